# Optimizing a Trainium2 kernel written in Bass

```python
import jax, jax.numpy as jnp
from jax import lax
import numpy as np

D_MODEL = 1024
BATCH = 2
SEQ = 8192
DEPTH = 1
DEC_BATCH = 128
DEC_SEQ = 1
PAST_LEN = 8192
PAGE_SIZE = 128

HEAD_DIM = 64
N_HG = 4
DIL_GROUPS = ((128, 1), (512, 4), (2048, 16))
N_DIL = len(DIL_GROUPS)
ATT_W = N_DIL * N_HG * HEAD_DIM
ATT_OUT = N_HG * HEAD_DIM
CHUNK = 128
N_GA = 4
GA_CH = 128
D_A = N_GA * GA_CH
D_FF = 2816
CONV_W = 3
QBLK = 128
EPS = 1e-6
IN_SPLITS = (ATT_W, ATT_W, ATT_W, D_A, D_A, D_MODEL, D_MODEL)
IN_COLS = sum(IN_SPLITS)

kernel_name = "gated_chunkgmlp_dilated_swa_convffn_step"


def rmsnorm(x, g):
    xf = x.astype(jnp.float32)
    y = xf * lax.rsqrt(jnp.mean(xf * xf, axis=-1, keepdims=True) + EPS)
    return (y * g.astype(jnp.float32)).astype(x.dtype)


def layernorm(x, g, b):
    xf = x.astype(jnp.float32)
    mu = jnp.mean(xf, axis=-1, keepdims=True)
    var = jnp.mean(jnp.square(xf - mu), axis=-1, keepdims=True)
    y = (xf - mu) * lax.rsqrt(var + EPS)
    return (y * g.astype(jnp.float32) + b.astype(jnp.float32)).astype(x.dtype)


def chunk_spatial_gating(u, va, ln_g, ln_b, w_spatial, b_spatial):
    u = jax.nn.gelu(u)
    vn = layernorm(jax.nn.gelu(va), ln_g, ln_b)
    B, T, _ = u.shape
    n_chunks = -(-T // CHUNK)
    Tp = n_chunks * CHUNK
    vp = jnp.pad(vn, ((0, 0), (0, Tp - T), (0, 0))).reshape(B, n_chunks, CHUNK, N_GA, GA_CH)
    causal = jnp.tril(jnp.ones((CHUNK, CHUNK), dtype=bool))
    w_m = jnp.where(causal[None], w_spatial, jnp.zeros((), w_spatial.dtype)).astype(vp.dtype)
    mixed = jnp.einsum('gts,bnsgc->bntgc', w_m, vp) + b_spatial.T[None, None, :, :, None].astype(vp.dtype)
    mixed = mixed.reshape(B, Tp, D_A)[:, :T]
    return u * mixed, vn


def token_mixing_inputs(x, norm_mix, w_in, ln_g, ln_b, w_spatial, b_spatial):
    xn = rmsnorm(x, norm_mix)
    z = xn @ w_in
    cuts, acc = [], 0
    for width in IN_SPLITS[:-1]:
        acc += width
        cuts.append(acc)
    q, k, v, u, va, ga, gb = jnp.split(z, cuts, axis=-1)
    B, T = x.shape[:2]
    q, k, v = (t.reshape(B, T, N_DIL, N_HG, HEAD_DIM) for t in (q, k, v))
    a_out, v_rows = chunk_spatial_gating(u, va, ln_g, ln_b, w_spatial, b_spatial)
    return a_out, v_rows, q, k, v, ga, gb


def dilated_band_attention(q, k, v, window, dilation):
    B, S, H, E = q.shape
    nk = window // dilation
    M = -(-S // dilation)
    nb = -(-M // QBLK)
    Mp = nb * QBLK
    Sp = Mp * dilation

    def to_strided(t):
        t = jnp.pad(t, ((0, 0), (0, Sp - S), (0, 0), (0, 0)))
        return t.reshape(B, Mp, dilation, H, E).transpose(0, 2, 1, 3, 4)

    def band_keys(t):
        tp = jnp.pad(t, ((0, 0), (0, 0), (QBLK, 0), (0, 0), (0, 0)))
        prev = tp[:, :, :Mp].reshape(B, dilation, nb, QBLK, H, E)
        cur = t.reshape(B, dilation, nb, QBLK, H, E)
        return jnp.concatenate([prev, cur], axis=3)

    qb = to_strided(q).reshape(B, dilation, nb, QBLK, H, E).astype(jnp.float32)
    kb = band_keys(to_strided(k)).astype(jnp.float32)
    vb = band_keys(to_strided(v)).astype(jnp.float32)
    s = jnp.einsum('bdnqhe,bdnkhe->bdnhqk', qb, kb) * (HEAD_DIM ** -0.5)
    qi = jnp.arange(QBLK)[:, None]
    ki = jnp.arange(2 * QBLK)[None, :]
    rel = qi + QBLK - ki
    blk = jnp.arange(nb)[:, None, None]
    valid = (rel >= 0) & (rel <= nk) & (blk * QBLK - QBLK + ki >= 0)
    s = jnp.where(valid[None, None, :, None], s, -jnp.inf)
    lse = jax.nn.logsumexp(s, axis=-1)
    p = jnp.exp(s - lse[..., None])
    o = jnp.einsum('bdnhqk,bdnkhe->bdnqhe', p, vb)
    o = o.reshape(B, dilation, Mp, H, E).transpose(0, 2, 1, 3, 4).reshape(B, Sp, H, E)[:, :S]
    lse = lse.transpose(0, 1, 2, 4, 3).reshape(B, dilation, Mp, H).transpose(0, 2, 1, 3).reshape(B, Sp, H)[:, :S]
    return o, lse


def dilated_step_attention(q, k_new, v_new, kv_buf, window, dilation):
    T = q.shape[1]
    L = kv_buf.shape[1]
    nk = window // dilation
    kv_all = jnp.concatenate([kv_buf.astype(q.dtype), jnp.stack([k_new, v_new], axis=2)], axis=1)
    idx = L + jnp.arange(T)[:, None] - dilation * jnp.arange(nk + 1)[None, :]
    valid = idx >= 0
    g = jnp.take(kv_all, jnp.maximum(idx, 0), axis=1).astype(jnp.float32)
    s = jnp.einsum('bthe,btjhe->bthj', q.astype(jnp.float32), g[:, :, :, 0]) * (HEAD_DIM ** -0.5)
    s = jnp.where(valid[None, :, None, :], s, -jnp.inf)
    lse = jax.nn.logsumexp(s, axis=-1)
    p = jnp.exp(s - lse[..., None])
    o = jnp.einsum('bthj,btjhe->bthe', p, g[:, :, :, 1])
    return o, lse


def combine_dilations(outs, lses, dtype):
    w = jax.nn.softmax(jnp.stack(lses), axis=0)
    o = jnp.sum(w[..., None] * jnp.stack(outs), axis=0)
    return o.reshape(o.shape[0], o.shape[1], ATT_OUT).astype(dtype)


def merge_and_ffn(x, a_out, b_out, ga, gb, conv_hist, w_proj_a, w_proj_b, w_out,
                  norm_ffn, w_up, conv_w, conv_b, w_down):
    merged = jax.nn.sigmoid(ga) * (a_out @ w_proj_a) + jax.nn.sigmoid(gb) * (b_out @ w_proj_b)
    h = x + merged @ w_out
    up = rmsnorm(h, norm_ffn) @ w_up
    T = up.shape[1]
    xp = jnp.concatenate([conv_hist.astype(up.dtype), up], axis=1)
    c = conv_b + conv_w[CONV_W - 1] * xp[:, CONV_W - 1:CONV_W - 1 + T]
    for i in range(CONV_W - 1):
        c = c + conv_w[i] * xp[:, i:i + T]
    gate, val = jnp.split(c, 2, axis=-1)
    h = h + (jax.nn.gelu(gate) * val) @ w_down
    return h, xp[:, -(CONV_W - 1):]


def setup_inputs(seed: int = 0) -> dict:
    key = jax.random.key(seed)
    ks = jax.random.split(key, 24)
    f32 = jnp.float32

    def nrm(k, shape, scale):
        return scale * jax.random.normal(k, shape, f32)

    cache_kv = [nrm(ks[2 + i], (DEPTH, DEC_BATCH, min(w, PAST_LEN), 2, N_HG, HEAD_DIM), 1.0)
                for i, (w, _) in enumerate(DIL_GROUPS)]
    return {
        "x_prompt": nrm(ks[0], (BATCH, SEQ, D_MODEL), 1.0),
        "x_sample": nrm(ks[1], (DEC_BATCH, DEC_SEQ, D_MODEL), 1.0),
        "cache_kv_w128": cache_kv[0],
        "cache_kv_w512": cache_kv[1],
        "cache_kv_w2048": cache_kv[2],
        "state_conv_ffn": nrm(ks[5], (DEPTH, DEC_BATCH, CONV_W - 1, 2 * D_FF), 1.0),
        "norm_mix": 1.0 + nrm(ks[6], (DEPTH, D_MODEL), 0.05),
        "w_in": nrm(ks[7], (DEPTH, D_MODEL, IN_COLS), D_MODEL ** -0.5),
        "ln_v_gain": 1.0 + nrm(ks[8], (DEPTH, D_A), 0.05),
        "ln_v_bias": nrm(ks[9], (DEPTH, D_A), 0.01),
        "w_spatial": nrm(ks[10], (DEPTH, N_GA, CHUNK, CHUNK), CHUNK ** -0.5),
        "b_spatial": 1.0 + nrm(ks[11], (DEPTH, N_GA, CHUNK), 0.1),
        "w_proj_a": nrm(ks[12], (DEPTH, D_A, D_MODEL), D_A ** -0.5),
        "w_proj_b": nrm(ks[13], (DEPTH, ATT_OUT, D_MODEL), ATT_OUT ** -0.5),
        "w_out": nrm(ks[14], (DEPTH, D_MODEL, D_MODEL), D_MODEL ** -0.5),
        "norm_ffn": 1.0 + nrm(ks[15], (DEPTH, D_MODEL), 0.05),
        "w_up": nrm(ks[16], (DEPTH, D_MODEL, 2 * D_FF), D_MODEL ** -0.5),
        "conv_w": nrm(ks[17], (DEPTH, CONV_W, 2 * D_FF), CONV_W ** -0.5),
        "conv_b": nrm(ks[18], (DEPTH, 2 * D_FF), 0.01),
        "w_down": nrm(ks[19], (DEPTH, D_FF, D_MODEL), D_FF ** -0.5),
        "norm_final": 1.0 + nrm(ks[20], (D_MODEL,), 0.05),
    }


def reference(x_prompt, x_sample, cache_kv_w128, cache_kv_w512, cache_kv_w2048, state_conv_ffn,
              norm_mix, w_in, ln_v_gain, ln_v_bias, w_spatial, b_spatial, w_proj_a, w_proj_b,
              w_out, norm_ffn, w_up, conv_w, conv_b, w_down, norm_final):
    S = x_prompt.shape[1]
    kv_caches = (cache_kv_w128, cache_kv_w512, cache_kv_w2048)
    h_p, h_s = x_prompt, x_sample
    kv_p_rows = [[] for _ in DIL_GROUPS]
    kv_s_rows = [[] for _ in DIL_GROUPS]
    v_p_rows, v_s_rows, conv_p_rows, conv_s_rows = [], [], [], []
    last_chunk_start = ((S - 1) // CHUNK) * CHUNK
    for l in range(DEPTH):
        a_p, vr_p, q, k, v, ga, gb = token_mixing_inputs(
            h_p, norm_mix[l], w_in[l], ln_v_gain[l], ln_v_bias[l], w_spatial[l], b_spatial[l])
        outs, lses = [], []
        for g, (win, dil) in enumerate(DIL_GROUPS):
            o, lse = dilated_band_attention(q[:, :, g], k[:, :, g], v[:, :, g], win, dil)
            outs.append(o)
            lses.append(lse)
            kv_p_rows[g].append(jnp.stack([k[:, :, g], v[:, :, g]], axis=2)[:, S - min(win, S):])
        b_p = combine_dilations(outs, lses, h_p.dtype)
        zero_hist = jnp.zeros((h_p.shape[0], CONV_W - 1, 2 * D_FF), h_p.dtype)
        h_p, conv_p = merge_and_ffn(h_p, a_p, b_p, ga, gb, zero_hist, w_proj_a[l], w_proj_b[l], w_out[l],
                                    norm_ffn[l], w_up[l], conv_w[l], conv_b[l], w_down[l])
        v_p_rows.append(vr_p[:, last_chunk_start:])
        conv_p_rows.append(conv_p)

        a_s, vr_s, q, k, v, ga, gb = token_mixing_inputs(
            h_s, norm_mix[l], w_in[l], ln_v_gain[l], ln_v_bias[l], w_spatial[l], b_spatial[l])
        outs, lses = [], []
        for g, (win, dil) in enumerate(DIL_GROUPS):
            o, lse = dilated_step_attention(q[:, :, g], k[:, :, g], v[:, :, g], kv_caches[g][l], win, dil)
            outs.append(o)
            lses.append(lse)
            kv_s_rows[g].append(jnp.stack([k[:, :, g], v[:, :, g]], axis=2))
        b_s = combine_dilations(outs, lses, h_s.dtype)
        h_s, conv_s = merge_and_ffn(h_s, a_s, b_s, ga, gb, state_conv_ffn[l], w_proj_a[l], w_proj_b[l],
                                    w_out[l], norm_ffn[l], w_up[l], conv_w[l], conv_b[l], w_down[l])
        v_s_rows.append(vr_s)
        conv_s_rows.append(conv_s)

    y_prompt = rmsnorm(h_p, norm_final)
    y_sample = rmsnorm(h_s, norm_final)
    kv_w128_prompt = jnp.stack(kv_p_rows[0])
    kv_w128_sample = jnp.stack(kv_s_rows[0])
    kv_w512_prompt = jnp.stack(kv_p_rows[1])
    kv_w512_sample = jnp.stack(kv_s_rows[1])
    kv_w2048_prompt = jnp.stack(kv_p_rows[2])
    kv_w2048_sample = jnp.stack(kv_s_rows[2])
    v_chunk_prompt = jnp.stack(v_p_rows)
    v_chunk_sample = jnp.stack(v_s_rows)
    conv_prompt = jnp.stack(conv_p_rows)
    conv_sample = jnp.stack(conv_s_rows)
    return (y_prompt, y_sample, kv_w128_prompt, kv_w128_sample, kv_w512_prompt, kv_w512_sample,
            kv_w2048_prompt, kv_w2048_sample, v_chunk_prompt, v_chunk_sample, conv_prompt, conv_sample)
```

```python
import numpy as np
from contextlib import ExitStack
import concourse.bass as bass
import concourse.mybir as mybir
from concourse.bass_utils import run_bass_kernel_spmd

F32 = mybir.dt.float32
BF16 = mybir.dt.bfloat16
AF = mybir.ActivationFunctionType
ALU = mybir.AluOpType

NCORES = 8
HX = 2176
NM = 2048
NS = 16
SM0 = NM
NCOL = NM + 2 + NS
EPS = 1e-6
WIN = ((128, 1), (512, 4), (2048, 16))
KBASE = (1920, 1536, 0)
KLEN = (2304, 2688, 4224)


class Buf:
    __slots__ = ("name", "w", "r")

    def __init__(self, name=""):
        self.name = name
        self.w = None
        self.r = {}


class Sched:
    ENG = ("pe", "act", "dve", "pool", "sp")

    def __init__(self, nc, stack):
        self.nc = nc
        self.stack = stack
        self.prog = {e: [] for e in self.ENG}
        self.sems = {}
        self.cnt = {}
        self.seen = {e: {} for e in self.ENG}
        self.label = ""
        for e in self.ENG:
            self._sem("E_" + e)

    def _sem(self, name):
        if name not in self.sems:
            self.sems[name] = self.stack.enter_context(self.nc.semaphore(name))
            self.cnt[name] = 0
        return name

    def _need(self, eng, waits, tok):
        if tok is None:
            return
        sem, val = tok
        if eng == "pe" and sem == "E_pe":
            return
        if self.seen[eng].get(sem, 0) >= val:
            return
        self.seen[eng][sem] = val
        waits.append((sem, val))

    def op(self, eng, fn, reads=(), writes=(), dma=None):
        waits = []
        for b in reads:
            self._need(eng, waits, b.w)
        for b in writes:
            self._need(eng, waits, b.w)
            for s, v in b.r.items():
                self._need(eng, waits, (s, v))
        if dma is not None:
            sem = self._sem(dma)
            inc = 16
        else:
            sem = "E_" + eng
            inc = 1
        self.cnt[sem] += inc
        tok = (sem, self.cnt[sem])
        self.prog[eng].append((waits, fn, (sem, inc), self.label + " r:" + ",".join(b.name for b in reads) + " w:" + ",".join(b.name for b in writes)))
        for b in reads:
            if b.r.get(sem, 0) < tok[1]:
                b.r[sem] = tok[1]
        for b in writes:
            b.w = tok
            b.r = {}
        return tok

    def barrier(self):
        allw = [(s, v) for s, v in self.cnt.items() if v > 0]
        for e in self.ENG:
            waits = []
            for t in allw:
                self._need(e, waits, t)
            self.prog[e].append((waits, None, None, ""))

    def emit(self):
        nc = self.nc
        with nc.Block() as block:
            def run(name, e):
                for waits, fn, inc, lab in self.prog[name]:
                    for s, v in waits:
                        e.wait_ge(self.sems[s], v)
                    if fn is not None:
                        with nc.named_scope(lab.split(" ")[0] or "none"):
                            ins = fn(e)
                        ins.then_inc(self.sems[inc[0]], inc[1])

            @block.tensor
            def _(e):
                run("pe", e)

            @block.scalar
            def _(e):
                run("act", e)

            @block.vector
            def _(e):
                run("dve", e)

            @block.gpsimd
            def _(e):
                run("pool", e)

            @block.sync
            def _(e):
                run("sp", e)


def build_program(stop=None):
    nc = bass.Bass("TRN2", target_bir_lowering=False)

    def din(name, shape):
        return nc.dram_tensor(name, list(shape), F32, kind="ExternalInput").ap()

    def dout(name, shape):
        return nc.dram_tensor(name, list(shape), F32, kind="ExternalOutput").ap()

    xall = din("xall", [HX + NM, 1024])
    xs = din("xs", [NS, 1024])
    ck = [din("ck0", [NS, 128, 512]), din("ck1", [NS, 512, 512]), din("ck2", [NS, 2048, 512])]
    sconv = din("sconv", [NS, 2, 5632])
    w_in = din("w_in", [1024, 5376])
    w_pa = din("w_pa", [512, 1024])
    w_pb = din("w_pb", [256, 1024])
    w_out = din("w_out", [1024, 1024])
    w_up = din("w_up", [1024, 5632])
    w_dn = din("w_dn", [2816, 1024])
    vec8 = din("vec8", [16, 128])
    nfin = din("nfin", [1024])
    lng = din("lng", [512])
    lnb = din("lnb", [512])
    cwb = din("cwb", [4, 44, 128])
    wsp = din("wsp", [4, 128, 128])
    bsp = din("bsp", [4, 128])
    w00 = din("w00", [4])
    cst = din("cst", [7, 128, 128])
    flagd = din("flagd", [128, 1])

    y = dout("y", [NM, 1024])
    ys = dout("ys", [NS, 1024])
    kvo = [dout("kv0", [128, 512]), dout("kv1", [512, 512]), dout("kv2", [2048, 512])]
    kvs = [dout("kvs0", [NS, 512]), dout("kvs1", [NS, 512]), dout("kvs2", [NS, 512])]
    vch = dout("vch", [128, 512])
    vchs = dout("vchs", [NS, 512])
    convp = dout("convp", [2, 5632])
    convs = dout("convs", [NS, 2, 5632])

    st = ExitStack()
    S = Sched(nc, st)
    NF = 53100
    arena = st.enter_context(nc.sbuf_tensor("arena", [128, NF], F32))
    psum_all = st.enter_context(nc.psum_tensor("psall", [128, 4096], F32))
    pbank = [psum_all[:, i * 512:(i + 1) * 512] for i in range(8)]
    PB = [Buf("pb%d" % i) for i in range(8)]
    bank_i = [0]

    def nextbank():
        i = bank_i[0] % 8
        bank_i[0] += 1
        return pbank[i], PB[i]

    def nextpair():
        if bank_i[0] % 2:
            bank_i[0] += 1
        i = bank_i[0] % 8
        bank_i[0] += 2
        return pbank[i], PB[i], pbank[i + 1], PB[i + 1], psum_all[:, i * 512:(i + 2) * 512]

    class Arena:
        def __init__(self):
            self.top = 0

        def f32(self, n):
            a = arena[:, self.top:self.top + n]
            self.top += n
            assert self.top <= NF, self.top
            return a

        def bf(self, n):
            n2 = (n + 1) // 2
            a = arena[:, self.top:self.top + n2].bitcast(BF16)
            self.top += n2
            assert self.top <= NF, self.top
            return a

    A = Arena()
    TT = [NF - 2200]

    def tmp_f32(n):
        a = arena[:, TT[0]:TT[0] + n]
        TT[0] += n
        assert TT[0] <= NF
        return a
    out_bufs = []
    uid = [0]

    def finalize():
        S.barrier()
        S.emit()
        st.close()
        return nc

    def dma(eng, out, in_, rd, wr, sem=None, **kw):
        if sem is None:
            uid[0] += 1
            sem = "d%d" % (uid[0] % 24)
        return S.op(eng, lambda e: e.dma_start(out=out, in_=in_, **kw), reads=rd, writes=wr, dma=sem)

    def mm(out, lhsT, rhs, start, stop, rd, wr):
        S.op("pe", lambda e: e.matmul(out, lhsT=lhsT, rhs=rhs, start=start, stop=stop), reads=rd, writes=wr)

    def act(out, in_, func, rd, wr, **kw):
        S.op("act", lambda e: e.activation(out=out, in_=in_, func=func, **kw), reads=rd, writes=wr)

    def dve(fn, rd, wr):
        S.op("dve", fn, reads=rd, writes=wr)

    def tcopy(eng, out, in_, rd, wr):
        S.op(eng, lambda e: e.tensor_copy(out=out, in_=in_), reads=rd, writes=wr)

    Bc = Buf("const")
    cst_f = A.f32(7 * 128).rearrange("p (k n) -> p k n", k=7)
    dma("sp", cst_f, cst.rearrange("k p n -> p k n"), [], [Bc], sem="c0")
    ident_f = cst_f[:, 0, :]
    blockones_f = cst_f[:, 5, :]
    cst_b = A.bf(7 * 128).rearrange("p (k n) -> p k n", k=7)
    tcopy("dve", cst_b, cst_f, [Bc], [Bc])
    ident_b = cst_b[:, 0, :]
    mprev_b = cst_b[:, 1, :]
    mcur_b = cst_b[:, 2, :]
    medge_b = cst_b[:, 3, :]
    ones_b = cst_b[:, 6, :]
    flag = A.f32(1)
    dma("sp", flag, flagd, [], [Bc], sem="c1")
    gfin_bc = A.f32(1024)
    dma("sp", gfin_bc, nfin.partition_broadcast(128), [], [Bc], sem="c2")
    lng_bc = A.f32(512)
    lnb_bc = A.f32(512)
    dma("sp", lng_bc, lng.partition_broadcast(128), [], [Bc], sem="c3")
    dma("sp", lnb_bc, lnb.partition_broadcast(128), [], [Bc], sem="c4")
    w00_bc = A.f32(4)
    dma("sp", w00_bc, w00.partition_broadcast(128), [], [Bc], sem="c5")
    v8_sb = tmp_f32(128)
    dma("sp", v8_sb[0:16, :], vec8, [], [Bc], sem="c6")
    cw_sb = tmp_f32(4 * 128).rearrange("p (k n) -> p k n", k=4)
    dma("sp", cw_sb[0:44, :, :], cwb.rearrange("k c p -> c k p"), [], [Bc], sem="c7")
    gvec = A.f32(16)
    cwT = A.f32(4 * 44).rearrange("p (k c) -> p k c", k=4)
    bk, Bk = nextbank()
    mm(bk[:, 0:16], v8_sb[0:16, :], ident_f[0:16, 0:16], True, True, [Bc], [Bk])
    tcopy("dve", gvec, bk[:, 0:16], [Bk], [Bc])
    bk, Bk = nextbank()
    for k in range(4):
        mm(bk[:, k * 44:(k + 1) * 44], cw_sb[0:44, k, :], ident_f[0:44, 0:44], True, True, [Bc], [Bk])
    tcopy("dve", cwT, bk[:, 0:176].rearrange("p (k c) -> p k c", k=4), [Bk], [Bc])
    ones_f = A.f32(128)
    S.op("dve", lambda e: e.memset(ones_f, 1.0), writes=[Bc])
    gm_bc = A.bf(8 * 128).rearrange("p (c n) -> p c n", c=8)
    gf_bc = A.bf(8 * 128).rearrange("p (c n) -> p c n", c=8)
    for c in range(8):
        dve(lambda e, c=c: e.tensor_scalar(out=gm_bc[:, c, :], in0=ones_f, scalar1=gvec[:, c:c + 1], scalar2=None, op0=ALU.mult), [Bc], [Bc])
        dve(lambda e, c=c: e.tensor_scalar(out=gf_bc[:, c, :], in0=ones_f, scalar1=gvec[:, 8 + c:9 + c], scalar2=None, op0=ALU.mult), [Bc], [Bc])
    wsp_f = tmp_f32(512).rearrange("p (g n) -> p g n", g=4)
    dma("sp", wsp_f, wsp.rearrange("g t s -> t g s"), [], [Bc], sem="c8")
    wsp_b = tmp_f32(256).bitcast(BF16).rearrange("p (g n) -> p g n", g=4)
    for g in range(4):
        dve(lambda e, g=g: e.tensor_tensor(out=wsp_b[:, g, :], in0=wsp_f[:, g, :], in1=cst_f[:, 1, :], op=ALU.mult), [Bc], [Bc])
    WmT = A.bf(512).rearrange("p (g n) -> p g n", g=4)
    bk, Bk = nextbank()
    bkb = bk.bitcast(BF16).rearrange("p (g n) -> p g n", g=8)
    for g in range(4):
        S.op("pe", lambda e, g=g: e.transpose(out=bkb[:, g, :], in_=wsp_b[:, g, :], identity=ident_b), reads=[Bc], writes=[Bk])
    tcopy("dve", WmT, bkb[:, 0:4, :], [Bk], [Bc])
    bsp_f = tmp_f32(512)
    dma("sp", bsp_f[0:1, :], bsp.rearrange("(o g) t -> o (g t)", o=1), [], [Bc], sem="c9")
    bsp_b = A.bf(512)
    tcopy("dve", bsp_b[0:1, :], bsp_f[0:1, :], [Bc], [Bc])
    bsp0_f = A.f32(4 * 16)
    bsp0_b = A.bf(4 * 16)
    for g in range(4):
        dve(lambda e, g=g: e.tensor_scalar(out=bsp0_f[0:1, g * 16:(g + 1) * 16], in0=ones_f[0:1, 0:16], scalar1=bsp_f[0:1, g * 128:g * 128 + 1], scalar2=None, op0=ALU.mult), [Bc], [Bc])
    tcopy("dve", bsp0_b[0:1, :], bsp0_f[0:1, :], [Bc], [Bc])
    D16 = A.bf(4 * 16).rearrange("p (g n) -> p g n", g=4)
    for g in range(4):
        dve(lambda e, g=g: e.tensor_scalar(out=D16[0:16, g, :], in0=ident_f[0:16, 0:16], scalar1=w00_bc[0:16, g:g + 1], scalar2=None, op0=ALU.mult), [Bc], [Bc])
    stat = A.f32(8 * 8).rearrange("p (s n) -> p s n", s=8)
    Bstat = [Buf("st%d" % i) for i in range(8)]
    stat_i = [0]
    CONST_TOP = A.top
    print('CONST_TOP', CONST_TOP)

    if stop == 'const':
        return finalize()
    def rstd_of(ssq_ap, n, inv_n, sti, Bs):
        s_ = stat[:n, sti, :]
        dve(lambda e: e.tensor_scalar(out=s_[:, 1:2], in0=ssq_ap, scalar1=inv_n, scalar2=EPS, op0=ALU.mult, op1=ALU.add), [Bs], [Bs])
        act(s_[:, 2:3], s_[:, 1:2], AF.Ln, [Bs], [Bs])
        act(s_[:, 3:4], s_[:, 2:3], AF.Exp, [Bs], [Bs], scale=-0.5)
        return s_[:, 3:4]

    def norm_rows(xt_ap, n, Bx, out_bf, Bo):
        sti = stat_i[0] % 8
        stat_i[0] += 1
        Bs = Bstat[sti]
        act(out_bf, xt_ap, AF.Square, [Bx], [Bo, Bs], accum_out=stat[:n, sti, 0:1])
        r = rstd_of(stat[:n, sti, 0:1], n, 1.0 / 1024, sti, Bs)
        act(out_bf, xt_ap, AF.Copy, [Bx, Bs], [Bo], scale=r)

    def transpose_rows(src_bf, n, Bsrc, dstT, Bdst, g_bc):
        bk, Bk = nextbank()
        pt = bk.bitcast(BF16).rearrange("p (c t) -> p c t", c=8)
        for c in range(8):
            S.op("pe", lambda e, c=c: e.transpose(out=pt[:, c, 0:n], in_=src_bf[0:n, c * 128:(c + 1) * 128], identity=ident_b[0:n, 0:n]), reads=[Bsrc, Bc], writes=[Bk])
        dve(lambda e: e.tensor_tensor(out=dstT, in0=pt[:, :, 0:n], in1=g_bc[:, :, 0:n], op=ALU.mult), [Bk, Bc], [Bdst])

    xnT = A.bf(8 * HX).rearrange("p (c n) -> p c n", c=8)
    BxnT = Buf("xnT")
    xeT = A.bf(8 * 128).rearrange("p (c n) -> p c n", c=8)
    BxeT = Buf("xeT")
    P1 = A.top
    QT = A.bf(6 * NCOL).rearrange("p (c n) -> p c n", c=6)
    BQT = Buf("QT")
    KT = [A.bf(2 * KLEN[g]).rearrange("p (c n) -> p c n", c=2) for g in range(3)]
    BKT = [Buf("KT%d" % g) for g in range(3)]
    KTs = A.bf(6 * 18).rearrange("p (c n) -> p c n", c=6)
    VTs = A.bf(6 * 18).rearrange("p (c n) -> p c n", c=6)
    BKTs = Buf("KTs")
    NVB = 79
    Vb = A.bf(NVB * 256).rearrange("p (b n) -> p b n", b=NVB)
    BV = [Buf("V%d" % i) for i in range(NVB + 3)]
    vidx = {}
    PW = A.top
    wqkv = A.bf(8 * 2304).rearrange("p (c n) -> p c n", c=8)
    Bw = Buf("wqkv")
    NKV = 5
    kvst = [A.f32(512) for _ in range(NKV)]
    Bkvst = [Buf("kvst%d" % i) for i in range(NKV)]
    kvst_i = [0]
    PB_TOP = A.top

    dma("pool", wqkv, w_in.rearrange("(c p) n -> p c n", p=128)[:, :, 0:2304], [], [Bw], sem="w0")

    def wcol(kind, g, c):
        return kind * 768 + g * 256 + c * 128

    def norm_T_batch(items, xts, Bxts, xbs, Bxbs, semp):
        G = len(xts)
        for g0 in range(0, len(items), G):
            grp = items[g0:g0 + G]
            stis = []
            srcs = []
            for k, (src_rows, n, dstT, Bdst, g_bc, pre) in enumerate(grp):
                if src_rows is not None:
                    dma("sp", xts[k][0:n, :], src_rows, [], [Bxts[k]], sem="%s%d" % (semp, k))
                srcs.append((xts[k], Bxts[k]))
            for k, (src_rows, n, dstT, Bdst, g_bc, pre) in enumerate(grp):
                if pre is not None:
                    srcs[k] = pre(k)
            for k, (src_rows, n, dstT, Bdst, g_bc, pre) in enumerate(grp):
                sti = stat_i[0] % 8
                stat_i[0] += 1
                stis.append(sti)
                act(xbs[k][0:n, :], srcs[k][0][0:n, :], AF.Square, [srcs[k][1]], [Bxbs[k], Bstat[sti]], accum_out=stat[:n, sti, 0:1])
            for k, (src_rows, n, dstT, Bdst, g_bc, pre) in enumerate(grp):
                s_ = stat[:n, stis[k], :]
                dve(lambda e, s_=s_: e.tensor_scalar(out=s_[:, 1:2], in0=s_[:, 0:1], scalar1=1.0 / 1024, scalar2=EPS, op0=ALU.mult, op1=ALU.add), [Bstat[stis[k]]], [Bstat[stis[k]]])
            for k, (src_rows, n, dstT, Bdst, g_bc, pre) in enumerate(grp):
                s_ = stat[:n, stis[k], :]
                act(s_[:, 2:3], s_[:, 1:2], AF.Ln, [Bstat[stis[k]]], [Bstat[stis[k]]])
            for k, (src_rows, n, dstT, Bdst, g_bc, pre) in enumerate(grp):
                s_ = stat[:n, stis[k], :]
                act(s_[:, 3:4], s_[:, 2:3], AF.Exp, [Bstat[stis[k]]], [Bstat[stis[k]]], scale=-0.5)
            for k, (src_rows, n, dstT, Bdst, g_bc, pre) in enumerate(grp):
                s_ = stat[:n, stis[k], :]
                act(xbs[k][0:n, :], srcs[k][0][0:n, :], AF.Copy, [srcs[k][1], Bstat[stis[k]]], [Bxbs[k]], scale=s_[:, 3:4])
            for k, (src_rows, n, dstT, Bdst, g_bc, pre) in enumerate(grp):
                transpose_rows(xbs[k], n, Bxbs[k], dstT, Bdst, g_bc)

    def proj_fm(dst, Bdst, wt, Bwt, col0, src, Bsrc, c0, n, nk=8, func=AF.Copy, **kw):
        bk, Bk = nextbank()
        for kc in range(nk):
            mm(bk[:, 0:n], wt[:, kc, col0:col0 + 128], src[:, kc, c0:c0 + n], kc == 0, kc == nk - 1, [Bwt, Bsrc], [Bk])
        act(dst, bk[:, 0:n], func, [Bk], [Bdst], **kw)

    def vblock(g, start, step, n, src, Bsrc):
        idx = len(vidx)
        vidx[(g, start, step, "m" if src is xnT_main_marker[0] else "h")] = idx
        bk, Bk = nextbank()
        for kc in range(8):
            mm(bk[0:n, 0:256], src[:, kc, start:start + step * (n - 1) + 1:step], wqkv[:, kc, wcol(2, g, 0):wcol(2, g, 0) + 256], kc == 0, kc == 7, [Bsrc, Bw], [Bk])
        tcopy("dve", Vb[0:n, idx, :], bk[0:n, 0:256], [Bk], [BV[idx]])
        return idx

    xnT_main_marker = [None]

    S.label = 'B1'
    QT_f32 = arena[:, P1:P1 + 6144]
    xtA = [QT_f32[:, k * 1024:(k + 1) * 1024] for k in range(4)]
    xbA = [QT_f32[:, 4096 + k * 512:4096 + (k + 1) * 512].bitcast(BF16) for k in range(4)]
    BxtA = [Buf("xtA%d" % k) for k in range(4)]
    BxbA = [Buf("xbA%d" % k) for k in range(4)]
    norm_T_batch([(xall[t * 128:(t + 1) * 128, :], 128, xnT[:, :, t * 128:(t + 1) * 128], BxnT, gm_bc, None) for t in range(17)],
                 xtA, BxtA, xbA, BxbA, "xa")
    if stop == 'B1a':
        return finalize()
    for g in range(3):
        lt = KBASE[g]
        while lt < HX:
            n = min(512, HX - lt)
            for c in range(2):
                proj_fm(KT[g][:, c, lt - KBASE[g]:lt - KBASE[g] + n], BKT[g], wqkv, Bw, wcol(1, g, c), xnT, BxnT, lt, n)
            lt += n
    if stop == 'B1k':
        return finalize()
    for r in range(16):
        vblock(2, 128 + r, 16, 128, xnT, BxnT)
    for r in range(4):
        vblock(1, 1664 + r, 4, 128, xnT, BxnT)
    vblock(0, 2048, 1, 128, xnT, BxnT)
    vblock(2, 126, 16, 128, xnT, BxnT)
    vblock(2, 127, 16, 128, xnT, BxnT)
    vblock(1, 1662, 4, 128, xnT, BxnT)
    vblock(1, 1663, 4, 128, xnT, BxnT)
    vblock(0, 2046, 1, 128, xnT, BxnT)
    vblock(2, 2174, 1, 1, xnT, BxnT)
    vblock(2, 2175, 1, 1, xnT, BxnT)
    vblock(1, 2174, 1, 1, xnT, BxnT)
    vblock(1, 2175, 1, 1, xnT, BxnT)
    vblock(0, 2174, 1, 2, xnT, BxnT)
    tcopy("dve", xeT, xnT[:, :, 2048:2176], [BxnT], [BxeT])

    if stop == 'B1':
        return finalize()
    S.label = 'B2'
    xnT_main_marker[0] = xnT
    S.barrier()
    VB0 = PW - NVB * 128 + 31 * 128
    Vm_f32 = arena[:, VB0:VB0 + 6144]
    xtB = [Vm_f32[:, k * 1024:(k + 1) * 1024] for k in range(4)]
    xbB = [Vm_f32[:, 4096 + k * 512:4096 + (k + 1) * 512].bitcast(BF16) for k in range(4)]
    BxtB = [Buf("xtB%d" % k) for k in range(4)]
    BxbB = [Buf("xbB%d" % k) for k in range(4)]
    norm_T_batch([(xall[HX + t * 128:HX + (t + 1) * 128, :], 128, xnT[:, :, t * 128:(t + 1) * 128], BxnT, gm_bc, None) for t in range(16)],
                 xtB, BxtB, xbB, BxbB, "xb")
    tcopy("dve", xnT[:, :, SM0:SM0 + 2], xeT[:, :, 126:128], [BxeT], [BxnT])
    norm_T_batch([(xs, NS, xnT[:, :, SM0 + 2:SM0 + 2 + NS], BxnT, gm_bc, None)], xtB, BxtB, xbB, BxbB, "xb")
    slices = [(i * 512, 512) for i in range(4)] + [(SM0, 18)]
    if stop == 'B2a':
        return finalize()
    for (c0, n) in slices:
        for gc in range(6):
            proj_fm(QT[:, gc, c0:c0 + n], BQT, wqkv, Bw, wcol(0, gc // 2, gc % 2), xnT, BxnT, c0, n)
    for (c0, n) in slices[:4]:
        for g in range(3):
            for c in range(2):
                kc0 = HX + c0 - KBASE[g]
                proj_fm(KT[g][:, c, kc0:kc0 + n], BKT[g], wqkv, Bw, wcol(1, g, c), xnT, BxnT, c0, n)
    for gc in range(6):
        proj_fm(KTs[:, gc, :], BKTs, wqkv, Bw, wcol(1, gc // 2, gc % 2), xnT, BxnT, SM0, 18)
        proj_fm(VTs[:, gc, :], BKTs, wqkv, Bw, wcol(2, gc // 2, gc % 2), xnT, BxnT, SM0, 18)
    if stop == 'B2q':
        return finalize()
    assert len(vidx) == 31, len(vidx)
    S.barrier()
    for t in range(16):
        vblock(0, t * 128, 1, 128, xnT, BxnT)
    for i in range(4):
        for r in range(4):
            vblock(1, 512 * i + r, 4, 128, xnT, BxnT)
    for r in range(16):
        vblock(2, r, 16, 128, xnT, BxnT)

    if stop == 'B2v':
        return finalize()
    S.label = 'kvtok'
    def kv_tok(col0, n, g, dst_rows):
        i = kvst_i[0] % NKV
        kvst_i[0] += 1
        bk, Bk = nextbank()
        for half in range(2):
            for kc in range(8):
                mm(bk[0:n, half * 256:(half + 1) * 256], xnT[:, kc, col0:col0 + n], wqkv[:, kc, wcol(1 + half, g, 0):wcol(1 + half, g, 0) + 256], kc == 0, kc == 7, [BxnT, Bw], [Bk])
        tcopy("dve", kvst[i][0:n, :], bk[0:n, :], [Bk], [Bkvst[i]])
        dma("sp" if n == 128 else "pool", dst_rows, kvst[i][0:n, :], [Bkvst[i]], [], sem=("ko%d" if n == 128 else "kp%d") % i)
        out_bufs.append(Bkvst[i])

    for t in range(16):
        kv_tok(t * 128, 128, 2, kvo[2][t * 128:(t + 1) * 128, :])
    if stop == 'kv1':
        return finalize()
    for t in range(12, 16):
        kv_tok(t * 128, 128, 1, kvo[1][(t - 12) * 128:(t - 11) * 128, :])
    kv_tok(15 * 128, 128, 0, kvo[0])
    if stop == 'kv2':
        return finalize()
    for g in range(3):
        kv_tok(SM0 + 2, NS, g, kvs[g])

    if stop == 'B':
        return finalize()
    S.barrier()
    A.top = PW
    ACC0 = A.top
    acc_n = A.f32(2 * NCOL).rearrange("p (c n) -> p c n", c=2)
    acc_d = A.f32(2 * NCOL).rearrange("p (c n) -> p c n", c=2)
    acc_all = arena[:, ACC0:ACC0 + 4 * NCOL].rearrange("p (x c n) -> p x c n", x=2, c=2)
    NPT = 2
    PTb = [A.bf(1024).rearrange("p (h k q) -> p h k q", h=4, k=2) for _ in range(NPT)]
    BPT = [Buf("PT%d" % i) for i in range(NPT)]
    pt_i = [0]
    BOUT0 = A.top
    boutT = A.bf(2 * NCOL).rearrange("p (c n) -> p c n", c=2)
    BboutT = Buf("boutT")
    ATT_TOP = A.top
    A.top = BOUT0
    ckb = [[A.bf(512) for _ in range(3)] for _ in range(2)]
    Bckb = [[Buf("ckb%d%d" % (i, g)) for g in range(3)] for i in range(2)]
    KTc = [A.bf(6 * 128).rearrange("p (c n) -> p c n", c=6) for _ in range(2)]
    BKTc = [Buf("KTc%d" % i) for i in range(2)]
    PTs = [A.bf(16) for _ in range(2)]
    BPTs = [Buf("PTs%d" % i) for i in range(2)]
    prodf = A.f32(6 * 16).rearrange("p (c n) -> p c n", c=6)
    pself = A.f32(6 * 16).rearrange("p (c n) -> p c n", c=6)
    Bpr = Buf("prod")
    assert A.top <= NF, A.top
    acc_hist = {0: [], 1: [], 2: [], "x": [Buf("accx")], "s": []}
    mpair_main = cst_b[:, 1:3, :]
    mpair_edge = cst_b[:, 3:5, :]

    def acc_update(pair, Bn, Bd, nq, cols, first, key):
        if key == "x":
            rd_prev, wr = acc_hist["x"], acc_hist["x"]
        else:
            nb = Buf("acc%s" % str(key))
            rd_prev = [] if (key == "s" or key == 0) else acc_hist[key - 1]
            wr = [nb]
            acc_hist[key].append(nb)
        p4 = pair.rearrange("p (x h q) -> p x h q", x=2, h=4)
        for h2 in range(2):
            i_ap = p4[h2 * 64:(h2 + 1) * 64, :, h2::2, 0:nq]
            o_ap = acc_all[h2 * 64:(h2 + 1) * 64, :, :, cols]
            if first:
                dve(lambda e, i_ap=i_ap, o_ap=o_ap: e.tensor_copy(out=o_ap, in_=i_ap), [Bn, Bd] + rd_prev, wr)
            else:
                dve(lambda e, i_ap=i_ap, o_ap=o_ap: e.tensor_tensor(out=o_ap, in0=i_ap, in1=o_ap, op=ALU.add), [Bn, Bd] + rd_prev, wr)

    def band_p1(g, qsrc, Bq, qcols, nq, chunks, mpair):
        pi = pt_i[0] % NPT
        pt_i[0] += 1
        PT, Bp = PTb[pi], BPT[pi]
        b0, B0 = nextbank()
        b1, B1 = nextbank()
        sb = [b0, b1]
        SBf = [B0, B1]
        for h in range(4):
            c, h2 = h // 2, h % 2
            rows = slice(h2 * 64, (h2 + 1) * 64)
            for ci, (Kap, BK, vi, nk, mask) in enumerate(chunks):
                o = sb[h2][0:nk, (c * 2 + ci) * 128:(c * 2 + ci) * 128 + nq]
                mm(o, Kap[rows, c, :], qsrc[rows, g * 2 + c, qcols], True, True, [BK, Bq], [SBf[h2]])
        full = (nq == 128 and len(chunks) == 2 and all(ch[3] == 128 for ch in chunks))
        if full:
            for h2 in range(2):
                src = sb[h2].rearrange("p (h k q) -> p h k q", h=2, k=2)
                act(PT[:, h2::2, :, :], src, AF.Exp, [SBf[h2]], [Bp], scale=0.125)
            mb = mpair.unsqueeze(1).to_broadcast([128, 4, 2, 128])
            dve(lambda e, PT=PT, mb=mb: e.tensor_tensor(out=PT, in0=PT, in1=mb, op=ALU.mult), [Bp, Bc], [Bp])
        else:
            for ci, (Kap, BK, vi, nk, mask) in enumerate(chunks):
                for h2 in range(2):
                    src = sb[h2][0:nk, :].rearrange("p (h k q) -> p h k q", h=2, k=2)[:, :, ci, 0:nq]
                    act(PT[0:nk, h2::2, ci, 0:nq], src, AF.Exp, [SBf[h2]], [Bp], scale=0.125)
                if mask is not None:
                    for h in range(4):
                        dve(lambda e, h=h, ci=ci, nk=nk, mask=mask, PT=PT: e.tensor_tensor(out=PT[0:nk, h, ci, 0:nq], in0=PT[0:nk, h, ci, 0:nq], in1=mask[0:nk, 0:nq], op=ALU.mult), [Bp, Bc], [Bp])
        return PT, Bp

    def band_p2(PT, Bp, nq, chunks, acc_cols, first, key):
        nch = len(chunks)
        bn, Bn, bd, Bd, pair = nextpair()
        for h in range(4):
            c = h // 2
            for ci, (Kap, BK, vi, nk, mask) in enumerate(chunks):
                mm(bn[:, h * 128:h * 128 + nq], Vb[0:nk, vi, c * 128:(c + 1) * 128], PT[0:nk, h, ci, 0:nq], ci == 0, ci == nch - 1, [BV[vi], Bp], [Bn])
            for ci, (Kap, BK, vi, nk, mask) in enumerate(chunks):
                mm(bd[:, h * 128:h * 128 + nq], ones_b[0:nk, :], PT[0:nk, h, ci, 0:nq], ci == 0, ci == nch - 1, [Bc, Bp], [Bd])
        acc_update(pair, Bn, Bd, nq, acc_cols, first, key)

    def kslice(g, lt0, step, n):
        a = lt0 - KBASE[g]
        return KT[g][:, :, a:a + step * (n - 1) + 1:step]

    def samp_s0(b):
        sl = b % 2
        for g in range(3):
            L, d = WIN[g]
            dma("pool", ckb[sl][g], ck[g][b, 0:L:d, :], [], [Bckb[sl][g]], sem="ck%d%d" % (sl, g))

    def samp_s1(b):
        sl = b % 2
        bk, Bk = nextbank()
        pt = bk.bitcast(BF16).rearrange("p (c t) -> p c t", c=8)
        for g in range(3):
            for c in range(2):
                S.op("pe", lambda e, c=c, g=g, pt=pt, sl=sl: e.transpose(out=pt[:, g * 2 + c, :], in_=ckb[sl][g][:, c * 128:(c + 1) * 128], identity=ident_b), reads=[Bckb[sl][g], Bc], writes=[Bk])
        tcopy("dve", KTc[sl], pt[:, 0:6, :], [Bk], [BKTc[sl]])

    def samp_s2(b):
        sl = b % 2
        col = SM0 + 2 + b
        bs0, BS0 = nextbank()
        bs1, BS1 = nextbank()
        bsx = [bs0, bs1]
        BSx = [BS0, BS1]
        for g in range(3):
            for h in range(4):
                c, h2 = h // 2, h % 2
                rows = slice(h2 * 64, (h2 + 1) * 64)
                mm(bsx[h2][:, g * 2 + c:g * 2 + c + 1], KTc[sl][rows, g * 2 + c, :], QT[rows, g * 2 + c, col:col + 1], True, True, [BKTc[sl], BQT], [BSx[h2]])
        PTs3 = PTs[sl][:, 0:12].rearrange("p (g c t) -> p g c t", g=3, c=2)
        for h2 in range(2):
            act(PTs3[:, :, :, h2], bsx[h2][:, 0:6].rearrange("p (g c) -> p g c", g=3), AF.Exp, [BSx[h2]], [BPTs[sl]], scale=0.125)

    def samp_s3(b):
        sl = b % 2
        col = SM0 + 2 + b
        bn, Bn, bd, Bd, pair = nextpair()
        for h in range(4):
            c = h // 2
            for g in range(3):
                mm(bn[:, h * 128:h * 128 + 1], ckb[sl][g][:, 256 + c * 128:256 + (c + 1) * 128], PTs[sl][:, g * 4 + h:g * 4 + h + 1], g == 0, g == 2, [Bckb[sl][g], BPTs[sl]], [Bn])
            for g in range(3):
                mm(bd[:, h * 128:h * 128 + 1], ones_b, PTs[sl][:, g * 4 + h:g * 4 + h + 1], g == 0, g == 2, [Bc, BPTs[sl]], [Bd])
        acc_update(pair, Bn, Bd, 1, slice(col, col + 1), True, "s")

    S.label = 'att-main'
    mblocks = []
    for g in range(3):
        step = WIN[g][1]
        if g == 0:
            starts = [t * 128 for t in range(16)]
        elif g == 1:
            starts = [512 * i + r for i in range(4) for r in range(4)]
        else:
            starts = list(range(16))
        for m0 in starts:
            lt_q = HX + m0
            lt_p = lt_q - 128 * step
            if lt_p < HX:
                vp = vidx[(g, lt_p, step, "h")]
                mp = mpair_edge
            else:
                vp = vidx[(g, lt_p - HX, step, "m")]
                mp = mpair_main
            vc = vidx[(g, m0, step, "m")]
            chunks = [(kslice(g, lt_p, step, 128), BKT[g], vp, 128, None),
                      (kslice(g, lt_q, step, 128), BKT[g], vc, 128, None)]
            qc = slice(m0, m0 + step * 127 + 1, step)
            mblocks.append((g, qc, chunks, mp))
    samp_s0(0)
    pend = band_p1(mblocks[0][0], QT, BQT, mblocks[0][1], 128, mblocks[0][2], mblocks[0][3])
    for m, (g, qc, chunks, mp) in enumerate(mblocks):
        nxt = None
        if m + 1 < len(mblocks):
            g2_, qc2, ch2, mp2 = mblocks[m + 1]
            nxt = band_p1(g2_, QT, BQT, qc2, 128, ch2, mp2)
        band_p2(pend[0], pend[1], 128, chunks, qc, g == 0, g)
        pend = nxt
        b, k = m // 3, m % 3
        if k == 0:
            samp_s1(b)
            if b + 1 < NS:
                samp_s0(b + 1)
        elif k == 1:
            samp_s2(b)
        else:
            samp_s3(b)
    if stop == 'att-main':
        return finalize()
    S.label = 'att-ext2'
    vp = vidx[(0, 2046, 1, "h")]
    vc = vidx[(0, 2174, 1, "h")]
    chx = [(kslice(0, 2046, 1, 128), BKT[0], vp, 128, mprev_b), (kslice(0, 2174, 1, 2), BKT[0], vc, 2, mcur_b)]
    p_ = band_p1(0, QT, BQT, slice(SM0, SM0 + 2), 2, chx, None)
    band_p2(p_[0], p_[1], 2, chx, slice(SM0, SM0 + 2), True, "x")
    for g in (1, 2):
        step = WIN[g][1]
        for j in range(2):
            ltq = 2174 + j
            vp = vidx[(g, ltq - 128 * step, step, "h")]
            vc = vidx[(g, ltq, 1, "h")]
            chx = [(kslice(g, ltq - 128 * step, step, 128), BKT[g], vp, 128, None), (kslice(g, ltq, 1, 1), BKT[g], vc, 1, None)]
            p_ = band_p1(g, QT, BQT, slice(SM0 + j, SM0 + j + 1), 1, chx, None)
            band_p2(p_[0], p_[1], 1, chx, slice(SM0 + j, SM0 + j + 1), False, "x")
    Bacc = Buf("accall")
    S.op("dve", lambda e: e.memset(prodf[:, 0, 0:1], 0.0), reads=[b_ for k_ in acc_hist for b_ in acc_hist[k_]], writes=[Bacc, Bpr])
    S.label = 'att-self'
    dve(lambda e: e.tensor_tensor(out=prodf, in0=QT[:, :, SM0 + 2:SM0 + 18], in1=KTs[:, :, 2:18], op=ALU.mult), [BQT, BKTs], [Bpr])
    bk, Bk = nextbank()
    mm(bk[:, 0:96], blockones_f, prodf.rearrange("p c n -> p (c n)"), True, True, [Bc, Bpr], [Bk])
    act(pself.rearrange("p c n -> p (c n)"), bk[:, 0:96], AF.Exp, [Bk], [Bpr], scale=0.125)
    dve(lambda e: e.tensor_tensor(out=prodf, in0=pself, in1=VTs[:, :, 2:18], op=ALU.mult), [Bpr, BKTs], [Bpr])
    for g in range(3):
        for c in range(2):
            dve(lambda e, g=g, c=c: e.tensor_tensor(out=acc_n[:, c, SM0 + 2:SM0 + 18], in0=acc_n[:, c, SM0 + 2:SM0 + 18], in1=prodf[:, g * 2 + c, :], op=ALU.add), [Bacc, Bpr], [Bacc])
            dve(lambda e, g=g, c=c: e.tensor_tensor(out=acc_d[:, c, SM0 + 2:SM0 + 18], in0=acc_d[:, c, SM0 + 2:SM0 + 18], in1=pself[:, g * 2 + c, :], op=ALU.add), [Bacc, Bpr], [Bacc])
    if stop == 'att-self':
        return finalize()
    S.barrier()
    S.label = 'att-norm'
    for c in range(2):
        for (c0, n) in slices:
            dve(lambda e, c=c, c0=c0, n=n: e.reciprocal(out=acc_d[:, c, c0:c0 + n], in_=acc_d[:, c, c0:c0 + n]), [Bacc], [Bacc])
            dve(lambda e, c=c, c0=c0, n=n: e.tensor_tensor(out=boutT[:, c, c0:c0 + n], in0=acc_n[:, c, c0:c0 + n], in1=acc_d[:, c, c0:c0 + n], op=ALU.mult), [Bacc], [BboutT])

    if stop == 'att':
        return finalize()
    S.label = 'C1'
    S.barrier()
    A.top = P1
    boutT2 = A.bf(2 * NCOL).rearrange("p (c n) -> p c n", c=2)
    assert A.top <= PW
    tcopy("dve", boutT2, boutT, [BboutT], [BboutT])
    S.barrier()
    aoutT = A.bf(4 * NCOL).rearrange("p (c n) -> p c n", c=4)
    BaoutT = Buf("aoutT")
    C_TOP = A.top
    wuv = A.bf(8 * 1024).rearrange("p (c n) -> p c n", c=8)
    Bwuv = Buf("wuv")
    wv_in = w_in.rearrange("(c p) n -> p c n", p=128)
    dma("pool", wuv, wv_in[:, :, 2304:3328], [], [Bwuv], sem="w1")
    MT0 = NF - 4 * NCOL
    W2A = MT0 - (8192 + 2048 + 1024)
    wg = arena[:, W2A:W2A + 8192].bitcast(BF16).rearrange("p (c n) -> p c n", c=8)
    wpa = arena[:, W2A + 8192:W2A + 10240].bitcast(BF16).rearrange("p (c n) -> p c n", c=4)
    wpb = arena[:, W2A + 10240:W2A + 11264].bitcast(BF16).rearrange("p (c n) -> p c n", c=2)
    Bwg = Buf("wg")
    dma("pool", wg, wv_in[:, :, 3328:5376], [], [Bwg], sem="w2")
    dma("pool", wpa, w_pa.rearrange("(c p) n -> p c n", p=128), [], [Bwg], sem="w3")
    dma("pool", wpb, w_pb.rearrange("(c p) n -> p c n", p=128), [], [Bwg], sem="w4")
    uT = A.bf(4 * NCOL).rearrange("p (c n) -> p c n", c=4)
    BuT = Buf("uT")
    NG = 3
    gv = [A.f32(512) for _ in range(NG)]
    Bgv = [Buf("gv%d" % i) for i in range(NG)]
    vn = [A.f32(512) for _ in range(NG)]
    Bvn = [Buf("vn%d" % i) for i in range(NG)]
    vnb = [A.bf(512) for _ in range(NG)]
    Bvnb = [Buf("vnb%d" % i) for i in range(NG)]
    uxe = A.bf(4 * 128).rearrange("p (c n) -> p c n", c=4)
    aoe = A.bf(4 * 128).rearrange("p (c n) -> p c n", c=4)
    Buxe = Buf("uxe")
    C1_TOP = A.top
    for (c0, n) in slices:
        for c in range(4):
            proj_fm(uT[:, c, c0:c0 + n], BuT, wuv, Bwuv, c * 128, xnT, BxnT, c0, n, func=AF.Gelu_apprx_tanh)
    for c in range(4):
        proj_fm(uxe[:, c, :], Buxe, wuv, Bwuv, c * 128, xeT, BxeT, 0, 128, func=AF.Gelu_apprx_tanh)
    gi = [0]

    def gmlp_batch(items):
        G = len(gv)
        for g0 in range(0, len(items), G):
            grp = items[g0:g0 + G]
            stis, banks = [], []
            for k, (src, Bsrc, c0, n, sample, u_ap, Bu, out_ap, Bout, vn_dst) in enumerate(grp):
                sti = stat_i[0] % 8
                stat_i[0] += 1
                stis.append(sti)
                bk, Bk = nextbank()
                banks.append((bk, Bk))
                for kc in range(8):
                    mm(bk[0:n, :], src[:, kc, c0:c0 + n], wuv[:, kc, 512:1024], kc == 0, kc == 7, [Bsrc, Bwuv], [Bk])
            for k, (src, Bsrc, c0, n, sample, u_ap, Bu, out_ap, Bout, vn_dst) in enumerate(grp):
                s_ = stat[:n, stis[k], :]
                act(gv[k][0:n, :], banks[k][0][0:n, :], AF.Gelu_apprx_tanh, [banks[k][1]], [Bgv[k], Bstat[stis[k]]], accum_out=s_[:, 4:5])
            for k, (src, Bsrc, c0, n, sample, u_ap, Bu, out_ap, Bout, vn_dst) in enumerate(grp):
                s_ = stat[:n, stis[k], :]
                dve(lambda e, s_=s_: e.tensor_scalar(out=s_[:, 5:6], in0=s_[:, 4:5], scalar1=-1.0 / 512, scalar2=None, op0=ALU.mult), [Bstat[stis[k]]], [Bstat[stis[k]]])
            for k, (src, Bsrc, c0, n, sample, u_ap, Bu, out_ap, Bout, vn_dst) in enumerate(grp):
                s_ = stat[:n, stis[k], :]
                act(vn[k][0:n, :], gv[k][0:n, :], AF.Identity, [Bgv[k], Bstat[stis[k]]], [Bvn[k]], bias=s_[:, 5:6], scale=1.0)
            for k, (src, Bsrc, c0, n, sample, u_ap, Bu, out_ap, Bout, vn_dst) in enumerate(grp):
                s_ = stat[:n, stis[k], :]
                act(gv[k][0:n, :], vn[k][0:n, :], AF.Square, [Bvn[k]], [Bgv[k], Bstat[stis[k]]], accum_out=s_[:, 0:1])
            for k, (src, Bsrc, c0, n, sample, u_ap, Bu, out_ap, Bout, vn_dst) in enumerate(grp):
                s_ = stat[:n, stis[k], :]
                dve(lambda e, s_=s_: e.tensor_scalar(out=s_[:, 1:2], in0=s_[:, 0:1], scalar1=1.0 / 512, scalar2=EPS, op0=ALU.mult, op1=ALU.add), [Bstat[stis[k]]], [Bstat[stis[k]]])
            for k, (src, Bsrc, c0, n, sample, u_ap, Bu, out_ap, Bout, vn_dst) in enumerate(grp):
                s_ = stat[:n, stis[k], :]
                act(s_[:, 2:3], s_[:, 1:2], AF.Ln, [Bstat[stis[k]]], [Bstat[stis[k]]])
            for k, (src, Bsrc, c0, n, sample, u_ap, Bu, out_ap, Bout, vn_dst) in enumerate(grp):
                s_ = stat[:n, stis[k], :]
                act(s_[:, 3:4], s_[:, 2:3], AF.Exp, [Bstat[stis[k]]], [Bstat[stis[k]]], scale=-0.5)
            for k, (src, Bsrc, c0, n, sample, u_ap, Bu, out_ap, Bout, vn_dst) in enumerate(grp):
                s_ = stat[:n, stis[k], :]
                dve(lambda e, k=k, n=n, s_=s_: e.scalar_tensor_tensor(out=vn[k][0:n, :], in0=vn[k][0:n, :], scalar=s_[:, 3:4], in1=lng_bc[0:n, :], op0=ALU.mult, op1=ALU.mult), [Bvn[k], Bstat[stis[k]], Bc], [Bvn[k]])
            for k, (src, Bsrc, c0, n, sample, u_ap, Bu, out_ap, Bout, vn_dst) in enumerate(grp):
                dve(lambda e, k=k, n=n: e.tensor_tensor(out=vn[k][0:n, :], in0=vn[k][0:n, :], in1=lnb_bc[0:n, :], op=ALU.add), [Bvn[k], Bc], [Bvn[k]])
            for k, (src, Bsrc, c0, n, sample, u_ap, Bu, out_ap, Bout, vn_dst) in enumerate(grp):
                tcopy("dve", vnb[k][0:n, :], vn[k][0:n, :], [Bvn[k]], [Bvnb[k]])
                if vn_dst is not None:
                    dma("sp" if n == 128 else "pool", vn_dst, vn[k][0:n, :], [Bvn[k]], [], sem=("vo%d" if n == 128 else "vp%d") % k)
                    out_bufs.append(Bvn[k])
            mbanks = []
            for k, (src, Bsrc, c0, n, sample, u_ap, Bu, out_ap, Bout, vn_dst) in enumerate(grp):
                bm, Bm = nextbank()
                mbanks.append((bm, Bm))
                nt = 16 if sample else 128
                for g in range(4):
                    o = bm[:, g * 128:g * 128 + nt]
                    if sample:
                        mm(o, vnb[k][0:n, g * 128:(g + 1) * 128], D16[0:16, g, :], True, False, [Bvnb[k], Bc], [Bm])
                        mm(o, ones_b[0:1, :], bsp0_b[0:1, g * 16:(g + 1) * 16], False, True, [Bc], [Bm])
                    else:
                        mm(o, vnb[k][0:n, g * 128:(g + 1) * 128], WmT[:, g, :], True, False, [Bvnb[k], Bc], [Bm])
                        mm(o, ones_b[0:1, :], bsp_b[0:1, g * 128:(g + 1) * 128], False, True, [Bc], [Bm])
            for k, (src, Bsrc, c0, n, sample, u_ap, Bu, out_ap, Bout, vn_dst) in enumerate(grp):
                nt = 16 if sample else 128
                m4 = mbanks[k][0].rearrange("p (g t) -> p g t", g=4)[:, :, 0:nt]
                dve(lambda e, m4=m4, out_ap=out_ap, u_ap=u_ap: e.tensor_tensor(out=out_ap, in0=m4, in1=u_ap, op=ALU.mult), [mbanks[k][1], Bu], [Bout])

    gitems = []
    for t in range(16):
        cs = slice(t * 128, (t + 1) * 128)
        gitems.append((xnT, BxnT, t * 128, 128, False, uT[:, :, cs], BuT, aoutT[:, :, cs], BaoutT, vch if t == 15 else None))
    gitems.append((xeT, BxeT, 0, 128, False, uxe, Buxe, aoe, Buxe, None))
    gitems.append((xnT, BxnT, SM0 + 2, NS, True, uT[:, :, SM0 + 2:SM0 + 18], BuT, aoutT[:, :, SM0 + 2:SM0 + 18], BaoutT, vchs))
    gmlp_batch(gitems)
    tcopy("dve", aoutT[:, :, SM0:SM0 + 2], aoe[:, :, 126:128], [Buxe], [BaoutT])
    if stop == 'C1':
        return finalize()
    S.label = 'C2a'
    S.barrier()
    A.top = C_TOP
    assert C1_TOP <= W2A, (C1_TOP, W2A)
    tg = [A.f32(512) for _ in range(2)]
    Btg = [Buf("tg0"), Buf("tg1")]
    t1 = [A.f32(512) for _ in range(2)]
    Bt1 = [Buf("t10"), Buf("t11")]
    mt = arena[:, MT0:NF].bitcast(BF16).rearrange("p (c n) -> p c n", c=8)
    Bmt = Buf("mT")
    oc_i = [0]
    for si, (c0, n) in enumerate(slices):
        for oc in range(8):
            i = oc_i[0] % 2
            oc_i[0] += 1
            ba, Ba = nextbank()
            for kc in range(4):
                mm(ba[:, 0:n], wpa[:, kc, oc * 128:(oc + 1) * 128], aoutT[:, kc, c0:c0 + n], kc == 0, kc == 3, [Bwg, BaoutT], [Ba])
            bb, Bb = nextbank()
            for kc in range(2):
                mm(bb[:, 0:n], wpb[:, kc, oc * 128:(oc + 1) * 128], boutT2[:, kc, c0:c0 + n], kc == 0, kc == 1, [Bwg, BboutT], [Bb])
            proj_fm(tg[i][:, 0:n], Btg[i], wg, Bwg, oc * 128, xnT, BxnT, c0, n, func=AF.Tanh, scale=0.5)
            dve(lambda e, i=i, ba=ba, n=n: e.scalar_tensor_tensor(out=t1[i][:, 0:n], in0=tg[i][:, 0:n], scalar=1.0, in1=ba[:, 0:n], op0=ALU.add, op1=ALU.mult), [Btg[i], Ba], [Bt1[i]])
            proj_fm(tg[i][:, 0:n], Btg[i], wg, Bwg, 1024 + oc * 128, xnT, BxnT, c0, n, func=AF.Tanh, scale=0.5)
            dve(lambda e, i=i, bb=bb, n=n: e.scalar_tensor_tensor(out=tg[i][:, 0:n], in0=tg[i][:, 0:n], scalar=1.0, in1=bb[:, 0:n], op0=ALU.add, op1=ALU.mult), [Btg[i], Bb], [Btg[i]])
            dve(lambda e, i=i, n=n, oc=oc, c0=c0: e.tensor_tensor(out=mt[:, oc, c0:c0 + n], in0=t1[i][:, 0:n], in1=tg[i][:, 0:n], op=ALU.add), [Bt1[i], Btg[i]], [Bmt])

    if stop == 'C2a':
        return finalize()
    S.label = 'C2b'
    S.barrier()
    A.top = CONST_TOP
    hnT = A.bf(8 * NCOL).rearrange("p (c n) -> p c n", c=8)
    BhnT = Buf("hnT")
    hbuf = A.f32(16 * 1024).rearrange("p (t n) -> p t n", t=16)
    hsm = A.f32(1024)
    Bh = [Buf("h%d" % t) for t in range(16)]
    Bhsm = Buf("hsm")
    HB_TOP = A.top
    wo = A.bf(8 * 1024).rearrange("p (c n) -> p c n", c=8)
    Bwo = Buf("wo")
    dma("pool", wo, w_out.rearrange("(c p) n -> p c n", p=128), [], [Bwo], sem="w5")
    NX = 3
    xt = [A.f32(1024) for _ in range(NX)]
    Bxt = [Buf("xt%db" % i) for i in range(NX)]
    xb = [A.bf(1024) for _ in range(NX)]
    Bxb = [Buf("xb%db" % i) for i in range(NX)]
    assert A.top <= MT0, (A.top, MT0)

    def mk_pre(t, c0o, m):
        def pre(k):
            if t >= 0:
                hdst, Bhd = hbuf[:, t, :], Bh[t]
            else:
                hdst, Bhd = hsm, Bhsm
            for half in range(2):
                bk, Bk = nextbank()
                for kc in range(8):
                    mm(bk[0:m, :], mt[:, kc, c0o:c0o + m], wo[:, kc, half * 512:(half + 1) * 512], kc == 0, kc == 7, [Bmt, Bwo], [Bk])
                dve(lambda e, bk=bk, half=half, k=k, hdst=hdst: e.scalar_tensor_tensor(out=hdst[0:m, half * 512:(half + 1) * 512], in0=bk[0:m, :], scalar=0.5, in1=xt[k][0:m, half * 512:(half + 1) * 512], op0=ALU.mult, op1=ALU.add), [Bk, Bxt[k]], [Bhd])
            return (hdst, Bhd)
        return pre

    citems = [(xall[HX + t * 128:HX + (t + 1) * 128, :], 128, hnT[:, :, t * 128:(t + 1) * 128], BhnT, gf_bc, mk_pre(t, t * 128, 128)) for t in range(16)]
    norm_T_batch(citems, xt, Bxt, xb, Bxb, "xc")
    dma("sp", xt[0][0:2, :], xall[HX - 2:HX, :], [], [Bxt[0]], sem="xc0")
    dma("sp", xt[0][2:18, :], xs, [], [Bxt[0]], sem="xq0")
    norm_T_batch([(None, 18, hnT[:, :, SM0:SM0 + 18], BhnT, gf_bc, mk_pre(-1, SM0, 18))], xt, Bxt, xb, Bxb, "xc")
    if stop == 'C2b':
        return finalize()
    S.label = 'D'
    S.barrier()
    A.top = HB_TOP
    HT = 1024
    KG = [(0, 4), (4, 4), (8, 4), (12, 4), (16, 3), (19, 3)]
    prodT = A.bf(4 * HT).rearrange("p (j n) -> p j n", j=4)
    BprodT = [Buf("prodT%d" % i) for i in range(6)]
    prodS = A.bf(22 * 18).rearrange("p (j n) -> p j n", j=22)
    BprodS = Buf("prodS")
    upb = [[A.f32(2 + HT) for _ in range(2)] for _ in range(2)]
    Bupb = [[Buf("up%d%d" % (a_, b_)) for b_ in range(2)] for a_ in range(2)]
    ups = [[A.f32(18) for _ in range(2)] for _ in range(3)]
    cbuf = [[A.f32(1024) for _ in range(2)] for _ in range(3)]
    Bcb = [[Buf("c%d%d" % (a_, b_)) for b_ in range(2)] for a_ in range(3)]
    cs_ = [[A.f32(18) for _ in range(2)] for _ in range(3)]
    Bcs = [[Buf("cs%d%d" % (a_, b_)) for b_ in range(2)] for a_ in range(3)]
    hist = A.f32(44 * 2).rearrange("p (c n) -> p c n", c=44)
    Bhist = Buf("hist")
    hsT = A.f32(44 * 2 * 16).rearrange("p (c k n) -> p c k n", c=44, k=2)
    BhsT = Buf("hsT")
    NWU = 3
    wup = [A.bf(8 * 256).rearrange("p (c n) -> p c n", c=8) for _ in range(NWU)]
    Bwup = [Buf("wup%d" % i) for i in range(NWU)]
    wdns = [A.bf(4 * 1024).rearrange("p (j n) -> p j n", j=4) for _ in range(2)]
    Bwdns = [Buf("wdn0"), Buf("wdn1")]
    wdi = [0]
    wdq = []

    def load_wdn(j0, gs):
        i = wdi[0] % 2
        wdi[0] += 1
        dma("pool", wdns[i][:, 0:gs, :], w_dn.rearrange("(j p) n -> p j n", p=128)[:, j0:j0 + gs, :], [], [Bwdns[i]], sem="wd%d" % i)
        wdq.append((wdns[i], Bwdns[i]))
    upst = [A.f32(256) for _ in range(2)]
    Bupst = [Buf("upst0"), Buf("upst1")]
    scst = cbuf[2][0].rearrange("p (k n) -> p k n", k=2)
    Bscst = Bcb[2][0]
    assert A.top <= NF, A.top
    print('D_TOP', A.top, 'HB_TOP', HB_TOP)
    for q in range(11):
        dma("sp", scst[0:16, :, :], sconv[:, :, q * 512:(q + 1) * 512], [], [Bscst], sem="sc")
        for cc in range(4):
            ch = q * 4 + cc
            bk, Bk = nextbank()
            for k in range(2):
                mm(bk[:, k * 16:(k + 1) * 16], scst[0:16, k, cc * 128:(cc + 1) * 128], ident_f[0:16, 0:16], True, True, [Bscst, Bc], [Bk])
            tcopy("dve", hsT[:, ch, :, :], bk[:, 0:32].rearrange("p (k n) -> p k n", k=2), [Bk], [BhsT])
        dma("pool", convs[:, 0, q * 512:(q + 1) * 512], scst[0:16, 1, :], [Bscst], [], sem="sco")
    out_bufs.append(Bscst)

    wv = w_up.rearrange("(c p) n -> p c n", p=128)
    wslot = {}
    wi = [0]

    def load_wup(j):
        wsl = wi[0] % NWU
        wi[0] += 1
        w_, Bw_ = wup[wsl], Bwup[wsl]
        dma("pool", w_[:, :, 0:128], wv[:, :, j * 128:(j + 1) * 128], [], [Bw_], sem="wu%da" % wsl)
        dma("pool", w_[:, :, 128:256], wv[:, :, 2816 + j * 128:2816 + (j + 1) * 128], [], [Bw_], sem="wu%db" % wsl)
        return w_, Bw_

    def stageA(H, j):
        base = H * HT
        w_, Bw_ = wslot[(H, j)]
        sl = j % 2
        s3 = j % 3
        for gv_ in range(2):
            ch = gv_ * 22 + j
            ub, Bub = upb[sl][gv_], Bupb[sl][gv_]
            if H == 0:
                if gv_ == 0:
                    bks_, Bks_ = nextbank()
                so = gv_ * 32
                for kc in range(8):
                    mm(bks_[:, so:so + 18], w_[:, kc, gv_ * 128:(gv_ + 1) * 128], hnT[:, kc, SM0:SM0 + 18], kc == 0, kc == 7, [Bw_, BhnT], [Bks_])
                if gv_ == 1:
                    for kc in range(8):
                        mm(bks_[0:20, 64:320], hnT[:, kc, NM - 2:NM + 18], w_[:, kc, :], kc == 0, kc == 7, [BhnT, Bw_], [Bks_])
                    for g2_ in range(2):
                        ub2, Bub2 = upb[sl][g2_], Bupb[sl][g2_]
                        ch2 = g2_ * 22 + j
                        so2 = g2_ * 32
                        act(ub2[:, 0:2], bks_[:, so2:so2 + 2], AF.Copy, [Bks_, Bc], [Bub2], scale=flag[:, 0:1])
                        act(ups[s3][g2_], bks_[:, so2:so2 + 18], AF.Copy, [Bks_], [Bcs[s3][g2_]])
                        cs = cs_[s3][g2_]
                        act(cs, ups[s3][g2_], AF.Identity, [Bcs[s3][g2_], Bc], [Bcs[s3][g2_]], scale=cwT[:, 2, ch2:ch2 + 1], bias=cwT[:, 3, ch2:ch2 + 1])
                        for k in range(2):
                            dve(lambda e, cs=cs, ch2=ch2, k=k: e.scalar_tensor_tensor(out=cs[:, 2:18], in0=hsT[:, ch2, k, :], scalar=cwT[:, k, ch2:ch2 + 1], in1=cs[:, 2:18], op0=ALU.mult, op1=ALU.add), [BhsT, Bc, Bcs[s3][g2_]], [Bcs[s3][g2_]])
                    ui = j % 2
                    tcopy("dve", upst[ui][0:20, :], bks_[0:20, 64:320], [Bks_], [Bupst[ui]])
                    for g2_ in range(2):
                        cc0 = g2_ * 2816 + j * 128
                        dma("pool", convp[:, cc0:cc0 + 128], upst[ui][0:2, g2_ * 128:(g2_ + 1) * 128], [Bupst[ui]], [], sem="uo%d" % ui)
                        dma("pool", convs[:, 1, cc0:cc0 + 128], upst[ui][4:20, g2_ * 128:(g2_ + 1) * 128], [Bupst[ui]], [], sem="uo%d" % ui)
                    out_bufs.append(Bupst[ui])
            else:
                tcopy("dve", ub[:, 0:2], hist[:, ch, :], [Bhist], [Bub])
            for s2 in range(2):
                c0 = base + s2 * 512
                bk, Bk = nextbank()
                for kc in range(8):
                    mm(bk, w_[:, kc, gv_ * 128:(gv_ + 1) * 128], hnT[:, kc, c0:c0 + 512], kc == 0, kc == 7, [Bw_, BhnT], [Bk])
                act(ub[:, 2 + s2 * 512:2 + (s2 + 1) * 512], bk, AF.Copy, [Bk], [Bub])
            if H == 0:
                tcopy("dve", hist[:, ch, :], ub[:, HT:HT + 2], [Bub], [Bhist])
        for gv_ in range(2):
            ch = gv_ * 22 + j
            ub, Bub = upb[sl][gv_], Bupb[sl][gv_]
            cc, Bcc = cbuf[s3][gv_], Bcb[s3][gv_]
            act(cc, ub[:, 2:2 + HT], AF.Identity, [Bub, Bc], [Bcc], scale=cwT[:, 2, ch:ch + 1], bias=cwT[:, 3, ch:ch + 1])
        for gv_ in range(2):
            ch = gv_ * 22 + j
            ub, Bub = upb[sl][gv_], Bupb[sl][gv_]
            cc, Bcc = cbuf[s3][gv_], Bcb[s3][gv_]
            for k in range(2):
                dve(lambda e, cc=cc, ub=ub, k=k, ch=ch: e.scalar_tensor_tensor(out=cc, in0=ub[:, k:k + HT], scalar=cwT[:, k, ch:ch + 1], in1=cc, op0=ALU.mult, op1=ALU.add), [Bub, Bc, Bcc], [Bcc])

    def stageB(H, j, jj):
        s3 = j % 3
        cg, cv = cbuf[s3][0], cbuf[s3][1]
        act(cg, cg, AF.Gelu_apprx_tanh, [Bcb[s3][0]], [Bcb[s3][0]])
        dve(lambda e, cg=cg, cv=cv, jj=jj: e.tensor_tensor(out=prodT[:, jj, :], in0=cg, in1=cv, op=ALU.mult), [Bcb[s3][0], Bcb[s3][1]], [BprodT[jj]])
        if H == 0:
            act(cs_[s3][0], cs_[s3][0], AF.Gelu_apprx_tanh, [Bcs[s3][0]], [Bcs[s3][0]])
            dve(lambda e, s3=s3, j=j: e.tensor_tensor(out=prodS[:, j, :], in0=cs_[s3][0], in1=cs_[s3][1], op=ALU.mult), [Bcs[s3][0], Bcs[s3][1]], [BprodS])

    def wdown_group(H, j0, gs):
        wdn, Bwdn = wdq.pop(0)
        for tb in range(2):
            ths = [(tb * 4 + q_, half) for q_ in range(4) for half in range(2)]
            bks = [nextbank() for _ in ths]
            for (tl, half), (bk, Bk) in zip(ths, bks):
                for jj in range(gs - 1):
                    mm(bk, prodT[:, jj, tl * 128:(tl + 1) * 128], wdn[:, jj, half * 512:(half + 1) * 512], jj == 0, False, [BprodT[jj], Bwdn], [Bk])
            for (tl, half), (bk, Bk) in zip(ths, bks):
                jj = gs - 1
                mm(bk, prodT[:, jj, tl * 128:(tl + 1) * 128], wdn[:, jj, half * 512:(half + 1) * 512], False, True, [BprodT[jj], Bwdn], [Bk])
            for (tl, half), (bk, Bk) in zip(ths, bks):
                t = H * 8 + tl
                dve(lambda e, bk=bk, t=t, half=half: e.tensor_tensor(out=hbuf[:, t, half * 512:(half + 1) * 512], in0=bk, in1=hbuf[:, t, half * 512:(half + 1) * 512], op=ALU.add), [Bk, Bh[t]], [Bh[t]])
        if H == 0:
            for half in range(2):
                bk, Bk = nextbank()
                for jj in range(gs):
                    mm(bk[0:18, :], prodS[:, j0 + jj, :], wdn[:, jj, half * 512:(half + 1) * 512], jj == 0, jj == gs - 1, [BprodS, Bwdn], [Bk])
                dve(lambda e, bk=bk, half=half: e.tensor_tensor(out=hsm[0:18, half * 512:(half + 1) * 512], in0=bk[0:18, :], in1=hsm[0:18, half * 512:(half + 1) * 512], op=ALU.add), [Bk, Bhsm], [Bhsm])

    def final_rows(haps, dsts):
        stis = []
        for k, (hap, m, Bh_) in enumerate(haps):
            sti = stat_i[0] % 8
            stat_i[0] += 1
            stis.append(sti)
            scr = cbuf[k % 3][(k // 3) % 2]
            Bscr = Bcb[k % 3][(k // 3) % 2]
            act(scr.bitcast(BF16)[0:m, 0:1024], hap, AF.Square, [Bh_], [Bscr, Bstat[sti]], accum_out=stat[:m, sti, 0:1])
        rs = []
        for k, (hap, m, Bh_) in enumerate(haps):
            rs.append(rstd_of(stat[:m, stis[k], 0:1], m, 1.0 / 1024, stis[k], Bstat[stis[k]]))
        for k, (hap, m, Bh_) in enumerate(haps):
            dve(lambda e, hap=hap, r=rs[k], m=m: e.scalar_tensor_tensor(out=hap, in0=hap, scalar=r, in1=gfin_bc[0:m, :], op0=ALU.mult, op1=ALU.mult), [Bh_, Bstat[stis[k]], Bc], [Bh_])
        for k, (hap, m, Bh_) in enumerate(haps):
            dst, r0 = dsts[k]
            dma("sp" if m == 128 else "pool", dst, hap[r0:m, :], [Bh_], [], sem=("yo%d" if m == 128 else "yp%d") % (k % 4))
            out_bufs.append(Bh_)

    for H in range(2):
        order = [(j0, gs, jj) for (j0, gs) in KG for jj in range(gs)]
        PRE = 3
        jof = lambda q_: order[q_][0] + order[q_][2]
        for q_ in range(min(PRE, len(order))):
            wslot[(H, jof(q_))] = load_wup(jof(q_))
        doneA = set()

        def doA(q_):
            if q_ < len(order) and q_ not in doneA:
                doneA.add(q_)
                stageA(H, jof(q_))

        load_wdn(*KG[0])
        gnext = [1]
        doA(0)
        doA(1)
        for idx, (j0, gs, jj) in enumerate(order):
            j = j0 + jj
            if idx + PRE < len(order):
                wslot[(H, jof(idx + PRE))] = load_wup(jof(idx + PRE))
            if jj == gs - 1:
                stageB(H, j, jj)
                doA(idx + 2)
                doA(idx + 3)
                if gnext[0] < len(KG):
                    load_wdn(*KG[gnext[0]])
                    gnext[0] += 1
                wdown_group(H, j0, gs)
            else:
                doA(idx + 2)
                stageB(H, j, jj)
        final_rows([(hbuf[:, H * 8 + tl, :], 128, Bh[H * 8 + tl]) for tl in range(8)],
                   [(y[(H * 8 + tl) * 128:(H * 8 + tl + 1) * 128, :], 0) for tl in range(8)])
        if H == 0:
            final_rows([(hsm[0:18, :], 18, Bhsm)], [(ys, 2)])

    return finalize()


_NC_CACHE = {}


def make_in_maps(x_prompt, x_sample, cache_kv_w128, cache_kv_w512, cache_kv_w2048, state_conv_ffn,
                 norm_mix, w_in, ln_v_gain, ln_v_bias, w_spatial, b_spatial, w_proj_a, w_proj_b,
                 w_out, norm_ffn, w_up, conv_w, conv_b, w_down, norm_final, cores=None):
    f = lambda a: np.ascontiguousarray(np.asarray(a, dtype=np.float32))
    x_prompt = f(x_prompt)
    B, SEQ, D = x_prompt.shape
    xp = np.zeros((B, HX + SEQ, D), np.float32)
    xp[:, HX:] = x_prompt
    k = np.arange(128)[:, None]
    q = np.arange(128)[None, :]
    mcur = (k <= q).astype(np.float32)
    mprev = (k >= q).astype(np.float32)
    blockones = ((k // 64) == (q // 64)).astype(np.float32)
    caches = [f(cache_kv_w128)[0], f(cache_kv_w512)[0], f(cache_kv_w2048)[0]]
    common = {
        "w_in": f(w_in)[0], "w_pa": f(w_proj_a)[0], "w_pb": f(w_proj_b)[0], "w_out": f(w_out)[0],
        "w_up": f(w_up)[0], "w_dn": f(w_down)[0],
        "vec8": np.concatenate([f(norm_mix)[0].reshape(8, 128), f(norm_ffn)[0].reshape(8, 128)], 0),
        "nfin": f(norm_final), "lng": f(ln_v_gain)[0], "lnb": f(ln_v_bias)[0],
        "cwb": np.concatenate([f(conv_w)[0], f(conv_b)], 0).reshape(4, 44, 128),
        "wsp": f(w_spatial)[0], "bsp": f(b_spatial)[0],
        "w00": np.ascontiguousarray(f(w_spatial)[0][:, 0, 0]),
    }
    in_maps = []
    for c in (range(NCORES) if cores is None else cores):
        b, qi = c // 4, c % 4
        fl = 0.0 if qi == 0 else 1.0
        m = dict(common)
        m["xall"] = np.ascontiguousarray(xp[b, qi * NM:qi * NM + HX + NM])
        m["xs"] = f(x_sample)[c * NS:(c + 1) * NS, 0]
        for g in range(3):
            cg = caches[g][c * NS:(c + 1) * NS]
            m["ck%d" % g] = np.ascontiguousarray(cg.reshape(NS, cg.shape[1], 512))
        m["sconv"] = f(state_conv_ffn)[0, c * NS:(c + 1) * NS]
        m["cst"] = np.stack([np.eye(128, dtype=np.float32), mprev, mcur, mprev * fl, mcur, blockones, np.ones((128, 128), np.float32)])
        m["flagd"] = np.full((128, 1), fl, np.float32)
        in_maps.append(m)
    return in_maps


def assemble(R, B=2):
    y_prompt = np.stack([np.concatenate([R[b * 4 + qi]["y"] for qi in range(4)], 0) for b in range(B)])
    y_sample = np.concatenate([R[c]["ys"] for c in range(NCORES)], 0)[:, None, :]
    outs = [y_prompt, y_sample]
    for g in range(3):
        L = WIN[g][0]
        kvp = np.stack([R[b * 4 + 3]["kv%d" % g] for b in range(B)]).reshape(1, B, L, 2, 4, 64)
        kvsm = np.concatenate([R[c]["kvs%d" % g] for c in range(NCORES)], 0).reshape(1, NCORES * NS, 1, 2, 4, 64)
        outs += [kvp, kvsm]
    vcp = np.stack([R[b * 4 + 3]["vch"] for b in range(B)])[None]
    vcs = np.concatenate([R[c]["vchs"] for c in range(NCORES)], 0)[None, :, None, :]
    cp = np.stack([R[b * 4 + 3]["convp"] for b in range(B)])[None]
    cs = np.concatenate([R[c]["convs"] for c in range(NCORES)], 0)[None]
    outs += [vcp, vcs, cp, cs]
    return tuple(np.ascontiguousarray(np.asarray(o, dtype=np.float32)) for o in outs)


def kernel(**inputs):
    in_maps = make_in_maps(**inputs)
    if "nc" not in _NC_CACHE:
        _NC_CACHE["nc"] = build_program()
    res = run_bass_kernel_spmd(_NC_CACHE["nc"], in_maps, core_ids=list(range(NCORES)))
    return assemble(res.results)
```

```python
import numpy as np
from contextlib import ExitStack
import concourse.bass as bass
import concourse.mybir as mybir
from concourse.bass_utils import run_bass_kernel_spmd

F32 = mybir.dt.float32
BF16 = mybir.dt.bfloat16
AF = mybir.ActivationFunctionType
ALU = mybir.AluOpType

NCORES = 8
HX = 2176
NM = 2048
NS = 16
SM0 = NM
NCOL = NM + 2 + NS
EPS = 1e-6
WIN = ((128, 1), (512, 4), (2048, 16))
KBASE = (1920, 1536, 0)
KLEN = (2304, 2688, 4224)


class Buf:
    __slots__ = ("name", "w", "r")

    def __init__(self, name=""):
        self.name = name
        self.w = None
        self.r = {}


class Sched:
    ENG = ("pe", "act", "dve", "pool", "sp")

    def __init__(self, nc, stack):
        self.nc = nc
        self.stack = stack
        self.prog = {e: [] for e in self.ENG}
        self.sems = {}
        self.cnt = {}
        self.seen = {e: {} for e in self.ENG}
        self.label = ""
        for e in self.ENG:
            self._sem("E_" + e)

    def _sem(self, name):
        if name not in self.sems:
            self.sems[name] = self.stack.enter_context(self.nc.semaphore(name))
            self.cnt[name] = 0
        return name

    def _need(self, eng, waits, tok):
        if tok is None:
            return
        sem, val = tok
        if eng == "pe" and sem == "E_pe":
            return
        if self.seen[eng].get(sem, 0) >= val:
            return
        self.seen[eng][sem] = val
        waits.append((sem, val))

    def op(self, eng, fn, reads=(), writes=(), dma=None):
        waits = []
        for b in reads:
            self._need(eng, waits, b.w)
        for b in writes:
            self._need(eng, waits, b.w)
            for s, v in b.r.items():
                self._need(eng, waits, (s, v))
        if dma is not None:
            sem = self._sem(dma)
            inc = 16
        else:
            sem = "E_" + eng
            inc = 1
        self.cnt[sem] += inc
        tok = (sem, self.cnt[sem])
        self.prog[eng].append((waits, fn, (sem, inc), self.label + " r:" + ",".join(b.name for b in reads) + " w:" + ",".join(b.name for b in writes)))
        for b in reads:
            if b.r.get(sem, 0) < tok[1]:
                b.r[sem] = tok[1]
        for b in writes:
            b.w = tok
            b.r = {}
        return tok

    def barrier(self):
        allw = [(s, v) for s, v in self.cnt.items() if v > 0]
        for e in self.ENG:
            waits = []
            for t in allw:
                self._need(e, waits, t)
            self.prog[e].append((waits, None, None, ""))

    def emit(self):
        nc = self.nc
        with nc.Block() as block:
            def run(name, e):
                for waits, fn, inc, lab in self.prog[name]:
                    for s, v in waits:
                        e.wait_ge(self.sems[s], v)
                    if fn is not None:
                        with nc.named_scope(lab.split(" ")[0] or "none"):
                            ins = fn(e)
                        ins.then_inc(self.sems[inc[0]], inc[1])

            @block.tensor
            def _(e):
                run("pe", e)

            @block.scalar
            def _(e):
                run("act", e)

            @block.vector
            def _(e):
                run("dve", e)

            @block.gpsimd
            def _(e):
                run("pool", e)

            @block.sync
            def _(e):
                run("sp", e)


def build_program(stop=None):
    nc = bass.Bass("TRN2", target_bir_lowering=False)

    def din(name, shape):
        return nc.dram_tensor(name, list(shape), F32, kind="ExternalInput").ap()

    def dout(name, shape):
        return nc.dram_tensor(name, list(shape), F32, kind="ExternalOutput").ap()

    xall = din("xall", [HX + NM, 1024])
    xs = din("xs", [NS, 1024])
    ck = [din("ck0", [NS, 128, 512]), din("ck1", [NS, 512, 512]), din("ck2", [NS, 2048, 512])]
    sconv = din("sconv", [NS, 2, 5632])
    w_in = din("w_in", [1024, 5376])
    w_pa = din("w_pa", [512, 1024])
    w_pb = din("w_pb", [256, 1024])
    w_out = din("w_out", [1024, 1024])
    w_up = din("w_up", [1024, 5632])
    w_dn = din("w_dn", [2816, 1024])
    vec8 = din("vec8", [16, 128])
    nfin = din("nfin", [1024])
    lng = din("lng", [512])
    lnb = din("lnb", [512])
    cwb = din("cwb", [4, 44, 128])
    wsp = din("wsp", [4, 128, 128])
    bsp = din("bsp", [4, 128])
    w00 = din("w00", [4])
    cst = din("cst", [7, 128, 128])
    flagd = din("flagd", [128, 1])

    y = dout("y", [NM, 1024])
    ys = dout("ys", [NS, 1024])
    kvo = [dout("kv0", [128, 512]), dout("kv1", [512, 512]), dout("kv2", [2048, 512])]
    kvs = [dout("kvs0", [NS, 512]), dout("kvs1", [NS, 512]), dout("kvs2", [NS, 512])]
    vch = dout("vch", [128, 512])
    vchs = dout("vchs", [NS, 512])
    convp = dout("convp", [2, 5632])
    convs = dout("convs", [NS, 2, 5632])

    st = ExitStack()
    S = Sched(nc, st)
    NF = 53100
    arena = st.enter_context(nc.sbuf_tensor("arena", [128, NF], F32))
    psum_all = st.enter_context(nc.psum_tensor("psall", [128, 4096], F32))
    pbank = [psum_all[:, i * 512:(i + 1) * 512] for i in range(8)]
    PB = [Buf("pb%d" % i) for i in range(8)]
    bank_i = [0]

    def nextbank():
        i = bank_i[0] % 8
        bank_i[0] += 1
        return pbank[i], PB[i]

    def nextpair():
        if bank_i[0] % 2:
            bank_i[0] += 1
        i = bank_i[0] % 8
        bank_i[0] += 2
        return pbank[i], PB[i], pbank[i + 1], PB[i + 1], psum_all[:, i * 512:(i + 2) * 512]

    class Arena:
        def __init__(self):
            self.top = 0

        def f32(self, n):
            a = arena[:, self.top:self.top + n]
            self.top += n
            assert self.top <= NF, self.top
            return a

        def bf(self, n):
            n2 = (n + 1) // 2
            a = arena[:, self.top:self.top + n2].bitcast(BF16)
            self.top += n2
            assert self.top <= NF, self.top
            return a

    A = Arena()
    TT = [NF - 2200]

    def tmp_f32(n):
        a = arena[:, TT[0]:TT[0] + n]
        TT[0] += n
        assert TT[0] <= NF
        return a
    out_bufs = []
    uid = [0]

    def finalize():
        S.barrier()
        S.emit()
        st.close()
        return nc

    def dma(eng, out, in_, rd, wr, sem=None, **kw):
        if sem is None:
            uid[0] += 1
            sem = "d%d" % (uid[0] % 24)
        return S.op(eng, lambda e: e.dma_start(out=out, in_=in_, **kw), reads=rd, writes=wr, dma=sem)

    def mm(out, lhsT, rhs, start, stop, rd, wr):
        S.op("pe", lambda e: e.matmul(out, lhsT=lhsT, rhs=rhs, start=start, stop=stop), reads=rd, writes=wr)

    def act(out, in_, func, rd, wr, **kw):
        S.op("act", lambda e: e.activation(out=out, in_=in_, func=func, **kw), reads=rd, writes=wr)

    def dve(fn, rd, wr):
        S.op("dve", fn, reads=rd, writes=wr)

    def tcopy(eng, out, in_, rd, wr):
        S.op(eng, lambda e: e.tensor_copy(out=out, in_=in_), reads=rd, writes=wr)

    Bc = Buf("const")
    cbs = []

    def CW():
        nb = Buf("c%d" % len(cbs))
        cbs.append(nb)
        return [nb]

    def CR():
        return list(cbs)
    cst_f = A.f32(7 * 128).rearrange("p (k n) -> p k n", k=7)
    dma("sp", cst_f, cst.rearrange("k p n -> p k n"), [], CW(), sem="c0")
    ident_f = cst_f[:, 0, :]
    blockones_f = cst_f[:, 5, :]
    cst_b = A.bf(7 * 128).rearrange("p (k n) -> p k n", k=7)
    tcopy("dve", cst_b, cst_f, CR(), CW())
    ident_b = cst_b[:, 0, :]
    mprev_b = cst_b[:, 1, :]
    mcur_b = cst_b[:, 2, :]
    medge_b = cst_b[:, 3, :]
    ones_b = cst_b[:, 6, :]
    flag = A.f32(1)
    dma("sp", flag, flagd, [], CW(), sem="c1")
    gfin_bc = A.f32(1024)
    dma("sp", gfin_bc, nfin.partition_broadcast(128), [], CW(), sem="c2")
    lng_bc = A.f32(512)
    lnb_bc = A.f32(512)
    dma("sp", lng_bc, lng.partition_broadcast(128), [], CW(), sem="c3")
    dma("sp", lnb_bc, lnb.partition_broadcast(128), [], CW(), sem="c4")
    w00_bc = A.f32(4)
    dma("sp", w00_bc, w00.partition_broadcast(128), [], CW(), sem="c5")
    v8_sb = tmp_f32(128)
    dma("sp", v8_sb[0:16, :], vec8, [], CW(), sem="c6")
    cw_sb = tmp_f32(4 * 128).rearrange("p (k n) -> p k n", k=4)
    dma("sp", cw_sb[0:44, :, :], cwb.rearrange("k c p -> c k p"), [], CW(), sem="c7")
    gvec = A.f32(16)
    cwT = A.f32(4 * 44).rearrange("p (k c) -> p k c", k=4)
    bk, Bk = nextbank()
    mm(bk[:, 0:16], v8_sb[0:16, :], ident_f[0:16, 0:16], True, True, CR(), [Bk])
    tcopy("dve", gvec, bk[:, 0:16], [Bk], CW())
    bk, Bk = nextbank()
    for k in range(4):
        mm(bk[:, k * 44:(k + 1) * 44], cw_sb[0:44, k, :], ident_f[0:44, 0:44], True, True, CR(), [Bk])
    tcopy("dve", cwT, bk[:, 0:176].rearrange("p (k c) -> p k c", k=4), [Bk], CW())
    ones_f = A.f32(128)
    S.op("dve", lambda e: e.memset(ones_f, 1.0), writes=CW())
    gm_bc = A.bf(8 * 128).rearrange("p (c n) -> p c n", c=8)
    gf_bc = A.bf(8 * 128).rearrange("p (c n) -> p c n", c=8)
    for c in range(8):
        dve(lambda e, c=c: e.tensor_scalar(out=gm_bc[:, c, :], in0=ones_f, scalar1=gvec[:, c:c + 1], scalar2=None, op0=ALU.mult), CR(), CW())
        dve(lambda e, c=c: e.tensor_scalar(out=gf_bc[:, c, :], in0=ones_f, scalar1=gvec[:, 8 + c:9 + c], scalar2=None, op0=ALU.mult), CR(), CW())
    wsp_f = tmp_f32(512).rearrange("p (g n) -> p g n", g=4)
    dma("sp", wsp_f, wsp.rearrange("g t s -> t g s"), [], CW(), sem="c8")
    wsp_b = tmp_f32(256).bitcast(BF16).rearrange("p (g n) -> p g n", g=4)
    for g in range(4):
        dve(lambda e, g=g: e.tensor_tensor(out=wsp_b[:, g, :], in0=wsp_f[:, g, :], in1=cst_f[:, 1, :], op=ALU.mult), CR(), CW())
    WmT = A.bf(512).rearrange("p (g n) -> p g n", g=4)
    bk, Bk = nextbank()
    bkb = bk.bitcast(BF16).rearrange("p (g n) -> p g n", g=8)
    for g in range(4):
        S.op("pe", lambda e, g=g: e.transpose(out=bkb[:, g, :], in_=wsp_b[:, g, :], identity=ident_b), reads=CR(), writes=[Bk])
    tcopy("dve", WmT, bkb[:, 0:4, :], [Bk], CW())
    bsp_f = tmp_f32(512)
    dma("sp", bsp_f[0:1, :], bsp.rearrange("(o g) t -> o (g t)", o=1), [], CW(), sem="c9")
    bsp_b = A.bf(512)
    tcopy("dve", bsp_b[0:1, :], bsp_f[0:1, :], CR(), CW())
    bsp0_f = A.f32(4 * 16)
    bsp0_b = A.bf(4 * 16)
    for g in range(4):
        dve(lambda e, g=g: e.tensor_scalar(out=bsp0_f[0:1, g * 16:(g + 1) * 16], in0=ones_f[0:1, 0:16], scalar1=bsp_f[0:1, g * 128:g * 128 + 1], scalar2=None, op0=ALU.mult), CR(), CW())
    tcopy("dve", bsp0_b[0:1, :], bsp0_f[0:1, :], CR(), CW())
    D16 = A.bf(4 * 16).rearrange("p (g n) -> p g n", g=4)
    for g in range(4):
        dve(lambda e, g=g: e.tensor_scalar(out=D16[0:16, g, :], in0=ident_f[0:16, 0:16], scalar1=w00_bc[0:16, g:g + 1], scalar2=None, op0=ALU.mult), CR(), CW())
    stat = A.f32(8 * 8).rearrange("p (s n) -> p s n", s=8)
    Bstat = [Buf("st%d" % i) for i in range(8)]
    stat_i = [0]
    S.op("dve", lambda e: e.memset(stat[:, 7, 7:8], 0.0), reads=CR(), writes=[Bc])
    CONST_TOP = A.top
    print('CONST_TOP', CONST_TOP)

    if stop == 'const':
        return finalize()
    def rstd_of(ssq_ap, n, inv_n, sti, Bs):
        s_ = stat[:n, sti, :]
        dve(lambda e: e.tensor_scalar(out=s_[:, 1:2], in0=ssq_ap, scalar1=inv_n, scalar2=EPS, op0=ALU.mult, op1=ALU.add), [Bs], [Bs])
        act(s_[:, 2:3], s_[:, 1:2], AF.Ln, [Bs], [Bs])
        act(s_[:, 3:4], s_[:, 2:3], AF.Exp, [Bs], [Bs], scale=-0.5)
        return s_[:, 3:4]

    def norm_rows(xt_ap, n, Bx, out_bf, Bo):
        sti = stat_i[0] % 8
        stat_i[0] += 1
        Bs = Bstat[sti]
        act(out_bf, xt_ap, AF.Square, [Bx], [Bo, Bs], accum_out=stat[:n, sti, 0:1])
        r = rstd_of(stat[:n, sti, 0:1], n, 1.0 / 1024, sti, Bs)
        act(out_bf, xt_ap, AF.Copy, [Bx, Bs], [Bo], scale=r)

    def transpose_rows(src_bf, n, Bsrc, dstT, Bdst, g_bc):
        bk, Bk = nextbank()
        pt = bk.bitcast(BF16).rearrange("p (c t) -> p c t", c=8)
        for c in range(8):
            S.op("pe", lambda e, c=c: e.transpose(out=pt[:, c, 0:n], in_=src_bf[0:n, c * 128:(c + 1) * 128], identity=ident_b[0:n, 0:n]), reads=[Bsrc, Bc], writes=[Bk])
        dve(lambda e: e.tensor_tensor(out=dstT, in0=pt[:, :, 0:n], in1=g_bc[:, :, 0:n], op=ALU.mult), [Bk, Bc], [Bdst])

    xnT = A.bf(8 * HX).rearrange("p (c n) -> p c n", c=8)
    BxnT = Buf("xnT")
    xeT = A.bf(8 * 128).rearrange("p (c n) -> p c n", c=8)
    BxeT = Buf("xeT")
    P1 = A.top
    QT = A.bf(6 * NCOL).rearrange("p (c n) -> p c n", c=6)
    BQT = Buf("QT")
    KT = [A.bf(2 * KLEN[g]).rearrange("p (c n) -> p c n", c=2) for g in range(3)]
    BKT = [Buf("KT%d" % g) for g in range(3)]
    KTs = A.bf(6 * 18).rearrange("p (c n) -> p c n", c=6)
    VTs = A.bf(6 * 18).rearrange("p (c n) -> p c n", c=6)
    BKTs = Buf("KTs")
    NVB = 79
    Vb = A.bf(NVB * 256).rearrange("p (b n) -> p b n", b=NVB)
    BV = [Buf("V%d" % i) for i in range(NVB + 3)]
    vidx = {}
    PW = A.top
    wqkv = A.bf(8 * 2304).rearrange("p (c n) -> p c n", c=8)
    Bw = Buf("wqkv")
    NKV = 5
    kvst = [A.f32(512) for _ in range(NKV)]
    Bkvst = [Buf("kvst%d" % i) for i in range(NKV)]
    kvst_i = [0]
    PB_TOP = A.top

    dma("pool", wqkv, w_in.rearrange("(c p) n -> p c n", p=128)[:, :, 0:2304], [], [Bw], sem="w0")

    def wcol(kind, g, c):
        return kind * 768 + g * 256 + c * 128

    def norm_T_batch(items, xts, Bxts, xbs, Bxbs, semp):
        G = len(xts)
        for g0 in range(0, len(items), G):
            grp = items[g0:g0 + G]
            stis = []
            srcs = []
            for k, (src_rows, n, dstT, Bdst, g_bc, pre) in enumerate(grp):
                if src_rows is not None:
                    dma("sp", xts[k][0:n, :], src_rows, [], [Bxts[k]], sem="%s%d" % (semp, k))
                srcs.append((xts[k], Bxts[k]))
            for k, (src_rows, n, dstT, Bdst, g_bc, pre) in enumerate(grp):
                if pre is not None:
                    srcs[k] = pre(k)
            for k, (src_rows, n, dstT, Bdst, g_bc, pre) in enumerate(grp):
                sti = stat_i[0] % 8
                stat_i[0] += 1
                stis.append(sti)
                act(xbs[k][0:n, :], srcs[k][0][0:n, :], AF.Square, [srcs[k][1]], [Bxbs[k], Bstat[sti]], accum_out=stat[:n, sti, 0:1])
            for k, (src_rows, n, dstT, Bdst, g_bc, pre) in enumerate(grp):
                s_ = stat[:n, stis[k], :]
                dve(lambda e, s_=s_: e.tensor_scalar(out=s_[:, 1:2], in0=s_[:, 0:1], scalar1=1.0 / 1024, scalar2=EPS, op0=ALU.mult, op1=ALU.add), [Bstat[stis[k]]], [Bstat[stis[k]]])
            for k, (src_rows, n, dstT, Bdst, g_bc, pre) in enumerate(grp):
                s_ = stat[:n, stis[k], :]
                act(s_[:, 2:3], s_[:, 1:2], AF.Ln, [Bstat[stis[k]]], [Bstat[stis[k]]])
            for k, (src_rows, n, dstT, Bdst, g_bc, pre) in enumerate(grp):
                s_ = stat[:n, stis[k], :]
                act(s_[:, 3:4], s_[:, 2:3], AF.Exp, [Bstat[stis[k]]], [Bstat[stis[k]]], scale=-0.5)
            for k, (src_rows, n, dstT, Bdst, g_bc, pre) in enumerate(grp):
                s_ = stat[:n, stis[k], :]
                act(xbs[k][0:n, :], srcs[k][0][0:n, :], AF.Copy, [srcs[k][1], Bstat[stis[k]]], [Bxbs[k]], scale=s_[:, 3:4])
            for k, (src_rows, n, dstT, Bdst, g_bc, pre) in enumerate(grp):
                transpose_rows(xbs[k], n, Bxbs[k], dstT, Bdst, g_bc)

    def proj_fm(dst, Bdst, wt, Bwt, col0, src, Bsrc, c0, n, nk=8, func=AF.Copy, **kw):
        bk, Bk = nextbank()
        for kc in range(nk):
            mm(bk[:, 0:n], wt[:, kc, col0:col0 + 128], src[:, kc, c0:c0 + n], kc == 0, kc == nk - 1, [Bwt, Bsrc], [Bk])
        act(dst, bk[:, 0:n], func, [Bk], [Bdst], **kw)

    def vblock(g, start, step, n, src, Bsrc):
        idx = len(vidx)
        vidx[(g, start, step, "m" if src is xnT_main_marker[0] else "h")] = idx
        bk, Bk = nextbank()
        for kc in range(8):
            mm(bk[0:n, 0:256], src[:, kc, start:start + step * (n - 1) + 1:step], wqkv[:, kc, wcol(2, g, 0):wcol(2, g, 0) + 256], kc == 0, kc == 7, [Bsrc, Bw], [Bk])
        tcopy("dve", Vb[0:n, idx, :], bk[0:n, 0:256], [Bk], [BV[idx]])
        return idx

    xnT_main_marker = [None]

    S.label = 'B1'
    QT_f32 = arena[:, P1:P1 + 6144]
    xtA = [QT_f32[:, k * 1024:(k + 1) * 1024] for k in range(4)]
    xbA = [QT_f32[:, 4096 + k * 512:4096 + (k + 1) * 512].bitcast(BF16) for k in range(4)]
    BxtA = [Buf("xtA%d" % k) for k in range(4)]
    BxbA = [Buf("xbA%d" % k) for k in range(4)]
    norm_T_batch([(xall[t * 128:(t + 1) * 128, :], 128, xnT[:, :, t * 128:(t + 1) * 128], BxnT, gm_bc, None) for t in range(17)],
                 xtA, BxtA, xbA, BxbA, "xa")
    if stop == 'B1a':
        return finalize()
    for g in range(3):
        lt = KBASE[g]
        while lt < HX:
            n = min(512, HX - lt)
            for c in range(2):
                proj_fm(KT[g][:, c, lt - KBASE[g]:lt - KBASE[g] + n], BKT[g], wqkv, Bw, wcol(1, g, c), xnT, BxnT, lt, n)
            lt += n
    if stop == 'B1k':
        return finalize()
    for r in range(16):
        vblock(2, 128 + r, 16, 128, xnT, BxnT)
    for r in range(4):
        vblock(1, 1664 + r, 4, 128, xnT, BxnT)
    vblock(0, 2048, 1, 128, xnT, BxnT)
    vblock(2, 126, 16, 128, xnT, BxnT)
    vblock(2, 127, 16, 128, xnT, BxnT)
    vblock(1, 1662, 4, 128, xnT, BxnT)
    vblock(1, 1663, 4, 128, xnT, BxnT)
    vblock(0, 2046, 1, 128, xnT, BxnT)
    vblock(2, 2174, 1, 1, xnT, BxnT)
    vblock(2, 2175, 1, 1, xnT, BxnT)
    vblock(1, 2174, 1, 1, xnT, BxnT)
    vblock(1, 2175, 1, 1, xnT, BxnT)
    vblock(0, 2174, 1, 2, xnT, BxnT)
    tcopy("dve", xeT, xnT[:, :, 2048:2176], [BxnT], [BxeT])

    if stop == 'B1':
        return finalize()
    S.label = 'B2'
    xnT_main_marker[0] = xnT
    S.barrier()
    VB0 = PW - NVB * 128 + 31 * 128
    Vm_f32 = arena[:, VB0:VB0 + 6144]
    xtB = [Vm_f32[:, k * 1024:(k + 1) * 1024] for k in range(4)]
    xbB = [Vm_f32[:, 4096 + k * 512:4096 + (k + 1) * 512].bitcast(BF16) for k in range(4)]
    BxtB = [Buf("xtB%d" % k) for k in range(4)]
    BxbB = [Buf("xbB%d" % k) for k in range(4)]
    norm_T_batch([(xall[HX + t * 128:HX + (t + 1) * 128, :], 128, xnT[:, :, t * 128:(t + 1) * 128], BxnT, gm_bc, None) for t in range(16)],
                 xtB, BxtB, xbB, BxbB, "xb")
    tcopy("dve", xnT[:, :, SM0:SM0 + 2], xeT[:, :, 126:128], [BxeT], [BxnT])
    norm_T_batch([(xs, NS, xnT[:, :, SM0 + 2:SM0 + 2 + NS], BxnT, gm_bc, None)], xtB, BxtB, xbB, BxbB, "xb")
    slices = [(i * 512, 512) for i in range(4)] + [(SM0, 18)]
    if stop == 'B2a':
        return finalize()
    for (c0, n) in slices:
        for gc in range(6):
            proj_fm(QT[:, gc, c0:c0 + n], BQT, wqkv, Bw, wcol(0, gc // 2, gc % 2), xnT, BxnT, c0, n)
    for (c0, n) in slices[:4]:
        for g in range(3):
            for c in range(2):
                kc0 = HX + c0 - KBASE[g]
                proj_fm(KT[g][:, c, kc0:kc0 + n], BKT[g], wqkv, Bw, wcol(1, g, c), xnT, BxnT, c0, n)
    for gc in range(6):
        proj_fm(KTs[:, gc, :], BKTs, wqkv, Bw, wcol(1, gc // 2, gc % 2), xnT, BxnT, SM0, 18)
        proj_fm(VTs[:, gc, :], BKTs, wqkv, Bw, wcol(2, gc // 2, gc % 2), xnT, BxnT, SM0, 18)
    if stop == 'B2q':
        return finalize()
    assert len(vidx) == 31, len(vidx)
    S.barrier()
    for t in range(16):
        vblock(0, t * 128, 1, 128, xnT, BxnT)
    for i in range(4):
        for r in range(4):
            vblock(1, 512 * i + r, 4, 128, xnT, BxnT)
    for r in range(16):
        vblock(2, r, 16, 128, xnT, BxnT)

    if stop == 'B2v':
        return finalize()
    S.label = 'kvtok'
    def kv_tok(col0, n, g, dst_rows):
        i = kvst_i[0] % NKV
        kvst_i[0] += 1
        bk, Bk = nextbank()
        for half in range(2):
            for kc in range(8):
                mm(bk[0:n, half * 256:(half + 1) * 256], xnT[:, kc, col0:col0 + n], wqkv[:, kc, wcol(1 + half, g, 0):wcol(1 + half, g, 0) + 256], kc == 0, kc == 7, [BxnT, Bw], [Bk])
        tcopy("dve", kvst[i][0:n, :], bk[0:n, :], [Bk], [Bkvst[i]])
        dma("sp" if n == 128 else "pool", dst_rows, kvst[i][0:n, :], [Bkvst[i]], [], sem=("ko%d" if n == 128 else "kp%d") % i)
        out_bufs.append(Bkvst[i])

    for t in range(16):
        kv_tok(t * 128, 128, 2, kvo[2][t * 128:(t + 1) * 128, :])
    if stop == 'kv1':
        return finalize()
    for t in range(12, 16):
        kv_tok(t * 128, 128, 1, kvo[1][(t - 12) * 128:(t - 11) * 128, :])
    kv_tok(15 * 128, 128, 0, kvo[0])
    if stop == 'kv2':
        return finalize()
    for g in range(3):
        kv_tok(SM0 + 2, NS, g, kvs[g])

    if stop == 'B':
        return finalize()
    S.barrier()
    A.top = PW
    ACC0 = A.top
    acc_n = A.f32(2 * NCOL).rearrange("p (c n) -> p c n", c=2)
    acc_d = A.f32(2 * NCOL).rearrange("p (c n) -> p c n", c=2)
    acc_all = arena[:, ACC0:ACC0 + 4 * NCOL].rearrange("p (x c n) -> p x c n", x=2, c=2)
    NPT = 2
    PTb = [A.bf(1024).rearrange("p (h k q) -> p h k q", h=4, k=2) for _ in range(NPT)]
    BPT = [Buf("PT%d" % i) for i in range(NPT)]
    pt_i = [0]
    BOUT0 = A.top
    boutT = A.bf(2 * NCOL).rearrange("p (c n) -> p c n", c=2)
    BboutT = Buf("boutT")
    ATT_TOP = A.top
    A.top = BOUT0
    ckb = [[A.bf(512) for _ in range(3)] for _ in range(2)]
    Bckb = [[Buf("ckb%d%d" % (i, g)) for g in range(3)] for i in range(2)]
    KTc = [A.bf(6 * 128).rearrange("p (c n) -> p c n", c=6) for _ in range(2)]
    BKTc = [Buf("KTc%d" % i) for i in range(2)]
    PTs = [A.bf(16) for _ in range(2)]
    BPTs = [Buf("PTs%d" % i) for i in range(2)]
    prodf = A.f32(6 * 16).rearrange("p (c n) -> p c n", c=6)
    pself = A.f32(6 * 16).rearrange("p (c n) -> p c n", c=6)
    Bpr = Buf("prod")
    assert A.top <= NF, A.top
    acc_hist = {0: [], 1: [], 2: [], "x": [Buf("accx")], "s": []}
    mpair_main = cst_b[:, 1:3, :]
    mpair_edge = cst_b[:, 3:5, :]

    def acc_update(pair, Bn, Bd, nq, cols, first, key):
        if key == "x":
            rd_prev, wr = acc_hist["x"], acc_hist["x"]
        else:
            nb = Buf("acc%s" % str(key))
            rd_prev = [] if (key == "s" or key == 0) else acc_hist[key - 1]
            wr = [nb]
            acc_hist[key].append(nb)
        p4 = pair.rearrange("p (x h q) -> p x h q", x=2, h=4)
        for h2 in range(2):
            i_ap = p4[h2 * 64:(h2 + 1) * 64, :, h2::2, 0:nq]
            o_ap = acc_all[h2 * 64:(h2 + 1) * 64, :, :, cols]
            if first:
                dve(lambda e, i_ap=i_ap, o_ap=o_ap: e.tensor_copy(out=o_ap, in_=i_ap), [Bn, Bd] + rd_prev, wr)
            else:
                dve(lambda e, i_ap=i_ap, o_ap=o_ap: e.tensor_tensor(out=o_ap, in0=i_ap, in1=o_ap, op=ALU.add), [Bn, Bd] + rd_prev, wr)

    def band_p1(g, qsrc, Bq, qcols, nq, chunks, mpair):
        pi = pt_i[0] % NPT
        pt_i[0] += 1
        PT, Bp = PTb[pi], BPT[pi]
        b0, B0 = nextbank()
        b1, B1 = nextbank()
        sb = [b0, b1]
        SBf = [B0, B1]
        for h in range(4):
            c, h2 = h // 2, h % 2
            rows = slice(h2 * 64, (h2 + 1) * 64)
            for ci, (Kap, BK, vi, nk, mask) in enumerate(chunks):
                o = sb[h2][0:nk, (c * 2 + ci) * 128:(c * 2 + ci) * 128 + nq]
                mm(o, Kap[rows, c, :], qsrc[rows, g * 2 + c, qcols], True, True, [BK, Bq], [SBf[h2]])
        full = (nq == 128 and len(chunks) == 2 and all(ch[3] == 128 for ch in chunks))
        if full:
            for h2 in range(2):
                src = sb[h2].rearrange("p (h k q) -> p h k q", h=2, k=2)
                act(PT[:, h2::2, :, :], src, AF.Exp, [SBf[h2]], [Bp], scale=0.125)
            mb = mpair.unsqueeze(1).to_broadcast([128, 4, 2, 128])
            dve(lambda e, PT=PT, mb=mb: e.tensor_tensor(out=PT, in0=PT, in1=mb, op=ALU.mult), [Bp, Bc], [Bp])
        else:
            for ci, (Kap, BK, vi, nk, mask) in enumerate(chunks):
                for h2 in range(2):
                    src = sb[h2][0:nk, :].rearrange("p (h k q) -> p h k q", h=2, k=2)[:, :, ci, 0:nq]
                    act(PT[0:nk, h2::2, ci, 0:nq], src, AF.Exp, [SBf[h2]], [Bp], scale=0.125)
                if mask is not None:
                    for h in range(4):
                        dve(lambda e, h=h, ci=ci, nk=nk, mask=mask, PT=PT: e.tensor_tensor(out=PT[0:nk, h, ci, 0:nq], in0=PT[0:nk, h, ci, 0:nq], in1=mask[0:nk, 0:nq], op=ALU.mult), [Bp, Bc], [Bp])
        return PT, Bp

    def band_p2(PT, Bp, nq, chunks, acc_cols, first, key):
        nch = len(chunks)
        bn, Bn, bd, Bd, pair = nextpair()
        for h in range(4):
            c = h // 2
            for ci, (Kap, BK, vi, nk, mask) in enumerate(chunks):
                mm(bn[:, h * 128:h * 128 + nq], Vb[0:nk, vi, c * 128:(c + 1) * 128], PT[0:nk, h, ci, 0:nq], ci == 0, ci == nch - 1, [BV[vi], Bp], [Bn])
            for ci, (Kap, BK, vi, nk, mask) in enumerate(chunks):
                mm(bd[:, h * 128:h * 128 + nq], ones_b[0:nk, :], PT[0:nk, h, ci, 0:nq], ci == 0, ci == nch - 1, [Bc, Bp], [Bd])
        acc_update(pair, Bn, Bd, nq, acc_cols, first, key)

    def kslice(g, lt0, step, n):
        a = lt0 - KBASE[g]
        return KT[g][:, :, a:a + step * (n - 1) + 1:step]

    def samp_s0(b):
        sl = b % 2
        for g in range(3):
            L, d = WIN[g]
            dma("pool", ckb[sl][g], ck[g][b, 0:L:d, :], [], [Bckb[sl][g]], sem="ck%d%d" % (sl, g))

    def samp_s1(b):
        sl = b % 2
        bk, Bk = nextbank()
        pt = bk.bitcast(BF16).rearrange("p (c t) -> p c t", c=8)
        for g in range(3):
            for c in range(2):
                S.op("pe", lambda e, c=c, g=g, pt=pt, sl=sl: e.transpose(out=pt[:, g * 2 + c, :], in_=ckb[sl][g][:, c * 128:(c + 1) * 128], identity=ident_b), reads=[Bckb[sl][g], Bc], writes=[Bk])
        tcopy("dve", KTc[sl], pt[:, 0:6, :], [Bk], [BKTc[sl]])

    def samp_s2(b):
        sl = b % 2
        col = SM0 + 2 + b
        bs0, BS0 = nextbank()
        bs1, BS1 = nextbank()
        bsx = [bs0, bs1]
        BSx = [BS0, BS1]
        for g in range(3):
            for h in range(4):
                c, h2 = h // 2, h % 2
                rows = slice(h2 * 64, (h2 + 1) * 64)
                mm(bsx[h2][:, g * 2 + c:g * 2 + c + 1], KTc[sl][rows, g * 2 + c, :], QT[rows, g * 2 + c, col:col + 1], True, True, [BKTc[sl], BQT], [BSx[h2]])
        PTs3 = PTs[sl][:, 0:12].rearrange("p (g c t) -> p g c t", g=3, c=2)
        for h2 in range(2):
            act(PTs3[:, :, :, h2], bsx[h2][:, 0:6].rearrange("p (g c) -> p g c", g=3), AF.Exp, [BSx[h2]], [BPTs[sl]], scale=0.125)

    def samp_s3(b):
        sl = b % 2
        col = SM0 + 2 + b
        bn, Bn, bd, Bd, pair = nextpair()
        for h in range(4):
            c = h // 2
            for g in range(3):
                mm(bn[:, h * 128:h * 128 + 1], ckb[sl][g][:, 256 + c * 128:256 + (c + 1) * 128], PTs[sl][:, g * 4 + h:g * 4 + h + 1], g == 0, g == 2, [Bckb[sl][g], BPTs[sl]], [Bn])
            for g in range(3):
                mm(bd[:, h * 128:h * 128 + 1], ones_b, PTs[sl][:, g * 4 + h:g * 4 + h + 1], g == 0, g == 2, [Bc, BPTs[sl]], [Bd])
        acc_update(pair, Bn, Bd, 1, slice(col, col + 1), True, "s")

    S.label = 'att-main'
    mblocks = []
    for g in range(3):
        step = WIN[g][1]
        if g == 0:
            starts = [t * 128 for t in range(16)]
        elif g == 1:
            starts = [512 * i + r for i in range(4) for r in range(4)]
        else:
            starts = list(range(16))
        for m0 in starts:
            lt_q = HX + m0
            lt_p = lt_q - 128 * step
            if lt_p < HX:
                vp = vidx[(g, lt_p, step, "h")]
                mp = mpair_edge
            else:
                vp = vidx[(g, lt_p - HX, step, "m")]
                mp = mpair_main
            vc = vidx[(g, m0, step, "m")]
            chunks = [(kslice(g, lt_p, step, 128), BKT[g], vp, 128, None),
                      (kslice(g, lt_q, step, 128), BKT[g], vc, 128, None)]
            qc = slice(m0, m0 + step * 127 + 1, step)
            mblocks.append((g, qc, chunks, mp))
    samp_s0(0)
    pend = band_p1(mblocks[0][0], QT, BQT, mblocks[0][1], 128, mblocks[0][2], mblocks[0][3])
    for m, (g, qc, chunks, mp) in enumerate(mblocks):
        nxt = None
        if m + 1 < len(mblocks):
            g2_, qc2, ch2, mp2 = mblocks[m + 1]
            nxt = band_p1(g2_, QT, BQT, qc2, 128, ch2, mp2)
        band_p2(pend[0], pend[1], 128, chunks, qc, g == 0, g)
        pend = nxt
        b, k = m // 3, m % 3
        if k == 0:
            samp_s1(b)
            if b + 1 < NS:
                samp_s0(b + 1)
        elif k == 1:
            samp_s2(b)
        else:
            samp_s3(b)
    if stop == 'att-main':
        return finalize()
    S.label = 'att-ext2'
    vp = vidx[(0, 2046, 1, "h")]
    vc = vidx[(0, 2174, 1, "h")]
    chx = [(kslice(0, 2046, 1, 128), BKT[0], vp, 128, mprev_b), (kslice(0, 2174, 1, 2), BKT[0], vc, 2, mcur_b)]
    p_ = band_p1(0, QT, BQT, slice(SM0, SM0 + 2), 2, chx, None)
    band_p2(p_[0], p_[1], 2, chx, slice(SM0, SM0 + 2), True, "x")
    for g in (1, 2):
        step = WIN[g][1]
        for j in range(2):
            ltq = 2174 + j
            vp = vidx[(g, ltq - 128 * step, step, "h")]
            vc = vidx[(g, ltq, 1, "h")]
            chx = [(kslice(g, ltq - 128 * step, step, 128), BKT[g], vp, 128, None), (kslice(g, ltq, 1, 1), BKT[g], vc, 1, None)]
            p_ = band_p1(g, QT, BQT, slice(SM0 + j, SM0 + j + 1), 1, chx, None)
            band_p2(p_[0], p_[1], 1, chx, slice(SM0 + j, SM0 + j + 1), False, "x")
    Bacc = Buf("accall")
    S.op("dve", lambda e: e.memset(prodf[:, 0, 0:1], 0.0), reads=[b_ for k_ in acc_hist for b_ in acc_hist[k_]], writes=[Bacc, Bpr])
    S.label = 'att-self'
    dve(lambda e: e.tensor_tensor(out=prodf, in0=QT[:, :, SM0 + 2:SM0 + 18], in1=KTs[:, :, 2:18], op=ALU.mult), [BQT, BKTs], [Bpr])
    bk, Bk = nextbank()
    mm(bk[:, 0:96], blockones_f, prodf.rearrange("p c n -> p (c n)"), True, True, [Bc, Bpr], [Bk])
    act(pself.rearrange("p c n -> p (c n)"), bk[:, 0:96], AF.Exp, [Bk], [Bpr], scale=0.125)
    dve(lambda e: e.tensor_tensor(out=prodf, in0=pself, in1=VTs[:, :, 2:18], op=ALU.mult), [Bpr, BKTs], [Bpr])
    for g in range(3):
        for c in range(2):
            dve(lambda e, g=g, c=c: e.tensor_tensor(out=acc_n[:, c, SM0 + 2:SM0 + 18], in0=acc_n[:, c, SM0 + 2:SM0 + 18], in1=prodf[:, g * 2 + c, :], op=ALU.add), [Bacc, Bpr], [Bacc])
            dve(lambda e, g=g, c=c: e.tensor_tensor(out=acc_d[:, c, SM0 + 2:SM0 + 18], in0=acc_d[:, c, SM0 + 2:SM0 + 18], in1=pself[:, g * 2 + c, :], op=ALU.add), [Bacc, Bpr], [Bacc])
    if stop == 'att-self':
        return finalize()
    S.barrier()
    A.top = P1
    boutT2 = A.bf(2 * NCOL).rearrange("p (c n) -> p c n", c=2)
    assert A.top <= PW
    aoutT = A.bf(4 * NCOL).rearrange("p (c n) -> p c n", c=4)
    BaoutT = Buf("aoutT")
    C_TOP = A.top
    wuv = A.bf(8 * 1024).rearrange("p (c n) -> p c n", c=8)
    Bwuv = Buf("wuv")
    wv_in = w_in.rearrange("(c p) n -> p c n", p=128)
    dma("pool", wuv, wv_in[:, :, 2304:3328], [], [Bwuv], sem="w1")
    S.label = 'att-norm'
    for c in range(2):
        for (c0, n) in slices:
            dve(lambda e, c=c, c0=c0, n=n: e.reciprocal(out=acc_d[:, c, c0:c0 + n], in_=acc_d[:, c, c0:c0 + n]), [Bacc], [Bacc])
            dve(lambda e, c=c, c0=c0, n=n: e.tensor_tensor(out=boutT2[:, c, c0:c0 + n], in0=acc_n[:, c, c0:c0 + n], in1=acc_d[:, c, c0:c0 + n], op=ALU.mult), [Bacc], [BboutT])
    if stop == 'att':
        return finalize()
    S.label = 'C1'
    MT0 = NF - 4 * NCOL
    W2A = MT0 - (8192 + 2048 + 1024)
    wg = arena[:, W2A:W2A + 8192].bitcast(BF16).rearrange("p (c n) -> p c n", c=8)
    wpa = arena[:, W2A + 8192:W2A + 10240].bitcast(BF16).rearrange("p (c n) -> p c n", c=4)
    wpb = arena[:, W2A + 10240:W2A + 11264].bitcast(BF16).rearrange("p (c n) -> p c n", c=2)
    Bwg = Buf("wg")
    dma("pool", wg, wv_in[:, :, 3328:5376], [], [Bwg, Bacc], sem="w2")
    dma("pool", wpa, w_pa.rearrange("(c p) n -> p c n", p=128), [], [Bwg, Bacc], sem="w3")
    dma("pool", wpb, w_pb.rearrange("(c p) n -> p c n", p=128), [], [Bwg, Bacc], sem="w4")
    uT = A.bf(4 * NCOL).rearrange("p (c n) -> p c n", c=4)
    BuT = Buf("uT")
    NG = 3
    gv = [A.f32(512) for _ in range(NG)]
    Bgv = [Buf("gv%d" % i) for i in range(NG)]
    vn = [A.f32(512) for _ in range(NG)]
    Bvn = [Buf("vn%d" % i) for i in range(NG)]
    vnb = [A.bf(512) for _ in range(NG)]
    Bvnb = [Buf("vnb%d" % i) for i in range(NG)]
    uxe = A.bf(4 * 128).rearrange("p (c n) -> p c n", c=4)
    aoe = A.bf(4 * 128).rearrange("p (c n) -> p c n", c=4)
    Buxe = Buf("uxe")
    C1_TOP = A.top
    for (c0, n) in slices:
        for c in range(4):
            proj_fm(uT[:, c, c0:c0 + n], BuT, wuv, Bwuv, c * 128, xnT, BxnT, c0, n, func=AF.Gelu_apprx_tanh)
    for c in range(4):
        proj_fm(uxe[:, c, :], Buxe, wuv, Bwuv, c * 128, xeT, BxeT, 0, 128, func=AF.Gelu_apprx_tanh)
    gi = [0]

    def gmlp_batch(items):
        G = len(gv)
        for g0 in range(0, len(items), G):
            grp = items[g0:g0 + G]
            stis, banks = [], []
            for k, (src, Bsrc, c0, n, sample, u_ap, Bu, out_ap, Bout, vn_dst) in enumerate(grp):
                sti = stat_i[0] % 8
                stat_i[0] += 1
                stis.append(sti)
                bk, Bk = nextbank()
                banks.append((bk, Bk))
                for kc in range(8):
                    mm(bk[0:n, :], src[:, kc, c0:c0 + n], wuv[:, kc, 512:1024], kc == 0, kc == 7, [Bsrc, Bwuv], [Bk])
            for k, (src, Bsrc, c0, n, sample, u_ap, Bu, out_ap, Bout, vn_dst) in enumerate(grp):
                s_ = stat[:n, stis[k], :]
                act(gv[k][0:n, :], banks[k][0][0:n, :], AF.Gelu_apprx_tanh, [banks[k][1]], [Bgv[k], Bstat[stis[k]]], accum_out=s_[:, 4:5])
            for k, (src, Bsrc, c0, n, sample, u_ap, Bu, out_ap, Bout, vn_dst) in enumerate(grp):
                s_ = stat[:n, stis[k], :]
                dve(lambda e, s_=s_: e.tensor_scalar(out=s_[:, 5:6], in0=s_[:, 4:5], scalar1=-1.0 / 512, scalar2=None, op0=ALU.mult), [Bstat[stis[k]]], [Bstat[stis[k]]])
            for k, (src, Bsrc, c0, n, sample, u_ap, Bu, out_ap, Bout, vn_dst) in enumerate(grp):
                s_ = stat[:n, stis[k], :]
                act(vn[k][0:n, :], gv[k][0:n, :], AF.Identity, [Bgv[k], Bstat[stis[k]]], [Bvn[k]], bias=s_[:, 5:6], scale=1.0)
            for k, (src, Bsrc, c0, n, sample, u_ap, Bu, out_ap, Bout, vn_dst) in enumerate(grp):
                s_ = stat[:n, stis[k], :]
                act(gv[k][0:n, :], vn[k][0:n, :], AF.Square, [Bvn[k]], [Bgv[k], Bstat[stis[k]]], accum_out=s_[:, 0:1])
            for k, (src, Bsrc, c0, n, sample, u_ap, Bu, out_ap, Bout, vn_dst) in enumerate(grp):
                s_ = stat[:n, stis[k], :]
                dve(lambda e, s_=s_: e.tensor_scalar(out=s_[:, 1:2], in0=s_[:, 0:1], scalar1=1.0 / 512, scalar2=EPS, op0=ALU.mult, op1=ALU.add), [Bstat[stis[k]]], [Bstat[stis[k]]])
            for k, (src, Bsrc, c0, n, sample, u_ap, Bu, out_ap, Bout, vn_dst) in enumerate(grp):
                s_ = stat[:n, stis[k], :]
                act(s_[:, 2:3], s_[:, 1:2], AF.Ln, [Bstat[stis[k]]], [Bstat[stis[k]]])
            for k, (src, Bsrc, c0, n, sample, u_ap, Bu, out_ap, Bout, vn_dst) in enumerate(grp):
                s_ = stat[:n, stis[k], :]
                act(s_[:, 3:4], s_[:, 2:3], AF.Exp, [Bstat[stis[k]]], [Bstat[stis[k]]], scale=-0.5)
            for k, (src, Bsrc, c0, n, sample, u_ap, Bu, out_ap, Bout, vn_dst) in enumerate(grp):
                s_ = stat[:n, stis[k], :]
                dve(lambda e, k=k, n=n, s_=s_: e.scalar_tensor_tensor(out=vn[k][0:n, :], in0=vn[k][0:n, :], scalar=s_[:, 3:4], in1=lng_bc[0:n, :], op0=ALU.mult, op1=ALU.mult), [Bvn[k], Bstat[stis[k]], Bc], [Bvn[k]])
            for k, (src, Bsrc, c0, n, sample, u_ap, Bu, out_ap, Bout, vn_dst) in enumerate(grp):
                dve(lambda e, k=k, n=n: e.tensor_tensor(out=vn[k][0:n, :], in0=vn[k][0:n, :], in1=lnb_bc[0:n, :], op=ALU.add), [Bvn[k], Bc], [Bvn[k]])
            for k, (src, Bsrc, c0, n, sample, u_ap, Bu, out_ap, Bout, vn_dst) in enumerate(grp):
                tcopy("dve", vnb[k][0:n, :], vn[k][0:n, :], [Bvn[k]], [Bvnb[k]])
                if vn_dst is not None:
                    dma("sp" if n == 128 else "pool", vn_dst, vn[k][0:n, :], [Bvn[k]], [], sem=("vo%d" if n == 128 else "vp%d") % k)
                    out_bufs.append(Bvn[k])
            mbanks = []
            for k, (src, Bsrc, c0, n, sample, u_ap, Bu, out_ap, Bout, vn_dst) in enumerate(grp):
                bm, Bm = nextbank()
                mbanks.append((bm, Bm))
                nt = 16 if sample else 128
                for g in range(4):
                    o = bm[:, g * 128:g * 128 + nt]
                    if sample:
                        mm(o, vnb[k][0:n, g * 128:(g + 1) * 128], D16[0:16, g, :], True, False, [Bvnb[k], Bc], [Bm])
                        mm(o, ones_b[0:1, :], bsp0_b[0:1, g * 16:(g + 1) * 16], False, True, [Bc], [Bm])
                    else:
                        mm(o, vnb[k][0:n, g * 128:(g + 1) * 128], WmT[:, g, :], True, False, [Bvnb[k], Bc], [Bm])
                        mm(o, ones_b[0:1, :], bsp_b[0:1, g * 128:(g + 1) * 128], False, True, [Bc], [Bm])
            for k, (src, Bsrc, c0, n, sample, u_ap, Bu, out_ap, Bout, vn_dst) in enumerate(grp):
                nt = 16 if sample else 128
                m4 = mbanks[k][0].rearrange("p (g t) -> p g t", g=4)[:, :, 0:nt]
                dve(lambda e, m4=m4, out_ap=out_ap, u_ap=u_ap: e.tensor_tensor(out=out_ap, in0=m4, in1=u_ap, op=ALU.mult), [mbanks[k][1], Bu], [Bout])

    gitems = []
    for t in range(16):
        cs = slice(t * 128, (t + 1) * 128)
        gitems.append((xnT, BxnT, t * 128, 128, False, uT[:, :, cs], BuT, aoutT[:, :, cs], BaoutT, vch if t == 15 else None))
    gitems.append((xeT, BxeT, 0, 128, False, uxe, Buxe, aoe, Buxe, None))
    gitems.append((xnT, BxnT, SM0 + 2, NS, True, uT[:, :, SM0 + 2:SM0 + 18], BuT, aoutT[:, :, SM0 + 2:SM0 + 18], BaoutT, vchs))
    gmlp_batch(gitems)
    tcopy("dve", aoutT[:, :, SM0:SM0 + 2], aoe[:, :, 126:128], [Buxe], [BaoutT])
    if stop == 'C1':
        return finalize()
    S.label = 'C2a'
    S.barrier()
    A.top = C_TOP
    assert C1_TOP <= W2A, (C1_TOP, W2A)
    tg = [A.f32(512) for _ in range(2)]
    Btg = [Buf("tg0"), Buf("tg1")]
    t1 = [A.f32(512) for _ in range(2)]
    Bt1 = [Buf("t10"), Buf("t11")]
    mt = arena[:, MT0:NF].bitcast(BF16).rearrange("p (c n) -> p c n", c=8)
    Bmt = Buf("mT")
    oc_i = [0]
    for si, (c0, n) in enumerate(slices):
        for oc in range(8):
            i = oc_i[0] % 2
            oc_i[0] += 1
            ba, Ba = nextbank()
            for kc in range(4):
                mm(ba[:, 0:n], wpa[:, kc, oc * 128:(oc + 1) * 128], aoutT[:, kc, c0:c0 + n], kc == 0, kc == 3, [Bwg, BaoutT], [Ba])
            bb, Bb = nextbank()
            for kc in range(2):
                mm(bb[:, 0:n], wpb[:, kc, oc * 128:(oc + 1) * 128], boutT2[:, kc, c0:c0 + n], kc == 0, kc == 1, [Bwg, BboutT], [Bb])
            proj_fm(tg[i][:, 0:n], Btg[i], wg, Bwg, oc * 128, xnT, BxnT, c0, n, func=AF.Tanh, scale=0.5)
            dve(lambda e, i=i, ba=ba, n=n: e.scalar_tensor_tensor(out=t1[i][:, 0:n], in0=tg[i][:, 0:n], scalar=1.0, in1=ba[:, 0:n], op0=ALU.add, op1=ALU.mult), [Btg[i], Ba], [Bt1[i]])
            proj_fm(tg[i][:, 0:n], Btg[i], wg, Bwg, 1024 + oc * 128, xnT, BxnT, c0, n, func=AF.Tanh, scale=0.5)
            dve(lambda e, i=i, bb=bb, n=n: e.scalar_tensor_tensor(out=tg[i][:, 0:n], in0=tg[i][:, 0:n], scalar=1.0, in1=bb[:, 0:n], op0=ALU.add, op1=ALU.mult), [Btg[i], Bb], [Btg[i]])
            dve(lambda e, i=i, n=n, oc=oc, c0=c0: e.tensor_tensor(out=mt[:, oc, c0:c0 + n], in0=t1[i][:, 0:n], in1=tg[i][:, 0:n], op=ALU.add), [Bt1[i], Btg[i]], [Bmt])

    if stop == 'C2a':
        return finalize()
    S.label = 'C2b'
    S.barrier()
    A.top = CONST_TOP
    hnT = A.bf(8 * NCOL).rearrange("p (c n) -> p c n", c=8)
    BhnT = Buf("hnT")
    hbuf = A.f32(16 * 1024).rearrange("p (t n) -> p t n", t=16)
    hsm = A.f32(1024)
    Bh = [Buf("h%d" % t) for t in range(16)]
    Bhsm = Buf("hsm")
    HB_TOP = A.top
    wo = A.bf(8 * 1024).rearrange("p (c n) -> p c n", c=8)
    Bwo = Buf("wo")
    dma("pool", wo, w_out.rearrange("(c p) n -> p c n", p=128), [], [Bwo], sem="w5")
    NX = 3
    xt = [A.f32(1024) for _ in range(NX)]
    Bxt = [Buf("xt%db" % i) for i in range(NX)]
    xb = [A.bf(1024) for _ in range(NX)]
    Bxb = [Buf("xb%db" % i) for i in range(NX)]
    assert A.top <= MT0, (A.top, MT0)

    def mk_pre(t, c0o, m):
        def pre(k):
            if t >= 0:
                hdst, Bhd = hbuf[:, t, :], Bh[t]
            else:
                hdst, Bhd = hsm, Bhsm
            for half in range(2):
                bk, Bk = nextbank()
                for kc in range(8):
                    mm(bk[0:m, :], mt[:, kc, c0o:c0o + m], wo[:, kc, half * 512:(half + 1) * 512], kc == 0, kc == 7, [Bmt, Bwo], [Bk])
                dve(lambda e, bk=bk, half=half, k=k, hdst=hdst: e.scalar_tensor_tensor(out=hdst[0:m, half * 512:(half + 1) * 512], in0=bk[0:m, :], scalar=0.5, in1=xt[k][0:m, half * 512:(half + 1) * 512], op0=ALU.mult, op1=ALU.add), [Bk, Bxt[k]], [Bhd])
            return (hdst, Bhd)
        return pre

    citems = [(xall[HX + t * 128:HX + (t + 1) * 128, :], 128, hnT[:, :, t * 128:(t + 1) * 128], BhnT, gf_bc, mk_pre(t, t * 128, 128)) for t in range(16)]
    norm_T_batch(citems, xt, Bxt, xb, Bxb, "xc")
    dma("sp", xt[0][0:2, :], xall[HX - 2:HX, :], [], [Bxt[0]], sem="xc0")
    dma("sp", xt[0][2:18, :], xs, [], [Bxt[0]], sem="xq0")
    norm_T_batch([(None, 18, hnT[:, :, SM0:SM0 + 18], BhnT, gf_bc, mk_pre(-1, SM0, 18))], xt, Bxt, xb, Bxb, "xc")
    if stop == 'C2b':
        return finalize()
    S.label = 'D'
    S.barrier()
    A.top = HB_TOP
    HT = 1024
    KG = [(0, 4), (4, 4), (8, 4), (12, 4), (16, 3), (19, 3)]
    prodT = A.bf(4 * HT).rearrange("p (j n) -> p j n", j=4)
    BprodT = [Buf("prodT%d" % i) for i in range(6)]
    prodS = A.bf(22 * 18).rearrange("p (j n) -> p j n", j=22)
    BprodS = Buf("prodS")
    upb = [[A.f32(2 + HT) for _ in range(2)] for _ in range(2)]
    Bupb = [[Buf("up%d%d" % (a_, b_)) for b_ in range(2)] for a_ in range(2)]
    ups = [[A.f32(18) for _ in range(2)] for _ in range(3)]
    cbuf = [[A.f32(1024) for _ in range(2)] for _ in range(3)]
    Bcb = [[Buf("c%d%d" % (a_, b_)) for b_ in range(2)] for a_ in range(3)]
    cs_ = [[A.f32(18) for _ in range(2)] for _ in range(3)]
    Bcs = [[Buf("cs%d%d" % (a_, b_)) for b_ in range(2)] for a_ in range(3)]
    hist = A.f32(44 * 2).rearrange("p (c n) -> p c n", c=44)
    Bhist = Buf("hist")
    hsT = A.f32(44 * 2 * 16).rearrange("p (c k n) -> p c k n", c=44, k=2)
    BhsT = Buf("hsT")
    NWU = 3
    wup = [A.bf(8 * 256).rearrange("p (c n) -> p c n", c=8) for _ in range(NWU)]
    Bwup = [Buf("wup%d" % i) for i in range(NWU)]
    wdns = [A.bf(4 * 1024).rearrange("p (j n) -> p j n", j=4) for _ in range(2)]
    Bwdns = [Buf("wdn0"), Buf("wdn1")]
    wdi = [0]
    wdq = []

    def load_wdn(j0, gs):
        i = wdi[0] % 2
        wdi[0] += 1
        dma("pool", wdns[i][:, 0:gs, :], w_dn.rearrange("(j p) n -> p j n", p=128)[:, j0:j0 + gs, :], [], [Bwdns[i]], sem="wd%d" % i)
        wdq.append((wdns[i], Bwdns[i]))
    upst = [A.f32(256) for _ in range(2)]
    Bupst = [Buf("upst0"), Buf("upst1")]
    scst = cbuf[2][0].rearrange("p (k n) -> p k n", k=2)
    Bscst = Bcb[2][0]
    assert A.top <= NF, A.top
    print('D_TOP', A.top, 'HB_TOP', HB_TOP)
    for q in range(11):
        dma("sp", scst[0:16, :, :], sconv[:, :, q * 512:(q + 1) * 512], [], [Bscst], sem="sc")
        for cc in range(4):
            ch = q * 4 + cc
            bk, Bk = nextbank()
            for k in range(2):
                mm(bk[:, k * 16:(k + 1) * 16], scst[0:16, k, cc * 128:(cc + 1) * 128], ident_f[0:16, 0:16], True, True, [Bscst, Bc], [Bk])
            tcopy("dve", hsT[:, ch, :, :], bk[:, 0:32].rearrange("p (k n) -> p k n", k=2), [Bk], [BhsT])
        dma("pool", convs[:, 0, q * 512:(q + 1) * 512], scst[0:16, 1, :], [Bscst], [], sem="sco")
    out_bufs.append(Bscst)

    wv = w_up.rearrange("(c p) n -> p c n", p=128)
    wslot = {}
    wi = [0]

    def load_wup(j):
        wsl = wi[0] % NWU
        wi[0] += 1
        w_, Bw_ = wup[wsl], Bwup[wsl]
        dma("pool", w_[:, :, 0:128], wv[:, :, j * 128:(j + 1) * 128], [], [Bw_], sem="wu%da" % wsl)
        dma("pool", w_[:, :, 128:256], wv[:, :, 2816 + j * 128:2816 + (j + 1) * 128], [], [Bw_], sem="wu%db" % wsl)
        return w_, Bw_

    def stageA(H, j):
        base = H * HT
        w_, Bw_ = wslot[(H, j)]
        sl = j % 2
        s3 = j % 3
        for gv_ in range(2):
            ch = gv_ * 22 + j
            ub, Bub = upb[sl][gv_], Bupb[sl][gv_]
            if H == 0:
                if gv_ == 0:
                    bks_, Bks_ = nextbank()
                so = gv_ * 32
                for kc in range(8):
                    mm(bks_[:, so:so + 18], w_[:, kc, gv_ * 128:(gv_ + 1) * 128], hnT[:, kc, SM0:SM0 + 18], kc == 0, kc == 7, [Bw_, BhnT], [Bks_])
                if gv_ == 1:
                    for kc in range(8):
                        mm(bks_[0:20, 64:320], hnT[:, kc, NM - 2:NM + 18], w_[:, kc, :], kc == 0, kc == 7, [BhnT, Bw_], [Bks_])
                    for g2_ in range(2):
                        ub2, Bub2 = upb[sl][g2_], Bupb[sl][g2_]
                        ch2 = g2_ * 22 + j
                        so2 = g2_ * 32
                        act(ub2[:, 0:2], bks_[:, so2:so2 + 2], AF.Copy, [Bks_, Bc], [Bub2], scale=flag[:, 0:1])
                        act(ups[s3][g2_], bks_[:, so2:so2 + 18], AF.Copy, [Bks_], [Bcs[s3][g2_]])
                        cs = cs_[s3][g2_]
                        act(cs, ups[s3][g2_], AF.Identity, [Bcs[s3][g2_], Bc], [Bcs[s3][g2_]], scale=cwT[:, 2, ch2:ch2 + 1], bias=cwT[:, 3, ch2:ch2 + 1])
                        for k in range(2):
                            dve(lambda e, cs=cs, ch2=ch2, k=k: e.scalar_tensor_tensor(out=cs[:, 2:18], in0=hsT[:, ch2, k, :], scalar=cwT[:, k, ch2:ch2 + 1], in1=cs[:, 2:18], op0=ALU.mult, op1=ALU.add), [BhsT, Bc, Bcs[s3][g2_]], [Bcs[s3][g2_]])
                    ui = j % 2
                    tcopy("dve", upst[ui][0:20, :], bks_[0:20, 64:320], [Bks_], [Bupst[ui]])
                    for g2_ in range(2):
                        cc0 = g2_ * 2816 + j * 128
                        dma("pool", convp[:, cc0:cc0 + 128], upst[ui][0:2, g2_ * 128:(g2_ + 1) * 128], [Bupst[ui]], [], sem="uo%d" % ui)
                        dma("pool", convs[:, 1, cc0:cc0 + 128], upst[ui][4:20, g2_ * 128:(g2_ + 1) * 128], [Bupst[ui]], [], sem="uo%d" % ui)
                    out_bufs.append(Bupst[ui])
            else:
                tcopy("dve", ub[:, 0:2], hist[:, ch, :], [Bhist], [Bub])
            for s2 in range(2):
                c0 = base + s2 * 512
                bk, Bk = nextbank()
                for kc in range(8):
                    mm(bk, w_[:, kc, gv_ * 128:(gv_ + 1) * 128], hnT[:, kc, c0:c0 + 512], kc == 0, kc == 7, [Bw_, BhnT], [Bk])
                act(ub[:, 2 + s2 * 512:2 + (s2 + 1) * 512], bk, AF.Copy, [Bk], [Bub])
            if H == 0:
                tcopy("dve", hist[:, ch, :], ub[:, HT:HT + 2], [Bub], [Bhist])
        for gv_ in range(2):
            ch = gv_ * 22 + j
            ub, Bub = upb[sl][gv_], Bupb[sl][gv_]
            cc, Bcc = cbuf[s3][gv_], Bcb[s3][gv_]
            act(cc, ub[:, 2:2 + HT], AF.Identity, [Bub, Bc], [Bcc], scale=cwT[:, 2, ch:ch + 1], bias=cwT[:, 3, ch:ch + 1])
        for gv_ in range(2):
            ch = gv_ * 22 + j
            ub, Bub = upb[sl][gv_], Bupb[sl][gv_]
            cc, Bcc = cbuf[s3][gv_], Bcb[s3][gv_]
            for k in range(2):
                dve(lambda e, cc=cc, ub=ub, k=k, ch=ch: e.scalar_tensor_tensor(out=cc, in0=ub[:, k:k + HT], scalar=cwT[:, k, ch:ch + 1], in1=cc, op0=ALU.mult, op1=ALU.add), [Bub, Bc, Bcc], [Bcc])

    def stageB(H, j, jj):
        s3 = j % 3
        cg, cv = cbuf[s3][0], cbuf[s3][1]
        act(cg, cg, AF.Gelu_apprx_tanh, [Bcb[s3][0]], [Bcb[s3][0]])
        dve(lambda e, cg=cg, cv=cv, jj=jj: e.tensor_tensor(out=prodT[:, jj, :], in0=cg, in1=cv, op=ALU.mult), [Bcb[s3][0], Bcb[s3][1]], [BprodT[jj]])
        if H == 0:
            act(cs_[s3][0], cs_[s3][0], AF.Gelu_apprx_tanh, [Bcs[s3][0]], [Bcs[s3][0]])
            dve(lambda e, s3=s3, j=j: e.tensor_tensor(out=prodS[:, j, :], in0=cs_[s3][0], in1=cs_[s3][1], op=ALU.mult), [Bcs[s3][0], Bcs[s3][1]], [BprodS])

    def wdown_group(H, j0, gs):
        wdn, Bwdn = wdq.pop(0)
        for tb in range(4):
            ths = [(tb * 2 + q_, half) for q_ in range(2) for half in range(2)]
            bks = [nextbank() for _ in ths]
            for (tl, half), (bk, Bk) in zip(ths, bks):
                for jj in range(gs - 1):
                    mm(bk, prodT[:, jj, tl * 128:(tl + 1) * 128], wdn[:, jj, half * 512:(half + 1) * 512], jj == 0, False, [BprodT[jj], Bwdn], [Bk])
            for (tl, half), (bk, Bk) in zip(ths, bks):
                jj = gs - 1
                mm(bk, prodT[:, jj, tl * 128:(tl + 1) * 128], wdn[:, jj, half * 512:(half + 1) * 512], False, True, [BprodT[jj], Bwdn], [Bk])
            for (tl, half), (bk, Bk) in zip(ths, bks):
                t = H * 8 + tl
                dve(lambda e, bk=bk, t=t, half=half: e.tensor_tensor(out=hbuf[:, t, half * 512:(half + 1) * 512], in0=bk, in1=hbuf[:, t, half * 512:(half + 1) * 512], op=ALU.add), [Bk, Bh[t]], [Bh[t]])
        if H == 0:
            for half in range(2):
                bk, Bk = nextbank()
                for jj in range(gs):
                    mm(bk[0:18, :], prodS[:, j0 + jj, :], wdn[:, jj, half * 512:(half + 1) * 512], jj == 0, jj == gs - 1, [BprodS, Bwdn], [Bk])
                dve(lambda e, bk=bk, half=half: e.tensor_tensor(out=hsm[0:18, half * 512:(half + 1) * 512], in0=bk[0:18, :], in1=hsm[0:18, half * 512:(half + 1) * 512], op=ALU.add), [Bk, Bhsm], [Bhsm])

    def final_rows(haps, dsts):
        stis = []
        for k, (hap, m, Bh_) in enumerate(haps):
            sti = stat_i[0] % 8
            stat_i[0] += 1
            stis.append(sti)
            scr = cbuf[k % 3][(k // 3) % 2]
            Bscr = Bcb[k % 3][(k // 3) % 2]
            act(scr.bitcast(BF16)[0:m, 0:1024], hap, AF.Square, [Bh_], [Bscr, Bstat[sti]], accum_out=stat[:m, sti, 0:1])
        rs = []
        for k, (hap, m, Bh_) in enumerate(haps):
            rs.append(rstd_of(stat[:m, stis[k], 0:1], m, 1.0 / 1024, stis[k], Bstat[stis[k]]))
        for k, (hap, m, Bh_) in enumerate(haps):
            dve(lambda e, hap=hap, r=rs[k], m=m: e.scalar_tensor_tensor(out=hap, in0=hap, scalar=r, in1=gfin_bc[0:m, :], op0=ALU.mult, op1=ALU.mult), [Bh_, Bstat[stis[k]], Bc], [Bh_])
        for k, (hap, m, Bh_) in enumerate(haps):
            dst, r0 = dsts[k]
            dma("sp" if m == 128 else "pool", dst, hap[r0:m, :], [Bh_], [], sem=("yo%d" if m == 128 else "yp%d") % (k % 4))
            out_bufs.append(Bh_)

    for H in range(2):
        order = [(j0, gs, jj) for (j0, gs) in KG for jj in range(gs)]
        PRE = 3
        jof = lambda q_: order[q_][0] + order[q_][2]
        for q_ in range(min(PRE, len(order))):
            wslot[(H, jof(q_))] = load_wup(jof(q_))
        doneA = set()

        def doA(q_):
            if q_ < len(order) and q_ not in doneA:
                doneA.add(q_)
                stageA(H, jof(q_))

        load_wdn(*KG[0])
        gnext = [1]
        doA(0)
        doA(1)
        for idx, (j0, gs, jj) in enumerate(order):
            j = j0 + jj
            if idx + PRE < len(order):
                wslot[(H, jof(idx + PRE))] = load_wup(jof(idx + PRE))
            if jj == gs - 1:
                stageB(H, j, jj)
                doA(idx + 2)
                doA(idx + 3)
                if gnext[0] < len(KG):
                    load_wdn(*KG[gnext[0]])
                    gnext[0] += 1
                wdown_group(H, j0, gs)
            else:
                doA(idx + 2)
                stageB(H, j, jj)
        final_rows([(hbuf[:, H * 8 + tl, :], 128, Bh[H * 8 + tl]) for tl in range(8)],
                   [(y[(H * 8 + tl) * 128:(H * 8 + tl + 1) * 128, :], 0) for tl in range(8)])
        if H == 0:
            final_rows([(hsm[0:18, :], 18, Bhsm)], [(ys, 2)])

    return finalize()


_NC_CACHE = {}


def make_in_maps(x_prompt, x_sample, cache_kv_w128, cache_kv_w512, cache_kv_w2048, state_conv_ffn,
                 norm_mix, w_in, ln_v_gain, ln_v_bias, w_spatial, b_spatial, w_proj_a, w_proj_b,
                 w_out, norm_ffn, w_up, conv_w, conv_b, w_down, norm_final, cores=None):
    f = lambda a: np.ascontiguousarray(np.asarray(a, dtype=np.float32))
    x_prompt = f(x_prompt)
    B, SEQ, D = x_prompt.shape
    xp = np.zeros((B, HX + SEQ, D), np.float32)
    xp[:, HX:] = x_prompt
    k = np.arange(128)[:, None]
    q = np.arange(128)[None, :]
    mcur = (k <= q).astype(np.float32)
    mprev = (k >= q).astype(np.float32)
    blockones = ((k // 64) == (q // 64)).astype(np.float32)
    caches = [f(cache_kv_w128)[0], f(cache_kv_w512)[0], f(cache_kv_w2048)[0]]
    common = {
        "w_in": f(w_in)[0], "w_pa": f(w_proj_a)[0], "w_pb": f(w_proj_b)[0], "w_out": f(w_out)[0],
        "w_up": f(w_up)[0], "w_dn": f(w_down)[0],
        "vec8": np.concatenate([f(norm_mix)[0].reshape(8, 128), f(norm_ffn)[0].reshape(8, 128)], 0),
        "nfin": f(norm_final), "lng": f(ln_v_gain)[0], "lnb": f(ln_v_bias)[0],
        "cwb": np.concatenate([f(conv_w)[0], f(conv_b)], 0).reshape(4, 44, 128),
        "wsp": f(w_spatial)[0], "bsp": f(b_spatial)[0],
        "w00": np.ascontiguousarray(f(w_spatial)[0][:, 0, 0]),
    }
    in_maps = []
    for c in (range(NCORES) if cores is None else cores):
        b, qi = c // 4, c % 4
        fl = 0.0 if qi == 0 else 1.0
        m = dict(common)
        m["xall"] = np.ascontiguousarray(xp[b, qi * NM:qi * NM + HX + NM])
        m["xs"] = f(x_sample)[c * NS:(c + 1) * NS, 0]
        for g in range(3):
            cg = caches[g][c * NS:(c + 1) * NS]
            m["ck%d" % g] = np.ascontiguousarray(cg.reshape(NS, cg.shape[1], 512))
        m["sconv"] = f(state_conv_ffn)[0, c * NS:(c + 1) * NS]
        m["cst"] = np.stack([np.eye(128, dtype=np.float32), mprev, mcur, mprev * fl, mcur, blockones, np.ones((128, 128), np.float32)])
        m["flagd"] = np.full((128, 1), fl, np.float32)
        in_maps.append(m)
    return in_maps


def assemble(R, B=2):
    y_prompt = np.stack([np.concatenate([R[b * 4 + qi]["y"] for qi in range(4)], 0) for b in range(B)])
    y_sample = np.concatenate([R[c]["ys"] for c in range(NCORES)], 0)[:, None, :]
    outs = [y_prompt, y_sample]
    for g in range(3):
        L = WIN[g][0]
        kvp = np.stack([R[b * 4 + 3]["kv%d" % g] for b in range(B)]).reshape(1, B, L, 2, 4, 64)
        kvsm = np.concatenate([R[c]["kvs%d" % g] for c in range(NCORES)], 0).reshape(1, NCORES * NS, 1, 2, 4, 64)
        outs += [kvp, kvsm]
    vcp = np.stack([R[b * 4 + 3]["vch"] for b in range(B)])[None]
    vcs = np.concatenate([R[c]["vchs"] for c in range(NCORES)], 0)[None, :, None, :]
    cp = np.stack([R[b * 4 + 3]["convp"] for b in range(B)])[None]
    cs = np.concatenate([R[c]["convs"] for c in range(NCORES)], 0)[None]
    outs += [vcp, vcs, cp, cs]
    return tuple(np.ascontiguousarray(np.asarray(o, dtype=np.float32)) for o in outs)


def kernel(**inputs):
    in_maps = make_in_maps(**inputs)
    if "nc" not in _NC_CACHE:
        _NC_CACHE["nc"] = build_program()
    res = run_bass_kernel_spmd(_NC_CACHE["nc"], in_maps, core_ids=list(range(NCORES)))
    return assemble(res.results)
```

```python
import numpy as np
from contextlib import ExitStack
import concourse.bass as bass
import concourse.mybir as mybir
from concourse.bass_utils import run_bass_kernel_spmd

F32 = mybir.dt.float32
BF16 = mybir.dt.bfloat16
AF = mybir.ActivationFunctionType
ALU = mybir.AluOpType

NCORES = 8
HX = 2176
NM = 2048
NS = 16
SM0 = NM
NCOL = NM + 2 + NS
EPS = 1e-6
WIN = ((128, 1), (512, 4), (2048, 16))
KBASE = (1920, 1536, 0)
KLEN = (2304, 2688, 4224)


class Buf:
    __slots__ = ("name", "w", "r")

    def __init__(self, name=""):
        self.name = name
        self.w = None
        self.r = {}


class Sched:
    ENG = ("pe", "act", "dve", "pool", "sp")

    def __init__(self, nc, stack):
        self.nc = nc
        self.stack = stack
        self.prog = {e: [] for e in self.ENG}
        self.sems = {}
        self.cnt = {}
        self.seen = {e: {} for e in self.ENG}
        self.label = ""
        for e in self.ENG:
            self._sem("E_" + e)

    def _sem(self, name):
        if name not in self.sems:
            self.sems[name] = self.stack.enter_context(self.nc.semaphore(name))
            self.cnt[name] = 0
        return name

    def _need(self, eng, waits, tok):
        if tok is None:
            return
        sem, val = tok
        if eng == "pe" and sem == "E_pe":
            return
        if self.seen[eng].get(sem, 0) >= val:
            return
        self.seen[eng][sem] = val
        waits.append((sem, val))

    def op(self, eng, fn, reads=(), writes=(), dma=None):
        waits = []
        for b in reads:
            self._need(eng, waits, b.w)
        for b in writes:
            self._need(eng, waits, b.w)
            for s, v in b.r.items():
                self._need(eng, waits, (s, v))
        if dma is not None:
            sem = self._sem(dma)
            inc = 16
        else:
            sem = "E_" + eng
            inc = 1
        self.cnt[sem] += inc
        tok = (sem, self.cnt[sem])
        self.prog[eng].append((waits, fn, (sem, inc), self.label + " r:" + ",".join(b.name for b in reads) + " w:" + ",".join(b.name for b in writes)))
        for b in reads:
            if b.r.get(sem, 0) < tok[1]:
                b.r[sem] = tok[1]
        for b in writes:
            b.w = tok
            b.r = {}
        return tok

    def barrier(self):
        allw = [(s, v) for s, v in self.cnt.items() if v > 0]
        for e in self.ENG:
            waits = []
            for t in allw:
                self._need(e, waits, t)
            self.prog[e].append((waits, None, None, ""))

    def emit(self):
        nc = self.nc
        with nc.Block() as block:
            def run(name, e):
                for waits, fn, inc, lab in self.prog[name]:
                    for s, v in waits:
                        e.wait_ge(self.sems[s], v)
                    if fn is not None:
                        with nc.named_scope(lab.split(" ")[0] or "none"):
                            ins = fn(e)
                        ins.then_inc(self.sems[inc[0]], inc[1])

            @block.tensor
            def _(e):
                run("pe", e)

            @block.scalar
            def _(e):
                run("act", e)

            @block.vector
            def _(e):
                run("dve", e)

            @block.gpsimd
            def _(e):
                run("pool", e)

            @block.sync
            def _(e):
                run("sp", e)


def build_program(stop=None):
    nc = bass.Bass("TRN2", target_bir_lowering=False)

    def din(name, shape):
        return nc.dram_tensor(name, list(shape), F32, kind="ExternalInput").ap()

    def dout(name, shape):
        return nc.dram_tensor(name, list(shape), F32, kind="ExternalOutput").ap()

    xall = din("xall", [HX + NM, 1024])
    xs = din("xs", [NS, 1024])
    ck = [din("ck0", [NS, 128, 512]), din("ck1", [NS, 512, 512]), din("ck2", [NS, 2048, 512])]
    sconv = din("sconv", [NS, 2, 5632])
    w_in = din("w_in", [1024, 5376])
    w_pa = din("w_pa", [512, 1024])
    w_pb = din("w_pb", [256, 1024])
    w_out = din("w_out", [1024, 1024])
    w_up = din("w_up", [1024, 5632])
    w_dn = din("w_dn", [2816, 1024])
    vec8 = din("vec8", [16, 128])
    nfin = din("nfin", [1024])
    lng = din("lng", [512])
    lnb = din("lnb", [512])
    cwb = din("cwb", [4, 44, 128])
    wsp = din("wsp", [4, 128, 128])
    bsp = din("bsp", [4, 128])
    w00 = din("w00", [4])
    cst = din("cst", [7, 128, 128])
    flagd = din("flagd", [128, 1])

    y = dout("y", [NM, 1024])
    ys = dout("ys", [NS, 1024])
    kvo = [dout("kv0", [128, 512]), dout("kv1", [512, 512]), dout("kv2", [2048, 512])]
    kvs = [dout("kvs0", [NS, 512]), dout("kvs1", [NS, 512]), dout("kvs2", [NS, 512])]
    vch = dout("vch", [128, 512])
    vchs = dout("vchs", [NS, 512])
    convp = dout("convp", [2, 5632])
    convs = dout("convs", [NS, 2, 5632])

    st = ExitStack()
    S = Sched(nc, st)
    NF = 53100
    arena = st.enter_context(nc.sbuf_tensor("arena", [128, NF], F32))
    psum_all = st.enter_context(nc.psum_tensor("psall", [128, 4096], F32))
    pbank = [psum_all[:, i * 512:(i + 1) * 512] for i in range(8)]
    PB = [Buf("pb%d" % i) for i in range(8)]
    bank_i = [0]

    def nextbank():
        i = bank_i[0] % 8
        bank_i[0] += 1
        return pbank[i], PB[i]

    def nextpair():
        if bank_i[0] % 2:
            bank_i[0] += 1
        i = bank_i[0] % 8
        bank_i[0] += 2
        return pbank[i], PB[i], pbank[i + 1], PB[i + 1], psum_all[:, i * 512:(i + 2) * 512]

    class Arena:
        def __init__(self):
            self.top = 0

        def f32(self, n):
            a = arena[:, self.top:self.top + n]
            self.top += n
            assert self.top <= NF, self.top
            return a

        def bf(self, n):
            n2 = (n + 1) // 2
            a = arena[:, self.top:self.top + n2].bitcast(BF16)
            self.top += n2
            assert self.top <= NF, self.top
            return a

    A = Arena()
    TT = [NF - 2200]

    def tmp_f32(n):
        a = arena[:, TT[0]:TT[0] + n]
        TT[0] += n
        assert TT[0] <= NF
        return a
    out_bufs = []
    uid = [0]

    def finalize():
        S.barrier()
        S.emit()
        st.close()
        return nc

    def dma(eng, out, in_, rd, wr, sem=None, **kw):
        if sem is None:
            uid[0] += 1
            sem = "d%d" % (uid[0] % 24)
        return S.op(eng, lambda e: e.dma_start(out=out, in_=in_, **kw), reads=rd, writes=wr, dma=sem)

    def mm(out, lhsT, rhs, start, stop, rd, wr):
        S.op("pe", lambda e: e.matmul(out, lhsT=lhsT, rhs=rhs, start=start, stop=stop), reads=rd, writes=wr)

    def act(out, in_, func, rd, wr, **kw):
        S.op("act", lambda e: e.activation(out=out, in_=in_, func=func, **kw), reads=rd, writes=wr)

    def dve(fn, rd, wr):
        S.op("dve", fn, reads=rd, writes=wr)

    def tcopy(eng, out, in_, rd, wr):
        S.op(eng, lambda e: e.tensor_copy(out=out, in_=in_), reads=rd, writes=wr)

    Bc = Buf("const")
    cbs = []

    def CW():
        nb = Buf("c%d" % len(cbs))
        cbs.append(nb)
        return [nb]

    def CR():
        return list(cbs)
    cst_f = A.f32(7 * 128).rearrange("p (k n) -> p k n", k=7)
    dma("sp", cst_f, cst.rearrange("k p n -> p k n"), [], CW(), sem="c0")
    ident_f = cst_f[:, 0, :]
    blockones_f = cst_f[:, 5, :]
    cst_b = A.bf(7 * 128).rearrange("p (k n) -> p k n", k=7)
    tcopy("dve", cst_b, cst_f, CR(), CW())
    ident_b = cst_b[:, 0, :]
    mprev_b = cst_b[:, 1, :]
    mcur_b = cst_b[:, 2, :]
    medge_b = cst_b[:, 3, :]
    ones_b = cst_b[:, 6, :]
    flag = A.f32(1)
    dma("sp", flag, flagd, [], CW(), sem="c1")
    gfin_bc = A.f32(1024)
    dma("sp", gfin_bc, nfin.partition_broadcast(128), [], CW(), sem="c2")
    lng_bc = A.f32(512)
    lnb_bc = A.f32(512)
    dma("sp", lng_bc, lng.partition_broadcast(128), [], CW(), sem="c3")
    dma("sp", lnb_bc, lnb.partition_broadcast(128), [], CW(), sem="c4")
    w00_bc = A.f32(4)
    dma("sp", w00_bc, w00.partition_broadcast(128), [], CW(), sem="c5")
    v8_sb = tmp_f32(128)
    dma("sp", v8_sb[0:16, :], vec8, [], CW(), sem="c6")
    cw_sb = tmp_f32(4 * 128).rearrange("p (k n) -> p k n", k=4)
    dma("sp", cw_sb[0:44, :, :], cwb.rearrange("k c p -> c k p"), [], CW(), sem="c7")
    gvec = A.f32(16)
    cwT = A.f32(4 * 44).rearrange("p (k c) -> p k c", k=4)
    bk, Bk = nextbank()
    mm(bk[:, 0:16], v8_sb[0:16, :], ident_f[0:16, 0:16], True, True, CR(), [Bk])
    tcopy("dve", gvec, bk[:, 0:16], [Bk], CW())
    bk, Bk = nextbank()
    for k in range(4):
        mm(bk[:, k * 44:(k + 1) * 44], cw_sb[0:44, k, :], ident_f[0:44, 0:44], True, True, CR(), [Bk])
    tcopy("dve", cwT, bk[:, 0:176].rearrange("p (k c) -> p k c", k=4), [Bk], CW())
    ones_f = A.f32(128)
    S.op("dve", lambda e: e.memset(ones_f, 1.0), writes=CW())
    gm_bc = A.bf(8 * 128).rearrange("p (c n) -> p c n", c=8)
    gf_bc = A.bf(8 * 128).rearrange("p (c n) -> p c n", c=8)
    for c in range(8):
        dve(lambda e, c=c: e.tensor_scalar(out=gm_bc[:, c, :], in0=ones_f, scalar1=gvec[:, c:c + 1], scalar2=None, op0=ALU.mult), CR(), CW())
        dve(lambda e, c=c: e.tensor_scalar(out=gf_bc[:, c, :], in0=ones_f, scalar1=gvec[:, 8 + c:9 + c], scalar2=None, op0=ALU.mult), CR(), CW())
    wsp_f = tmp_f32(512).rearrange("p (g n) -> p g n", g=4)
    dma("sp", wsp_f, wsp.rearrange("g t s -> t g s"), [], CW(), sem="c8")
    wsp_b = tmp_f32(256).bitcast(BF16).rearrange("p (g n) -> p g n", g=4)
    for g in range(4):
        dve(lambda e, g=g: e.tensor_tensor(out=wsp_b[:, g, :], in0=wsp_f[:, g, :], in1=cst_f[:, 1, :], op=ALU.mult), CR(), CW())
    WmT = A.bf(512).rearrange("p (g n) -> p g n", g=4)
    bk, Bk = nextbank()
    bkb = bk.bitcast(BF16).rearrange("p (g n) -> p g n", g=8)
    for g in range(4):
        S.op("pe", lambda e, g=g: e.transpose(out=bkb[:, g, :], in_=wsp_b[:, g, :], identity=ident_b), reads=CR(), writes=[Bk])
    tcopy("dve", WmT, bkb[:, 0:4, :], [Bk], CW())
    bsp_f = tmp_f32(512)
    dma("sp", bsp_f[0:1, :], bsp.rearrange("(o g) t -> o (g t)", o=1), [], CW(), sem="c9")
    bsp_b = A.bf(512)
    tcopy("dve", bsp_b[0:1, :], bsp_f[0:1, :], CR(), CW())
    bsp0_f = A.f32(4 * 16)
    bsp0_b = A.bf(4 * 16)
    for g in range(4):
        dve(lambda e, g=g: e.tensor_scalar(out=bsp0_f[0:1, g * 16:(g + 1) * 16], in0=ones_f[0:1, 0:16], scalar1=bsp_f[0:1, g * 128:g * 128 + 1], scalar2=None, op0=ALU.mult), CR(), CW())
    tcopy("dve", bsp0_b[0:1, :], bsp0_f[0:1, :], CR(), CW())
    D16 = A.bf(4 * 16).rearrange("p (g n) -> p g n", g=4)
    for g in range(4):
        dve(lambda e, g=g: e.tensor_scalar(out=D16[0:16, g, :], in0=ident_f[0:16, 0:16], scalar1=w00_bc[0:16, g:g + 1], scalar2=None, op0=ALU.mult), CR(), CW())
    stat = A.f32(8 * 8).rearrange("p (s n) -> p s n", s=8)
    Bstat = [Buf("st%d" % i) for i in range(8)]
    stat_i = [0]
    S.op("dve", lambda e: e.memset(stat[:, 7, 7:8], 0.0), reads=CR(), writes=[Bc])
    CONST_TOP = A.top
    print('CONST_TOP', CONST_TOP)

    if stop == 'const':
        return finalize()
    def rstd_of(ssq_ap, n, inv_n, sti, Bs):
        s_ = stat[:n, sti, :]
        dve(lambda e: e.tensor_scalar(out=s_[:, 1:2], in0=ssq_ap, scalar1=inv_n, scalar2=EPS, op0=ALU.mult, op1=ALU.add), [Bs], [Bs])
        act(s_[:, 2:3], s_[:, 1:2], AF.Ln, [Bs], [Bs])
        act(s_[:, 3:4], s_[:, 2:3], AF.Exp, [Bs], [Bs], scale=-0.5)
        return s_[:, 3:4]

    def norm_rows(xt_ap, n, Bx, out_bf, Bo):
        sti = stat_i[0] % 8
        stat_i[0] += 1
        Bs = Bstat[sti]
        act(out_bf, xt_ap, AF.Square, [Bx], [Bo, Bs], accum_out=stat[:n, sti, 0:1])
        r = rstd_of(stat[:n, sti, 0:1], n, 1.0 / 1024, sti, Bs)
        act(out_bf, xt_ap, AF.Copy, [Bx, Bs], [Bo], scale=r)

    def transpose_rows(src_bf, n, Bsrc, dstT, Bdst, g_bc):
        bk, Bk = nextbank()
        pt = bk.bitcast(BF16).rearrange("p (c t) -> p c t", c=8)
        for c in range(8):
            S.op("pe", lambda e, c=c: e.transpose(out=pt[:, c, 0:n], in_=src_bf[0:n, c * 128:(c + 1) * 128], identity=ident_b[0:n, 0:n]), reads=[Bsrc, Bc], writes=[Bk])
        dve(lambda e: e.tensor_tensor(out=dstT, in0=pt[:, :, 0:n], in1=g_bc[:, :, 0:n], op=ALU.mult), [Bk, Bc], [Bdst])

    xnT = A.bf(8 * HX).rearrange("p (c n) -> p c n", c=8)
    BxnT = Buf("xnT")
    xeT = A.bf(8 * 128).rearrange("p (c n) -> p c n", c=8)
    BxeT = Buf("xeT")
    P1 = A.top
    QT = A.bf(6 * NCOL).rearrange("p (c n) -> p c n", c=6)
    BQT = Buf("QT")
    KT = [A.bf(2 * KLEN[g]).rearrange("p (c n) -> p c n", c=2) for g in range(3)]
    BKT = [Buf("KT%d" % g) for g in range(3)]
    KTs = A.bf(6 * 18).rearrange("p (c n) -> p c n", c=6)
    VTs = A.bf(6 * 18).rearrange("p (c n) -> p c n", c=6)
    BKTs = Buf("KTs")
    NVB = 79
    Vb = A.bf(NVB * 256).rearrange("p (b n) -> p b n", b=NVB)
    BV = [Buf("V%d" % i) for i in range(NVB + 3)]
    vidx = {}
    PW = A.top
    wqkv = A.bf(8 * 2304).rearrange("p (c n) -> p c n", c=8)
    Bw = Buf("wqkv")
    NKV = 5
    kvst = [A.f32(512) for _ in range(NKV)]
    Bkvst = [Buf("kvst%d" % i) for i in range(NKV)]
    kvst_i = [0]
    PB_TOP = A.top

    dma("pool", wqkv, w_in.rearrange("(c p) n -> p c n", p=128)[:, :, 0:2304], [], [Bw], sem="w0")

    def wcol(kind, g, c):
        return kind * 768 + g * 256 + c * 128

    def norm_T_batch(items, xts, Bxts, xbs, Bxbs, semp, group_hook=None):
        G = len(xts)
        for g0 in range(0, len(items), G):
            grp = items[g0:g0 + G]
            stis = []
            srcs = []
            for k, (src_rows, n, dstT, Bdst, g_bc, pre) in enumerate(grp):
                if src_rows is not None:
                    dma("sp", xts[k][0:n, :], src_rows, [], [Bxts[k]], sem="%s%d" % (semp, k))
                srcs.append((xts[k], Bxts[k]))
            for k, (src_rows, n, dstT, Bdst, g_bc, pre) in enumerate(grp):
                if pre is not None:
                    srcs[k] = pre(k)
            for k, (src_rows, n, dstT, Bdst, g_bc, pre) in enumerate(grp):
                sti = stat_i[0] % 8
                stat_i[0] += 1
                stis.append(sti)
                act(xbs[k][0:n, :], srcs[k][0][0:n, :], AF.Square, [srcs[k][1]], [Bxbs[k], Bstat[sti]], accum_out=stat[:n, sti, 0:1])
            for k, (src_rows, n, dstT, Bdst, g_bc, pre) in enumerate(grp):
                s_ = stat[:n, stis[k], :]
                dve(lambda e, s_=s_: e.tensor_scalar(out=s_[:, 1:2], in0=s_[:, 0:1], scalar1=1.0 / 1024, scalar2=EPS, op0=ALU.mult, op1=ALU.add), [Bstat[stis[k]]], [Bstat[stis[k]]])
            for k, (src_rows, n, dstT, Bdst, g_bc, pre) in enumerate(grp):
                s_ = stat[:n, stis[k], :]
                act(s_[:, 2:3], s_[:, 1:2], AF.Ln, [Bstat[stis[k]]], [Bstat[stis[k]]])
            for k, (src_rows, n, dstT, Bdst, g_bc, pre) in enumerate(grp):
                s_ = stat[:n, stis[k], :]
                act(s_[:, 3:4], s_[:, 2:3], AF.Exp, [Bstat[stis[k]]], [Bstat[stis[k]]], scale=-0.5)
            for k, (src_rows, n, dstT, Bdst, g_bc, pre) in enumerate(grp):
                s_ = stat[:n, stis[k], :]
                act(xbs[k][0:n, :], srcs[k][0][0:n, :], AF.Copy, [srcs[k][1], Bstat[stis[k]]], [Bxbs[k]], scale=s_[:, 3:4])
            if group_hook is not None:
                group_hook(g0 // G)
            for k, (src_rows, n, dstT, Bdst, g_bc, pre) in enumerate(grp):
                transpose_rows(xbs[k], n, Bxbs[k], dstT, Bdst, g_bc)

    def proj_fm(dst, Bdst, wt, Bwt, col0, src, Bsrc, c0, n, nk=8, func=AF.Copy, **kw):
        bk, Bk = nextbank()
        for kc in range(nk):
            mm(bk[:, 0:n], wt[:, kc, col0:col0 + 128], src[:, kc, c0:c0 + n], kc == 0, kc == nk - 1, [Bwt, Bsrc], [Bk])
        act(dst, bk[:, 0:n], func, [Bk], [Bdst], **kw)

    def vblock(g, start, step, n, src, Bsrc):
        idx = len(vidx)
        vidx[(g, start, step, "m" if src is xnT_main_marker[0] else "h")] = idx
        bk, Bk = nextbank()
        for kc in range(8):
            mm(bk[0:n, 0:256], src[:, kc, start:start + step * (n - 1) + 1:step], wqkv[:, kc, wcol(2, g, 0):wcol(2, g, 0) + 256], kc == 0, kc == 7, [Bsrc, Bw], [Bk])
        tcopy("dve", Vb[0:n, idx, :], bk[0:n, 0:256], [Bk], [BV[idx]])
        return idx

    xnT_main_marker = [None]

    S.label = 'B1'
    QT_f32 = arena[:, P1:P1 + 6144]
    xtA = [QT_f32[:, k * 1024:(k + 1) * 1024] for k in range(4)]
    xbA = [QT_f32[:, 4096 + k * 512:4096 + (k + 1) * 512].bitcast(BF16) for k in range(4)]
    BxtA = [Buf("xtA%d" % k) for k in range(4)]
    BxbA = [Buf("xbA%d" % k) for k in range(4)]
    norm_T_batch([(xall[t * 128:(t + 1) * 128, :], 128, xnT[:, :, t * 128:(t + 1) * 128], BxnT, gm_bc, None) for t in range(17)],
                 xtA, BxtA, xbA, BxbA, "xa")
    if stop == 'B1a':
        return finalize()
    for g in range(3):
        lt = KBASE[g]
        while lt < HX:
            n = min(512, HX - lt)
            for c in range(2):
                proj_fm(KT[g][:, c, lt - KBASE[g]:lt - KBASE[g] + n], BKT[g], wqkv, Bw, wcol(1, g, c), xnT, BxnT, lt, n)
            lt += n
    if stop == 'B1k':
        return finalize()
    for r in range(16):
        vblock(2, 128 + r, 16, 128, xnT, BxnT)
    for r in range(4):
        vblock(1, 1664 + r, 4, 128, xnT, BxnT)
    vblock(0, 2048, 1, 128, xnT, BxnT)
    vblock(2, 126, 16, 128, xnT, BxnT)
    vblock(2, 127, 16, 128, xnT, BxnT)
    vblock(1, 1662, 4, 128, xnT, BxnT)
    vblock(1, 1663, 4, 128, xnT, BxnT)
    vblock(0, 2046, 1, 128, xnT, BxnT)
    vblock(2, 2174, 1, 1, xnT, BxnT)
    vblock(2, 2175, 1, 1, xnT, BxnT)
    vblock(1, 2174, 1, 1, xnT, BxnT)
    vblock(1, 2175, 1, 1, xnT, BxnT)
    vblock(0, 2174, 1, 2, xnT, BxnT)
    tcopy("dve", xeT, xnT[:, :, 2048:2176], [BxnT], [BxeT])

    if stop == 'B1':
        return finalize()
    S.label = 'B2'
    xnT_main_marker[0] = xnT
    S.barrier()
    VB0 = PW - NVB * 128 + 31 * 128
    Vm_f32 = arena[:, VB0:VB0 + 6144]
    xtB = [Vm_f32[:, k * 1024:(k + 1) * 1024] for k in range(4)]
    xbB = [Vm_f32[:, 4096 + k * 512:4096 + (k + 1) * 512].bitcast(BF16) for k in range(4)]
    BxtB = [Buf("xtB%d" % k) for k in range(4)]
    BxbB = [Buf("xbB%d" % k) for k in range(4)]
    norm_T_batch([(xall[HX + t * 128:HX + (t + 1) * 128, :], 128, xnT[:, :, t * 128:(t + 1) * 128], BxnT, gm_bc, None) for t in range(16)],
                 xtB, BxtB, xbB, BxbB, "xb")
    tcopy("dve", xnT[:, :, SM0:SM0 + 2], xeT[:, :, 126:128], [BxeT], [BxnT])
    norm_T_batch([(xs, NS, xnT[:, :, SM0 + 2:SM0 + 2 + NS], BxnT, gm_bc, None)], xtB, BxtB, xbB, BxbB, "xb")
    slices = [(i * 512, 512) for i in range(4)] + [(SM0, 18)]
    if stop == 'B2a':
        return finalize()
    for (c0, n) in slices:
        for gc in range(6):
            proj_fm(QT[:, gc, c0:c0 + n], BQT, wqkv, Bw, wcol(0, gc // 2, gc % 2), xnT, BxnT, c0, n)
    for (c0, n) in slices[:4]:
        for g in range(3):
            for c in range(2):
                kc0 = HX + c0 - KBASE[g]
                proj_fm(KT[g][:, c, kc0:kc0 + n], BKT[g], wqkv, Bw, wcol(1, g, c), xnT, BxnT, c0, n)
    for gc in range(6):
        proj_fm(KTs[:, gc, :], BKTs, wqkv, Bw, wcol(1, gc // 2, gc % 2), xnT, BxnT, SM0, 18)
        proj_fm(VTs[:, gc, :], BKTs, wqkv, Bw, wcol(2, gc // 2, gc % 2), xnT, BxnT, SM0, 18)
    if stop == 'B2q':
        return finalize()
    assert len(vidx) == 31, len(vidx)
    S.barrier()
    for t in range(16):
        vblock(0, t * 128, 1, 128, xnT, BxnT)
    for i in range(4):
        for r in range(4):
            vblock(1, 512 * i + r, 4, 128, xnT, BxnT)
    for r in range(16):
        vblock(2, r, 16, 128, xnT, BxnT)

    if stop == 'B2v':
        return finalize()
    S.label = 'kvtok'
    def kv_tok(col0, n, g, dst_rows):
        i = kvst_i[0] % NKV
        kvst_i[0] += 1
        bk, Bk = nextbank()
        for half in range(2):
            for kc in range(8):
                mm(bk[0:n, half * 256:(half + 1) * 256], xnT[:, kc, col0:col0 + n], wqkv[:, kc, wcol(1 + half, g, 0):wcol(1 + half, g, 0) + 256], kc == 0, kc == 7, [BxnT, Bw], [Bk])
        tcopy("dve", kvst[i][0:n, :], bk[0:n, :], [Bk], [Bkvst[i]])
        dma("sp" if n == 128 else "pool", dst_rows, kvst[i][0:n, :], [Bkvst[i]], [], sem=("ko%d" if n == 128 else "kp%d") % i)
        out_bufs.append(Bkvst[i])

    for t in range(16):
        kv_tok(t * 128, 128, 2, kvo[2][t * 128:(t + 1) * 128, :])
    if stop == 'kv1':
        return finalize()
    for t in range(12, 16):
        kv_tok(t * 128, 128, 1, kvo[1][(t - 12) * 128:(t - 11) * 128, :])
    kv_tok(15 * 128, 128, 0, kvo[0])
    if stop == 'kv2':
        return finalize()
    for g in range(3):
        kv_tok(SM0 + 2, NS, g, kvs[g])

    if stop == 'B':
        return finalize()
    S.barrier()
    A.top = PW
    ACC0 = A.top
    acc_n = A.f32(2 * NCOL).rearrange("p (c n) -> p c n", c=2)
    acc_d = A.f32(2 * NCOL).rearrange("p (c n) -> p c n", c=2)
    acc_all = arena[:, ACC0:ACC0 + 4 * NCOL].rearrange("p (x c n) -> p x c n", x=2, c=2)
    NPT = 2
    PTb = [A.bf(1024).rearrange("p (h k q) -> p h k q", h=4, k=2) for _ in range(NPT)]
    BPT = [Buf("PT%d" % i) for i in range(NPT)]
    pt_i = [0]
    BOUT0 = A.top
    boutT = A.bf(2 * NCOL).rearrange("p (c n) -> p c n", c=2)
    BboutT = Buf("boutT")
    ATT_TOP = A.top
    A.top = BOUT0
    ckb = [[A.bf(512) for _ in range(3)] for _ in range(2)]
    Bckb = [[Buf("ckb%d%d" % (i, g)) for g in range(3)] for i in range(2)]
    KTc = [A.bf(6 * 128).rearrange("p (c n) -> p c n", c=6) for _ in range(2)]
    BKTc = [Buf("KTc%d" % i) for i in range(2)]
    PTs = [A.bf(16) for _ in range(2)]
    BPTs = [Buf("PTs%d" % i) for i in range(2)]
    prodf = A.f32(6 * 16).rearrange("p (c n) -> p c n", c=6)
    pself = A.f32(6 * 16).rearrange("p (c n) -> p c n", c=6)
    Bpr = Buf("prod")
    assert A.top <= NF, A.top
    acc_hist = {0: [], 1: [], 2: [], "x": [Buf("accx")], "s": []}
    mpair_main = cst_b[:, 1:3, :]
    mpair_edge = cst_b[:, 3:5, :]

    def acc_update(pair, Bn, Bd, nq, cols, first, key):
        if key == "x":
            rd_prev, wr = acc_hist["x"], acc_hist["x"]
        else:
            nb = Buf("acc%s" % str(key))
            rd_prev = [] if (key == "s" or key == 0) else acc_hist[key - 1]
            wr = [nb]
            acc_hist[key].append(nb)
        p4 = pair.rearrange("p (x h q) -> p x h q", x=2, h=4)
        for h2 in range(2):
            i_ap = p4[h2 * 64:(h2 + 1) * 64, :, h2::2, 0:nq]
            o_ap = acc_all[h2 * 64:(h2 + 1) * 64, :, :, cols]
            if first:
                dve(lambda e, i_ap=i_ap, o_ap=o_ap: e.tensor_copy(out=o_ap, in_=i_ap), [Bn, Bd] + rd_prev, wr)
            else:
                dve(lambda e, i_ap=i_ap, o_ap=o_ap: e.tensor_tensor(out=o_ap, in0=i_ap, in1=o_ap, op=ALU.add), [Bn, Bd] + rd_prev, wr)

    def band_p1(g, qsrc, Bq, qcols, nq, chunks, mpair):
        pi = pt_i[0] % NPT
        pt_i[0] += 1
        PT, Bp = PTb[pi], BPT[pi]
        b0, B0 = nextbank()
        b1, B1 = nextbank()
        sb = [b0, b1]
        SBf = [B0, B1]
        for h in range(4):
            c, h2 = h // 2, h % 2
            rows = slice(h2 * 64, (h2 + 1) * 64)
            for ci, (Kap, BK, vi, nk, mask) in enumerate(chunks):
                o = sb[h2][0:nk, (c * 2 + ci) * 128:(c * 2 + ci) * 128 + nq]
                mm(o, Kap[rows, c, :], qsrc[rows, g * 2 + c, qcols], True, True, [BK, Bq], [SBf[h2]])
        full = (nq == 128 and len(chunks) == 2 and all(ch[3] == 128 for ch in chunks))
        if full:
            for h2 in range(2):
                src = sb[h2].rearrange("p (h k q) -> p h k q", h=2, k=2)
                act(PT[:, h2::2, :, :], src, AF.Exp, [SBf[h2]], [Bp], scale=0.125)
            mb = mpair.unsqueeze(1).to_broadcast([128, 4, 2, 128])
            dve(lambda e, PT=PT, mb=mb: e.tensor_tensor(out=PT, in0=PT, in1=mb, op=ALU.mult), [Bp, Bc], [Bp])
        else:
            for ci, (Kap, BK, vi, nk, mask) in enumerate(chunks):
                for h2 in range(2):
                    src = sb[h2][0:nk, :].rearrange("p (h k q) -> p h k q", h=2, k=2)[:, :, ci, 0:nq]
                    act(PT[0:nk, h2::2, ci, 0:nq], src, AF.Exp, [SBf[h2]], [Bp], scale=0.125)
                if mask is not None:
                    for h in range(4):
                        dve(lambda e, h=h, ci=ci, nk=nk, mask=mask, PT=PT: e.tensor_tensor(out=PT[0:nk, h, ci, 0:nq], in0=PT[0:nk, h, ci, 0:nq], in1=mask[0:nk, 0:nq], op=ALU.mult), [Bp, Bc], [Bp])
        return PT, Bp

    def band_p2(PT, Bp, nq, chunks, acc_cols, first, key):
        nch = len(chunks)
        bn, Bn, bd, Bd, pair = nextpair()
        for h in range(4):
            c = h // 2
            for ci, (Kap, BK, vi, nk, mask) in enumerate(chunks):
                mm(bn[:, h * 128:h * 128 + nq], Vb[0:nk, vi, c * 128:(c + 1) * 128], PT[0:nk, h, ci, 0:nq], ci == 0, ci == nch - 1, [BV[vi], Bp], [Bn])
            for ci, (Kap, BK, vi, nk, mask) in enumerate(chunks):
                mm(bd[:, h * 128:h * 128 + nq], ones_b[0:nk, :], PT[0:nk, h, ci, 0:nq], ci == 0, ci == nch - 1, [Bc, Bp], [Bd])
        acc_update(pair, Bn, Bd, nq, acc_cols, first, key)

    def kslice(g, lt0, step, n):
        a = lt0 - KBASE[g]
        return KT[g][:, :, a:a + step * (n - 1) + 1:step]

    def samp_s0(b):
        sl = b % 2
        for g in range(3):
            L, d = WIN[g]
            dma("pool", ckb[sl][g], ck[g][b, 0:L:d, :], [], [Bckb[sl][g]], sem="ck%d%d" % (sl, g))

    def samp_s1(b):
        sl = b % 2
        bk, Bk = nextbank()
        pt = bk.bitcast(BF16).rearrange("p (c t) -> p c t", c=8)
        for g in range(3):
            for c in range(2):
                S.op("pe", lambda e, c=c, g=g, pt=pt, sl=sl: e.transpose(out=pt[:, g * 2 + c, :], in_=ckb[sl][g][:, c * 128:(c + 1) * 128], identity=ident_b), reads=[Bckb[sl][g], Bc], writes=[Bk])
        tcopy("dve", KTc[sl], pt[:, 0:6, :], [Bk], [BKTc[sl]])

    def samp_s2(b):
        sl = b % 2
        col = SM0 + 2 + b
        bs0, BS0 = nextbank()
        bs1, BS1 = nextbank()
        bsx = [bs0, bs1]
        BSx = [BS0, BS1]
        for g in range(3):
            for h in range(4):
                c, h2 = h // 2, h % 2
                rows = slice(h2 * 64, (h2 + 1) * 64)
                mm(bsx[h2][:, g * 2 + c:g * 2 + c + 1], KTc[sl][rows, g * 2 + c, :], QT[rows, g * 2 + c, col:col + 1], True, True, [BKTc[sl], BQT], [BSx[h2]])
        PTs3 = PTs[sl][:, 0:12].rearrange("p (g c t) -> p g c t", g=3, c=2)
        for h2 in range(2):
            act(PTs3[:, :, :, h2], bsx[h2][:, 0:6].rearrange("p (g c) -> p g c", g=3), AF.Exp, [BSx[h2]], [BPTs[sl]], scale=0.125)

    def samp_s3(b):
        sl = b % 2
        col = SM0 + 2 + b
        bn, Bn, bd, Bd, pair = nextpair()
        for h in range(4):
            c = h // 2
            for g in range(3):
                mm(bn[:, h * 128:h * 128 + 1], ckb[sl][g][:, 256 + c * 128:256 + (c + 1) * 128], PTs[sl][:, g * 4 + h:g * 4 + h + 1], g == 0, g == 2, [Bckb[sl][g], BPTs[sl]], [Bn])
            for g in range(3):
                mm(bd[:, h * 128:h * 128 + 1], ones_b, PTs[sl][:, g * 4 + h:g * 4 + h + 1], g == 0, g == 2, [Bc, BPTs[sl]], [Bd])
        acc_update(pair, Bn, Bd, 1, slice(col, col + 1), True, "s")

    S.label = 'att-main'
    mblocks = []
    for g in range(3):
        step = WIN[g][1]
        if g == 0:
            starts = [t * 128 for t in range(16)]
        elif g == 1:
            starts = [512 * i + r for i in range(4) for r in range(4)]
        else:
            starts = list(range(16))
        for m0 in starts:
            lt_q = HX + m0
            lt_p = lt_q - 128 * step
            if lt_p < HX:
                vp = vidx[(g, lt_p, step, "h")]
                mp = mpair_edge
            else:
                vp = vidx[(g, lt_p - HX, step, "m")]
                mp = mpair_main
            vc = vidx[(g, m0, step, "m")]
            chunks = [(kslice(g, lt_p, step, 128), BKT[g], vp, 128, None),
                      (kslice(g, lt_q, step, 128), BKT[g], vc, 128, None)]
            qc = slice(m0, m0 + step * 127 + 1, step)
            mblocks.append((g, qc, chunks, mp))
    samp_s0(0)
    pend = band_p1(mblocks[0][0], QT, BQT, mblocks[0][1], 128, mblocks[0][2], mblocks[0][3])
    for m, (g, qc, chunks, mp) in enumerate(mblocks):
        nxt = None
        if m + 1 < len(mblocks):
            g2_, qc2, ch2, mp2 = mblocks[m + 1]
            nxt = band_p1(g2_, QT, BQT, qc2, 128, ch2, mp2)
        band_p2(pend[0], pend[1], 128, chunks, qc, g == 0, g)
        pend = nxt
        b, k = m // 3, m % 3
        if k == 0:
            samp_s1(b)
            if b + 1 < NS:
                samp_s0(b + 1)
        elif k == 1:
            samp_s2(b)
        else:
            samp_s3(b)
    if stop == 'att-main':
        return finalize()
    S.label = 'att-ext2'
    vp = vidx[(0, 2046, 1, "h")]
    vc = vidx[(0, 2174, 1, "h")]
    chx = [(kslice(0, 2046, 1, 128), BKT[0], vp, 128, mprev_b), (kslice(0, 2174, 1, 2), BKT[0], vc, 2, mcur_b)]
    p_ = band_p1(0, QT, BQT, slice(SM0, SM0 + 2), 2, chx, None)
    band_p2(p_[0], p_[1], 2, chx, slice(SM0, SM0 + 2), True, "x")
    for g in (1, 2):
        step = WIN[g][1]
        for j in range(2):
            ltq = 2174 + j
            vp = vidx[(g, ltq - 128 * step, step, "h")]
            vc = vidx[(g, ltq, 1, "h")]
            chx = [(kslice(g, ltq - 128 * step, step, 128), BKT[g], vp, 128, None), (kslice(g, ltq, 1, 1), BKT[g], vc, 1, None)]
            p_ = band_p1(g, QT, BQT, slice(SM0 + j, SM0 + j + 1), 1, chx, None)
            band_p2(p_[0], p_[1], 1, chx, slice(SM0 + j, SM0 + j + 1), False, "x")
    Bacc = Buf("accall")
    S.op("dve", lambda e: e.memset(prodf[:, 0, 0:1], 0.0), reads=[b_ for k_ in acc_hist for b_ in acc_hist[k_]], writes=[Bacc, Bpr])
    S.label = 'att-self'
    dve(lambda e: e.tensor_tensor(out=prodf, in0=QT[:, :, SM0 + 2:SM0 + 18], in1=KTs[:, :, 2:18], op=ALU.mult), [BQT, BKTs], [Bpr])
    bk, Bk = nextbank()
    mm(bk[:, 0:96], blockones_f, prodf.rearrange("p c n -> p (c n)"), True, True, [Bc, Bpr], [Bk])
    act(pself.rearrange("p c n -> p (c n)"), bk[:, 0:96], AF.Exp, [Bk], [Bpr], scale=0.125)
    dve(lambda e: e.tensor_tensor(out=prodf, in0=pself, in1=VTs[:, :, 2:18], op=ALU.mult), [Bpr, BKTs], [Bpr])
    for g in range(3):
        for c in range(2):
            dve(lambda e, g=g, c=c: e.tensor_tensor(out=acc_n[:, c, SM0 + 2:SM0 + 18], in0=acc_n[:, c, SM0 + 2:SM0 + 18], in1=prodf[:, g * 2 + c, :], op=ALU.add), [Bacc, Bpr], [Bacc])
            dve(lambda e, g=g, c=c: e.tensor_tensor(out=acc_d[:, c, SM0 + 2:SM0 + 18], in0=acc_d[:, c, SM0 + 2:SM0 + 18], in1=pself[:, g * 2 + c, :], op=ALU.add), [Bacc, Bpr], [Bacc])
    if stop == 'att-self':
        return finalize()
    S.barrier()
    A.top = P1
    boutT2 = A.bf(2 * NCOL).rearrange("p (c n) -> p c n", c=2)
    assert A.top <= PW
    aoutT = A.bf(4 * NCOL).rearrange("p (c n) -> p c n", c=4)
    BaoutT = Buf("aoutT")
    C_TOP = A.top
    wuv = A.bf(8 * 1024).rearrange("p (c n) -> p c n", c=8)
    Bwuv = Buf("wuv")
    wv_in = w_in.rearrange("(c p) n -> p c n", p=128)
    dma("pool", wuv, wv_in[:, :, 2304:3328], [], [Bwuv], sem="w1")
    S.label = 'att-norm'
    for c in range(2):
        for (c0, n) in slices:
            dve(lambda e, c=c, c0=c0, n=n: e.reciprocal(out=acc_d[:, c, c0:c0 + n], in_=acc_d[:, c, c0:c0 + n]), [Bacc], [Bacc])
            dve(lambda e, c=c, c0=c0, n=n: e.tensor_tensor(out=boutT2[:, c, c0:c0 + n], in0=acc_n[:, c, c0:c0 + n], in1=acc_d[:, c, c0:c0 + n], op=ALU.mult), [Bacc], [BboutT])
    if stop == 'att':
        return finalize()
    S.label = 'C1'
    MT0 = NF - 4 * NCOL
    W2A = MT0 - (8192 + 2048 + 1024)
    wg = arena[:, W2A:W2A + 8192].bitcast(BF16).rearrange("p (c n) -> p c n", c=8)
    wpa = arena[:, W2A + 8192:W2A + 10240].bitcast(BF16).rearrange("p (c n) -> p c n", c=4)
    wpb = arena[:, W2A + 10240:W2A + 11264].bitcast(BF16).rearrange("p (c n) -> p c n", c=2)
    Bwg = Buf("wg")
    dma("pool", wg, wv_in[:, :, 3328:5376], [], [Bwg, Bacc], sem="w2")
    dma("pool", wpa, w_pa.rearrange("(c p) n -> p c n", p=128), [], [Bwg, Bacc], sem="w3")
    dma("pool", wpb, w_pb.rearrange("(c p) n -> p c n", p=128), [], [Bwg, Bacc], sem="w4")
    uT = A.bf(4 * NCOL).rearrange("p (c n) -> p c n", c=4)
    BuT = Buf("uT")
    NG = 3
    gv = [A.f32(512) for _ in range(NG)]
    Bgv = [Buf("gv%d" % i) for i in range(NG)]
    vn = [A.f32(512) for _ in range(NG)]
    Bvn = [Buf("vn%d" % i) for i in range(NG)]
    vnb = [A.bf(512) for _ in range(NG)]
    Bvnb = [Buf("vnb%d" % i) for i in range(NG)]
    uxe = A.bf(4 * 128).rearrange("p (c n) -> p c n", c=4)
    aoe = A.bf(4 * 128).rearrange("p (c n) -> p c n", c=4)
    Buxe = Buf("uxe")
    C1_TOP = A.top
    for (c0, n) in slices:
        for c in range(4):
            proj_fm(uT[:, c, c0:c0 + n], BuT, wuv, Bwuv, c * 128, xnT, BxnT, c0, n, func=AF.Gelu_apprx_tanh)
    for c in range(4):
        proj_fm(uxe[:, c, :], Buxe, wuv, Bwuv, c * 128, xeT, BxeT, 0, 128, func=AF.Gelu_apprx_tanh)
    gi = [0]

    def gmlp_batch(items):
        G = len(gv)
        for g0 in range(0, len(items), G):
            grp = items[g0:g0 + G]
            stis, banks = [], []
            for k, (src, Bsrc, c0, n, sample, u_ap, Bu, out_ap, Bout, vn_dst) in enumerate(grp):
                sti = stat_i[0] % 8
                stat_i[0] += 1
                stis.append(sti)
                bk, Bk = nextbank()
                banks.append((bk, Bk))
                for kc in range(8):
                    mm(bk[0:n, :], src[:, kc, c0:c0 + n], wuv[:, kc, 512:1024], kc == 0, kc == 7, [Bsrc, Bwuv], [Bk])
            for k, (src, Bsrc, c0, n, sample, u_ap, Bu, out_ap, Bout, vn_dst) in enumerate(grp):
                s_ = stat[:n, stis[k], :]
                act(gv[k][0:n, :], banks[k][0][0:n, :], AF.Gelu_apprx_tanh, [banks[k][1]], [Bgv[k], Bstat[stis[k]]], accum_out=s_[:, 4:5])
            for k, (src, Bsrc, c0, n, sample, u_ap, Bu, out_ap, Bout, vn_dst) in enumerate(grp):
                s_ = stat[:n, stis[k], :]
                dve(lambda e, s_=s_: e.tensor_scalar(out=s_[:, 5:6], in0=s_[:, 4:5], scalar1=-1.0 / 512, scalar2=None, op0=ALU.mult), [Bstat[stis[k]]], [Bstat[stis[k]]])
            for k, (src, Bsrc, c0, n, sample, u_ap, Bu, out_ap, Bout, vn_dst) in enumerate(grp):
                s_ = stat[:n, stis[k], :]
                act(vn[k][0:n, :], gv[k][0:n, :], AF.Identity, [Bgv[k], Bstat[stis[k]]], [Bvn[k]], bias=s_[:, 5:6], scale=1.0)
            for k, (src, Bsrc, c0, n, sample, u_ap, Bu, out_ap, Bout, vn_dst) in enumerate(grp):
                s_ = stat[:n, stis[k], :]
                act(gv[k][0:n, :], vn[k][0:n, :], AF.Square, [Bvn[k]], [Bgv[k], Bstat[stis[k]]], accum_out=s_[:, 0:1])
            for k, (src, Bsrc, c0, n, sample, u_ap, Bu, out_ap, Bout, vn_dst) in enumerate(grp):
                s_ = stat[:n, stis[k], :]
                dve(lambda e, s_=s_: e.tensor_scalar(out=s_[:, 1:2], in0=s_[:, 0:1], scalar1=1.0 / 512, scalar2=EPS, op0=ALU.mult, op1=ALU.add), [Bstat[stis[k]]], [Bstat[stis[k]]])
            for k, (src, Bsrc, c0, n, sample, u_ap, Bu, out_ap, Bout, vn_dst) in enumerate(grp):
                s_ = stat[:n, stis[k], :]
                act(s_[:, 2:3], s_[:, 1:2], AF.Ln, [Bstat[stis[k]]], [Bstat[stis[k]]])
            for k, (src, Bsrc, c0, n, sample, u_ap, Bu, out_ap, Bout, vn_dst) in enumerate(grp):
                s_ = stat[:n, stis[k], :]
                act(s_[:, 3:4], s_[:, 2:3], AF.Exp, [Bstat[stis[k]]], [Bstat[stis[k]]], scale=-0.5)
            for k, (src, Bsrc, c0, n, sample, u_ap, Bu, out_ap, Bout, vn_dst) in enumerate(grp):
                s_ = stat[:n, stis[k], :]
                dve(lambda e, k=k, n=n, s_=s_: e.scalar_tensor_tensor(out=vn[k][0:n, :], in0=vn[k][0:n, :], scalar=s_[:, 3:4], in1=lng_bc[0:n, :], op0=ALU.mult, op1=ALU.mult), [Bvn[k], Bstat[stis[k]], Bc], [Bvn[k]])
            for k, (src, Bsrc, c0, n, sample, u_ap, Bu, out_ap, Bout, vn_dst) in enumerate(grp):
                dve(lambda e, k=k, n=n: e.tensor_tensor(out=vn[k][0:n, :], in0=vn[k][0:n, :], in1=lnb_bc[0:n, :], op=ALU.add), [Bvn[k], Bc], [Bvn[k]])
            for k, (src, Bsrc, c0, n, sample, u_ap, Bu, out_ap, Bout, vn_dst) in enumerate(grp):
                tcopy("dve", vnb[k][0:n, :], vn[k][0:n, :], [Bvn[k]], [Bvnb[k]])
                if vn_dst is not None:
                    dma("sp" if n == 128 else "pool", vn_dst, vn[k][0:n, :], [Bvn[k]], [], sem=("vo%d" if n == 128 else "vp%d") % k)
                    out_bufs.append(Bvn[k])
            mbanks = []
            for k, (src, Bsrc, c0, n, sample, u_ap, Bu, out_ap, Bout, vn_dst) in enumerate(grp):
                bm, Bm = nextbank()
                mbanks.append((bm, Bm))
                nt = 16 if sample else 128
                for g in range(4):
                    o = bm[:, g * 128:g * 128 + nt]
                    if sample:
                        mm(o, vnb[k][0:n, g * 128:(g + 1) * 128], D16[0:16, g, :], True, False, [Bvnb[k], Bc], [Bm])
                        mm(o, ones_b[0:1, :], bsp0_b[0:1, g * 16:(g + 1) * 16], False, True, [Bc], [Bm])
                    else:
                        mm(o, vnb[k][0:n, g * 128:(g + 1) * 128], WmT[:, g, :], True, False, [Bvnb[k], Bc], [Bm])
                        mm(o, ones_b[0:1, :], bsp_b[0:1, g * 128:(g + 1) * 128], False, True, [Bc], [Bm])
            for k, (src, Bsrc, c0, n, sample, u_ap, Bu, out_ap, Bout, vn_dst) in enumerate(grp):
                nt = 16 if sample else 128
                m4 = mbanks[k][0].rearrange("p (g t) -> p g t", g=4)[:, :, 0:nt]
                dve(lambda e, m4=m4, out_ap=out_ap, u_ap=u_ap: e.tensor_tensor(out=out_ap, in0=m4, in1=u_ap, op=ALU.mult), [mbanks[k][1], Bu], [Bout])

    gitems = []
    for t in range(16):
        cs = slice(t * 128, (t + 1) * 128)
        gitems.append((xnT, BxnT, t * 128, 128, False, uT[:, :, cs], BuT, aoutT[:, :, cs], BaoutT, vch if t == 15 else None))
    gitems.append((xeT, BxeT, 0, 128, False, uxe, Buxe, aoe, Buxe, None))
    gitems.append((xnT, BxnT, SM0 + 2, NS, True, uT[:, :, SM0 + 2:SM0 + 18], BuT, aoutT[:, :, SM0 + 2:SM0 + 18], BaoutT, vchs))
    gmlp_batch(gitems)
    tcopy("dve", aoutT[:, :, SM0:SM0 + 2], aoe[:, :, 126:128], [Buxe], [BaoutT])
    if stop == 'C1':
        return finalize()
    S.label = 'C2a'
    S.barrier()
    A.top = C_TOP
    assert C1_TOP <= W2A, (C1_TOP, W2A)
    tg = [A.f32(512) for _ in range(2)]
    Btg = [Buf("tg0"), Buf("tg1")]
    t1 = [A.f32(512) for _ in range(2)]
    Bt1 = [Buf("t10"), Buf("t11")]
    mt = arena[:, MT0:NF].bitcast(BF16).rearrange("p (c n) -> p c n", c=8)
    Bmt = Buf("mT")
    oc_i = [0]
    for si, (c0, n) in enumerate(slices):
        for oc in range(8):
            i = oc_i[0] % 2
            oc_i[0] += 1
            ba, Ba = nextbank()
            for kc in range(4):
                mm(ba[:, 0:n], wpa[:, kc, oc * 128:(oc + 1) * 128], aoutT[:, kc, c0:c0 + n], kc == 0, kc == 3, [Bwg, BaoutT], [Ba])
            bb, Bb = nextbank()
            for kc in range(2):
                mm(bb[:, 0:n], wpb[:, kc, oc * 128:(oc + 1) * 128], boutT2[:, kc, c0:c0 + n], kc == 0, kc == 1, [Bwg, BboutT], [Bb])
            proj_fm(tg[i][:, 0:n], Btg[i], wg, Bwg, oc * 128, xnT, BxnT, c0, n, func=AF.Tanh, scale=0.5)
            dve(lambda e, i=i, ba=ba, n=n: e.scalar_tensor_tensor(out=t1[i][:, 0:n], in0=tg[i][:, 0:n], scalar=1.0, in1=ba[:, 0:n], op0=ALU.add, op1=ALU.mult), [Btg[i], Ba], [Bt1[i]])
            proj_fm(tg[i][:, 0:n], Btg[i], wg, Bwg, 1024 + oc * 128, xnT, BxnT, c0, n, func=AF.Tanh, scale=0.5)
            dve(lambda e, i=i, bb=bb, n=n: e.scalar_tensor_tensor(out=tg[i][:, 0:n], in0=tg[i][:, 0:n], scalar=1.0, in1=bb[:, 0:n], op0=ALU.add, op1=ALU.mult), [Btg[i], Bb], [Btg[i]])
            dve(lambda e, i=i, n=n, oc=oc, c0=c0: e.tensor_tensor(out=mt[:, oc, c0:c0 + n], in0=t1[i][:, 0:n], in1=tg[i][:, 0:n], op=ALU.add), [Bt1[i], Btg[i]], [Bmt])

    if stop == 'C2a':
        return finalize()
    S.label = 'C2b'
    S.barrier()
    A.top = CONST_TOP
    hnT = A.bf(8 * NCOL).rearrange("p (c n) -> p c n", c=8)
    BhnT = Buf("hnT")
    hbuf = A.f32(16 * 1024).rearrange("p (t n) -> p t n", t=16)
    hsm = A.f32(1024)
    Bh = [Buf("h%d" % t) for t in range(16)]
    Bhsm = Buf("hsm")
    HB_TOP = A.top
    wo = A.bf(8 * 1024).rearrange("p (c n) -> p c n", c=8)
    Bwo = Buf("wo")
    dma("pool", wo, w_out.rearrange("(c p) n -> p c n", p=128), [], [Bwo], sem="w5")
    NX = 3
    xt = [A.f32(1024) for _ in range(NX)]
    Bxt = [Buf("xt%db" % i) for i in range(NX)]
    xb = [A.bf(1024) for _ in range(NX)]
    Bxb = [Buf("xb%db" % i) for i in range(NX)]
    assert A.top <= MT0, (A.top, MT0)

    def mk_pre(t, c0o, m):
        def pre(k):
            if t >= 0:
                hdst, Bhd = hbuf[:, t, :], Bh[t]
            else:
                hdst, Bhd = hsm, Bhsm
            for half in range(2):
                bk, Bk = nextbank()
                for kc in range(8):
                    mm(bk[0:m, :], mt[:, kc, c0o:c0o + m], wo[:, kc, half * 512:(half + 1) * 512], kc == 0, kc == 7, [Bmt, Bwo], [Bk])
                dve(lambda e, bk=bk, half=half, k=k, hdst=hdst: e.scalar_tensor_tensor(out=hdst[0:m, half * 512:(half + 1) * 512], in0=bk[0:m, :], scalar=0.5, in1=xt[k][0:m, half * 512:(half + 1) * 512], op0=ALU.mult, op1=ALU.add), [Bk, Bxt[k]], [Bhd])
            return (hdst, Bhd)
        return pre

    HS0 = 42404
    assert A.top <= HS0 and HS0 + 1408 + 1024 <= MT0, (A.top, MT0)
    hsT = arena[:, HS0:HS0 + 1408].rearrange("p (c k n) -> p c k n", c=44, k=2)
    BhsT = Buf("hsT")
    scst2 = [arena[:, HS0 + 1408 + i * 512:HS0 + 1408 + (i + 1) * 512].rearrange("p (k n) -> p k n", k=2) for i in range(2)]
    Bscst2 = [Buf("scst0"), Buf("scst1")]
    def sconv_piece(q):
        sc_, Bsc_ = scst2[q % 2], Bscst2[q % 2]
        dma("sp", sc_[0:16, :, :], sconv[:, :, q * 256:(q + 1) * 256], [], [Bsc_], sem="sc%d" % (q % 2))
        for cc in range(2):
            ch = q * 2 + cc
            bk, Bk = nextbank()
            for k in range(2):
                mm(bk[:, k * 16:(k + 1) * 16], sc_[0:16, k, cc * 128:(cc + 1) * 128], ident_f[0:16, 0:16], True, True, [Bsc_, Bc], [Bk])
            tcopy("dve", hsT[:, ch, :, :], bk[:, 0:32].rearrange("p (k n) -> p k n", k=2), [Bk], [BhsT])
        dma("pool", convs[:, 0, q * 256:(q + 1) * 256], sc_[0:16, 1, :], [Bsc_], [], sem="sco%d" % (q % 2))
        out_bufs.append(Bsc_)


    sc_done = [0]

    def sc_hook(gi_):
        for _ in range(4):
            if sc_done[0] < 22:
                sconv_piece(sc_done[0])
                sc_done[0] += 1

    citems = [(xall[HX + t * 128:HX + (t + 1) * 128, :], 128, hnT[:, :, t * 128:(t + 1) * 128], BhnT, gf_bc, mk_pre(t, t * 128, 128)) for t in range(16)]
    norm_T_batch(citems, xt, Bxt, xb, Bxb, "xc", group_hook=sc_hook)
    dma("sp", xt[0][0:2, :], xall[HX - 2:HX, :], [], [Bxt[0]], sem="xc0")
    dma("sp", xt[0][2:18, :], xs, [], [Bxt[0]], sem="xq0")
    norm_T_batch([(None, 18, hnT[:, :, SM0:SM0 + 18], BhnT, gf_bc, mk_pre(-1, SM0, 18))], xt, Bxt, xb, Bxb, "xc")
    if stop == 'C2b':
        return finalize()
    for q in range(sc_done[0], 22):
        sconv_piece(q)

    S.label = 'D'
    S.barrier()
    A.top = HB_TOP
    HT = 1024
    KG = [(0, 4), (4, 4), (8, 4), (12, 4), (16, 3), (19, 3)]
    cbuf = [[A.f32(1024) for _ in range(2)] for _ in range(3)]
    Bcb = [[Buf("c%d%d" % (a_, b_)) for b_ in range(2)] for a_ in range(3)]
    upb = [[A.f32(2 + HT) for _ in range(2)] for _ in range(2)]
    Bupb = [[Buf("up%d%d" % (a_, b_)) for b_ in range(2)] for a_ in range(2)]
    prodS = A.bf(22 * 18).rearrange("p (j n) -> p j n", j=22)
    BprodS = Buf("prodS")
    ups = [[A.f32(18) for _ in range(2)] for _ in range(3)]
    cs_ = [[A.f32(18) for _ in range(2)] for _ in range(3)]
    Bcs = [[Buf("cs%d%d" % (a_, b_)) for b_ in range(2)] for a_ in range(3)]
    hist = A.f32(44 * 2).rearrange("p (c n) -> p c n", c=44)
    Bhist = Buf("hist")
    upst = [A.f32(256) for _ in range(2)]
    Bupst = [Buf("upst0"), Buf("upst1")]
    assert A.top <= HS0, A.top
    A.top = HS0 + 1408
    prodT = A.bf(4 * HT).rearrange("p (j n) -> p j n", j=4)
    BprodT = [Buf("prodT%d" % i) for i in range(6)]
    NWU = 3
    wup = [A.bf(8 * 256).rearrange("p (c n) -> p c n", c=8) for _ in range(NWU)]
    Bwup = [Buf("wup%d" % i) for i in range(NWU)]
    wdns = [A.bf(4 * 1024).rearrange("p (j n) -> p j n", j=4) for _ in range(2)]
    Bwdns = [Buf("wdn0"), Buf("wdn1")]
    wdi = [0]
    wdq = []

    def load_wdn(j0, gs):
        i = wdi[0] % 2
        wdi[0] += 1
        dma("pool", wdns[i][:, 0:gs, :], w_dn.rearrange("(j p) n -> p j n", p=128)[:, j0:j0 + gs, :], [], [Bwdns[i]], sem="wd%d" % i)
        wdq.append((wdns[i], Bwdns[i]))
    assert A.top <= NF, A.top

    wv = w_up.rearrange("(c p) n -> p c n", p=128)
    wslot = {}
    wi = [0]

    def load_wup(j):
        wsl = wi[0] % NWU
        wi[0] += 1
        w_, Bw_ = wup[wsl], Bwup[wsl]
        dma("pool", w_[:, :, 0:128], wv[:, :, j * 128:(j + 1) * 128], [], [Bw_], sem="wu%da" % wsl)
        dma("pool", w_[:, :, 128:256], wv[:, :, 2816 + j * 128:2816 + (j + 1) * 128], [], [Bw_], sem="wu%db" % wsl)
        return w_, Bw_

    def stageA(H, j):
        base = H * HT
        w_, Bw_ = wslot[(H, j)]
        sl = j % 2
        s3 = j % 3
        for gv_ in range(2):
            ch = gv_ * 22 + j
            ub, Bub = upb[sl][gv_], Bupb[sl][gv_]
            if H == 0:
                if gv_ == 0:
                    bks_, Bks_ = nextbank()
                so = gv_ * 32
                for kc in range(8):
                    mm(bks_[:, so:so + 18], w_[:, kc, gv_ * 128:(gv_ + 1) * 128], hnT[:, kc, SM0:SM0 + 18], kc == 0, kc == 7, [Bw_, BhnT], [Bks_])
                if gv_ == 1:
                    for kc in range(8):
                        mm(bks_[0:20, 64:320], hnT[:, kc, NM - 2:NM + 18], w_[:, kc, :], kc == 0, kc == 7, [BhnT, Bw_], [Bks_])
                    for g2_ in range(2):
                        ub2, Bub2 = upb[sl][g2_], Bupb[sl][g2_]
                        ch2 = g2_ * 22 + j
                        so2 = g2_ * 32
                        act(ub2[:, 0:2], bks_[:, so2:so2 + 2], AF.Copy, [Bks_, Bc], [Bub2], scale=flag[:, 0:1])
                        act(ups[s3][g2_], bks_[:, so2:so2 + 18], AF.Copy, [Bks_], [Bcs[s3][g2_]])
                        cs = cs_[s3][g2_]
                        act(cs, ups[s3][g2_], AF.Identity, [Bcs[s3][g2_], Bc], [Bcs[s3][g2_]], scale=cwT[:, 2, ch2:ch2 + 1], bias=cwT[:, 3, ch2:ch2 + 1])
                        for k in range(2):
                            dve(lambda e, cs=cs, ch2=ch2, k=k: e.scalar_tensor_tensor(out=cs[:, 2:18], in0=hsT[:, ch2, k, :], scalar=cwT[:, k, ch2:ch2 + 1], in1=cs[:, 2:18], op0=ALU.mult, op1=ALU.add), [BhsT, Bc, Bcs[s3][g2_]], [Bcs[s3][g2_]])
                    ui = j % 2
                    tcopy("dve", upst[ui][0:20, :], bks_[0:20, 64:320], [Bks_], [Bupst[ui]])
                    for g2_ in range(2):
                        cc0 = g2_ * 2816 + j * 128
                        dma("pool", convp[:, cc0:cc0 + 128], upst[ui][0:2, g2_ * 128:(g2_ + 1) * 128], [Bupst[ui]], [], sem="uo%d" % ui)
                        dma("pool", convs[:, 1, cc0:cc0 + 128], upst[ui][4:20, g2_ * 128:(g2_ + 1) * 128], [Bupst[ui]], [], sem="uo%d" % ui)
                    out_bufs.append(Bupst[ui])
            else:
                tcopy("dve", ub[:, 0:2], hist[:, ch, :], [Bhist], [Bub])
            for s2 in range(2):
                c0 = base + s2 * 512
                bk, Bk = nextbank()
                for kc in range(8):
                    mm(bk, w_[:, kc, gv_ * 128:(gv_ + 1) * 128], hnT[:, kc, c0:c0 + 512], kc == 0, kc == 7, [Bw_, BhnT], [Bk])
                act(ub[:, 2 + s2 * 512:2 + (s2 + 1) * 512], bk, AF.Copy, [Bk], [Bub])
            if H == 0:
                tcopy("dve", hist[:, ch, :], ub[:, HT:HT + 2], [Bub], [Bhist])
        for gv_ in range(2):
            ch = gv_ * 22 + j
            ub, Bub = upb[sl][gv_], Bupb[sl][gv_]
            cc, Bcc = cbuf[s3][gv_], Bcb[s3][gv_]
            act(cc, ub[:, 2:2 + HT], AF.Identity, [Bub, Bc], [Bcc], scale=cwT[:, 2, ch:ch + 1], bias=cwT[:, 3, ch:ch + 1])
        for gv_ in range(2):
            ch = gv_ * 22 + j
            ub, Bub = upb[sl][gv_], Bupb[sl][gv_]
            cc, Bcc = cbuf[s3][gv_], Bcb[s3][gv_]
            for k in range(2):
                dve(lambda e, cc=cc, ub=ub, k=k, ch=ch: e.scalar_tensor_tensor(out=cc, in0=ub[:, k:k + HT], scalar=cwT[:, k, ch:ch + 1], in1=cc, op0=ALU.mult, op1=ALU.add), [Bub, Bc, Bcc], [Bcc])

    def stageB(H, j, jj):
        s3 = j % 3
        cg, cv = cbuf[s3][0], cbuf[s3][1]
        act(cg, cg, AF.Gelu_apprx_tanh, [Bcb[s3][0]], [Bcb[s3][0]])
        dve(lambda e, cg=cg, cv=cv, jj=jj: e.tensor_tensor(out=prodT[:, jj, :], in0=cg, in1=cv, op=ALU.mult), [Bcb[s3][0], Bcb[s3][1]], [BprodT[jj]])
        if H == 0:
            act(cs_[s3][0], cs_[s3][0], AF.Gelu_apprx_tanh, [Bcs[s3][0]], [Bcs[s3][0]])
            dve(lambda e, s3=s3, j=j: e.tensor_tensor(out=prodS[:, j, :], in0=cs_[s3][0], in1=cs_[s3][1], op=ALU.mult), [Bcs[s3][0], Bcs[s3][1]], [BprodS])

    def wdown_group(H, j0, gs):
        wdn, Bwdn = wdq.pop(0)
        for tb in range(4):
            ths = [(tb * 2 + q_, half) for q_ in range(2) for half in range(2)]
            bks = [nextbank() for _ in ths]
            for (tl, half), (bk, Bk) in zip(ths, bks):
                for jj in range(gs - 1):
                    mm(bk, prodT[:, jj, tl * 128:(tl + 1) * 128], wdn[:, jj, half * 512:(half + 1) * 512], jj == 0, False, [BprodT[jj], Bwdn], [Bk])
            for (tl, half), (bk, Bk) in zip(ths, bks):
                jj = gs - 1
                mm(bk, prodT[:, jj, tl * 128:(tl + 1) * 128], wdn[:, jj, half * 512:(half + 1) * 512], False, True, [BprodT[jj], Bwdn], [Bk])
            for (tl, half), (bk, Bk) in zip(ths, bks):
                t = H * 8 + tl
                dve(lambda e, bk=bk, t=t, half=half: e.tensor_tensor(out=hbuf[:, t, half * 512:(half + 1) * 512], in0=bk, in1=hbuf[:, t, half * 512:(half + 1) * 512], op=ALU.add), [Bk, Bh[t]], [Bh[t]])
        if H == 0:
            for half in range(2):
                bk, Bk = nextbank()
                for jj in range(gs):
                    mm(bk[0:18, :], prodS[:, j0 + jj, :], wdn[:, jj, half * 512:(half + 1) * 512], jj == 0, jj == gs - 1, [BprodS, Bwdn], [Bk])
                dve(lambda e, bk=bk, half=half: e.tensor_tensor(out=hsm[0:18, half * 512:(half + 1) * 512], in0=bk[0:18, :], in1=hsm[0:18, half * 512:(half + 1) * 512], op=ALU.add), [Bk, Bhsm], [Bhsm])

    def final_rows(haps, dsts):
        stis = []
        for k, (hap, m, Bh_) in enumerate(haps):
            sti = stat_i[0] % 8
            stat_i[0] += 1
            stis.append(sti)
            scr = cbuf[k % 3][(k // 3) % 2]
            Bscr = Bcb[k % 3][(k // 3) % 2]
            act(scr.bitcast(BF16)[0:m, 0:1024], hap, AF.Square, [Bh_], [Bscr, Bstat[sti]], accum_out=stat[:m, sti, 0:1])
        rs = []
        for k, (hap, m, Bh_) in enumerate(haps):
            rs.append(rstd_of(stat[:m, stis[k], 0:1], m, 1.0 / 1024, stis[k], Bstat[stis[k]]))
        for k, (hap, m, Bh_) in enumerate(haps):
            dve(lambda e, hap=hap, r=rs[k], m=m: e.scalar_tensor_tensor(out=hap, in0=hap, scalar=r, in1=gfin_bc[0:m, :], op0=ALU.mult, op1=ALU.mult), [Bh_, Bstat[stis[k]], Bc], [Bh_])
        for k, (hap, m, Bh_) in enumerate(haps):
            dst, r0 = dsts[k]
            dma("sp" if m == 128 else "pool", dst, hap[r0:m, :], [Bh_], [], sem=("yo%d" if m == 128 else "yp%d") % (k % 4))
            out_bufs.append(Bh_)

    for H in range(2):
        order = [(j0, gs, jj) for (j0, gs) in KG for jj in range(gs)]
        PRE = 3
        jof = lambda q_: order[q_][0] + order[q_][2]
        for q_ in range(min(PRE, len(order))):
            wslot[(H, jof(q_))] = load_wup(jof(q_))
        doneA = set()

        def doA(q_):
            if q_ < len(order) and q_ not in doneA:
                doneA.add(q_)
                stageA(H, jof(q_))

        load_wdn(*KG[0])
        gnext = [1]
        doA(0)
        doA(1)
        for idx, (j0, gs, jj) in enumerate(order):
            j = j0 + jj
            if idx + PRE < len(order):
                wslot[(H, jof(idx + PRE))] = load_wup(jof(idx + PRE))
            if jj == gs - 1:
                stageB(H, j, jj)
                doA(idx + 2)
                doA(idx + 3)
                if gnext[0] < len(KG):
                    load_wdn(*KG[gnext[0]])
                    gnext[0] += 1
                wdown_group(H, j0, gs)
            else:
                doA(idx + 2)
                stageB(H, j, jj)
        final_rows([(hbuf[:, H * 8 + tl, :], 128, Bh[H * 8 + tl]) for tl in range(8)],
                   [(y[(H * 8 + tl) * 128:(H * 8 + tl + 1) * 128, :], 0) for tl in range(8)])
        if H == 0:
            final_rows([(hsm[0:18, :], 18, Bhsm)], [(ys, 2)])

    return finalize()


_NC_CACHE = {}


def make_in_maps(x_prompt, x_sample, cache_kv_w128, cache_kv_w512, cache_kv_w2048, state_conv_ffn,
                 norm_mix, w_in, ln_v_gain, ln_v_bias, w_spatial, b_spatial, w_proj_a, w_proj_b,
                 w_out, norm_ffn, w_up, conv_w, conv_b, w_down, norm_final, cores=None):
    f = lambda a: np.ascontiguousarray(np.asarray(a, dtype=np.float32))
    x_prompt = f(x_prompt)
    B, SEQ, D = x_prompt.shape
    xp = np.zeros((B, HX + SEQ, D), np.float32)
    xp[:, HX:] = x_prompt
    k = np.arange(128)[:, None]
    q = np.arange(128)[None, :]
    mcur = (k <= q).astype(np.float32)
    mprev = (k >= q).astype(np.float32)
    blockones = ((k // 64) == (q // 64)).astype(np.float32)
    caches = [f(cache_kv_w128)[0], f(cache_kv_w512)[0], f(cache_kv_w2048)[0]]
    common = {
        "w_in": f(w_in)[0], "w_pa": f(w_proj_a)[0], "w_pb": f(w_proj_b)[0], "w_out": f(w_out)[0],
        "w_up": f(w_up)[0], "w_dn": f(w_down)[0],
        "vec8": np.concatenate([f(norm_mix)[0].reshape(8, 128), f(norm_ffn)[0].reshape(8, 128)], 0),
        "nfin": f(norm_final), "lng": f(ln_v_gain)[0], "lnb": f(ln_v_bias)[0],
        "cwb": np.concatenate([f(conv_w)[0], f(conv_b)], 0).reshape(4, 44, 128),
        "wsp": f(w_spatial)[0], "bsp": f(b_spatial)[0],
        "w00": np.ascontiguousarray(f(w_spatial)[0][:, 0, 0]),
    }
    in_maps = []
    for c in (range(NCORES) if cores is None else cores):
        b, qi = c // 4, c % 4
        fl = 0.0 if qi == 0 else 1.0
        m = dict(common)
        m["xall"] = np.ascontiguousarray(xp[b, qi * NM:qi * NM + HX + NM])
        m["xs"] = f(x_sample)[c * NS:(c + 1) * NS, 0]
        for g in range(3):
            cg = caches[g][c * NS:(c + 1) * NS]
            m["ck%d" % g] = np.ascontiguousarray(cg.reshape(NS, cg.shape[1], 512))
        m["sconv"] = f(state_conv_ffn)[0, c * NS:(c + 1) * NS]
        m["cst"] = np.stack([np.eye(128, dtype=np.float32), mprev, mcur, mprev * fl, mcur, blockones, np.ones((128, 128), np.float32)])
        m["flagd"] = np.full((128, 1), fl, np.float32)
        in_maps.append(m)
    return in_maps


def assemble(R, B=2):
    y_prompt = np.stack([np.concatenate([R[b * 4 + qi]["y"] for qi in range(4)], 0) for b in range(B)])
    y_sample = np.concatenate([R[c]["ys"] for c in range(NCORES)], 0)[:, None, :]
    outs = [y_prompt, y_sample]
    for g in range(3):
        L = WIN[g][0]
        kvp = np.stack([R[b * 4 + 3]["kv%d" % g] for b in range(B)]).reshape(1, B, L, 2, 4, 64)
        kvsm = np.concatenate([R[c]["kvs%d" % g] for c in range(NCORES)], 0).reshape(1, NCORES * NS, 1, 2, 4, 64)
        outs += [kvp, kvsm]
    vcp = np.stack([R[b * 4 + 3]["vch"] for b in range(B)])[None]
    vcs = np.concatenate([R[c]["vchs"] for c in range(NCORES)], 0)[None, :, None, :]
    cp = np.stack([R[b * 4 + 3]["convp"] for b in range(B)])[None]
    cs = np.concatenate([R[c]["convs"] for c in range(NCORES)], 0)[None]
    outs += [vcp, vcs, cp, cs]
    return tuple(np.ascontiguousarray(np.asarray(o, dtype=np.float32)) for o in outs)


def kernel(**inputs):
    in_maps = make_in_maps(**inputs)
    if "nc" not in _NC_CACHE:
        _NC_CACHE["nc"] = build_program()
    res = run_bass_kernel_spmd(_NC_CACHE["nc"], in_maps, core_ids=list(range(NCORES)))
    return assemble(res.results)
```

```python
import numpy as np
from contextlib import ExitStack
import concourse.bass as bass
import concourse.mybir as mybir
from concourse.bass_utils import run_bass_kernel_spmd

F32 = mybir.dt.float32
BF16 = mybir.dt.bfloat16
AF = mybir.ActivationFunctionType
ALU = mybir.AluOpType

NCORES = 8
HX = 2176
NM = 2048
NS = 16
SM0 = NM
NCOL = NM + 2 + NS
EPS = 1e-6
WIN = ((128, 1), (512, 4), (2048, 16))
KBASE = (1920, 1536, 0)
KLEN = (2304, 2688, 4224)


class Buf:
    __slots__ = ("name", "w", "r")

    def __init__(self, name=""):
        self.name = name
        self.w = None
        self.r = {}


class Sched:
    ENG = ("pe", "act", "dve", "pool", "sp")

    def __init__(self, nc, stack):
        self.nc = nc
        self.stack = stack
        self.prog = {e: [] for e in self.ENG}
        self.sems = {}
        self.cnt = {}
        self.seen = {e: {} for e in self.ENG}
        self.label = ""
        for e in self.ENG:
            self._sem("E_" + e)

    def _sem(self, name):
        if name not in self.sems:
            self.sems[name] = self.stack.enter_context(self.nc.semaphore(name))
            self.cnt[name] = 0
        return name

    def _need(self, eng, waits, tok):
        if tok is None:
            return
        sem, val = tok
        if eng == "pe" and sem == "E_pe":
            return
        if self.seen[eng].get(sem, 0) >= val:
            return
        self.seen[eng][sem] = val
        waits.append((sem, val))

    def op(self, eng, fn, reads=(), writes=(), dma=None):
        waits = []
        for b in reads:
            self._need(eng, waits, b.w)
        for b in writes:
            self._need(eng, waits, b.w)
            for s, v in b.r.items():
                self._need(eng, waits, (s, v))
        if dma is not None:
            sem = self._sem(dma)
            inc = 16
        else:
            sem = "E_" + eng
            inc = 1
        self.cnt[sem] += inc
        tok = (sem, self.cnt[sem])
        self.prog[eng].append((waits, fn, (sem, inc), self.label + " r:" + ",".join(b.name for b in reads) + " w:" + ",".join(b.name for b in writes)))
        for b in reads:
            if b.r.get(sem, 0) < tok[1]:
                b.r[sem] = tok[1]
        for b in writes:
            b.w = tok
            b.r = {}
        return tok

    def barrier(self):
        allw = [(s, v) for s, v in self.cnt.items() if v > 0]
        for e in self.ENG:
            waits = []
            for t in allw:
                self._need(e, waits, t)
            self.prog[e].append((waits, None, None, ""))

    def emit(self):
        nc = self.nc
        with nc.Block() as block:
            def run(name, e):
                for waits, fn, inc, lab in self.prog[name]:
                    for s, v in waits:
                        e.wait_ge(self.sems[s], v)
                    if fn is not None:
                        with nc.named_scope(lab.split(" ")[0] or "none"):
                            ins = fn(e)
                        ins.then_inc(self.sems[inc[0]], inc[1])

            @block.tensor
            def _(e):
                run("pe", e)

            @block.scalar
            def _(e):
                run("act", e)

            @block.vector
            def _(e):
                run("dve", e)

            @block.gpsimd
            def _(e):
                run("pool", e)

            @block.sync
            def _(e):
                run("sp", e)


def build_program(stop=None):
    nc = bass.Bass("TRN2", target_bir_lowering=False)

    def din(name, shape):
        return nc.dram_tensor(name, list(shape), F32, kind="ExternalInput").ap()

    def dout(name, shape):
        return nc.dram_tensor(name, list(shape), F32, kind="ExternalOutput").ap()

    xall = din("xall", [HX + NM, 1024])
    xs = din("xs", [NS, 1024])
    ck = [din("ck0", [NS, 128, 512]), din("ck1", [NS, 512, 512]), din("ck2", [NS, 2048, 512])]
    sconv = din("sconv", [NS, 2, 5632])
    w_in = din("w_in", [1024, 5376])
    w_pa = din("w_pa", [512, 1024])
    w_pb = din("w_pb", [256, 1024])
    w_out = din("w_out", [1024, 1024])
    w_up = din("w_up", [1024, 5632])
    w_dn = din("w_dn", [2816, 1024])
    vec8 = din("vec8", [16, 128])
    nfin = din("nfin", [1024])
    lng = din("lng", [512])
    lnb = din("lnb", [512])
    cwb = din("cwb", [4, 44, 128])
    wsp = din("wsp", [4, 128, 128])
    bsp = din("bsp", [4, 128])
    w00 = din("w00", [4])
    cst = din("cst", [7, 128, 128])
    flagd = din("flagd", [128, 1])

    y = dout("y", [NM, 1024])
    ys = dout("ys", [NS, 1024])
    kvo = [dout("kv0", [128, 512]), dout("kv1", [512, 512]), dout("kv2", [2048, 512])]
    kvs = [dout("kvs0", [NS, 512]), dout("kvs1", [NS, 512]), dout("kvs2", [NS, 512])]
    vch = dout("vch", [128, 512])
    vchs = dout("vchs", [NS, 512])
    convp = dout("convp", [2, 5632])
    convs = dout("convs", [NS, 2, 5632])

    st = ExitStack()
    S = Sched(nc, st)
    NF = 53100
    arena = st.enter_context(nc.sbuf_tensor("arena", [128, NF], F32))
    psum_all = st.enter_context(nc.psum_tensor("psall", [128, 4096], F32))
    pbank = [psum_all[:, i * 512:(i + 1) * 512] for i in range(8)]
    PB = [Buf("pb%d" % i) for i in range(8)]
    bank_i = [0]

    def nextbank():
        i = bank_i[0] % 8
        bank_i[0] += 1
        return pbank[i], PB[i]

    def nextpair():
        if bank_i[0] % 2:
            bank_i[0] += 1
        i = bank_i[0] % 8
        bank_i[0] += 2
        return pbank[i], PB[i], pbank[i + 1], PB[i + 1], psum_all[:, i * 512:(i + 2) * 512]

    class Arena:
        def __init__(self):
            self.top = 0

        def f32(self, n):
            a = arena[:, self.top:self.top + n]
            self.top += n
            assert self.top <= NF, self.top
            return a

        def bf(self, n):
            n2 = (n + 1) // 2
            a = arena[:, self.top:self.top + n2].bitcast(BF16)
            self.top += n2
            assert self.top <= NF, self.top
            return a

    A = Arena()
    TT = [NF - 2200]

    def tmp_f32(n):
        a = arena[:, TT[0]:TT[0] + n]
        TT[0] += n
        assert TT[0] <= NF
        return a
    out_bufs = []
    uid = [0]

    def finalize():
        S.barrier()
        S.emit()
        st.close()
        return nc

    def dma(eng, out, in_, rd, wr, sem=None, **kw):
        if sem is None:
            uid[0] += 1
            sem = "d%d" % (uid[0] % 24)
        return S.op(eng, lambda e: e.dma_start(out=out, in_=in_, **kw), reads=rd, writes=wr, dma=sem)

    def mm(out, lhsT, rhs, start, stop, rd, wr):
        S.op("pe", lambda e: e.matmul(out, lhsT=lhsT, rhs=rhs, start=start, stop=stop), reads=rd, writes=wr)

    def act(out, in_, func, rd, wr, **kw):
        S.op("act", lambda e: e.activation(out=out, in_=in_, func=func, **kw), reads=rd, writes=wr)

    def dve(fn, rd, wr):
        S.op("dve", fn, reads=rd, writes=wr)

    def tcopy(eng, out, in_, rd, wr):
        S.op(eng, lambda e: e.tensor_copy(out=out, in_=in_), reads=rd, writes=wr)

    Bc = Buf("const")
    cbs = []

    def CW():
        nb = Buf("c%d" % len(cbs))
        cbs.append(nb)
        return [nb]

    def CR():
        return list(cbs)
    cst_f = A.f32(7 * 128).rearrange("p (k n) -> p k n", k=7)
    dma("sp", cst_f, cst.rearrange("k p n -> p k n"), [], CW(), sem="c0")
    ident_f = cst_f[:, 0, :]
    blockones_f = cst_f[:, 5, :]
    cst_b = A.bf(7 * 128).rearrange("p (k n) -> p k n", k=7)
    tcopy("dve", cst_b, cst_f, CR(), CW())
    ident_b = cst_b[:, 0, :]
    mprev_b = cst_b[:, 1, :]
    mcur_b = cst_b[:, 2, :]
    medge_b = cst_b[:, 3, :]
    ones_b = cst_b[:, 6, :]
    flag = A.f32(1)
    dma("sp", flag, flagd, [], CW(), sem="c1")
    gfin_bc = A.f32(1024)
    dma("sp", gfin_bc, nfin.partition_broadcast(128), [], CW(), sem="c2")
    lng_bc = A.f32(512)
    lnb_bc = A.f32(512)
    dma("sp", lng_bc, lng.partition_broadcast(128), [], CW(), sem="c3")
    dma("sp", lnb_bc, lnb.partition_broadcast(128), [], CW(), sem="c4")
    w00_bc = A.f32(4)
    dma("sp", w00_bc, w00.partition_broadcast(128), [], CW(), sem="c5")
    v8_sb = tmp_f32(128)
    dma("sp", v8_sb[0:16, :], vec8, [], CW(), sem="c6")
    cw_sb = tmp_f32(4 * 128).rearrange("p (k n) -> p k n", k=4)
    dma("sp", cw_sb[0:44, :, :], cwb.rearrange("k c p -> c k p"), [], CW(), sem="c7")
    gvec = A.f32(16)
    cwT = A.f32(4 * 44).rearrange("p (k c) -> p k c", k=4)
    bk, Bk = nextbank()
    mm(bk[:, 0:16], v8_sb[0:16, :], ident_f[0:16, 0:16], True, True, CR(), [Bk])
    tcopy("dve", gvec, bk[:, 0:16], [Bk], CW())
    bk, Bk = nextbank()
    for k in range(4):
        mm(bk[:, k * 44:(k + 1) * 44], cw_sb[0:44, k, :], ident_f[0:44, 0:44], True, True, CR(), [Bk])
    tcopy("dve", cwT, bk[:, 0:176].rearrange("p (k c) -> p k c", k=4), [Bk], CW())
    ones_f = A.f32(128)
    S.op("dve", lambda e: e.memset(ones_f, 1.0), writes=CW())
    gm_bc = A.bf(8 * 128).rearrange("p (c n) -> p c n", c=8)
    gf_bc = A.bf(8 * 128).rearrange("p (c n) -> p c n", c=8)
    for c in range(8):
        dve(lambda e, c=c: e.tensor_scalar(out=gm_bc[:, c, :], in0=ones_f, scalar1=gvec[:, c:c + 1], scalar2=None, op0=ALU.mult), CR(), CW())
        dve(lambda e, c=c: e.tensor_scalar(out=gf_bc[:, c, :], in0=ones_f, scalar1=gvec[:, 8 + c:9 + c], scalar2=None, op0=ALU.mult), CR(), CW())
    wsp_f = tmp_f32(512).rearrange("p (g n) -> p g n", g=4)
    dma("sp", wsp_f, wsp.rearrange("g t s -> t g s"), [], CW(), sem="c8")
    wsp_b = tmp_f32(256).bitcast(BF16).rearrange("p (g n) -> p g n", g=4)
    for g in range(4):
        dve(lambda e, g=g: e.tensor_tensor(out=wsp_b[:, g, :], in0=wsp_f[:, g, :], in1=cst_f[:, 1, :], op=ALU.mult), CR(), CW())
    WmT = A.bf(512).rearrange("p (g n) -> p g n", g=4)
    bk, Bk = nextbank()
    bkb = bk.bitcast(BF16).rearrange("p (g n) -> p g n", g=8)
    for g in range(4):
        S.op("pe", lambda e, g=g: e.transpose(out=bkb[:, g, :], in_=wsp_b[:, g, :], identity=ident_b), reads=CR(), writes=[Bk])
    tcopy("dve", WmT, bkb[:, 0:4, :], [Bk], CW())
    bsp_f = tmp_f32(512)
    dma("sp", bsp_f[0:1, :], bsp.rearrange("(o g) t -> o (g t)", o=1), [], CW(), sem="c9")
    bsp_b = A.bf(512)
    tcopy("dve", bsp_b[0:1, :], bsp_f[0:1, :], CR(), CW())
    bsp0_f = A.f32(4 * 16)
    bsp0_b = A.bf(4 * 16)
    for g in range(4):
        dve(lambda e, g=g: e.tensor_scalar(out=bsp0_f[0:1, g * 16:(g + 1) * 16], in0=ones_f[0:1, 0:16], scalar1=bsp_f[0:1, g * 128:g * 128 + 1], scalar2=None, op0=ALU.mult), CR(), CW())
    tcopy("dve", bsp0_b[0:1, :], bsp0_f[0:1, :], CR(), CW())
    D16 = A.bf(4 * 16).rearrange("p (g n) -> p g n", g=4)
    for g in range(4):
        dve(lambda e, g=g: e.tensor_scalar(out=D16[0:16, g, :], in0=ident_f[0:16, 0:16], scalar1=w00_bc[0:16, g:g + 1], scalar2=None, op0=ALU.mult), CR(), CW())
    stat = A.f32(8 * 8).rearrange("p (s n) -> p s n", s=8)
    Bstat = [Buf("st%d" % i) for i in range(8)]
    stat_i = [0]
    S.op("dve", lambda e: e.memset(stat[:, 7, 7:8], 0.0), reads=CR(), writes=[Bc])
    CONST_TOP = A.top
    print('CONST_TOP', CONST_TOP)

    if stop == 'const':
        return finalize()
    def rstd_of(ssq_ap, n, inv_n, sti, Bs):
        s_ = stat[:n, sti, :]
        dve(lambda e: e.tensor_scalar(out=s_[:, 1:2], in0=ssq_ap, scalar1=inv_n, scalar2=EPS, op0=ALU.mult, op1=ALU.add), [Bs], [Bs])
        act(s_[:, 2:3], s_[:, 1:2], AF.Ln, [Bs], [Bs])
        act(s_[:, 3:4], s_[:, 2:3], AF.Exp, [Bs], [Bs], scale=-0.5)
        return s_[:, 3:4]

    def norm_rows(xt_ap, n, Bx, out_bf, Bo):
        sti = stat_i[0] % 8
        stat_i[0] += 1
        Bs = Bstat[sti]
        act(out_bf, xt_ap, AF.Square, [Bx], [Bo, Bs], accum_out=stat[:n, sti, 0:1])
        r = rstd_of(stat[:n, sti, 0:1], n, 1.0 / 1024, sti, Bs)
        act(out_bf, xt_ap, AF.Copy, [Bx, Bs], [Bo], scale=r)

    def transpose_rows(src_bf, n, Bsrc, dstT, Bdst, g_bc):
        bk, Bk = nextbank()
        pt = bk.bitcast(BF16).rearrange("p (c t) -> p c t", c=8)
        for c in range(8):
            S.op("pe", lambda e, c=c: e.transpose(out=pt[:, c, 0:n], in_=src_bf[0:n, c * 128:(c + 1) * 128], identity=ident_b[0:n, 0:n]), reads=[Bsrc, Bc], writes=[Bk])
        dve(lambda e: e.tensor_tensor(out=dstT, in0=pt[:, :, 0:n], in1=g_bc[:, :, 0:n], op=ALU.mult), [Bk, Bc], [Bdst])

    xnT = A.bf(8 * HX).rearrange("p (c n) -> p c n", c=8)
    BxnT = Buf("xnT")
    xeT = A.bf(8 * 128).rearrange("p (c n) -> p c n", c=8)
    BxeT = Buf("xeT")
    P1 = A.top
    QT = A.bf(6 * NCOL).rearrange("p (c n) -> p c n", c=6)
    BQT = Buf("QT")
    KT = [A.bf(2 * KLEN[g]).rearrange("p (c n) -> p c n", c=2) for g in range(3)]
    BKT = [Buf("KT%d" % g) for g in range(3)]
    KTs = A.bf(6 * 18).rearrange("p (c n) -> p c n", c=6)
    VTs = A.bf(6 * 18).rearrange("p (c n) -> p c n", c=6)
    BKTs = Buf("KTs")
    NVB = 79
    Vb = A.bf(NVB * 256).rearrange("p (b n) -> p b n", b=NVB)
    BV = [Buf("V%d" % i) for i in range(NVB + 3)]
    vidx = {}
    PW = A.top
    wqkv = A.bf(8 * 2304).rearrange("p (c n) -> p c n", c=8)
    Bw = Buf("wqkv")
    NKV = 5
    kvst = [A.f32(512) for _ in range(NKV)]
    Bkvst = [Buf("kvst%d" % i) for i in range(NKV)]
    kvst_i = [0]
    PB_TOP = A.top

    dma("pool", wqkv, w_in.rearrange("(c p) n -> p c n", p=128)[:, :, 0:2304], [], [Bw], sem="w0")

    def wcol(kind, g, c):
        return kind * 768 + g * 256 + c * 128

    def norm_T_batch(items, xts, Bxts, xbs, Bxbs, semp, group_hook=None):
        sets = xts if isinstance(xts[0], list) else None
        G = len(xts[0]) if sets is not None else len(xts)
        all_sets = (xts, Bxts, xbs, Bxbs)
        for g0 in range(0, len(items), G):
            grp = items[g0:g0 + G]
            si_ = 0
            if sets is not None:
                si_ = (g0 // G) % len(sets)
                xts, Bxts, xbs, Bxbs = (all_sets[0][si_], all_sets[1][si_], all_sets[2][si_], all_sets[3][si_])
            stis = []
            srcs = []
            for k, (src_rows, n, dstT, Bdst, g_bc, pre) in enumerate(grp):
                if src_rows is not None:
                    dma("sp", xts[k][0:n, :], src_rows, [], [Bxts[k]], sem="%s%d%d" % (semp, si_, k))
                srcs.append((xts[k], Bxts[k]))
            for k, (src_rows, n, dstT, Bdst, g_bc, pre) in enumerate(grp):
                if pre is not None:
                    srcs[k] = pre(k)
            for k, (src_rows, n, dstT, Bdst, g_bc, pre) in enumerate(grp):
                sti = stat_i[0] % 8
                stat_i[0] += 1
                stis.append(sti)
                act(xbs[k][0:n, :], srcs[k][0][0:n, :], AF.Square, [srcs[k][1]], [Bxbs[k], Bstat[sti]], accum_out=stat[:n, sti, 0:1])
            for k, (src_rows, n, dstT, Bdst, g_bc, pre) in enumerate(grp):
                s_ = stat[:n, stis[k], :]
                dve(lambda e, s_=s_: e.tensor_scalar(out=s_[:, 1:2], in0=s_[:, 0:1], scalar1=1.0 / 1024, scalar2=EPS, op0=ALU.mult, op1=ALU.add), [Bstat[stis[k]]], [Bstat[stis[k]]])
            for k, (src_rows, n, dstT, Bdst, g_bc, pre) in enumerate(grp):
                s_ = stat[:n, stis[k], :]
                act(s_[:, 2:3], s_[:, 1:2], AF.Ln, [Bstat[stis[k]]], [Bstat[stis[k]]])
            for k, (src_rows, n, dstT, Bdst, g_bc, pre) in enumerate(grp):
                s_ = stat[:n, stis[k], :]
                act(s_[:, 3:4], s_[:, 2:3], AF.Exp, [Bstat[stis[k]]], [Bstat[stis[k]]], scale=-0.5)
            for k, (src_rows, n, dstT, Bdst, g_bc, pre) in enumerate(grp):
                s_ = stat[:n, stis[k], :]
                dve(lambda e, k=k, n=n, s_=s_, xbs=xbs, src_=srcs[k][0]: e.tensor_scalar(out=xbs[k][0:n, :], in0=src_[0:n, :], scalar1=s_[:, 3:4], scalar2=None, op0=ALU.mult), [srcs[k][1], Bstat[stis[k]]], [Bxbs[k]])
            if group_hook is not None:
                group_hook(g0 // G)
            for k, (src_rows, n, dstT, Bdst, g_bc, pre) in enumerate(grp):
                transpose_rows(xbs[k], n, Bxbs[k], dstT, Bdst, g_bc)

    def proj_fm(dst, Bdst, wt, Bwt, col0, src, Bsrc, c0, n, nk=8, func=AF.Copy, **kw):
        bk, Bk = nextbank()
        for kc in range(nk):
            mm(bk[:, 0:n], wt[:, kc, col0:col0 + 128], src[:, kc, c0:c0 + n], kc == 0, kc == nk - 1, [Bwt, Bsrc], [Bk])
        act(dst, bk[:, 0:n], func, [Bk], [Bdst], **kw)

    def vblock(g, start, step, n, src, Bsrc):
        idx = len(vidx)
        vidx[(g, start, step, "m" if src is xnT_main_marker[0] else "h")] = idx
        bk, Bk = nextbank()
        for kc in range(8):
            mm(bk[0:n, 0:256], src[:, kc, start:start + step * (n - 1) + 1:step], wqkv[:, kc, wcol(2, g, 0):wcol(2, g, 0) + 256], kc == 0, kc == 7, [Bsrc, Bw], [Bk])
        tcopy("dve", Vb[0:n, idx, :], bk[0:n, 0:256], [Bk], [BV[idx]])
        return idx

    xnT_main_marker = [None]

    S.label = 'B1'
    QT_f32 = arena[:, P1:P1 + 6144]
    xtA = [QT_f32[:, k * 1024:(k + 1) * 1024] for k in range(4)]
    xbA = [QT_f32[:, 4096 + k * 512:4096 + (k + 1) * 512].bitcast(BF16) for k in range(4)]
    BxtA = [Buf("xtA%d" % k) for k in range(4)]
    BxbA = [Buf("xbA%d" % k) for k in range(4)]
    VBa = PW - NVB * 128
    Va_f32 = arena[:, VBa:VBa + 6144]
    xtA2 = [Va_f32[:, k * 1024:(k + 1) * 1024] for k in range(4)]
    xbA2 = [Va_f32[:, 4096 + k * 512:4096 + (k + 1) * 512].bitcast(BF16) for k in range(4)]
    BxtA2 = [Buf("xtA2%d" % k) for k in range(4)]
    BxbA2 = [Buf("xbA2%d" % k) for k in range(4)]
    norm_T_batch([(xall[t * 128:(t + 1) * 128, :], 128, xnT[:, :, t * 128:(t + 1) * 128], BxnT, gm_bc, None) for t in range(17)],
                 [xtA, xtA2], [BxtA, BxtA2], [xbA, xbA2], [BxbA, BxbA2], "xa")
    if stop == 'B1a':
        return finalize()
    for g in range(3):
        lt = KBASE[g]
        while lt < HX:
            n = min(512, HX - lt)
            for c in range(2):
                proj_fm(KT[g][:, c, lt - KBASE[g]:lt - KBASE[g] + n], BKT[g], wqkv, Bw, wcol(1, g, c), xnT, BxnT, lt, n)
            lt += n
    if stop == 'B1k':
        return finalize()
    S.barrier()
    for r in range(16):
        vblock(2, 128 + r, 16, 128, xnT, BxnT)
    for r in range(4):
        vblock(1, 1664 + r, 4, 128, xnT, BxnT)
    vblock(0, 2048, 1, 128, xnT, BxnT)
    vblock(2, 126, 16, 128, xnT, BxnT)
    vblock(2, 127, 16, 128, xnT, BxnT)
    vblock(1, 1662, 4, 128, xnT, BxnT)
    vblock(1, 1663, 4, 128, xnT, BxnT)
    vblock(0, 2046, 1, 128, xnT, BxnT)
    vblock(2, 2174, 1, 1, xnT, BxnT)
    vblock(2, 2175, 1, 1, xnT, BxnT)
    vblock(1, 2174, 1, 1, xnT, BxnT)
    vblock(1, 2175, 1, 1, xnT, BxnT)
    vblock(0, 2174, 1, 2, xnT, BxnT)
    tcopy("dve", xeT, xnT[:, :, 2048:2176], [BxnT], [BxeT])

    if stop == 'B1':
        return finalize()
    S.label = 'B2'
    xnT_main_marker[0] = xnT
    S.barrier()
    VB0 = PW - NVB * 128 + 31 * 128
    Vm_f32 = arena[:, VB0:VB0 + 6144]
    xtB = [Vm_f32[:, k * 1024:(k + 1) * 1024] for k in range(4)]
    xbB = [Vm_f32[:, 4096 + k * 512:4096 + (k + 1) * 512].bitcast(BF16) for k in range(4)]
    BxtB = [Buf("xtB%d" % k) for k in range(4)]
    BxbB = [Buf("xbB%d" % k) for k in range(4)]
    norm_T_batch([(xall[HX + t * 128:HX + (t + 1) * 128, :], 128, xnT[:, :, t * 128:(t + 1) * 128], BxnT, gm_bc, None) for t in range(16)],
                 [xtB, xtA], [BxtB, BxtA], [xbB, xbA], [BxbB, BxbA], "xb")
    tcopy("dve", xnT[:, :, SM0:SM0 + 2], xeT[:, :, 126:128], [BxeT], [BxnT])
    norm_T_batch([(xs, NS, xnT[:, :, SM0 + 2:SM0 + 2 + NS], BxnT, gm_bc, None)], xtB, BxtB, xbB, BxbB, "xb")
    slices = [(i * 512, 512) for i in range(4)] + [(SM0, 18)]
    if stop == 'B2a':
        return finalize()
    S.barrier()
    for (c0, n) in slices:
        for gc in range(6):
            proj_fm(QT[:, gc, c0:c0 + n], BQT, wqkv, Bw, wcol(0, gc // 2, gc % 2), xnT, BxnT, c0, n)
    for (c0, n) in slices[:4]:
        for g in range(3):
            for c in range(2):
                kc0 = HX + c0 - KBASE[g]
                proj_fm(KT[g][:, c, kc0:kc0 + n], BKT[g], wqkv, Bw, wcol(1, g, c), xnT, BxnT, c0, n)
    for gc in range(6):
        proj_fm(KTs[:, gc, :], BKTs, wqkv, Bw, wcol(1, gc // 2, gc % 2), xnT, BxnT, SM0, 18)
        proj_fm(VTs[:, gc, :], BKTs, wqkv, Bw, wcol(2, gc // 2, gc % 2), xnT, BxnT, SM0, 18)
    if stop == 'B2q':
        return finalize()
    assert len(vidx) == 31, len(vidx)
    S.barrier()
    for t in range(16):
        vblock(0, t * 128, 1, 128, xnT, BxnT)
    for i in range(4):
        for r in range(4):
            vblock(1, 512 * i + r, 4, 128, xnT, BxnT)
    for r in range(16):
        vblock(2, r, 16, 128, xnT, BxnT)

    if stop == 'B2v':
        return finalize()
    S.label = 'kvtok'
    def kv_tok(col0, n, g, dst_rows):
        i = kvst_i[0] % NKV
        kvst_i[0] += 1
        bk, Bk = nextbank()
        for half in range(2):
            for kc in range(8):
                mm(bk[0:n, half * 256:(half + 1) * 256], xnT[:, kc, col0:col0 + n], wqkv[:, kc, wcol(1 + half, g, 0):wcol(1 + half, g, 0) + 256], kc == 0, kc == 7, [BxnT, Bw], [Bk])
        tcopy("dve", kvst[i][0:n, :], bk[0:n, :], [Bk], [Bkvst[i]])
        dma("sp" if n == 128 else "pool", dst_rows, kvst[i][0:n, :], [Bkvst[i]], [], sem=("ko%d" if n == 128 else "kp%d") % i)
        out_bufs.append(Bkvst[i])

    for t in range(16):
        kv_tok(t * 128, 128, 2, kvo[2][t * 128:(t + 1) * 128, :])
    if stop == 'kv1':
        return finalize()
    for t in range(12, 16):
        kv_tok(t * 128, 128, 1, kvo[1][(t - 12) * 128:(t - 11) * 128, :])
    kv_tok(15 * 128, 128, 0, kvo[0])
    if stop == 'kv2':
        return finalize()
    for g in range(3):
        kv_tok(SM0 + 2, NS, g, kvs[g])

    if stop == 'B':
        return finalize()
    S.barrier()
    A.top = PW
    ACC0 = A.top
    acc_n = A.f32(2 * NCOL).rearrange("p (c n) -> p c n", c=2)
    acc_d = A.f32(2 * NCOL).rearrange("p (c n) -> p c n", c=2)
    acc_all = arena[:, ACC0:ACC0 + 4 * NCOL].rearrange("p (x c n) -> p x c n", x=2, c=2)
    NPT = 2
    PTb = [A.bf(1024).rearrange("p (h k q) -> p h k q", h=4, k=2) for _ in range(NPT)]
    BPT = [Buf("PT%d" % i) for i in range(NPT)]
    pt_i = [0]
    BOUT0 = A.top
    boutT = A.bf(2 * NCOL).rearrange("p (c n) -> p c n", c=2)
    BboutT = Buf("boutT")
    ATT_TOP = A.top
    A.top = BOUT0
    ckb = [[A.bf(512) for _ in range(3)] for _ in range(2)]
    Bckb = [[Buf("ckb%d%d" % (i, g)) for g in range(3)] for i in range(2)]
    KTc = [A.bf(6 * 128).rearrange("p (c n) -> p c n", c=6) for _ in range(2)]
    BKTc = [Buf("KTc%d" % i) for i in range(2)]
    PTs = [A.bf(16) for _ in range(2)]
    BPTs = [Buf("PTs%d" % i) for i in range(2)]
    prodf = A.f32(6 * 16).rearrange("p (c n) -> p c n", c=6)
    pself = A.f32(6 * 16).rearrange("p (c n) -> p c n", c=6)
    Bpr = Buf("prod")
    assert A.top <= NF, A.top
    acc_hist = {0: [], 1: [], 2: [], "x": [Buf("accx")], "s": []}
    mpair_main = cst_b[:, 1:3, :]
    mpair_edge = cst_b[:, 3:5, :]

    def acc_update(pair, Bn, Bd, nq, cols, first, key):
        if key == "x":
            rd_prev, wr = acc_hist["x"], acc_hist["x"]
        else:
            nb = Buf("acc%s" % str(key))
            rd_prev = [] if (key == "s" or key == 0) else acc_hist[key - 1]
            wr = [nb]
            acc_hist[key].append(nb)
        p4 = pair.rearrange("p (x h q) -> p x h q", x=2, h=4)
        for h2 in range(2):
            i_ap = p4[h2 * 64:(h2 + 1) * 64, :, h2::2, 0:nq]
            o_ap = acc_all[h2 * 64:(h2 + 1) * 64, :, :, cols]
            if first:
                dve(lambda e, i_ap=i_ap, o_ap=o_ap: e.tensor_copy(out=o_ap, in_=i_ap), [Bn, Bd] + rd_prev, wr)
            else:
                dve(lambda e, i_ap=i_ap, o_ap=o_ap: e.tensor_tensor(out=o_ap, in0=i_ap, in1=o_ap, op=ALU.add), [Bn, Bd] + rd_prev, wr)

    def band_p1(g, qsrc, Bq, qcols, nq, chunks, mpair):
        pi = pt_i[0] % NPT
        pt_i[0] += 1
        PT, Bp = PTb[pi], BPT[pi]
        b0, B0 = nextbank()
        b1, B1 = nextbank()
        sb = [b0, b1]
        SBf = [B0, B1]
        for h in range(4):
            c, h2 = h // 2, h % 2
            rows = slice(h2 * 64, (h2 + 1) * 64)
            for ci, (Kap, BK, vi, nk, mask) in enumerate(chunks):
                o = sb[h2][0:nk, (c * 2 + ci) * 128:(c * 2 + ci) * 128 + nq]
                mm(o, Kap[rows, c, :], qsrc[rows, g * 2 + c, qcols], True, True, [BK, Bq], [SBf[h2]])
        full = (nq == 128 and len(chunks) == 2 and all(ch[3] == 128 for ch in chunks))
        if full:
            for h2 in range(2):
                src = sb[h2].rearrange("p (h k q) -> p h k q", h=2, k=2)
                act(PT[:, h2::2, :, :], src, AF.Exp, [SBf[h2]], [Bp], scale=0.125)
            mb = mpair.unsqueeze(1).to_broadcast([128, 4, 2, 128])
            dve(lambda e, PT=PT, mb=mb: e.tensor_tensor(out=PT, in0=PT, in1=mb, op=ALU.mult), [Bp, Bc], [Bp])
        else:
            for ci, (Kap, BK, vi, nk, mask) in enumerate(chunks):
                for h2 in range(2):
                    src = sb[h2][0:nk, :].rearrange("p (h k q) -> p h k q", h=2, k=2)[:, :, ci, 0:nq]
                    act(PT[0:nk, h2::2, ci, 0:nq], src, AF.Exp, [SBf[h2]], [Bp], scale=0.125)
                if mask is not None:
                    for h in range(4):
                        dve(lambda e, h=h, ci=ci, nk=nk, mask=mask, PT=PT: e.tensor_tensor(out=PT[0:nk, h, ci, 0:nq], in0=PT[0:nk, h, ci, 0:nq], in1=mask[0:nk, 0:nq], op=ALU.mult), [Bp, Bc], [Bp])
        return PT, Bp

    def band_p2(PT, Bp, nq, chunks, acc_cols, first, key):
        nch = len(chunks)
        bn, Bn, bd, Bd, pair = nextpair()
        for h in range(4):
            c = h // 2
            for ci, (Kap, BK, vi, nk, mask) in enumerate(chunks):
                mm(bn[:, h * 128:h * 128 + nq], Vb[0:nk, vi, c * 128:(c + 1) * 128], PT[0:nk, h, ci, 0:nq], ci == 0, ci == nch - 1, [BV[vi], Bp], [Bn])
            for ci, (Kap, BK, vi, nk, mask) in enumerate(chunks):
                mm(bd[:, h * 128:h * 128 + nq], ones_b[0:nk, :], PT[0:nk, h, ci, 0:nq], ci == 0, ci == nch - 1, [Bc, Bp], [Bd])
        acc_update(pair, Bn, Bd, nq, acc_cols, first, key)

    def kslice(g, lt0, step, n):
        a = lt0 - KBASE[g]
        return KT[g][:, :, a:a + step * (n - 1) + 1:step]

    def samp_s0(b):
        sl = b % 2
        for g in range(3):
            L, d = WIN[g]
            dma("pool", ckb[sl][g], ck[g][b, 0:L:d, :], [], [Bckb[sl][g]], sem="ck%d%d" % (sl, g))

    def samp_s1(b):
        sl = b % 2
        bk, Bk = nextbank()
        pt = bk.bitcast(BF16).rearrange("p (c t) -> p c t", c=8)
        for g in range(3):
            for c in range(2):
                S.op("pe", lambda e, c=c, g=g, pt=pt, sl=sl: e.transpose(out=pt[:, g * 2 + c, :], in_=ckb[sl][g][:, c * 128:(c + 1) * 128], identity=ident_b), reads=[Bckb[sl][g], Bc], writes=[Bk])
        tcopy("dve", KTc[sl], pt[:, 0:6, :], [Bk], [BKTc[sl]])

    def samp_s2(b):
        sl = b % 2
        col = SM0 + 2 + b
        bs0, BS0 = nextbank()
        bs1, BS1 = nextbank()
        bsx = [bs0, bs1]
        BSx = [BS0, BS1]
        for g in range(3):
            for h in range(4):
                c, h2 = h // 2, h % 2
                rows = slice(h2 * 64, (h2 + 1) * 64)
                mm(bsx[h2][:, g * 2 + c:g * 2 + c + 1], KTc[sl][rows, g * 2 + c, :], QT[rows, g * 2 + c, col:col + 1], True, True, [BKTc[sl], BQT], [BSx[h2]])
        PTs3 = PTs[sl][:, 0:12].rearrange("p (g c t) -> p g c t", g=3, c=2)
        for h2 in range(2):
            act(PTs3[:, :, :, h2], bsx[h2][:, 0:6].rearrange("p (g c) -> p g c", g=3), AF.Exp, [BSx[h2]], [BPTs[sl]], scale=0.125)

    def samp_s3(b):
        sl = b % 2
        col = SM0 + 2 + b
        bn, Bn, bd, Bd, pair = nextpair()
        for h in range(4):
            c = h // 2
            for g in range(3):
                mm(bn[:, h * 128:h * 128 + 1], ckb[sl][g][:, 256 + c * 128:256 + (c + 1) * 128], PTs[sl][:, g * 4 + h:g * 4 + h + 1], g == 0, g == 2, [Bckb[sl][g], BPTs[sl]], [Bn])
            for g in range(3):
                mm(bd[:, h * 128:h * 128 + 1], ones_b, PTs[sl][:, g * 4 + h:g * 4 + h + 1], g == 0, g == 2, [Bc, BPTs[sl]], [Bd])
        acc_update(pair, Bn, Bd, 1, slice(col, col + 1), True, "s")

    S.label = 'att-main'
    mblocks = []
    for g in range(3):
        step = WIN[g][1]
        if g == 0:
            starts = [t * 128 for t in range(16)]
        elif g == 1:
            starts = [512 * i + r for i in range(4) for r in range(4)]
        else:
            starts = list(range(16))
        for m0 in starts:
            lt_q = HX + m0
            lt_p = lt_q - 128 * step
            if lt_p < HX:
                vp = vidx[(g, lt_p, step, "h")]
                mp = mpair_edge
            else:
                vp = vidx[(g, lt_p - HX, step, "m")]
                mp = mpair_main
            vc = vidx[(g, m0, step, "m")]
            chunks = [(kslice(g, lt_p, step, 128), BKT[g], vp, 128, None),
                      (kslice(g, lt_q, step, 128), BKT[g], vc, 128, None)]
            qc = slice(m0, m0 + step * 127 + 1, step)
            mblocks.append((g, qc, chunks, mp))
    samp_s0(0)
    pend = band_p1(mblocks[0][0], QT, BQT, mblocks[0][1], 128, mblocks[0][2], mblocks[0][3])
    for m, (g, qc, chunks, mp) in enumerate(mblocks):
        nxt = None
        if m + 1 < len(mblocks):
            g2_, qc2, ch2, mp2 = mblocks[m + 1]
            nxt = band_p1(g2_, QT, BQT, qc2, 128, ch2, mp2)
        band_p2(pend[0], pend[1], 128, chunks, qc, g == 0, g)
        pend = nxt
        b, k = m // 3, m % 3
        if k == 0:
            samp_s1(b)
            if b + 1 < NS:
                samp_s0(b + 1)
        elif k == 1:
            samp_s2(b)
        else:
            samp_s3(b)
    if stop == 'att-main':
        return finalize()
    S.label = 'att-ext2'
    vp = vidx[(0, 2046, 1, "h")]
    vc = vidx[(0, 2174, 1, "h")]
    chx = [(kslice(0, 2046, 1, 128), BKT[0], vp, 128, mprev_b), (kslice(0, 2174, 1, 2), BKT[0], vc, 2, mcur_b)]
    p_ = band_p1(0, QT, BQT, slice(SM0, SM0 + 2), 2, chx, None)
    band_p2(p_[0], p_[1], 2, chx, slice(SM0, SM0 + 2), True, "x")
    for g in (1, 2):
        step = WIN[g][1]
        for j in range(2):
            ltq = 2174 + j
            vp = vidx[(g, ltq - 128 * step, step, "h")]
            vc = vidx[(g, ltq, 1, "h")]
            chx = [(kslice(g, ltq - 128 * step, step, 128), BKT[g], vp, 128, None), (kslice(g, ltq, 1, 1), BKT[g], vc, 1, None)]
            p_ = band_p1(g, QT, BQT, slice(SM0 + j, SM0 + j + 1), 1, chx, None)
            band_p2(p_[0], p_[1], 1, chx, slice(SM0 + j, SM0 + j + 1), False, "x")
    Bacc = Buf("accall")
    S.op("dve", lambda e: e.memset(prodf[:, 0, 0:1], 0.0), reads=[b_ for k_ in acc_hist for b_ in acc_hist[k_]], writes=[Bacc, Bpr])
    S.label = 'att-self'
    dve(lambda e: e.tensor_tensor(out=prodf, in0=QT[:, :, SM0 + 2:SM0 + 18], in1=KTs[:, :, 2:18], op=ALU.mult), [BQT, BKTs], [Bpr])
    bk, Bk = nextbank()
    mm(bk[:, 0:96], blockones_f, prodf.rearrange("p c n -> p (c n)"), True, True, [Bc, Bpr], [Bk])
    act(pself.rearrange("p c n -> p (c n)"), bk[:, 0:96], AF.Exp, [Bk], [Bpr], scale=0.125)
    dve(lambda e: e.tensor_tensor(out=prodf, in0=pself, in1=VTs[:, :, 2:18], op=ALU.mult), [Bpr, BKTs], [Bpr])
    for g in range(3):
        for c in range(2):
            dve(lambda e, g=g, c=c: e.tensor_tensor(out=acc_n[:, c, SM0 + 2:SM0 + 18], in0=acc_n[:, c, SM0 + 2:SM0 + 18], in1=prodf[:, g * 2 + c, :], op=ALU.add), [Bacc, Bpr], [Bacc])
            dve(lambda e, g=g, c=c: e.tensor_tensor(out=acc_d[:, c, SM0 + 2:SM0 + 18], in0=acc_d[:, c, SM0 + 2:SM0 + 18], in1=pself[:, g * 2 + c, :], op=ALU.add), [Bacc, Bpr], [Bacc])
    if stop == 'att-self':
        return finalize()
    S.barrier()
    A.top = P1
    boutT2 = A.bf(2 * NCOL).rearrange("p (c n) -> p c n", c=2)
    assert A.top <= PW
    aoutT = A.bf(4 * NCOL).rearrange("p (c n) -> p c n", c=4)
    BaoutT = Buf("aoutT")
    C_TOP = A.top
    wuv = A.bf(8 * 1024).rearrange("p (c n) -> p c n", c=8)
    Bwuv = Buf("wuv")
    wv_in = w_in.rearrange("(c p) n -> p c n", p=128)
    dma("pool", wuv, wv_in[:, :, 2304:3328], [], [Bwuv], sem="w1")
    S.label = 'att-norm'
    for c in range(2):
        for (c0, n) in slices:
            dve(lambda e, c=c, c0=c0, n=n: e.reciprocal(out=acc_d[:, c, c0:c0 + n], in_=acc_d[:, c, c0:c0 + n]), [Bacc], [Bacc])
            dve(lambda e, c=c, c0=c0, n=n: e.tensor_tensor(out=boutT2[:, c, c0:c0 + n], in0=acc_n[:, c, c0:c0 + n], in1=acc_d[:, c, c0:c0 + n], op=ALU.mult), [Bacc], [BboutT])
    if stop == 'att':
        return finalize()
    S.label = 'C1'
    MT0 = NF - 4 * NCOL
    W2A = MT0 - (8192 + 2048 + 1024)
    wg = arena[:, W2A:W2A + 8192].bitcast(BF16).rearrange("p (c n) -> p c n", c=8)
    wpa = arena[:, W2A + 8192:W2A + 10240].bitcast(BF16).rearrange("p (c n) -> p c n", c=4)
    wpb = arena[:, W2A + 10240:W2A + 11264].bitcast(BF16).rearrange("p (c n) -> p c n", c=2)
    Bwg = Buf("wg")
    dma("pool", wg, wv_in[:, :, 3328:5376], [], [Bwg, Bacc], sem="w2")
    dma("pool", wpa, w_pa.rearrange("(c p) n -> p c n", p=128), [], [Bwg, Bacc], sem="w3")
    dma("pool", wpb, w_pb.rearrange("(c p) n -> p c n", p=128), [], [Bwg, Bacc], sem="w4")
    uT = A.bf(4 * NCOL).rearrange("p (c n) -> p c n", c=4)
    BuT = Buf("uT")
    NG = 3
    gv = [A.f32(512) for _ in range(NG)]
    Bgv = [Buf("gv%d" % i) for i in range(NG)]
    vn = [A.f32(512) for _ in range(NG)]
    Bvn = [Buf("vn%d" % i) for i in range(NG)]
    vnb = [A.bf(512) for _ in range(NG)]
    Bvnb = [Buf("vnb%d" % i) for i in range(NG)]
    uxe = A.bf(4 * 128).rearrange("p (c n) -> p c n", c=4)
    aoe = A.bf(4 * 128).rearrange("p (c n) -> p c n", c=4)
    Buxe = Buf("uxe")
    C1_TOP = A.top
    for (c0, n) in slices:
        for c in range(4):
            proj_fm(uT[:, c, c0:c0 + n], BuT, wuv, Bwuv, c * 128, xnT, BxnT, c0, n, func=AF.Gelu_apprx_tanh)
    for c in range(4):
        proj_fm(uxe[:, c, :], Buxe, wuv, Bwuv, c * 128, xeT, BxeT, 0, 128, func=AF.Gelu_apprx_tanh)
    gi = [0]

    def gmlp_batch(items):
        G = len(gv)
        for g0 in range(0, len(items), G):
            grp = items[g0:g0 + G]
            stis, banks = [], []
            for k, (src, Bsrc, c0, n, sample, u_ap, Bu, out_ap, Bout, vn_dst) in enumerate(grp):
                sti = stat_i[0] % 8
                stat_i[0] += 1
                stis.append(sti)
                bk, Bk = nextbank()
                banks.append((bk, Bk))
                for kc in range(8):
                    mm(bk[0:n, :], src[:, kc, c0:c0 + n], wuv[:, kc, 512:1024], kc == 0, kc == 7, [Bsrc, Bwuv], [Bk])
            for k, (src, Bsrc, c0, n, sample, u_ap, Bu, out_ap, Bout, vn_dst) in enumerate(grp):
                s_ = stat[:n, stis[k], :]
                act(gv[k][0:n, :], banks[k][0][0:n, :], AF.Gelu_apprx_tanh, [banks[k][1]], [Bgv[k], Bstat[stis[k]]], accum_out=s_[:, 4:5])
            for k, (src, Bsrc, c0, n, sample, u_ap, Bu, out_ap, Bout, vn_dst) in enumerate(grp):
                s_ = stat[:n, stis[k], :]
                dve(lambda e, s_=s_: e.tensor_scalar(out=s_[:, 5:6], in0=s_[:, 4:5], scalar1=-1.0 / 512, scalar2=None, op0=ALU.mult), [Bstat[stis[k]]], [Bstat[stis[k]]])
            for k, (src, Bsrc, c0, n, sample, u_ap, Bu, out_ap, Bout, vn_dst) in enumerate(grp):
                s_ = stat[:n, stis[k], :]
                act(vn[k][0:n, :], gv[k][0:n, :], AF.Identity, [Bgv[k], Bstat[stis[k]]], [Bvn[k]], bias=s_[:, 5:6], scale=1.0)
            for k, (src, Bsrc, c0, n, sample, u_ap, Bu, out_ap, Bout, vn_dst) in enumerate(grp):
                s_ = stat[:n, stis[k], :]
                act(gv[k][0:n, :], vn[k][0:n, :], AF.Square, [Bvn[k]], [Bgv[k], Bstat[stis[k]]], accum_out=s_[:, 0:1])
            for k, (src, Bsrc, c0, n, sample, u_ap, Bu, out_ap, Bout, vn_dst) in enumerate(grp):
                s_ = stat[:n, stis[k], :]
                dve(lambda e, s_=s_: e.tensor_scalar(out=s_[:, 1:2], in0=s_[:, 0:1], scalar1=1.0 / 512, scalar2=EPS, op0=ALU.mult, op1=ALU.add), [Bstat[stis[k]]], [Bstat[stis[k]]])
            for k, (src, Bsrc, c0, n, sample, u_ap, Bu, out_ap, Bout, vn_dst) in enumerate(grp):
                s_ = stat[:n, stis[k], :]
                act(s_[:, 2:3], s_[:, 1:2], AF.Ln, [Bstat[stis[k]]], [Bstat[stis[k]]])
            for k, (src, Bsrc, c0, n, sample, u_ap, Bu, out_ap, Bout, vn_dst) in enumerate(grp):
                s_ = stat[:n, stis[k], :]
                act(s_[:, 3:4], s_[:, 2:3], AF.Exp, [Bstat[stis[k]]], [Bstat[stis[k]]], scale=-0.5)
            for k, (src, Bsrc, c0, n, sample, u_ap, Bu, out_ap, Bout, vn_dst) in enumerate(grp):
                s_ = stat[:n, stis[k], :]
                dve(lambda e, k=k, n=n, s_=s_: e.scalar_tensor_tensor(out=vn[k][0:n, :], in0=vn[k][0:n, :], scalar=s_[:, 3:4], in1=lng_bc[0:n, :], op0=ALU.mult, op1=ALU.mult), [Bvn[k], Bstat[stis[k]], Bc], [Bvn[k]])
            for k, (src, Bsrc, c0, n, sample, u_ap, Bu, out_ap, Bout, vn_dst) in enumerate(grp):
                dve(lambda e, k=k, n=n: e.tensor_tensor(out=vn[k][0:n, :], in0=vn[k][0:n, :], in1=lnb_bc[0:n, :], op=ALU.add), [Bvn[k], Bc], [Bvn[k]])
            for k, (src, Bsrc, c0, n, sample, u_ap, Bu, out_ap, Bout, vn_dst) in enumerate(grp):
                tcopy("dve", vnb[k][0:n, :], vn[k][0:n, :], [Bvn[k]], [Bvnb[k]])
                if vn_dst is not None:
                    dma("sp" if n == 128 else "pool", vn_dst, vn[k][0:n, :], [Bvn[k]], [], sem=("vo%d" if n == 128 else "vp%d") % k)
                    out_bufs.append(Bvn[k])
            mbanks = []
            for k, (src, Bsrc, c0, n, sample, u_ap, Bu, out_ap, Bout, vn_dst) in enumerate(grp):
                bm, Bm = nextbank()
                mbanks.append((bm, Bm))
                nt = 16 if sample else 128
                for g in range(4):
                    o = bm[:, g * 128:g * 128 + nt]
                    if sample:
                        mm(o, vnb[k][0:n, g * 128:(g + 1) * 128], D16[0:16, g, :], True, False, [Bvnb[k], Bc], [Bm])
                        mm(o, ones_b[0:1, :], bsp0_b[0:1, g * 16:(g + 1) * 16], False, True, [Bc], [Bm])
                    else:
                        mm(o, vnb[k][0:n, g * 128:(g + 1) * 128], WmT[:, g, :], True, False, [Bvnb[k], Bc], [Bm])
                        mm(o, ones_b[0:1, :], bsp_b[0:1, g * 128:(g + 1) * 128], False, True, [Bc], [Bm])
            for k, (src, Bsrc, c0, n, sample, u_ap, Bu, out_ap, Bout, vn_dst) in enumerate(grp):
                nt = 16 if sample else 128
                m4 = mbanks[k][0].rearrange("p (g t) -> p g t", g=4)[:, :, 0:nt]
                dve(lambda e, m4=m4, out_ap=out_ap, u_ap=u_ap: e.tensor_tensor(out=out_ap, in0=m4, in1=u_ap, op=ALU.mult), [mbanks[k][1], Bu], [Bout])

    gitems = []
    for t in range(16):
        cs = slice(t * 128, (t + 1) * 128)
        gitems.append((xnT, BxnT, t * 128, 128, False, uT[:, :, cs], BuT, aoutT[:, :, cs], BaoutT, vch if t == 15 else None))
    gitems.append((xeT, BxeT, 0, 128, False, uxe, Buxe, aoe, Buxe, None))
    gitems.append((xnT, BxnT, SM0 + 2, NS, True, uT[:, :, SM0 + 2:SM0 + 18], BuT, aoutT[:, :, SM0 + 2:SM0 + 18], BaoutT, vchs))
    gmlp_batch(gitems)
    tcopy("dve", aoutT[:, :, SM0:SM0 + 2], aoe[:, :, 126:128], [Buxe], [BaoutT])
    if stop == 'C1':
        return finalize()
    S.label = 'C2a'
    S.barrier()
    A.top = C_TOP
    assert C1_TOP <= W2A, (C1_TOP, W2A)
    tg = [A.f32(512) for _ in range(2)]
    Btg = [Buf("tg0"), Buf("tg1")]
    t1 = [A.f32(512) for _ in range(2)]
    Bt1 = [Buf("t10"), Buf("t11")]
    mt = arena[:, MT0:NF].bitcast(BF16).rearrange("p (c n) -> p c n", c=8)
    Bmt = Buf("mT")
    oc_i = [0]
    for si, (c0, n) in enumerate(slices):
        for oc in range(8):
            i = oc_i[0] % 2
            oc_i[0] += 1
            ba, Ba = nextbank()
            for kc in range(4):
                mm(ba[:, 0:n], wpa[:, kc, oc * 128:(oc + 1) * 128], aoutT[:, kc, c0:c0 + n], kc == 0, kc == 3, [Bwg, BaoutT], [Ba])
            bb, Bb = nextbank()
            for kc in range(2):
                mm(bb[:, 0:n], wpb[:, kc, oc * 128:(oc + 1) * 128], boutT2[:, kc, c0:c0 + n], kc == 0, kc == 1, [Bwg, BboutT], [Bb])
            proj_fm(tg[i][:, 0:n], Btg[i], wg, Bwg, oc * 128, xnT, BxnT, c0, n, func=AF.Tanh, scale=0.5)
            dve(lambda e, i=i, ba=ba, n=n: e.scalar_tensor_tensor(out=t1[i][:, 0:n], in0=tg[i][:, 0:n], scalar=1.0, in1=ba[:, 0:n], op0=ALU.add, op1=ALU.mult), [Btg[i], Ba], [Bt1[i]])
            proj_fm(tg[i][:, 0:n], Btg[i], wg, Bwg, 1024 + oc * 128, xnT, BxnT, c0, n, func=AF.Tanh, scale=0.5)
            dve(lambda e, i=i, bb=bb, n=n: e.scalar_tensor_tensor(out=tg[i][:, 0:n], in0=tg[i][:, 0:n], scalar=1.0, in1=bb[:, 0:n], op0=ALU.add, op1=ALU.mult), [Btg[i], Bb], [Btg[i]])
            dve(lambda e, i=i, n=n, oc=oc, c0=c0: e.tensor_tensor(out=mt[:, oc, c0:c0 + n], in0=t1[i][:, 0:n], in1=tg[i][:, 0:n], op=ALU.add), [Bt1[i], Btg[i]], [Bmt])

    if stop == 'C2a':
        return finalize()
    S.label = 'C2b'
    S.barrier()
    A.top = CONST_TOP
    hnT = A.bf(8 * NCOL).rearrange("p (c n) -> p c n", c=8)
    BhnT = Buf("hnT")
    hbuf = A.f32(16 * 1024).rearrange("p (t n) -> p t n", t=16)
    hsm = A.f32(1024)
    Bh = [Buf("h%d" % t) for t in range(16)]
    Bhsm = Buf("hsm")
    HB_TOP = A.top
    wo = A.bf(8 * 1024).rearrange("p (c n) -> p c n", c=8)
    Bwo = Buf("wo")
    dma("pool", wo, w_out.rearrange("(c p) n -> p c n", p=128), [], [Bwo], sem="w5")
    NX = 3
    xt = [A.f32(1024) for _ in range(NX)]
    Bxt = [Buf("xt%db" % i) for i in range(NX)]
    xb = [A.bf(1024) for _ in range(NX)]
    Bxb = [Buf("xb%db" % i) for i in range(NX)]
    assert A.top <= MT0, (A.top, MT0)

    def mk_pre(t, c0o, m):
        def pre(k):
            if t >= 0:
                hdst, Bhd = hbuf[:, t, :], Bh[t]
            else:
                hdst, Bhd = hsm, Bhsm
            for half in range(2):
                bk, Bk = nextbank()
                for kc in range(8):
                    mm(bk[0:m, :], mt[:, kc, c0o:c0o + m], wo[:, kc, half * 512:(half + 1) * 512], kc == 0, kc == 7, [Bmt, Bwo], [Bk])
                dve(lambda e, bk=bk, half=half, k=k, hdst=hdst: e.scalar_tensor_tensor(out=hdst[0:m, half * 512:(half + 1) * 512], in0=bk[0:m, :], scalar=0.5, in1=xt[k][0:m, half * 512:(half + 1) * 512], op0=ALU.mult, op1=ALU.add), [Bk, Bxt[k]], [Bhd])
            return (hdst, Bhd)
        return pre

    HS0 = 42404
    assert A.top <= HS0 and HS0 + 1408 + 1024 <= MT0, (A.top, MT0)
    hsT = arena[:, HS0:HS0 + 1408].rearrange("p (c k n) -> p c k n", c=44, k=2)
    BhsT = Buf("hsT")
    scst2 = [arena[:, HS0 + 1408 + i * 512:HS0 + 1408 + (i + 1) * 512].rearrange("p (k n) -> p k n", k=2) for i in range(2)]
    Bscst2 = [Buf("scst0"), Buf("scst1")]
    def sconv_piece(q):
        sc_, Bsc_ = scst2[q % 2], Bscst2[q % 2]
        dma("sp", sc_[0:16, :, :], sconv[:, :, q * 256:(q + 1) * 256], [], [Bsc_], sem="sc%d" % (q % 2))
        for cc in range(2):
            ch = q * 2 + cc
            bk, Bk = nextbank()
            for k in range(2):
                mm(bk[:, k * 16:(k + 1) * 16], sc_[0:16, k, cc * 128:(cc + 1) * 128], ident_f[0:16, 0:16], True, True, [Bsc_, Bc], [Bk])
            tcopy("dve", hsT[:, ch, :, :], bk[:, 0:32].rearrange("p (k n) -> p k n", k=2), [Bk], [BhsT])
        dma("pool", convs[:, 0, q * 256:(q + 1) * 256], sc_[0:16, 1, :], [Bsc_], [], sem="sco%d" % (q % 2))
        out_bufs.append(Bsc_)


    sc_done = [0]

    def sc_hook(gi_):
        for _ in range(4):
            if sc_done[0] < 22:
                sconv_piece(sc_done[0])
                sc_done[0] += 1

    citems = [(xall[HX + t * 128:HX + (t + 1) * 128, :], 128, hnT[:, :, t * 128:(t + 1) * 128], BhnT, gf_bc, mk_pre(t, t * 128, 128)) for t in range(16)]
    norm_T_batch(citems, xt, Bxt, xb, Bxb, "xc", group_hook=sc_hook)
    dma("sp", xt[0][0:2, :], xall[HX - 2:HX, :], [], [Bxt[0]], sem="xc0")
    dma("sp", xt[0][2:18, :], xs, [], [Bxt[0]], sem="xq0")
    norm_T_batch([(None, 18, hnT[:, :, SM0:SM0 + 18], BhnT, gf_bc, mk_pre(-1, SM0, 18))], xt, Bxt, xb, Bxb, "xc")
    if stop == 'C2b':
        return finalize()
    for q in range(sc_done[0], 22):
        sconv_piece(q)

    S.label = 'D'
    S.barrier()
    A.top = HB_TOP
    HT = 1024
    KG = [(0, 4), (4, 4), (8, 4), (12, 4), (16, 3), (19, 3)]
    cbuf = [[A.f32(1024) for _ in range(2)] for _ in range(3)]
    Bcb = [[Buf("c%d%d" % (a_, b_)) for b_ in range(2)] for a_ in range(3)]
    upb = [[A.f32(2 + HT) for _ in range(2)] for _ in range(2)]
    Bupb = [[Buf("up%d%d" % (a_, b_)) for b_ in range(2)] for a_ in range(2)]
    prodS = A.bf(22 * 18).rearrange("p (j n) -> p j n", j=22)
    BprodS = Buf("prodS")
    ups = [[A.f32(18) for _ in range(2)] for _ in range(3)]
    cs_ = [[A.f32(18) for _ in range(2)] for _ in range(3)]
    Bcs = [[Buf("cs%d%d" % (a_, b_)) for b_ in range(2)] for a_ in range(3)]
    hist = A.f32(44 * 2).rearrange("p (c n) -> p c n", c=44)
    Bhist = Buf("hist")
    upst = [A.f32(256) for _ in range(2)]
    Bupst = [Buf("upst0"), Buf("upst1")]
    assert A.top <= HS0, A.top
    A.top = HS0 + 1408
    prodT = A.bf(4 * HT).rearrange("p (j n) -> p j n", j=4)
    BprodT = [Buf("prodT%d" % i) for i in range(6)]
    NWU = 3
    wup = [A.bf(8 * 256).rearrange("p (c n) -> p c n", c=8) for _ in range(NWU)]
    Bwup = [Buf("wup%d" % i) for i in range(NWU)]
    wdns = [A.bf(4 * 1024).rearrange("p (j n) -> p j n", j=4) for _ in range(2)]
    Bwdns = [Buf("wdn0"), Buf("wdn1")]
    wdi = [0]
    wdq = []

    def load_wdn(j0, gs):
        i = wdi[0] % 2
        wdi[0] += 1
        dma("pool", wdns[i][:, 0:gs, :], w_dn.rearrange("(j p) n -> p j n", p=128)[:, j0:j0 + gs, :], [], [Bwdns[i]], sem="wd%d" % i)
        wdq.append((wdns[i], Bwdns[i]))
    assert A.top <= NF, A.top

    wv = w_up.rearrange("(c p) n -> p c n", p=128)
    wslot = {}
    wi = [0]

    def load_wup(j):
        wsl = wi[0] % NWU
        wi[0] += 1
        w_, Bw_ = wup[wsl], Bwup[wsl]
        dma("pool", w_[:, :, 0:128], wv[:, :, j * 128:(j + 1) * 128], [], [Bw_], sem="wu%da" % wsl)
        dma("pool", w_[:, :, 128:256], wv[:, :, 2816 + j * 128:2816 + (j + 1) * 128], [], [Bw_], sem="wu%db" % wsl)
        return w_, Bw_

    def stageA(H, j):
        base = H * HT
        w_, Bw_ = wslot[(H, j)]
        sl = j % 2
        s3 = j % 3
        for gv_ in range(2):
            ch = gv_ * 22 + j
            ub, Bub = upb[sl][gv_], Bupb[sl][gv_]
            if H == 0:
                if gv_ == 0:
                    bks_, Bks_ = nextbank()
                so = gv_ * 32
                for kc in range(8):
                    mm(bks_[:, so:so + 18], w_[:, kc, gv_ * 128:(gv_ + 1) * 128], hnT[:, kc, SM0:SM0 + 18], kc == 0, kc == 7, [Bw_, BhnT], [Bks_])
                if gv_ == 1:
                    for kc in range(8):
                        mm(bks_[0:20, 64:320], hnT[:, kc, NM - 2:NM + 18], w_[:, kc, :], kc == 0, kc == 7, [BhnT, Bw_], [Bks_])
                    for g2_ in range(2):
                        ub2, Bub2 = upb[sl][g2_], Bupb[sl][g2_]
                        ch2 = g2_ * 22 + j
                        so2 = g2_ * 32
                        act(ub2[:, 0:2], bks_[:, so2:so2 + 2], AF.Copy, [Bks_, Bc], [Bub2], scale=flag[:, 0:1])
                        act(ups[s3][g2_], bks_[:, so2:so2 + 18], AF.Copy, [Bks_], [Bcs[s3][g2_]])
                        cs = cs_[s3][g2_]
                        act(cs, ups[s3][g2_], AF.Identity, [Bcs[s3][g2_], Bc], [Bcs[s3][g2_]], scale=cwT[:, 2, ch2:ch2 + 1], bias=cwT[:, 3, ch2:ch2 + 1])
                        for k in range(2):
                            dve(lambda e, cs=cs, ch2=ch2, k=k: e.scalar_tensor_tensor(out=cs[:, 2:18], in0=hsT[:, ch2, k, :], scalar=cwT[:, k, ch2:ch2 + 1], in1=cs[:, 2:18], op0=ALU.mult, op1=ALU.add), [BhsT, Bc, Bcs[s3][g2_]], [Bcs[s3][g2_]])
                    ui = j % 2
                    tcopy("dve", upst[ui][0:20, :], bks_[0:20, 64:320], [Bks_], [Bupst[ui]])
                    for g2_ in range(2):
                        cc0 = g2_ * 2816 + j * 128
                        dma("pool", convp[:, cc0:cc0 + 128], upst[ui][0:2, g2_ * 128:(g2_ + 1) * 128], [Bupst[ui]], [], sem="uo%d" % ui)
                        dma("pool", convs[:, 1, cc0:cc0 + 128], upst[ui][4:20, g2_ * 128:(g2_ + 1) * 128], [Bupst[ui]], [], sem="uo%d" % ui)
                    out_bufs.append(Bupst[ui])
            else:
                tcopy("dve", ub[:, 0:2], hist[:, ch, :], [Bhist], [Bub])
            for s2 in range(2):
                c0 = base + s2 * 512
                bk, Bk = nextbank()
                for kc in range(8):
                    mm(bk, w_[:, kc, gv_ * 128:(gv_ + 1) * 128], hnT[:, kc, c0:c0 + 512], kc == 0, kc == 7, [Bw_, BhnT], [Bk])
                act(ub[:, 2 + s2 * 512:2 + (s2 + 1) * 512], bk, AF.Copy, [Bk], [Bub])
            if H == 0:
                tcopy("dve", hist[:, ch, :], ub[:, HT:HT + 2], [Bub], [Bhist])
        for gv_ in range(2):
            ch = gv_ * 22 + j
            ub, Bub = upb[sl][gv_], Bupb[sl][gv_]
            cc, Bcc = cbuf[s3][gv_], Bcb[s3][gv_]
            act(cc, ub[:, 2:2 + HT], AF.Identity, [Bub, Bc], [Bcc], scale=cwT[:, 2, ch:ch + 1], bias=cwT[:, 3, ch:ch + 1])
        for gv_ in range(2):
            ch = gv_ * 22 + j
            ub, Bub = upb[sl][gv_], Bupb[sl][gv_]
            cc, Bcc = cbuf[s3][gv_], Bcb[s3][gv_]
            for k in range(2):
                dve(lambda e, cc=cc, ub=ub, k=k, ch=ch: e.scalar_tensor_tensor(out=cc, in0=ub[:, k:k + HT], scalar=cwT[:, k, ch:ch + 1], in1=cc, op0=ALU.mult, op1=ALU.add), [Bub, Bc, Bcc], [Bcc])

    def stageB(H, j, jj):
        s3 = j % 3
        cg, cv = cbuf[s3][0], cbuf[s3][1]
        act(cg, cg, AF.Gelu_apprx_tanh, [Bcb[s3][0]], [Bcb[s3][0]])
        dve(lambda e, cg=cg, cv=cv, jj=jj: e.tensor_tensor(out=prodT[:, jj, :], in0=cg, in1=cv, op=ALU.mult), [Bcb[s3][0], Bcb[s3][1]], [BprodT[jj]])
        if H == 0:
            act(cs_[s3][0], cs_[s3][0], AF.Gelu_apprx_tanh, [Bcs[s3][0]], [Bcs[s3][0]])
            dve(lambda e, s3=s3, j=j: e.tensor_tensor(out=prodS[:, j, :], in0=cs_[s3][0], in1=cs_[s3][1], op=ALU.mult), [Bcs[s3][0], Bcs[s3][1]], [BprodS])

    def wdown_group(H, j0, gs):
        wdn, Bwdn = wdq.pop(0)
        for tb in range(4):
            ths = [(tb * 2 + q_, half) for q_ in range(2) for half in range(2)]
            bks = [nextbank() for _ in ths]
            for (tl, half), (bk, Bk) in zip(ths, bks):
                for jj in range(gs - 1):
                    mm(bk, prodT[:, jj, tl * 128:(tl + 1) * 128], wdn[:, jj, half * 512:(half + 1) * 512], jj == 0, False, [BprodT[jj], Bwdn], [Bk])
            for (tl, half), (bk, Bk) in zip(ths, bks):
                jj = gs - 1
                mm(bk, prodT[:, jj, tl * 128:(tl + 1) * 128], wdn[:, jj, half * 512:(half + 1) * 512], False, True, [BprodT[jj], Bwdn], [Bk])
            for (tl, half), (bk, Bk) in zip(ths, bks):
                t = H * 8 + tl
                dve(lambda e, bk=bk, t=t, half=half: e.tensor_tensor(out=hbuf[:, t, half * 512:(half + 1) * 512], in0=bk, in1=hbuf[:, t, half * 512:(half + 1) * 512], op=ALU.add), [Bk, Bh[t]], [Bh[t]])
        if H == 0:
            for half in range(2):
                bk, Bk = nextbank()
                for jj in range(gs):
                    mm(bk[0:18, :], prodS[:, j0 + jj, :], wdn[:, jj, half * 512:(half + 1) * 512], jj == 0, jj == gs - 1, [BprodS, Bwdn], [Bk])
                dve(lambda e, bk=bk, half=half: e.tensor_tensor(out=hsm[0:18, half * 512:(half + 1) * 512], in0=bk[0:18, :], in1=hsm[0:18, half * 512:(half + 1) * 512], op=ALU.add), [Bk, Bhsm], [Bhsm])

    def final_rows(haps, dsts):
        stis = []
        for k, (hap, m, Bh_) in enumerate(haps):
            sti = stat_i[0] % 8
            stat_i[0] += 1
            stis.append(sti)
            scr = cbuf[k % 3][(k // 3) % 2]
            Bscr = Bcb[k % 3][(k // 3) % 2]
            act(scr.bitcast(BF16)[0:m, 0:1024], hap, AF.Square, [Bh_], [Bscr, Bstat[sti]], accum_out=stat[:m, sti, 0:1])
        rs = []
        for k, (hap, m, Bh_) in enumerate(haps):
            rs.append(rstd_of(stat[:m, stis[k], 0:1], m, 1.0 / 1024, stis[k], Bstat[stis[k]]))
        for k, (hap, m, Bh_) in enumerate(haps):
            dve(lambda e, hap=hap, r=rs[k], m=m: e.scalar_tensor_tensor(out=hap, in0=hap, scalar=r, in1=gfin_bc[0:m, :], op0=ALU.mult, op1=ALU.mult), [Bh_, Bstat[stis[k]], Bc], [Bh_])
        for k, (hap, m, Bh_) in enumerate(haps):
            dst, r0 = dsts[k]
            dma("sp" if m == 128 else "pool", dst, hap[r0:m, :], [Bh_], [], sem=("yo%d" if m == 128 else "yp%d") % (k % 4))
            out_bufs.append(Bh_)

    for H in range(2):
        order = [(j0, gs, jj) for (j0, gs) in KG for jj in range(gs)]
        PRE = 3
        jof = lambda q_: order[q_][0] + order[q_][2]
        for q_ in range(min(PRE, len(order))):
            wslot[(H, jof(q_))] = load_wup(jof(q_))
        doneA = set()

        def doA(q_):
            if q_ < len(order) and q_ not in doneA:
                doneA.add(q_)
                stageA(H, jof(q_))

        load_wdn(*KG[0])
        gnext = [1]
        doA(0)
        doA(1)
        for idx, (j0, gs, jj) in enumerate(order):
            j = j0 + jj
            if idx + PRE < len(order):
                wslot[(H, jof(idx + PRE))] = load_wup(jof(idx + PRE))
            if jj == gs - 1:
                stageB(H, j, jj)
                doA(idx + 2)
                doA(idx + 3)
                if gnext[0] < len(KG):
                    load_wdn(*KG[gnext[0]])
                    gnext[0] += 1
                wdown_group(H, j0, gs)
            else:
                doA(idx + 2)
                stageB(H, j, jj)
        final_rows([(hbuf[:, H * 8 + tl, :], 128, Bh[H * 8 + tl]) for tl in range(8)],
                   [(y[(H * 8 + tl) * 128:(H * 8 + tl + 1) * 128, :], 0) for tl in range(8)])
        if H == 0:
            final_rows([(hsm[0:18, :], 18, Bhsm)], [(ys, 2)])

    return finalize()


_NC_CACHE = {}


def make_in_maps(x_prompt, x_sample, cache_kv_w128, cache_kv_w512, cache_kv_w2048, state_conv_ffn,
                 norm_mix, w_in, ln_v_gain, ln_v_bias, w_spatial, b_spatial, w_proj_a, w_proj_b,
                 w_out, norm_ffn, w_up, conv_w, conv_b, w_down, norm_final, cores=None):
    f = lambda a: np.ascontiguousarray(np.asarray(a, dtype=np.float32))
    x_prompt = f(x_prompt)
    B, SEQ, D = x_prompt.shape
    xp = np.zeros((B, HX + SEQ, D), np.float32)
    xp[:, HX:] = x_prompt
    k = np.arange(128)[:, None]
    q = np.arange(128)[None, :]
    mcur = (k <= q).astype(np.float32)
    mprev = (k >= q).astype(np.float32)
    blockones = ((k // 64) == (q // 64)).astype(np.float32)
    caches = [f(cache_kv_w128)[0], f(cache_kv_w512)[0], f(cache_kv_w2048)[0]]
    common = {
        "w_in": f(w_in)[0], "w_pa": f(w_proj_a)[0], "w_pb": f(w_proj_b)[0], "w_out": f(w_out)[0],
        "w_up": f(w_up)[0], "w_dn": f(w_down)[0],
        "vec8": np.concatenate([f(norm_mix)[0].reshape(8, 128), f(norm_ffn)[0].reshape(8, 128)], 0),
        "nfin": f(norm_final), "lng": f(ln_v_gain)[0], "lnb": f(ln_v_bias)[0],
        "cwb": np.concatenate([f(conv_w)[0], f(conv_b)], 0).reshape(4, 44, 128),
        "wsp": f(w_spatial)[0], "bsp": f(b_spatial)[0],
        "w00": np.ascontiguousarray(f(w_spatial)[0][:, 0, 0]),
    }
    in_maps = []
    for c in (range(NCORES) if cores is None else cores):
        b, qi = c // 4, c % 4
        fl = 0.0 if qi == 0 else 1.0
        m = dict(common)
        m["xall"] = np.ascontiguousarray(xp[b, qi * NM:qi * NM + HX + NM])
        m["xs"] = f(x_sample)[c * NS:(c + 1) * NS, 0]
        for g in range(3):
            cg = caches[g][c * NS:(c + 1) * NS]
            m["ck%d" % g] = np.ascontiguousarray(cg.reshape(NS, cg.shape[1], 512))
        m["sconv"] = f(state_conv_ffn)[0, c * NS:(c + 1) * NS]
        m["cst"] = np.stack([np.eye(128, dtype=np.float32), mprev, mcur, mprev * fl, mcur, blockones, np.ones((128, 128), np.float32)])
        m["flagd"] = np.full((128, 1), fl, np.float32)
        in_maps.append(m)
    return in_maps


def assemble(R, B=2):
    y_prompt = np.stack([np.concatenate([R[b * 4 + qi]["y"] for qi in range(4)], 0) for b in range(B)])
    y_sample = np.concatenate([R[c]["ys"] for c in range(NCORES)], 0)[:, None, :]
    outs = [y_prompt, y_sample]
    for g in range(3):
        L = WIN[g][0]
        kvp = np.stack([R[b * 4 + 3]["kv%d" % g] for b in range(B)]).reshape(1, B, L, 2, 4, 64)
        kvsm = np.concatenate([R[c]["kvs%d" % g] for c in range(NCORES)], 0).reshape(1, NCORES * NS, 1, 2, 4, 64)
        outs += [kvp, kvsm]
    vcp = np.stack([R[b * 4 + 3]["vch"] for b in range(B)])[None]
    vcs = np.concatenate([R[c]["vchs"] for c in range(NCORES)], 0)[None, :, None, :]
    cp = np.stack([R[b * 4 + 3]["convp"] for b in range(B)])[None]
    cs = np.concatenate([R[c]["convs"] for c in range(NCORES)], 0)[None]
    outs += [vcp, vcs, cp, cs]
    return tuple(np.ascontiguousarray(np.asarray(o, dtype=np.float32)) for o in outs)


def kernel(**inputs):
    in_maps = make_in_maps(**inputs)
    if "nc" not in _NC_CACHE:
        _NC_CACHE["nc"] = build_program()
    res = run_bass_kernel_spmd(_NC_CACHE["nc"], in_maps, core_ids=list(range(NCORES)))
    return assemble(res.results)
```

```python
import numpy as np
from contextlib import ExitStack
import concourse.bass as bass
import concourse.mybir as mybir
from concourse.bass_utils import run_bass_kernel_spmd

F32 = mybir.dt.float32
BF16 = mybir.dt.bfloat16
AF = mybir.ActivationFunctionType
ALU = mybir.AluOpType

NCORES = 8
HX = 2176
NM = 2048
NS = 16
SM0 = NM
NCOL = NM + 2 + NS
EPS = 1e-6
WIN = ((128, 1), (512, 4), (2048, 16))
KBASE = (1920, 1536, 0)
KLEN = (2304, 2688, 4224)


class Buf:
    __slots__ = ("name", "w", "r")

    def __init__(self, name=""):
        self.name = name
        self.w = None
        self.r = {}


class Sched:
    ENG = ("pe", "act", "dve", "pool", "sp")

    def __init__(self, nc, stack):
        self.nc = nc
        self.stack = stack
        self.prog = {e: [] for e in self.ENG}
        self.sems = {}
        self.cnt = {}
        self.seen = {e: {} for e in self.ENG}
        self.label = ""
        for e in self.ENG:
            self._sem("E_" + e)

    def _sem(self, name):
        if name not in self.sems:
            self.sems[name] = self.stack.enter_context(self.nc.semaphore(name))
            self.cnt[name] = 0
        return name

    def _need(self, eng, waits, tok):
        if tok is None:
            return
        sem, val = tok
        if eng == "pe" and sem == "E_pe":
            return
        if self.seen[eng].get(sem, 0) >= val:
            return
        self.seen[eng][sem] = val
        waits.append((sem, val))

    def op(self, eng, fn, reads=(), writes=(), dma=None):
        waits = []
        for b in reads:
            self._need(eng, waits, b.w)
        for b in writes:
            self._need(eng, waits, b.w)
            for s, v in b.r.items():
                self._need(eng, waits, (s, v))
        if dma is not None:
            sem = self._sem(dma)
            inc = 16
        else:
            sem = "E_" + eng
            inc = 1
        self.cnt[sem] += inc
        tok = (sem, self.cnt[sem])
        self.prog[eng].append((waits, fn, (sem, inc), self.label + " r:" + ",".join(b.name for b in reads) + " w:" + ",".join(b.name for b in writes)))
        for b in reads:
            if b.r.get(sem, 0) < tok[1]:
                b.r[sem] = tok[1]
        for b in writes:
            b.w = tok
            b.r = {}
        return tok

    def barrier(self):
        allw = [(s, v) for s, v in self.cnt.items() if v > 0]
        for e in self.ENG:
            waits = []
            for t in allw:
                self._need(e, waits, t)
            self.prog[e].append((waits, None, None, ""))

    def emit(self):
        nc = self.nc
        with nc.Block() as block:
            def run(name, e):
                for waits, fn, inc, lab in self.prog[name]:
                    for s, v in waits:
                        e.wait_ge(self.sems[s], v)
                    if fn is not None:
                        with nc.named_scope(lab.split(" ")[0] or "none"):
                            ins = fn(e)
                        ins.then_inc(self.sems[inc[0]], inc[1])

            @block.tensor
            def _(e):
                run("pe", e)

            @block.scalar
            def _(e):
                run("act", e)

            @block.vector
            def _(e):
                run("dve", e)

            @block.gpsimd
            def _(e):
                run("pool", e)

            @block.sync
            def _(e):
                run("sp", e)


def build_program(stop=None):
    nc = bass.Bass("TRN2", target_bir_lowering=False)

    def din(name, shape):
        return nc.dram_tensor(name, list(shape), F32, kind="ExternalInput").ap()

    def dout(name, shape):
        return nc.dram_tensor(name, list(shape), F32, kind="ExternalOutput").ap()

    xall = din("xall", [HX + NM, 1024])
    xs = din("xs", [NS, 1024])
    ck = [din("ck0", [NS, 128, 512]), din("ck1", [NS, 512, 512]), din("ck2", [NS, 2048, 512])]
    sconv = din("sconv", [NS, 2, 5632])
    w_in = din("w_in", [1024, 5376])
    w_pa = din("w_pa", [512, 1024])
    w_pb = din("w_pb", [256, 1024])
    w_out = din("w_out", [1024, 1024])
    w_up = din("w_up", [1024, 5632])
    w_dn = din("w_dn", [2816, 1024])
    vec8 = din("vec8", [16, 128])
    nfin = din("nfin", [1024])
    lng = din("lng", [512])
    lnb = din("lnb", [512])
    cwb = din("cwb", [4, 44, 128])
    wsp = din("wsp", [4, 128, 128])
    bsp = din("bsp", [4, 128])
    w00 = din("w00", [4])
    cst = din("cst", [7, 128, 128])
    flagd = din("flagd", [128, 1])

    y = dout("y", [NM, 1024])
    ys = dout("ys", [NS, 1024])
    kvo = [dout("kv0", [128, 512]), dout("kv1", [512, 512]), dout("kv2", [2048, 512])]
    kvs = [dout("kvs0", [NS, 512]), dout("kvs1", [NS, 512]), dout("kvs2", [NS, 512])]
    vch = dout("vch", [128, 512])
    vchs = dout("vchs", [NS, 512])
    convp = dout("convp", [2, 5632])
    convs = dout("convs", [NS, 2, 5632])

    st = ExitStack()
    S = Sched(nc, st)
    NF = 53100
    arena = st.enter_context(nc.sbuf_tensor("arena", [128, NF], F32))
    psum_all = st.enter_context(nc.psum_tensor("psall", [128, 4096], F32))
    pbank = [psum_all[:, i * 512:(i + 1) * 512] for i in range(8)]
    PB = [Buf("pb%d" % i) for i in range(8)]
    bank_i = [0]

    def nextbank():
        i = bank_i[0] % 8
        bank_i[0] += 1
        return pbank[i], PB[i]

    def nextpair():
        if bank_i[0] % 2:
            bank_i[0] += 1
        i = bank_i[0] % 8
        bank_i[0] += 2
        return pbank[i], PB[i], pbank[i + 1], PB[i + 1], psum_all[:, i * 512:(i + 2) * 512]

    class Arena:
        def __init__(self):
            self.top = 0

        def f32(self, n):
            a = arena[:, self.top:self.top + n]
            self.top += n
            assert self.top <= NF, self.top
            return a

        def bf(self, n):
            n2 = (n + 1) // 2
            a = arena[:, self.top:self.top + n2].bitcast(BF16)
            self.top += n2
            assert self.top <= NF, self.top
            return a

    A = Arena()
    TT = [NF - 2200]

    def tmp_f32(n):
        a = arena[:, TT[0]:TT[0] + n]
        TT[0] += n
        assert TT[0] <= NF
        return a
    out_bufs = []
    uid = [0]

    def finalize():
        S.barrier()
        S.emit()
        st.close()
        return nc

    def dma(eng, out, in_, rd, wr, sem=None, **kw):
        if sem is None:
            uid[0] += 1
            sem = "d%d" % (uid[0] % 24)
        return S.op(eng, lambda e: e.dma_start(out=out, in_=in_, **kw), reads=rd, writes=wr, dma=sem)

    def mm(out, lhsT, rhs, start, stop, rd, wr):
        S.op("pe", lambda e: e.matmul(out, lhsT=lhsT, rhs=rhs, start=start, stop=stop), reads=rd, writes=wr)

    def act(out, in_, func, rd, wr, **kw):
        S.op("act", lambda e: e.activation(out=out, in_=in_, func=func, **kw), reads=rd, writes=wr)

    def dve(fn, rd, wr):
        S.op("dve", fn, reads=rd, writes=wr)

    def tcopy(eng, out, in_, rd, wr):
        S.op(eng, lambda e: e.tensor_copy(out=out, in_=in_), reads=rd, writes=wr)

    Bc = Buf("const")
    cbs = []

    def CW():
        nb = Buf("c%d" % len(cbs))
        cbs.append(nb)
        return [nb]

    def CR():
        return list(cbs)
    cst_f = A.f32(7 * 128).rearrange("p (k n) -> p k n", k=7)
    dma("sp", cst_f, cst.rearrange("k p n -> p k n"), [], CW(), sem="c0")
    ident_f = cst_f[:, 0, :]
    blockones_f = cst_f[:, 5, :]
    cst_b = A.bf(7 * 128).rearrange("p (k n) -> p k n", k=7)
    tcopy("dve", cst_b, cst_f, CR(), CW())
    ident_b = cst_b[:, 0, :]
    mprev_b = cst_b[:, 1, :]
    mcur_b = cst_b[:, 2, :]
    medge_b = cst_b[:, 3, :]
    ones_b = cst_b[:, 6, :]
    flag = A.f32(1)
    dma("sp", flag, flagd, [], CW(), sem="c1")
    gfin_bc = A.f32(1024)
    dma("sp", gfin_bc, nfin.partition_broadcast(128), [], CW(), sem="c2")
    lng_bc = A.f32(512)
    lnb_bc = A.f32(512)
    dma("sp", lng_bc, lng.partition_broadcast(128), [], CW(), sem="c3")
    dma("sp", lnb_bc, lnb.partition_broadcast(128), [], CW(), sem="c4")
    w00_bc = A.f32(4)
    dma("sp", w00_bc, w00.partition_broadcast(128), [], CW(), sem="c5")
    v8_sb = tmp_f32(128)
    dma("sp", v8_sb[0:16, :], vec8, [], CW(), sem="c6")
    cw_sb = tmp_f32(4 * 128).rearrange("p (k n) -> p k n", k=4)
    dma("sp", cw_sb[0:44, :, :], cwb.rearrange("k c p -> c k p"), [], CW(), sem="c7")
    gvec = A.f32(16)
    cwT = A.f32(4 * 44).rearrange("p (k c) -> p k c", k=4)
    bk, Bk = nextbank()
    mm(bk[:, 0:16], v8_sb[0:16, :], ident_f[0:16, 0:16], True, True, CR(), [Bk])
    tcopy("dve", gvec, bk[:, 0:16], [Bk], CW())
    bk, Bk = nextbank()
    for k in range(4):
        mm(bk[:, k * 44:(k + 1) * 44], cw_sb[0:44, k, :], ident_f[0:44, 0:44], True, True, CR(), [Bk])
    tcopy("dve", cwT, bk[:, 0:176].rearrange("p (k c) -> p k c", k=4), [Bk], CW())
    ones_f = A.f32(128)
    S.op("dve", lambda e: e.memset(ones_f, 1.0), writes=CW())
    gm_bc = A.bf(8 * 128).rearrange("p (c n) -> p c n", c=8)
    gf_bc = A.bf(8 * 128).rearrange("p (c n) -> p c n", c=8)
    for c in range(8):
        dve(lambda e, c=c: e.tensor_scalar(out=gm_bc[:, c, :], in0=ones_f, scalar1=gvec[:, c:c + 1], scalar2=None, op0=ALU.mult), CR(), CW())
        dve(lambda e, c=c: e.tensor_scalar(out=gf_bc[:, c, :], in0=ones_f, scalar1=gvec[:, 8 + c:9 + c], scalar2=None, op0=ALU.mult), CR(), CW())
    wsp_f = tmp_f32(512).rearrange("p (g n) -> p g n", g=4)
    dma("sp", wsp_f, wsp.rearrange("g t s -> t g s"), [], CW(), sem="c8")
    wsp_b = tmp_f32(256).bitcast(BF16).rearrange("p (g n) -> p g n", g=4)
    for g in range(4):
        dve(lambda e, g=g: e.tensor_tensor(out=wsp_b[:, g, :], in0=wsp_f[:, g, :], in1=cst_f[:, 1, :], op=ALU.mult), CR(), CW())
    WmT = A.bf(512).rearrange("p (g n) -> p g n", g=4)
    bk, Bk = nextbank()
    bkb = bk.bitcast(BF16).rearrange("p (g n) -> p g n", g=8)
    for g in range(4):
        S.op("pe", lambda e, g=g: e.transpose(out=bkb[:, g, :], in_=wsp_b[:, g, :], identity=ident_b), reads=CR(), writes=[Bk])
    tcopy("dve", WmT, bkb[:, 0:4, :], [Bk], CW())
    bsp_f = tmp_f32(512)
    dma("sp", bsp_f[0:1, :], bsp.rearrange("(o g) t -> o (g t)", o=1), [], CW(), sem="c9")
    bsp_b = A.bf(512)
    tcopy("dve", bsp_b[0:1, :], bsp_f[0:1, :], CR(), CW())
    bsp0_f = A.f32(4 * 16)
    bsp0_b = A.bf(4 * 16)
    for g in range(4):
        dve(lambda e, g=g: e.tensor_scalar(out=bsp0_f[0:1, g * 16:(g + 1) * 16], in0=ones_f[0:1, 0:16], scalar1=bsp_f[0:1, g * 128:g * 128 + 1], scalar2=None, op0=ALU.mult), CR(), CW())
    tcopy("dve", bsp0_b[0:1, :], bsp0_f[0:1, :], CR(), CW())
    D16 = A.bf(4 * 16).rearrange("p (g n) -> p g n", g=4)
    for g in range(4):
        dve(lambda e, g=g: e.tensor_scalar(out=D16[0:16, g, :], in0=ident_f[0:16, 0:16], scalar1=w00_bc[0:16, g:g + 1], scalar2=None, op0=ALU.mult), CR(), CW())
    stat = A.f32(8 * 8).rearrange("p (s n) -> p s n", s=8)
    Bstat = [Buf("st%d" % i) for i in range(8)]
    stat_i = [0]
    S.op("dve", lambda e: e.memset(stat[:, 7, 7:8], 0.0), reads=CR(), writes=[Bc])
    CONST_TOP = A.top
    print('CONST_TOP', CONST_TOP)

    if stop == 'const':
        return finalize()
    def rstd_of(ssq_ap, n, inv_n, sti, Bs):
        s_ = stat[:n, sti, :]
        dve(lambda e: e.tensor_scalar(out=s_[:, 1:2], in0=ssq_ap, scalar1=inv_n, scalar2=EPS, op0=ALU.mult, op1=ALU.add), [Bs], [Bs])
        act(s_[:, 2:3], s_[:, 1:2], AF.Ln, [Bs], [Bs])
        act(s_[:, 3:4], s_[:, 2:3], AF.Exp, [Bs], [Bs], scale=-0.5)
        return s_[:, 3:4]

    def norm_rows(xt_ap, n, Bx, out_bf, Bo):
        sti = stat_i[0] % 8
        stat_i[0] += 1
        Bs = Bstat[sti]
        act(out_bf, xt_ap, AF.Square, [Bx], [Bo, Bs], accum_out=stat[:n, sti, 0:1])
        r = rstd_of(stat[:n, sti, 0:1], n, 1.0 / 1024, sti, Bs)
        act(out_bf, xt_ap, AF.Copy, [Bx, Bs], [Bo], scale=r)

    def transpose_rows(src_bf, n, Bsrc, dstT, Bdst, g_bc):
        bk, Bk = nextbank()
        pt = bk.bitcast(BF16).rearrange("p (c t) -> p c t", c=8)
        for c in range(8):
            S.op("pe", lambda e, c=c: e.transpose(out=pt[:, c, 0:n], in_=src_bf[0:n, c * 128:(c + 1) * 128], identity=ident_b[0:n, 0:n]), reads=[Bsrc, Bc], writes=[Bk])
        dve(lambda e: e.tensor_tensor(out=dstT, in0=pt[:, :, 0:n], in1=g_bc[:, :, 0:n], op=ALU.mult), [Bk, Bc], [Bdst])

    xnT = A.bf(8 * HX).rearrange("p (c n) -> p c n", c=8)
    BxnT = Buf("xnT")
    xeT = A.bf(8 * 128).rearrange("p (c n) -> p c n", c=8)
    BxeT = Buf("xeT")
    P1 = A.top
    QT = A.bf(6 * NCOL).rearrange("p (c n) -> p c n", c=6)
    BQT = Buf("QT")
    KT = [A.bf(2 * KLEN[g]).rearrange("p (c n) -> p c n", c=2) for g in range(3)]
    BKT = [Buf("KT%d" % g) for g in range(3)]
    KTs = A.bf(6 * 18).rearrange("p (c n) -> p c n", c=6)
    VTs = A.bf(6 * 18).rearrange("p (c n) -> p c n", c=6)
    BKTs = Buf("KTs")
    NVB = 79
    Vb = A.bf(NVB * 256).rearrange("p (b n) -> p b n", b=NVB)
    BV = [Buf("V%d" % i) for i in range(NVB + 3)]
    vidx = {}
    PW = A.top
    wqkv = A.bf(8 * 2304).rearrange("p (c n) -> p c n", c=8)
    Bw = Buf("wqkv")
    NKV = 5
    kvst = [A.f32(512) for _ in range(NKV)]
    Bkvst = [Buf("kvst%d" % i) for i in range(NKV)]
    kvst_i = [0]
    PB_TOP = A.top

    dma("pool", wqkv, w_in.rearrange("(c p) n -> p c n", p=128)[:, :, 0:2304], [], [Bw], sem="w0")

    def wcol(kind, g, c):
        return kind * 768 + g * 256 + c * 128

    def norm_T_batch(items, xts, Bxts, xbs, Bxbs, semp, group_hook=None):
        sets = xts if isinstance(xts[0], list) else None
        G = len(xts[0]) if sets is not None else len(xts)
        all_sets = (xts, Bxts, xbs, Bxbs)
        for g0 in range(0, len(items), G):
            grp = items[g0:g0 + G]
            si_ = 0
            if sets is not None:
                si_ = (g0 // G) % len(sets)
                xts, Bxts, xbs, Bxbs = (all_sets[0][si_], all_sets[1][si_], all_sets[2][si_], all_sets[3][si_])
            stis = []
            srcs = []
            for k, (src_rows, n, dstT, Bdst, g_bc, pre) in enumerate(grp):
                if src_rows is not None:
                    dma("sp", xts[k][0:n, :], src_rows, [], [Bxts[k]], sem="%s%d%d" % (semp, si_, k))
                srcs.append((xts[k], Bxts[k]))
            for k, (src_rows, n, dstT, Bdst, g_bc, pre) in enumerate(grp):
                if pre is not None:
                    srcs[k] = pre(k)
            for k, (src_rows, n, dstT, Bdst, g_bc, pre) in enumerate(grp):
                sti = stat_i[0] % 8
                stat_i[0] += 1
                stis.append(sti)
                act(xbs[k][0:n, :], srcs[k][0][0:n, :], AF.Square, [srcs[k][1]], [Bxbs[k], Bstat[sti]], accum_out=stat[:n, sti, 0:1])
            for k, (src_rows, n, dstT, Bdst, g_bc, pre) in enumerate(grp):
                s_ = stat[:n, stis[k], :]
                dve(lambda e, s_=s_: e.tensor_scalar(out=s_[:, 1:2], in0=s_[:, 0:1], scalar1=1.0 / 1024, scalar2=EPS, op0=ALU.mult, op1=ALU.add), [Bstat[stis[k]]], [Bstat[stis[k]]])
            for k, (src_rows, n, dstT, Bdst, g_bc, pre) in enumerate(grp):
                s_ = stat[:n, stis[k], :]
                act(s_[:, 2:3], s_[:, 1:2], AF.Ln, [Bstat[stis[k]]], [Bstat[stis[k]]])
            for k, (src_rows, n, dstT, Bdst, g_bc, pre) in enumerate(grp):
                s_ = stat[:n, stis[k], :]
                act(s_[:, 3:4], s_[:, 2:3], AF.Exp, [Bstat[stis[k]]], [Bstat[stis[k]]], scale=-0.5)
            for k, (src_rows, n, dstT, Bdst, g_bc, pre) in enumerate(grp):
                s_ = stat[:n, stis[k], :]
                dve(lambda e, k=k, n=n, s_=s_, xbs=xbs, src_=srcs[k][0]: e.tensor_scalar(out=xbs[k][0:n, :], in0=src_[0:n, :], scalar1=s_[:, 3:4], scalar2=None, op0=ALU.mult), [srcs[k][1], Bstat[stis[k]]], [Bxbs[k]])
            if group_hook is not None:
                group_hook(g0 // G)
            for k, (src_rows, n, dstT, Bdst, g_bc, pre) in enumerate(grp):
                transpose_rows(xbs[k], n, Bxbs[k], dstT, Bdst, g_bc)

    def proj_fm(dst, Bdst, wt, Bwt, col0, src, Bsrc, c0, n, nk=8, func=AF.Copy, **kw):
        bk, Bk = nextbank()
        for kc in range(nk):
            mm(bk[:, 0:n], wt[:, kc, col0:col0 + 128], src[:, kc, c0:c0 + n], kc == 0, kc == nk - 1, [Bwt, Bsrc], [Bk])
        act(dst, bk[:, 0:n], func, [Bk], [Bdst], **kw)

    def vblock(g, start, step, n, src, Bsrc):
        idx = len(vidx)
        vidx[(g, start, step, "m" if src is xnT_main_marker[0] else "h")] = idx
        bk, Bk = nextbank()
        for kc in range(8):
            mm(bk[0:n, 0:256], src[:, kc, start:start + step * (n - 1) + 1:step], wqkv[:, kc, wcol(2, g, 0):wcol(2, g, 0) + 256], kc == 0, kc == 7, [Bsrc, Bw], [Bk])
        tcopy("dve", Vb[0:n, idx, :], bk[0:n, 0:256], [Bk], [BV[idx]])
        return idx

    xnT_main_marker = [None]

    S.label = 'B1'
    QT_f32 = arena[:, P1:P1 + 6144]
    xtA = [QT_f32[:, k * 1024:(k + 1) * 1024] for k in range(4)]
    xbA = [QT_f32[:, 4096 + k * 512:4096 + (k + 1) * 512].bitcast(BF16) for k in range(4)]
    BxtA = [Buf("xtA%d" % k) for k in range(4)]
    BxbA = [Buf("xbA%d" % k) for k in range(4)]
    VBa = PW - NVB * 128
    Va_f32 = arena[:, VBa:VBa + 6144]
    xtA2 = [Va_f32[:, k * 1024:(k + 1) * 1024] for k in range(4)]
    xbA2 = [Va_f32[:, 4096 + k * 512:4096 + (k + 1) * 512].bitcast(BF16) for k in range(4)]
    BxtA2 = [Buf("xtA2%d" % k) for k in range(4)]
    BxbA2 = [Buf("xbA2%d" % k) for k in range(4)]
    norm_T_batch([(xall[t * 128:(t + 1) * 128, :], 128, xnT[:, :, t * 128:(t + 1) * 128], BxnT, gm_bc, None) for t in range(17)],
                 [xtA, xtA2], [BxtA, BxtA2], [xbA, xbA2], [BxbA, BxbA2], "xa")
    if stop == 'B1a':
        return finalize()
    for g in range(3):
        lt = KBASE[g]
        while lt < HX:
            n = min(512, HX - lt)
            for c in range(2):
                proj_fm(KT[g][:, c, lt - KBASE[g]:lt - KBASE[g] + n], BKT[g], wqkv, Bw, wcol(1, g, c), xnT, BxnT, lt, n)
            lt += n
    if stop == 'B1k':
        return finalize()
    S.barrier()
    for r in range(16):
        vblock(2, 128 + r, 16, 128, xnT, BxnT)
    for r in range(4):
        vblock(1, 1664 + r, 4, 128, xnT, BxnT)
    vblock(0, 2048, 1, 128, xnT, BxnT)
    vblock(2, 126, 16, 128, xnT, BxnT)
    vblock(2, 127, 16, 128, xnT, BxnT)
    vblock(1, 1662, 4, 128, xnT, BxnT)
    vblock(1, 1663, 4, 128, xnT, BxnT)
    vblock(0, 2046, 1, 128, xnT, BxnT)
    vblock(2, 2174, 1, 1, xnT, BxnT)
    vblock(2, 2175, 1, 1, xnT, BxnT)
    vblock(1, 2174, 1, 1, xnT, BxnT)
    vblock(1, 2175, 1, 1, xnT, BxnT)
    vblock(0, 2174, 1, 2, xnT, BxnT)
    tcopy("dve", xeT, xnT[:, :, 2048:2176], [BxnT], [BxeT])

    if stop == 'B1':
        return finalize()
    S.label = 'B2'
    xnT_main_marker[0] = xnT
    S.barrier()
    VB0 = PW - NVB * 128 + 31 * 128
    Vm_f32 = arena[:, VB0:VB0 + 6144]
    xtB = [Vm_f32[:, k * 1024:(k + 1) * 1024] for k in range(4)]
    xbB = [Vm_f32[:, 4096 + k * 512:4096 + (k + 1) * 512].bitcast(BF16) for k in range(4)]
    BxtB = [Buf("xtB%d" % k) for k in range(4)]
    BxbB = [Buf("xbB%d" % k) for k in range(4)]
    norm_T_batch([(xall[HX + t * 128:HX + (t + 1) * 128, :], 128, xnT[:, :, t * 128:(t + 1) * 128], BxnT, gm_bc, None) for t in range(16)],
                 [xtB, xtA], [BxtB, BxtA], [xbB, xbA], [BxbB, BxbA], "xb")
    tcopy("dve", xnT[:, :, SM0:SM0 + 2], xeT[:, :, 126:128], [BxeT], [BxnT])
    norm_T_batch([(xs, NS, xnT[:, :, SM0 + 2:SM0 + 2 + NS], BxnT, gm_bc, None)], xtB, BxtB, xbB, BxbB, "xb")
    slices = [(i * 512, 512) for i in range(4)] + [(SM0, 18)]
    if stop == 'B2a':
        return finalize()
    S.barrier()
    for (c0, n) in slices:
        for gc in range(6):
            proj_fm(QT[:, gc, c0:c0 + n], BQT, wqkv, Bw, wcol(0, gc // 2, gc % 2), xnT, BxnT, c0, n)
    for (c0, n) in slices[:4]:
        for g in range(3):
            for c in range(2):
                kc0 = HX + c0 - KBASE[g]
                proj_fm(KT[g][:, c, kc0:kc0 + n], BKT[g], wqkv, Bw, wcol(1, g, c), xnT, BxnT, c0, n)
    for gc in range(6):
        proj_fm(KTs[:, gc, :], BKTs, wqkv, Bw, wcol(1, gc // 2, gc % 2), xnT, BxnT, SM0, 18)
        proj_fm(VTs[:, gc, :], BKTs, wqkv, Bw, wcol(2, gc // 2, gc % 2), xnT, BxnT, SM0, 18)
    if stop == 'B2q':
        return finalize()
    assert len(vidx) == 31, len(vidx)
    S.barrier()
    for t in range(16):
        vblock(0, t * 128, 1, 128, xnT, BxnT)
    for i in range(4):
        for r in range(4):
            vblock(1, 512 * i + r, 4, 128, xnT, BxnT)
    for r in range(16):
        vblock(2, r, 16, 128, xnT, BxnT)

    if stop == 'B2v':
        return finalize()
    S.label = 'kvtok'
    def kv_tok(col0, n, g, dst_rows):
        i = kvst_i[0] % NKV
        kvst_i[0] += 1
        bk, Bk = nextbank()
        for half in range(2):
            for kc in range(8):
                mm(bk[0:n, half * 256:(half + 1) * 256], xnT[:, kc, col0:col0 + n], wqkv[:, kc, wcol(1 + half, g, 0):wcol(1 + half, g, 0) + 256], kc == 0, kc == 7, [BxnT, Bw], [Bk])
        tcopy("dve", kvst[i][0:n, :], bk[0:n, :], [Bk], [Bkvst[i]])
        dma("sp" if n == 128 else "pool", dst_rows, kvst[i][0:n, :], [Bkvst[i]], [], sem=("ko%d" if n == 128 else "kp%d") % i)
        out_bufs.append(Bkvst[i])

    for t in range(16):
        kv_tok(t * 128, 128, 2, kvo[2][t * 128:(t + 1) * 128, :])
    if stop == 'kv1':
        return finalize()
    for t in range(12, 16):
        kv_tok(t * 128, 128, 1, kvo[1][(t - 12) * 128:(t - 11) * 128, :])
    kv_tok(15 * 128, 128, 0, kvo[0])
    if stop == 'kv2':
        return finalize()
    for g in range(3):
        kv_tok(SM0 + 2, NS, g, kvs[g])

    if stop == 'B':
        return finalize()
    S.barrier()
    A.top = PW
    ACC0 = A.top
    acc_n = A.f32(2 * NCOL).rearrange("p (c n) -> p c n", c=2)
    acc_d = A.f32(2 * NCOL).rearrange("p (c n) -> p c n", c=2)
    acc_all = arena[:, ACC0:ACC0 + 4 * NCOL].rearrange("p (x c n) -> p x c n", x=2, c=2)
    NPT = 2
    PTb = [A.bf(1024).rearrange("p (h k q) -> p h k q", h=4, k=2) for _ in range(NPT)]
    BPT = [Buf("PT%d" % i) for i in range(NPT)]
    pt_i = [0]
    BOUT0 = A.top
    boutT = A.bf(2 * NCOL).rearrange("p (c n) -> p c n", c=2)
    BboutT = Buf("boutT")
    ATT_TOP = A.top
    A.top = BOUT0
    ckb = [[A.bf(512) for _ in range(3)] for _ in range(2)]
    Bckb = [[Buf("ckb%d%d" % (i, g)) for g in range(3)] for i in range(2)]
    KTc = [A.bf(6 * 128).rearrange("p (c n) -> p c n", c=6) for _ in range(2)]
    BKTc = [Buf("KTc%d" % i) for i in range(2)]
    PTs = [A.bf(16) for _ in range(2)]
    BPTs = [Buf("PTs%d" % i) for i in range(2)]
    prodf = A.f32(6 * 16).rearrange("p (c n) -> p c n", c=6)
    pself = A.f32(6 * 16).rearrange("p (c n) -> p c n", c=6)
    Bpr = Buf("prod")
    assert A.top <= NF, A.top
    acc_hist = {0: [], 1: [], 2: [], "x": [Buf("accx")], "s": []}
    mpair_main = cst_b[:, 1:3, :]
    mpair_edge = cst_b[:, 3:5, :]

    def acc_update(pair, Bn, Bd, nq, cols, first, key):
        if key == "x":
            rd_prev, wr = acc_hist["x"], acc_hist["x"]
        else:
            nb = Buf("acc%s" % str(key))
            rd_prev = [] if (key == "s" or key == 0) else acc_hist[key - 1]
            wr = [nb]
            acc_hist[key].append(nb)
        p4 = pair.rearrange("p (x h q) -> p x h q", x=2, h=4)
        for h2 in range(2):
            i_ap = p4[h2 * 64:(h2 + 1) * 64, :, h2::2, 0:nq]
            o_ap = acc_all[h2 * 64:(h2 + 1) * 64, :, :, cols]
            if first:
                dve(lambda e, i_ap=i_ap, o_ap=o_ap: e.tensor_copy(out=o_ap, in_=i_ap), [Bn, Bd] + rd_prev, wr)
            else:
                dve(lambda e, i_ap=i_ap, o_ap=o_ap: e.tensor_tensor(out=o_ap, in0=i_ap, in1=o_ap, op=ALU.add), [Bn, Bd] + rd_prev, wr)

    def band_p1(g, qsrc, Bq, qcols, nq, chunks, mpair):
        pi = pt_i[0] % NPT
        pt_i[0] += 1
        PT, Bp = PTb[pi], BPT[pi]
        b0, B0 = nextbank()
        b1, B1 = nextbank()
        sb = [b0, b1]
        SBf = [B0, B1]
        for h in range(4):
            c, h2 = h // 2, h % 2
            rows = slice(h2 * 64, (h2 + 1) * 64)
            for ci, (Kap, BK, vi, nk, mask) in enumerate(chunks):
                o = sb[h2][0:nk, (c * 2 + ci) * 128:(c * 2 + ci) * 128 + nq]
                mm(o, Kap[rows, c, :], qsrc[rows, g * 2 + c, qcols], True, True, [BK, Bq], [SBf[h2]])
        full = (nq == 128 and len(chunks) == 2 and all(ch[3] == 128 for ch in chunks))
        if full:
            for h2 in range(2):
                src = sb[h2].rearrange("p (h k q) -> p h k q", h=2, k=2)
                act(PT[:, h2::2, :, :], src, AF.Exp, [SBf[h2]], [Bp], scale=0.125)
            mb = mpair.unsqueeze(1).to_broadcast([128, 4, 2, 128])
            dve(lambda e, PT=PT, mb=mb: e.tensor_tensor(out=PT, in0=PT, in1=mb, op=ALU.mult), [Bp, Bc], [Bp])
        else:
            for ci, (Kap, BK, vi, nk, mask) in enumerate(chunks):
                for h2 in range(2):
                    src = sb[h2][0:nk, :].rearrange("p (h k q) -> p h k q", h=2, k=2)[:, :, ci, 0:nq]
                    act(PT[0:nk, h2::2, ci, 0:nq], src, AF.Exp, [SBf[h2]], [Bp], scale=0.125)
                if mask is not None:
                    for h in range(4):
                        dve(lambda e, h=h, ci=ci, nk=nk, mask=mask, PT=PT: e.tensor_tensor(out=PT[0:nk, h, ci, 0:nq], in0=PT[0:nk, h, ci, 0:nq], in1=mask[0:nk, 0:nq], op=ALU.mult), [Bp, Bc], [Bp])
        return PT, Bp

    def band_p2(PT, Bp, nq, chunks, acc_cols, first, key):
        nch = len(chunks)
        bn, Bn, bd, Bd, pair = nextpair()
        for h in range(4):
            c = h // 2
            for ci, (Kap, BK, vi, nk, mask) in enumerate(chunks):
                mm(bn[:, h * 128:h * 128 + nq], Vb[0:nk, vi, c * 128:(c + 1) * 128], PT[0:nk, h, ci, 0:nq], ci == 0, ci == nch - 1, [BV[vi], Bp], [Bn])
            for ci, (Kap, BK, vi, nk, mask) in enumerate(chunks):
                mm(bd[:, h * 128:h * 128 + nq], ones_b[0:nk, :], PT[0:nk, h, ci, 0:nq], ci == 0, ci == nch - 1, [Bc, Bp], [Bd])
        acc_update(pair, Bn, Bd, nq, acc_cols, first, key)

    def kslice(g, lt0, step, n):
        a = lt0 - KBASE[g]
        return KT[g][:, :, a:a + step * (n - 1) + 1:step]

    def samp_s0(b):
        sl = b % 2
        for g in range(3):
            L, d = WIN[g]
            dma("pool", ckb[sl][g], ck[g][b, 0:L:d, :], [], [Bckb[sl][g]], sem="ck%d%d" % (sl, g))

    def samp_s1(b):
        sl = b % 2
        bk, Bk = nextbank()
        pt = bk.bitcast(BF16).rearrange("p (c t) -> p c t", c=8)
        for g in range(3):
            for c in range(2):
                S.op("pe", lambda e, c=c, g=g, pt=pt, sl=sl: e.transpose(out=pt[:, g * 2 + c, :], in_=ckb[sl][g][:, c * 128:(c + 1) * 128], identity=ident_b), reads=[Bckb[sl][g], Bc], writes=[Bk])
        tcopy("dve", KTc[sl], pt[:, 0:6, :], [Bk], [BKTc[sl]])

    def samp_s2(b):
        sl = b % 2
        col = SM0 + 2 + b
        bs0, BS0 = nextbank()
        bs1, BS1 = nextbank()
        bsx = [bs0, bs1]
        BSx = [BS0, BS1]
        for g in range(3):
            for h in range(4):
                c, h2 = h // 2, h % 2
                rows = slice(h2 * 64, (h2 + 1) * 64)
                mm(bsx[h2][:, g * 2 + c:g * 2 + c + 1], KTc[sl][rows, g * 2 + c, :], QT[rows, g * 2 + c, col:col + 1], True, True, [BKTc[sl], BQT], [BSx[h2]])
        PTs3 = PTs[sl][:, 0:12].rearrange("p (g c t) -> p g c t", g=3, c=2)
        for h2 in range(2):
            act(PTs3[:, :, :, h2], bsx[h2][:, 0:6].rearrange("p (g c) -> p g c", g=3), AF.Exp, [BSx[h2]], [BPTs[sl]], scale=0.125)

    def samp_s3(b):
        sl = b % 2
        col = SM0 + 2 + b
        bn, Bn, bd, Bd, pair = nextpair()
        for h in range(4):
            c = h // 2
            for g in range(3):
                mm(bn[:, h * 128:h * 128 + 1], ckb[sl][g][:, 256 + c * 128:256 + (c + 1) * 128], PTs[sl][:, g * 4 + h:g * 4 + h + 1], g == 0, g == 2, [Bckb[sl][g], BPTs[sl]], [Bn])
            for g in range(3):
                mm(bd[:, h * 128:h * 128 + 1], ones_b, PTs[sl][:, g * 4 + h:g * 4 + h + 1], g == 0, g == 2, [Bc, BPTs[sl]], [Bd])
        acc_update(pair, Bn, Bd, 1, slice(col, col + 1), True, "s")

    S.label = 'att-main'
    mblocks = []
    for g in range(3):
        step = WIN[g][1]
        if g == 0:
            starts = [t * 128 for t in range(16)]
        elif g == 1:
            starts = [512 * i + r for i in range(4) for r in range(4)]
        else:
            starts = list(range(16))
        for m0 in starts:
            lt_q = HX + m0
            lt_p = lt_q - 128 * step
            if lt_p < HX:
                vp = vidx[(g, lt_p, step, "h")]
                mp = mpair_edge
            else:
                vp = vidx[(g, lt_p - HX, step, "m")]
                mp = mpair_main
            vc = vidx[(g, m0, step, "m")]
            chunks = [(kslice(g, lt_p, step, 128), BKT[g], vp, 128, None),
                      (kslice(g, lt_q, step, 128), BKT[g], vc, 128, None)]
            qc = slice(m0, m0 + step * 127 + 1, step)
            mblocks.append((g, qc, chunks, mp))
    samp_s0(0)
    pend = band_p1(mblocks[0][0], QT, BQT, mblocks[0][1], 128, mblocks[0][2], mblocks[0][3])
    for m, (g, qc, chunks, mp) in enumerate(mblocks):
        nxt = None
        if m + 1 < len(mblocks):
            g2_, qc2, ch2, mp2 = mblocks[m + 1]
            nxt = band_p1(g2_, QT, BQT, qc2, 128, ch2, mp2)
        band_p2(pend[0], pend[1], 128, chunks, qc, g == 0, g)
        pend = nxt
        b, k = m // 3, m % 3
        if k == 0:
            samp_s1(b)
            if b + 1 < NS:
                samp_s0(b + 1)
        elif k == 1:
            samp_s2(b)
        else:
            samp_s3(b)
    if stop == 'att-main':
        return finalize()
    S.label = 'att-ext2'
    vp = vidx[(0, 2046, 1, "h")]
    vc = vidx[(0, 2174, 1, "h")]
    chx = [(kslice(0, 2046, 1, 128), BKT[0], vp, 128, mprev_b), (kslice(0, 2174, 1, 2), BKT[0], vc, 2, mcur_b)]
    p_ = band_p1(0, QT, BQT, slice(SM0, SM0 + 2), 2, chx, None)
    band_p2(p_[0], p_[1], 2, chx, slice(SM0, SM0 + 2), True, "x")
    for g in (1, 2):
        step = WIN[g][1]
        for j in range(2):
            ltq = 2174 + j
            vp = vidx[(g, ltq - 128 * step, step, "h")]
            vc = vidx[(g, ltq, 1, "h")]
            chx = [(kslice(g, ltq - 128 * step, step, 128), BKT[g], vp, 128, None), (kslice(g, ltq, 1, 1), BKT[g], vc, 1, None)]
            p_ = band_p1(g, QT, BQT, slice(SM0 + j, SM0 + j + 1), 1, chx, None)
            band_p2(p_[0], p_[1], 1, chx, slice(SM0 + j, SM0 + j + 1), False, "x")
    Bacc = Buf("accall")
    S.op("dve", lambda e: e.memset(prodf[:, 0, 0:1], 0.0), reads=[b_ for k_ in acc_hist for b_ in acc_hist[k_]], writes=[Bacc, Bpr])
    S.label = 'att-self'
    dve(lambda e: e.tensor_tensor(out=prodf, in0=QT[:, :, SM0 + 2:SM0 + 18], in1=KTs[:, :, 2:18], op=ALU.mult), [BQT, BKTs], [Bpr])
    bk, Bk = nextbank()
    mm(bk[:, 0:96], blockones_f, prodf.rearrange("p c n -> p (c n)"), True, True, [Bc, Bpr], [Bk])
    act(pself.rearrange("p c n -> p (c n)"), bk[:, 0:96], AF.Exp, [Bk], [Bpr], scale=0.125)
    dve(lambda e: e.tensor_tensor(out=prodf, in0=pself, in1=VTs[:, :, 2:18], op=ALU.mult), [Bpr, BKTs], [Bpr])
    for g in range(3):
        for c in range(2):
            dve(lambda e, g=g, c=c: e.tensor_tensor(out=acc_n[:, c, SM0 + 2:SM0 + 18], in0=acc_n[:, c, SM0 + 2:SM0 + 18], in1=prodf[:, g * 2 + c, :], op=ALU.add), [Bacc, Bpr], [Bacc])
            dve(lambda e, g=g, c=c: e.tensor_tensor(out=acc_d[:, c, SM0 + 2:SM0 + 18], in0=acc_d[:, c, SM0 + 2:SM0 + 18], in1=pself[:, g * 2 + c, :], op=ALU.add), [Bacc, Bpr], [Bacc])
    if stop == 'att-self':
        return finalize()
    S.barrier()
    A.top = P1
    boutT2 = A.bf(2 * NCOL).rearrange("p (c n) -> p c n", c=2)
    assert A.top <= PW
    aoutT = A.bf(4 * NCOL).rearrange("p (c n) -> p c n", c=4)
    BaoutT = Buf("aoutT")
    C_TOP = A.top
    wuv = A.bf(8 * 1024).rearrange("p (c n) -> p c n", c=8)
    Bwuv = Buf("wuv")
    wv_in = w_in.rearrange("(c p) n -> p c n", p=128)
    dma("pool", wuv, wv_in[:, :, 2304:3328], [], [Bwuv], sem="w1")
    S.label = 'att-norm'
    for c in range(2):
        for (c0, n) in slices:
            dve(lambda e, c=c, c0=c0, n=n: e.reciprocal(out=acc_d[:, c, c0:c0 + n], in_=acc_d[:, c, c0:c0 + n]), [Bacc], [Bacc])
            dve(lambda e, c=c, c0=c0, n=n: e.tensor_tensor(out=boutT2[:, c, c0:c0 + n], in0=acc_n[:, c, c0:c0 + n], in1=acc_d[:, c, c0:c0 + n], op=ALU.mult), [Bacc], [BboutT])
    if stop == 'att':
        return finalize()
    S.label = 'C1'
    MT0 = NF - 4 * NCOL
    W2A = MT0 - (8192 + 2048 + 1024)
    wg = arena[:, W2A:W2A + 8192].bitcast(BF16).rearrange("p (c n) -> p c n", c=8)
    wpa = arena[:, W2A + 8192:W2A + 10240].bitcast(BF16).rearrange("p (c n) -> p c n", c=4)
    wpb = arena[:, W2A + 10240:W2A + 11264].bitcast(BF16).rearrange("p (c n) -> p c n", c=2)
    Bwg = Buf("wg")
    dma("pool", wg, wv_in[:, :, 3328:5376], [], [Bwg, Bacc], sem="w2")
    dma("pool", wpa, w_pa.rearrange("(c p) n -> p c n", p=128), [], [Bwg, Bacc], sem="w3")
    dma("pool", wpb, w_pb.rearrange("(c p) n -> p c n", p=128), [], [Bwg, Bacc], sem="w4")
    uT = A.bf(4 * NCOL).rearrange("p (c n) -> p c n", c=4)
    BuT = Buf("uT")
    NG = 3
    gv = [A.f32(512) for _ in range(NG)]
    Bgv = [Buf("gv%d" % i) for i in range(NG)]
    vn = [A.f32(512) for _ in range(NG)]
    Bvn = [Buf("vn%d" % i) for i in range(NG)]
    vnb = [A.bf(512) for _ in range(NG)]
    Bvnb = [Buf("vnb%d" % i) for i in range(NG)]
    uxe = A.bf(4 * 128).rearrange("p (c n) -> p c n", c=4)
    aoe = A.bf(4 * 128).rearrange("p (c n) -> p c n", c=4)
    Buxe = Buf("uxe")
    C1_TOP = A.top
    for (c0, n) in slices:
        for c in range(4):
            proj_fm(uT[:, c, c0:c0 + n], BuT, wuv, Bwuv, c * 128, xnT, BxnT, c0, n, func=AF.Gelu_apprx_tanh)
    for c in range(4):
        proj_fm(uxe[:, c, :], Buxe, wuv, Bwuv, c * 128, xeT, BxeT, 0, 128, func=AF.Gelu_apprx_tanh)
    gi = [0]

    def gmlp_batch(items):
        G = len(gv)

        def stage1(grp, par):
            bks_ = []
            for k, (src, Bsrc, c0, n, sample, u_ap, Bu, out_ap, Bout, vn_dst) in enumerate(grp):
                bk, Bk = pbank[3 * par + k], PB[3 * par + k]
                bks_.append((bk, Bk))
                for kc in range(8):
                    mm(bk[0:n, :], src[:, kc, c0:c0 + n], wuv[:, kc, 512:1024], kc == 0, kc == 7, [Bsrc, Bwuv], [Bk])
            return bks_

        groups = [items[g0:g0 + G] for g0 in range(0, len(items), G)]
        banks_next = stage1(groups[0], 0)
        for gi_, grp in enumerate(groups):
            banks = banks_next
            if gi_ + 1 < len(groups):
                banks_next = stage1(groups[gi_ + 1], (gi_ + 1) % 2)
            stis = []
            for k, (src, Bsrc, c0, n, sample, u_ap, Bu, out_ap, Bout, vn_dst) in enumerate(grp):
                sti = stat_i[0] % 8
                stat_i[0] += 1
                stis.append(sti)
            for k, (src, Bsrc, c0, n, sample, u_ap, Bu, out_ap, Bout, vn_dst) in enumerate(grp):
                s_ = stat[:n, stis[k], :]
                act(gv[k][0:n, :], banks[k][0][0:n, :], AF.Gelu_apprx_tanh, [banks[k][1]], [Bgv[k], Bstat[stis[k]]], accum_out=s_[:, 4:5])
            for k, (src, Bsrc, c0, n, sample, u_ap, Bu, out_ap, Bout, vn_dst) in enumerate(grp):
                s_ = stat[:n, stis[k], :]
                dve(lambda e, s_=s_: e.tensor_scalar(out=s_[:, 5:6], in0=s_[:, 4:5], scalar1=-1.0 / 512, scalar2=None, op0=ALU.mult), [Bstat[stis[k]]], [Bstat[stis[k]]])
            for k, (src, Bsrc, c0, n, sample, u_ap, Bu, out_ap, Bout, vn_dst) in enumerate(grp):
                s_ = stat[:n, stis[k], :]
                act(vn[k][0:n, :], gv[k][0:n, :], AF.Identity, [Bgv[k], Bstat[stis[k]]], [Bvn[k]], bias=s_[:, 5:6], scale=1.0)
            for k, (src, Bsrc, c0, n, sample, u_ap, Bu, out_ap, Bout, vn_dst) in enumerate(grp):
                s_ = stat[:n, stis[k], :]
                act(gv[k][0:n, :], vn[k][0:n, :], AF.Square, [Bvn[k]], [Bgv[k], Bstat[stis[k]]], accum_out=s_[:, 0:1])
            for k, (src, Bsrc, c0, n, sample, u_ap, Bu, out_ap, Bout, vn_dst) in enumerate(grp):
                s_ = stat[:n, stis[k], :]
                dve(lambda e, s_=s_: e.tensor_scalar(out=s_[:, 1:2], in0=s_[:, 0:1], scalar1=1.0 / 512, scalar2=EPS, op0=ALU.mult, op1=ALU.add), [Bstat[stis[k]]], [Bstat[stis[k]]])
            for k, (src, Bsrc, c0, n, sample, u_ap, Bu, out_ap, Bout, vn_dst) in enumerate(grp):
                s_ = stat[:n, stis[k], :]
                act(s_[:, 2:3], s_[:, 1:2], AF.Ln, [Bstat[stis[k]]], [Bstat[stis[k]]])
            for k, (src, Bsrc, c0, n, sample, u_ap, Bu, out_ap, Bout, vn_dst) in enumerate(grp):
                s_ = stat[:n, stis[k], :]
                act(s_[:, 3:4], s_[:, 2:3], AF.Exp, [Bstat[stis[k]]], [Bstat[stis[k]]], scale=-0.5)
            for k, (src, Bsrc, c0, n, sample, u_ap, Bu, out_ap, Bout, vn_dst) in enumerate(grp):
                s_ = stat[:n, stis[k], :]
                dve(lambda e, k=k, n=n, s_=s_: e.scalar_tensor_tensor(out=vn[k][0:n, :], in0=vn[k][0:n, :], scalar=s_[:, 3:4], in1=lng_bc[0:n, :], op0=ALU.mult, op1=ALU.mult), [Bvn[k], Bstat[stis[k]], Bc], [Bvn[k]])
            for k, (src, Bsrc, c0, n, sample, u_ap, Bu, out_ap, Bout, vn_dst) in enumerate(grp):
                dve(lambda e, k=k, n=n: e.tensor_tensor(out=vn[k][0:n, :], in0=vn[k][0:n, :], in1=lnb_bc[0:n, :], op=ALU.add), [Bvn[k], Bc], [Bvn[k]])
            for k, (src, Bsrc, c0, n, sample, u_ap, Bu, out_ap, Bout, vn_dst) in enumerate(grp):
                tcopy("dve", vnb[k][0:n, :], vn[k][0:n, :], [Bvn[k]], [Bvnb[k]])
                if vn_dst is not None:
                    dma("sp" if n == 128 else "pool", vn_dst, vn[k][0:n, :], [Bvn[k]], [], sem=("vo%d" if n == 128 else "vp%d") % k)
                    out_bufs.append(Bvn[k])
            mbanks = []
            for k, (src, Bsrc, c0, n, sample, u_ap, Bu, out_ap, Bout, vn_dst) in enumerate(grp):
                mb_i = (6, 7, 3 * (gi_ % 2))[k]
                bm, Bm = pbank[mb_i], PB[mb_i]
                mbanks.append((bm, Bm))
                nt = 16 if sample else 128
                for g in range(4):
                    o = bm[:, g * 128:g * 128 + nt]
                    if sample:
                        mm(o, vnb[k][0:n, g * 128:(g + 1) * 128], D16[0:16, g, :], True, False, [Bvnb[k], Bc], [Bm])
                        mm(o, ones_b[0:1, :], bsp0_b[0:1, g * 16:(g + 1) * 16], False, True, [Bc], [Bm])
                    else:
                        mm(o, vnb[k][0:n, g * 128:(g + 1) * 128], WmT[:, g, :], True, False, [Bvnb[k], Bc], [Bm])
                        mm(o, ones_b[0:1, :], bsp_b[0:1, g * 128:(g + 1) * 128], False, True, [Bc], [Bm])
            for k, (src, Bsrc, c0, n, sample, u_ap, Bu, out_ap, Bout, vn_dst) in enumerate(grp):
                nt = 16 if sample else 128
                m4 = mbanks[k][0].rearrange("p (g t) -> p g t", g=4)[:, :, 0:nt]
                dve(lambda e, m4=m4, out_ap=out_ap, u_ap=u_ap: e.tensor_tensor(out=out_ap, in0=m4, in1=u_ap, op=ALU.mult), [mbanks[k][1], Bu], [Bout])

    gitems = []
    for t in range(16):
        cs = slice(t * 128, (t + 1) * 128)
        gitems.append((xnT, BxnT, t * 128, 128, False, uT[:, :, cs], BuT, aoutT[:, :, cs], BaoutT, vch if t == 15 else None))
    gitems.append((xeT, BxeT, 0, 128, False, uxe, Buxe, aoe, Buxe, None))
    gitems.append((xnT, BxnT, SM0 + 2, NS, True, uT[:, :, SM0 + 2:SM0 + 18], BuT, aoutT[:, :, SM0 + 2:SM0 + 18], BaoutT, vchs))
    gmlp_batch(gitems)
    tcopy("dve", aoutT[:, :, SM0:SM0 + 2], aoe[:, :, 126:128], [Buxe], [BaoutT])
    if stop == 'C1':
        return finalize()
    S.label = 'C2a'
    S.barrier()
    A.top = C_TOP
    assert C1_TOP <= W2A, (C1_TOP, W2A)
    tg = [A.f32(512) for _ in range(2)]
    Btg = [Buf("tg0"), Buf("tg1")]
    t1 = [A.f32(512) for _ in range(2)]
    Bt1 = [Buf("t10"), Buf("t11")]
    mt = arena[:, MT0:NF].bitcast(BF16).rearrange("p (c n) -> p c n", c=8)
    Bmt = Buf("mT")
    oc_i = [0]
    for si, (c0, n) in enumerate(slices):
        for oc in range(8):
            i = oc_i[0] % 2
            oc_i[0] += 1
            ba, Ba = nextbank()
            for kc in range(4):
                mm(ba[:, 0:n], wpa[:, kc, oc * 128:(oc + 1) * 128], aoutT[:, kc, c0:c0 + n], kc == 0, kc == 3, [Bwg, BaoutT], [Ba])
            bb, Bb = nextbank()
            for kc in range(2):
                mm(bb[:, 0:n], wpb[:, kc, oc * 128:(oc + 1) * 128], boutT2[:, kc, c0:c0 + n], kc == 0, kc == 1, [Bwg, BboutT], [Bb])
            proj_fm(tg[i][:, 0:n], Btg[i], wg, Bwg, oc * 128, xnT, BxnT, c0, n, func=AF.Tanh, scale=0.5)
            dve(lambda e, i=i, ba=ba, n=n: e.scalar_tensor_tensor(out=t1[i][:, 0:n], in0=tg[i][:, 0:n], scalar=1.0, in1=ba[:, 0:n], op0=ALU.add, op1=ALU.mult), [Btg[i], Ba], [Bt1[i]])
            proj_fm(tg[i][:, 0:n], Btg[i], wg, Bwg, 1024 + oc * 128, xnT, BxnT, c0, n, func=AF.Tanh, scale=0.5)
            dve(lambda e, i=i, bb=bb, n=n: e.scalar_tensor_tensor(out=tg[i][:, 0:n], in0=tg[i][:, 0:n], scalar=1.0, in1=bb[:, 0:n], op0=ALU.add, op1=ALU.mult), [Btg[i], Bb], [Btg[i]])
            dve(lambda e, i=i, n=n, oc=oc, c0=c0: e.tensor_tensor(out=mt[:, oc, c0:c0 + n], in0=t1[i][:, 0:n], in1=tg[i][:, 0:n], op=ALU.add), [Bt1[i], Btg[i]], [Bmt])

    if stop == 'C2a':
        return finalize()
    S.label = 'C2b'
    S.barrier()
    A.top = CONST_TOP
    hnT = A.bf(8 * NCOL).rearrange("p (c n) -> p c n", c=8)
    BhnT = Buf("hnT")
    hbuf = A.f32(16 * 1024).rearrange("p (t n) -> p t n", t=16)
    hsm = A.f32(1024)
    Bh = [Buf("h%d" % t) for t in range(16)]
    Bhsm = Buf("hsm")
    HB_TOP = A.top
    wo = A.bf(8 * 1024).rearrange("p (c n) -> p c n", c=8)
    Bwo = Buf("wo")
    dma("pool", wo, w_out.rearrange("(c p) n -> p c n", p=128), [], [Bwo], sem="w5")
    NX = 3
    xt = [A.f32(1024) for _ in range(NX)]
    Bxt = [Buf("xt%db" % i) for i in range(NX)]
    xb = [A.bf(1024) for _ in range(NX)]
    Bxb = [Buf("xb%db" % i) for i in range(NX)]
    assert A.top <= MT0, (A.top, MT0)

    def mk_pre(t, c0o, m):
        def pre(k):
            if t >= 0:
                hdst, Bhd = hbuf[:, t, :], Bh[t]
            else:
                hdst, Bhd = hsm, Bhsm
            for half in range(2):
                bk, Bk = nextbank()
                for kc in range(8):
                    mm(bk[0:m, :], mt[:, kc, c0o:c0o + m], wo[:, kc, half * 512:(half + 1) * 512], kc == 0, kc == 7, [Bmt, Bwo], [Bk])
                dve(lambda e, bk=bk, half=half, k=k, hdst=hdst: e.scalar_tensor_tensor(out=hdst[0:m, half * 512:(half + 1) * 512], in0=bk[0:m, :], scalar=0.5, in1=xt[k][0:m, half * 512:(half + 1) * 512], op0=ALU.mult, op1=ALU.add), [Bk, Bxt[k]], [Bhd])
            return (hdst, Bhd)
        return pre

    HS0 = 42404
    assert A.top <= HS0 and HS0 + 1408 + 1024 <= MT0, (A.top, MT0)
    hsT = arena[:, HS0:HS0 + 1408].rearrange("p (c k n) -> p c k n", c=44, k=2)
    BhsT = Buf("hsT")
    scst2 = [arena[:, HS0 + 1408 + i * 512:HS0 + 1408 + (i + 1) * 512].rearrange("p (k n) -> p k n", k=2) for i in range(2)]
    Bscst2 = [Buf("scst0"), Buf("scst1")]
    def sconv_piece(q):
        sc_, Bsc_ = scst2[q % 2], Bscst2[q % 2]
        dma("sp", sc_[0:16, :, :], sconv[:, :, q * 256:(q + 1) * 256], [], [Bsc_], sem="sc%d" % (q % 2))
        for cc in range(2):
            ch = q * 2 + cc
            bk, Bk = nextbank()
            for k in range(2):
                mm(bk[:, k * 16:(k + 1) * 16], sc_[0:16, k, cc * 128:(cc + 1) * 128], ident_f[0:16, 0:16], True, True, [Bsc_, Bc], [Bk])
            tcopy("dve", hsT[:, ch, :, :], bk[:, 0:32].rearrange("p (k n) -> p k n", k=2), [Bk], [BhsT])
        dma("pool", convs[:, 0, q * 256:(q + 1) * 256], sc_[0:16, 1, :], [Bsc_], [], sem="sco%d" % (q % 2))
        out_bufs.append(Bsc_)


    sc_done = [0]

    def sc_hook(gi_):
        for _ in range(4):
            if sc_done[0] < 22:
                sconv_piece(sc_done[0])
                sc_done[0] += 1

    citems = [(xall[HX + t * 128:HX + (t + 1) * 128, :], 128, hnT[:, :, t * 128:(t + 1) * 128], BhnT, gf_bc, mk_pre(t, t * 128, 128)) for t in range(16)]
    norm_T_batch(citems, xt, Bxt, xb, Bxb, "xc", group_hook=sc_hook)
    dma("sp", xt[0][0:2, :], xall[HX - 2:HX, :], [], [Bxt[0]], sem="xc0")
    dma("sp", xt[0][2:18, :], xs, [], [Bxt[0]], sem="xq0")
    norm_T_batch([(None, 18, hnT[:, :, SM0:SM0 + 18], BhnT, gf_bc, mk_pre(-1, SM0, 18))], xt, Bxt, xb, Bxb, "xc")
    if stop == 'C2b':
        return finalize()
    for q in range(sc_done[0], 22):
        sconv_piece(q)

    S.label = 'D'
    S.barrier()
    A.top = HB_TOP
    HT = 1024
    KG = [(0, 4), (4, 4), (8, 4), (12, 4), (16, 3), (19, 3)]
    cbuf = [[A.f32(1024) for _ in range(2)] for _ in range(3)]
    Bcb = [[Buf("c%d%d" % (a_, b_)) for b_ in range(2)] for a_ in range(3)]
    upb = [[A.f32(2 + HT) for _ in range(2)] for _ in range(2)]
    Bupb = [[Buf("up%d%d" % (a_, b_)) for b_ in range(2)] for a_ in range(2)]
    prodS = A.bf(22 * 18).rearrange("p (j n) -> p j n", j=22)
    BprodS = Buf("prodS")
    ups = [[A.f32(18) for _ in range(2)] for _ in range(3)]
    cs_ = [[A.f32(18) for _ in range(2)] for _ in range(3)]
    Bcs = [[Buf("cs%d%d" % (a_, b_)) for b_ in range(2)] for a_ in range(3)]
    hist = A.f32(44 * 2).rearrange("p (c n) -> p c n", c=44)
    Bhist = Buf("hist")
    upst = [A.f32(256) for _ in range(2)]
    Bupst = [Buf("upst0"), Buf("upst1")]
    assert A.top <= HS0, A.top
    A.top = HS0 + 1408
    prodT = A.bf(4 * HT).rearrange("p (j n) -> p j n", j=4)
    BprodT = [Buf("prodT%d" % i) for i in range(6)]
    NWU = 3
    wup = [A.bf(8 * 256).rearrange("p (c n) -> p c n", c=8) for _ in range(NWU)]
    Bwup = [Buf("wup%d" % i) for i in range(NWU)]
    wdns = [A.bf(4 * 1024).rearrange("p (j n) -> p j n", j=4) for _ in range(2)]
    Bwdns = [Buf("wdn0"), Buf("wdn1")]
    wdi = [0]
    wdq = []

    def load_wdn(j0, gs):
        i = wdi[0] % 2
        wdi[0] += 1
        dma("pool", wdns[i][:, 0:gs, :], w_dn.rearrange("(j p) n -> p j n", p=128)[:, j0:j0 + gs, :], [], [Bwdns[i]], sem="wd%d" % i)
        wdq.append((wdns[i], Bwdns[i]))
    assert A.top <= NF, A.top

    wv = w_up.rearrange("(c p) n -> p c n", p=128)
    wslot = {}
    wi = [0]

    def load_wup(j):
        wsl = wi[0] % NWU
        wi[0] += 1
        w_, Bw_ = wup[wsl], Bwup[wsl]
        dma("pool", w_[:, :, 0:128], wv[:, :, j * 128:(j + 1) * 128], [], [Bw_], sem="wu%da" % wsl)
        dma("pool", w_[:, :, 128:256], wv[:, :, 2816 + j * 128:2816 + (j + 1) * 128], [], [Bw_], sem="wu%db" % wsl)
        return w_, Bw_

    def stageA(H, j):
        base = H * HT
        w_, Bw_ = wslot[(H, j)]
        sl = j % 2
        s3 = j % 3
        for gv_ in range(2):
            ch = gv_ * 22 + j
            ub, Bub = upb[sl][gv_], Bupb[sl][gv_]
            if H == 0:
                if gv_ == 0:
                    bks_, Bks_ = nextbank()
                so = gv_ * 32
                for kc in range(8):
                    mm(bks_[:, so:so + 18], w_[:, kc, gv_ * 128:(gv_ + 1) * 128], hnT[:, kc, SM0:SM0 + 18], kc == 0, kc == 7, [Bw_, BhnT], [Bks_])
                if gv_ == 1:
                    for kc in range(8):
                        mm(bks_[0:20, 64:320], hnT[:, kc, NM - 2:NM + 18], w_[:, kc, :], kc == 0, kc == 7, [BhnT, Bw_], [Bks_])
                    for g2_ in range(2):
                        ub2, Bub2 = upb[sl][g2_], Bupb[sl][g2_]
                        ch2 = g2_ * 22 + j
                        so2 = g2_ * 32
                        act(ub2[:, 0:2], bks_[:, so2:so2 + 2], AF.Copy, [Bks_, Bc], [Bub2], scale=flag[:, 0:1])
                        act(ups[s3][g2_], bks_[:, so2:so2 + 18], AF.Copy, [Bks_], [Bcs[s3][g2_]])
                        cs = cs_[s3][g2_]
                        act(cs, ups[s3][g2_], AF.Identity, [Bcs[s3][g2_], Bc], [Bcs[s3][g2_]], scale=cwT[:, 2, ch2:ch2 + 1], bias=cwT[:, 3, ch2:ch2 + 1])
                        for k in range(2):
                            dve(lambda e, cs=cs, ch2=ch2, k=k: e.scalar_tensor_tensor(out=cs[:, 2:18], in0=hsT[:, ch2, k, :], scalar=cwT[:, k, ch2:ch2 + 1], in1=cs[:, 2:18], op0=ALU.mult, op1=ALU.add), [BhsT, Bc, Bcs[s3][g2_]], [Bcs[s3][g2_]])
                    ui = j % 2
                    tcopy("dve", upst[ui][0:20, :], bks_[0:20, 64:320], [Bks_], [Bupst[ui]])
                    for g2_ in range(2):
                        cc0 = g2_ * 2816 + j * 128
                        dma("pool", convp[:, cc0:cc0 + 128], upst[ui][0:2, g2_ * 128:(g2_ + 1) * 128], [Bupst[ui]], [], sem="uo%d" % ui)
                        dma("pool", convs[:, 1, cc0:cc0 + 128], upst[ui][4:20, g2_ * 128:(g2_ + 1) * 128], [Bupst[ui]], [], sem="uo%d" % ui)
                    out_bufs.append(Bupst[ui])
            else:
                tcopy("dve", ub[:, 0:2], hist[:, ch, :], [Bhist], [Bub])
            for s2 in range(2):
                c0 = base + s2 * 512
                bk, Bk = nextbank()
                for kc in range(8):
                    mm(bk, w_[:, kc, gv_ * 128:(gv_ + 1) * 128], hnT[:, kc, c0:c0 + 512], kc == 0, kc == 7, [Bw_, BhnT], [Bk])
                act(ub[:, 2 + s2 * 512:2 + (s2 + 1) * 512], bk, AF.Copy, [Bk], [Bub])
            if H == 0:
                tcopy("dve", hist[:, ch, :], ub[:, HT:HT + 2], [Bub], [Bhist])
        for gv_ in range(2):
            ch = gv_ * 22 + j
            ub, Bub = upb[sl][gv_], Bupb[sl][gv_]
            cc, Bcc = cbuf[s3][gv_], Bcb[s3][gv_]
            act(cc, ub[:, 2:2 + HT], AF.Identity, [Bub, Bc], [Bcc], scale=cwT[:, 2, ch:ch + 1], bias=cwT[:, 3, ch:ch + 1])
        for gv_ in range(2):
            ch = gv_ * 22 + j
            ub, Bub = upb[sl][gv_], Bupb[sl][gv_]
            cc, Bcc = cbuf[s3][gv_], Bcb[s3][gv_]
            for k in range(2):
                dve(lambda e, cc=cc, ub=ub, k=k, ch=ch: e.scalar_tensor_tensor(out=cc, in0=ub[:, k:k + HT], scalar=cwT[:, k, ch:ch + 1], in1=cc, op0=ALU.mult, op1=ALU.add), [Bub, Bc, Bcc], [Bcc])

    def stageB(H, j, jj):
        s3 = j % 3
        cg, cv = cbuf[s3][0], cbuf[s3][1]
        act(cg, cg, AF.Gelu_apprx_tanh, [Bcb[s3][0]], [Bcb[s3][0]])
        dve(lambda e, cg=cg, cv=cv, jj=jj: e.tensor_tensor(out=prodT[:, jj, :], in0=cg, in1=cv, op=ALU.mult), [Bcb[s3][0], Bcb[s3][1]], [BprodT[jj]])
        if H == 0:
            act(cs_[s3][0], cs_[s3][0], AF.Gelu_apprx_tanh, [Bcs[s3][0]], [Bcs[s3][0]])
            dve(lambda e, s3=s3, j=j: e.tensor_tensor(out=prodS[:, j, :], in0=cs_[s3][0], in1=cs_[s3][1], op=ALU.mult), [Bcs[s3][0], Bcs[s3][1]], [BprodS])

    def wdown_group(H, j0, gs):
        wdn, Bwdn = wdq.pop(0)
        for tb in range(4):
            ths = [(tb * 2 + q_, half) for q_ in range(2) for half in range(2)]
            bks = [nextbank() for _ in ths]
            for (tl, half), (bk, Bk) in zip(ths, bks):
                for jj in range(gs - 1):
                    mm(bk, prodT[:, jj, tl * 128:(tl + 1) * 128], wdn[:, jj, half * 512:(half + 1) * 512], jj == 0, False, [BprodT[jj], Bwdn], [Bk])
            for (tl, half), (bk, Bk) in zip(ths, bks):
                jj = gs - 1
                mm(bk, prodT[:, jj, tl * 128:(tl + 1) * 128], wdn[:, jj, half * 512:(half + 1) * 512], False, True, [BprodT[jj], Bwdn], [Bk])
            for (tl, half), (bk, Bk) in zip(ths, bks):
                t = H * 8 + tl
                dve(lambda e, bk=bk, t=t, half=half: e.tensor_tensor(out=hbuf[:, t, half * 512:(half + 1) * 512], in0=bk, in1=hbuf[:, t, half * 512:(half + 1) * 512], op=ALU.add), [Bk, Bh[t]], [Bh[t]])
        if H == 0:
            for half in range(2):
                bk, Bk = nextbank()
                for jj in range(gs):
                    mm(bk[0:18, :], prodS[:, j0 + jj, :], wdn[:, jj, half * 512:(half + 1) * 512], jj == 0, jj == gs - 1, [BprodS, Bwdn], [Bk])
                dve(lambda e, bk=bk, half=half: e.tensor_tensor(out=hsm[0:18, half * 512:(half + 1) * 512], in0=bk[0:18, :], in1=hsm[0:18, half * 512:(half + 1) * 512], op=ALU.add), [Bk, Bhsm], [Bhsm])

    def final_rows(haps, dsts):
        stis = []
        for k, (hap, m, Bh_) in enumerate(haps):
            sti = stat_i[0] % 8
            stat_i[0] += 1
            stis.append(sti)
            scr = cbuf[k % 3][(k // 3) % 2]
            Bscr = Bcb[k % 3][(k // 3) % 2]
            act(scr.bitcast(BF16)[0:m, 0:1024], hap, AF.Square, [Bh_], [Bscr, Bstat[sti]], accum_out=stat[:m, sti, 0:1])
        rs = []
        for k, (hap, m, Bh_) in enumerate(haps):
            rs.append(rstd_of(stat[:m, stis[k], 0:1], m, 1.0 / 1024, stis[k], Bstat[stis[k]]))
        for k, (hap, m, Bh_) in enumerate(haps):
            dve(lambda e, hap=hap, r=rs[k], m=m: e.scalar_tensor_tensor(out=hap, in0=hap, scalar=r, in1=gfin_bc[0:m, :], op0=ALU.mult, op1=ALU.mult), [Bh_, Bstat[stis[k]], Bc], [Bh_])
        for k, (hap, m, Bh_) in enumerate(haps):
            dst, r0 = dsts[k]
            dma("sp" if m == 128 else "pool", dst, hap[r0:m, :], [Bh_], [], sem=("yo%d" if m == 128 else "yp%d") % (k % 4))
            out_bufs.append(Bh_)

    for H in range(2):
        order = [(j0, gs, jj) for (j0, gs) in KG for jj in range(gs)]
        PRE = 3
        jof = lambda q_: order[q_][0] + order[q_][2]
        for q_ in range(min(PRE, len(order))):
            wslot[(H, jof(q_))] = load_wup(jof(q_))
        doneA = set()

        def doA(q_):
            if q_ < len(order) and q_ not in doneA:
                doneA.add(q_)
                stageA(H, jof(q_))

        load_wdn(*KG[0])
        gnext = [1]
        doA(0)
        doA(1)
        for idx, (j0, gs, jj) in enumerate(order):
            j = j0 + jj
            if idx + PRE < len(order):
                wslot[(H, jof(idx + PRE))] = load_wup(jof(idx + PRE))
            if jj == gs - 1:
                stageB(H, j, jj)
                doA(idx + 2)
                doA(idx + 3)
                if gnext[0] < len(KG):
                    load_wdn(*KG[gnext[0]])
                    gnext[0] += 1
                wdown_group(H, j0, gs)
            else:
                doA(idx + 2)
                stageB(H, j, jj)
        final_rows([(hbuf[:, H * 8 + tl, :], 128, Bh[H * 8 + tl]) for tl in range(8)],
                   [(y[(H * 8 + tl) * 128:(H * 8 + tl + 1) * 128, :], 0) for tl in range(8)])
        if H == 0:
            final_rows([(hsm[0:18, :], 18, Bhsm)], [(ys, 2)])

    return finalize()


_NC_CACHE = {}


def make_in_maps(x_prompt, x_sample, cache_kv_w128, cache_kv_w512, cache_kv_w2048, state_conv_ffn,
                 norm_mix, w_in, ln_v_gain, ln_v_bias, w_spatial, b_spatial, w_proj_a, w_proj_b,
                 w_out, norm_ffn, w_up, conv_w, conv_b, w_down, norm_final, cores=None):
    f = lambda a: np.ascontiguousarray(np.asarray(a, dtype=np.float32))
    x_prompt = f(x_prompt)
    B, SEQ, D = x_prompt.shape
    xp = np.zeros((B, HX + SEQ, D), np.float32)
    xp[:, HX:] = x_prompt
    k = np.arange(128)[:, None]
    q = np.arange(128)[None, :]
    mcur = (k <= q).astype(np.float32)
    mprev = (k >= q).astype(np.float32)
    blockones = ((k // 64) == (q // 64)).astype(np.float32)
    caches = [f(cache_kv_w128)[0], f(cache_kv_w512)[0], f(cache_kv_w2048)[0]]
    common = {
        "w_in": f(w_in)[0], "w_pa": f(w_proj_a)[0], "w_pb": f(w_proj_b)[0], "w_out": f(w_out)[0],
        "w_up": f(w_up)[0], "w_dn": f(w_down)[0],
        "vec8": np.concatenate([f(norm_mix)[0].reshape(8, 128), f(norm_ffn)[0].reshape(8, 128)], 0),
        "nfin": f(norm_final), "lng": f(ln_v_gain)[0], "lnb": f(ln_v_bias)[0],
        "cwb": np.concatenate([f(conv_w)[0], f(conv_b)], 0).reshape(4, 44, 128),
        "wsp": f(w_spatial)[0], "bsp": f(b_spatial)[0],
        "w00": np.ascontiguousarray(f(w_spatial)[0][:, 0, 0]),
    }
    in_maps = []
    for c in (range(NCORES) if cores is None else cores):
        b, qi = c // 4, c % 4
        fl = 0.0 if qi == 0 else 1.0
        m = dict(common)
        m["xall"] = np.ascontiguousarray(xp[b, qi * NM:qi * NM + HX + NM])
        m["xs"] = f(x_sample)[c * NS:(c + 1) * NS, 0]
        for g in range(3):
            cg = caches[g][c * NS:(c + 1) * NS]
            m["ck%d" % g] = np.ascontiguousarray(cg.reshape(NS, cg.shape[1], 512))
        m["sconv"] = f(state_conv_ffn)[0, c * NS:(c + 1) * NS]
        m["cst"] = np.stack([np.eye(128, dtype=np.float32), mprev, mcur, mprev * fl, mcur, blockones, np.ones((128, 128), np.float32)])
        m["flagd"] = np.full((128, 1), fl, np.float32)
        in_maps.append(m)
    return in_maps


def assemble(R, B=2):
    y_prompt = np.stack([np.concatenate([R[b * 4 + qi]["y"] for qi in range(4)], 0) for b in range(B)])
    y_sample = np.concatenate([R[c]["ys"] for c in range(NCORES)], 0)[:, None, :]
    outs = [y_prompt, y_sample]
    for g in range(3):
        L = WIN[g][0]
        kvp = np.stack([R[b * 4 + 3]["kv%d" % g] for b in range(B)]).reshape(1, B, L, 2, 4, 64)
        kvsm = np.concatenate([R[c]["kvs%d" % g] for c in range(NCORES)], 0).reshape(1, NCORES * NS, 1, 2, 4, 64)
        outs += [kvp, kvsm]
    vcp = np.stack([R[b * 4 + 3]["vch"] for b in range(B)])[None]
    vcs = np.concatenate([R[c]["vchs"] for c in range(NCORES)], 0)[None, :, None, :]
    cp = np.stack([R[b * 4 + 3]["convp"] for b in range(B)])[None]
    cs = np.concatenate([R[c]["convs"] for c in range(NCORES)], 0)[None]
    outs += [vcp, vcs, cp, cs]
    return tuple(np.ascontiguousarray(np.asarray(o, dtype=np.float32)) for o in outs)


def kernel(**inputs):
    in_maps = make_in_maps(**inputs)
    if "nc" not in _NC_CACHE:
        _NC_CACHE["nc"] = build_program()
    res = run_bass_kernel_spmd(_NC_CACHE["nc"], in_maps, core_ids=list(range(NCORES)))
    return assemble(res.results)
```

```python
import numpy as np
from contextlib import ExitStack
import concourse.bass as bass
import concourse.mybir as mybir
from concourse.bass_utils import run_bass_kernel_spmd

F32 = mybir.dt.float32
BF16 = mybir.dt.bfloat16
AF = mybir.ActivationFunctionType
ALU = mybir.AluOpType

NCORES = 8
HX = 2176
NM = 2048
NS = 16
SM0 = NM
NCOL = NM + 2 + NS
EPS = 1e-6
WIN = ((128, 1), (512, 4), (2048, 16))
KBASE = (1920, 1536, 0)
KLEN = (2304, 2688, 4224)


class Buf:
    __slots__ = ("name", "w", "r")

    def __init__(self, name=""):
        self.name = name
        self.w = None
        self.r = {}


class Sched:
    ENG = ("pe", "act", "dve", "pool", "sp")

    def __init__(self, nc, stack):
        self.nc = nc
        self.stack = stack
        self.prog = {e: [] for e in self.ENG}
        self.sems = {}
        self.cnt = {}
        self.seen = {e: {} for e in self.ENG}
        self.label = ""
        for e in self.ENG:
            self._sem("E_" + e)

    def _sem(self, name):
        if name not in self.sems:
            self.sems[name] = self.stack.enter_context(self.nc.semaphore(name))
            self.cnt[name] = 0
        return name

    def _need(self, eng, waits, tok):
        if tok is None:
            return
        sem, val = tok
        if eng == "pe" and sem == "E_pe":
            return
        if self.seen[eng].get(sem, 0) >= val:
            return
        self.seen[eng][sem] = val
        waits.append((sem, val))

    def op(self, eng, fn, reads=(), writes=(), dma=None):
        waits = []
        for b in reads:
            self._need(eng, waits, b.w)
        for b in writes:
            self._need(eng, waits, b.w)
            for s, v in b.r.items():
                self._need(eng, waits, (s, v))
        if dma is not None:
            sem = self._sem(dma)
            inc = 16
        else:
            sem = "E_" + eng
            inc = 1
        self.cnt[sem] += inc
        tok = (sem, self.cnt[sem])
        self.prog[eng].append((waits, fn, (sem, inc), self.label + " r:" + ",".join(b.name for b in reads) + " w:" + ",".join(b.name for b in writes)))
        for b in reads:
            if b.r.get(sem, 0) < tok[1]:
                b.r[sem] = tok[1]
        for b in writes:
            b.w = tok
            b.r = {}
        return tok

    def barrier(self):
        allw = [(s, v) for s, v in self.cnt.items() if v > 0]
        for e in self.ENG:
            waits = []
            for t in allw:
                self._need(e, waits, t)
            self.prog[e].append((waits, None, None, ""))

    def emit(self):
        nc = self.nc
        with nc.Block() as block:
            def run(name, e):
                for waits, fn, inc, lab in self.prog[name]:
                    for s, v in waits:
                        e.wait_ge(self.sems[s], v)
                    if fn is not None:
                        with nc.named_scope(lab.split(" ")[0] or "none"):
                            ins = fn(e)
                        ins.then_inc(self.sems[inc[0]], inc[1])

            @block.tensor
            def _(e):
                run("pe", e)

            @block.scalar
            def _(e):
                run("act", e)

            @block.vector
            def _(e):
                run("dve", e)

            @block.gpsimd
            def _(e):
                run("pool", e)

            @block.sync
            def _(e):
                run("sp", e)


def build_program(stop=None):
    nc = bass.Bass("TRN2", target_bir_lowering=False)

    def din(name, shape):
        return nc.dram_tensor(name, list(shape), F32, kind="ExternalInput").ap()

    def dout(name, shape):
        return nc.dram_tensor(name, list(shape), F32, kind="ExternalOutput").ap()

    xall = din("xall", [HX + NM, 1024])
    xs = din("xs", [NS, 1024])
    ck = [din("ck0", [NS, 128, 512]), din("ck1", [NS, 512, 512]), din("ck2", [NS, 2048, 512])]
    sconv = din("sconv", [NS, 2, 5632])
    w_in = din("w_in", [1024, 5376])
    w_pa = din("w_pa", [512, 1024])
    w_pb = din("w_pb", [256, 1024])
    w_out = din("w_out", [1024, 1024])
    w_up = din("w_up", [1024, 5632])
    w_dn = din("w_dn", [2816, 1024])
    vec8 = din("vec8", [16, 128])
    nfin = din("nfin", [1024])
    lng = din("lng", [512])
    lnb = din("lnb", [512])
    cwb = din("cwb", [4, 44, 128])
    wsp = din("wsp", [4, 128, 128])
    bsp = din("bsp", [4, 128])
    w00 = din("w00", [4])
    cst = din("cst", [7, 128, 128])
    flagd = din("flagd", [128, 1])

    y = dout("y", [NM, 1024])
    ys = dout("ys", [NS, 1024])
    kvo = [dout("kv0", [128, 512]), dout("kv1", [512, 512]), dout("kv2", [2048, 512])]
    kvs = [dout("kvs0", [NS, 512]), dout("kvs1", [NS, 512]), dout("kvs2", [NS, 512])]
    vch = dout("vch", [128, 512])
    vchs = dout("vchs", [NS, 512])
    convp = dout("convp", [2, 5632])
    convs = dout("convs", [NS, 2, 5632])

    st = ExitStack()
    S = Sched(nc, st)
    NF = 53100
    arena = st.enter_context(nc.sbuf_tensor("arena", [128, NF], F32))
    psum_all = st.enter_context(nc.psum_tensor("psall", [128, 4096], F32))
    pbank = [psum_all[:, i * 512:(i + 1) * 512] for i in range(8)]
    PB = [Buf("pb%d" % i) for i in range(8)]
    bank_i = [0]

    def nextbank():
        i = bank_i[0] % 8
        bank_i[0] += 1
        return pbank[i], PB[i]

    def nextpair():
        if bank_i[0] % 2:
            bank_i[0] += 1
        i = bank_i[0] % 8
        bank_i[0] += 2
        return pbank[i], PB[i], pbank[i + 1], PB[i + 1], psum_all[:, i * 512:(i + 2) * 512]

    class Arena:
        def __init__(self):
            self.top = 0

        def f32(self, n):
            a = arena[:, self.top:self.top + n]
            self.top += n
            assert self.top <= NF, self.top
            return a

        def bf(self, n):
            n2 = (n + 1) // 2
            a = arena[:, self.top:self.top + n2].bitcast(BF16)
            self.top += n2
            assert self.top <= NF, self.top
            return a

    A = Arena()
    TT = [NF - 2200]

    def tmp_f32(n):
        a = arena[:, TT[0]:TT[0] + n]
        TT[0] += n
        assert TT[0] <= NF
        return a
    out_bufs = []
    uid = [0]

    def finalize():
        S.barrier()
        S.emit()
        st.close()
        return nc

    def dma(eng, out, in_, rd, wr, sem=None, **kw):
        if sem is None:
            uid[0] += 1
            sem = "d%d" % (uid[0] % 24)
        return S.op(eng, lambda e: e.dma_start(out=out, in_=in_, **kw), reads=rd, writes=wr, dma=sem)

    def mm(out, lhsT, rhs, start, stop, rd, wr):
        S.op("pe", lambda e: e.matmul(out, lhsT=lhsT, rhs=rhs, start=start, stop=stop), reads=rd, writes=wr)

    def act(out, in_, func, rd, wr, **kw):
        S.op("act", lambda e: e.activation(out=out, in_=in_, func=func, **kw), reads=rd, writes=wr)

    def dve(fn, rd, wr):
        S.op("dve", fn, reads=rd, writes=wr)

    def tcopy(eng, out, in_, rd, wr):
        S.op(eng, lambda e: e.tensor_copy(out=out, in_=in_), reads=rd, writes=wr)

    Bc = Buf("const")
    cbs = []

    def CW():
        nb = Buf("c%d" % len(cbs))
        cbs.append(nb)
        return [nb]

    def CR():
        return list(cbs)
    cst_f = A.f32(7 * 128).rearrange("p (k n) -> p k n", k=7)
    dma("sp", cst_f, cst.rearrange("k p n -> p k n"), [], CW(), sem="c0")
    ident_f = cst_f[:, 0, :]
    blockones_f = cst_f[:, 5, :]
    cst_b = A.bf(7 * 128).rearrange("p (k n) -> p k n", k=7)
    tcopy("dve", cst_b, cst_f, CR(), CW())
    ident_b = cst_b[:, 0, :]
    mprev_b = cst_b[:, 1, :]
    mcur_b = cst_b[:, 2, :]
    medge_b = cst_b[:, 3, :]
    ones_b = cst_b[:, 6, :]
    flag = A.f32(1)
    dma("sp", flag, flagd, [], CW(), sem="c1")
    gfin_bc = A.f32(1024)
    dma("sp", gfin_bc, nfin.partition_broadcast(128), [], CW(), sem="c2")
    lng_bc = A.f32(512)
    lnb_bc = A.f32(512)
    dma("sp", lng_bc, lng.partition_broadcast(128), [], CW(), sem="c3")
    dma("sp", lnb_bc, lnb.partition_broadcast(128), [], CW(), sem="c4")
    w00_bc = A.f32(4)
    dma("sp", w00_bc, w00.partition_broadcast(128), [], CW(), sem="c5")
    v8_sb = tmp_f32(128)
    dma("sp", v8_sb[0:16, :], vec8, [], CW(), sem="c6")
    cw_sb = tmp_f32(4 * 128).rearrange("p (k n) -> p k n", k=4)
    dma("sp", cw_sb[0:44, :, :], cwb.rearrange("k c p -> c k p"), [], CW(), sem="c7")
    gvec = A.f32(16)
    cwT = A.f32(4 * 44).rearrange("p (k c) -> p k c", k=4)
    bk, Bk = nextbank()
    mm(bk[:, 0:16], v8_sb[0:16, :], ident_f[0:16, 0:16], True, True, CR(), [Bk])
    tcopy("dve", gvec, bk[:, 0:16], [Bk], CW())
    bk, Bk = nextbank()
    for k in range(4):
        mm(bk[:, k * 44:(k + 1) * 44], cw_sb[0:44, k, :], ident_f[0:44, 0:44], True, True, CR(), [Bk])
    tcopy("dve", cwT, bk[:, 0:176].rearrange("p (k c) -> p k c", k=4), [Bk], CW())
    ones_f = A.f32(128)
    S.op("dve", lambda e: e.memset(ones_f, 1.0), writes=CW())
    gm_bc = A.bf(8 * 128).rearrange("p (c n) -> p c n", c=8)
    gf_bc = A.bf(8 * 128).rearrange("p (c n) -> p c n", c=8)
    for c in range(8):
        dve(lambda e, c=c: e.tensor_scalar(out=gm_bc[:, c, :], in0=ones_f, scalar1=gvec[:, c:c + 1], scalar2=None, op0=ALU.mult), CR(), CW())
        dve(lambda e, c=c: e.tensor_scalar(out=gf_bc[:, c, :], in0=ones_f, scalar1=gvec[:, 8 + c:9 + c], scalar2=None, op0=ALU.mult), CR(), CW())
    wsp_f = tmp_f32(512).rearrange("p (g n) -> p g n", g=4)
    dma("sp", wsp_f, wsp.rearrange("g t s -> t g s"), [], CW(), sem="c8")
    wsp_b = tmp_f32(256).bitcast(BF16).rearrange("p (g n) -> p g n", g=4)
    for g in range(4):
        dve(lambda e, g=g: e.tensor_tensor(out=wsp_b[:, g, :], in0=wsp_f[:, g, :], in1=cst_f[:, 1, :], op=ALU.mult), CR(), CW())
    WmT = A.bf(512).rearrange("p (g n) -> p g n", g=4)
    bk, Bk = nextbank()
    bkb = bk.bitcast(BF16).rearrange("p (g n) -> p g n", g=8)
    for g in range(4):
        S.op("pe", lambda e, g=g: e.transpose(out=bkb[:, g, :], in_=wsp_b[:, g, :], identity=ident_b), reads=CR(), writes=[Bk])
    tcopy("dve", WmT, bkb[:, 0:4, :], [Bk], CW())
    bsp_f = tmp_f32(512)
    dma("sp", bsp_f[0:1, :], bsp.rearrange("(o g) t -> o (g t)", o=1), [], CW(), sem="c9")
    bsp_b = A.bf(512)
    tcopy("dve", bsp_b[0:1, :], bsp_f[0:1, :], CR(), CW())
    bsp0_f = A.f32(4 * 16)
    bsp0_b = A.bf(4 * 16)
    for g in range(4):
        dve(lambda e, g=g: e.tensor_scalar(out=bsp0_f[0:1, g * 16:(g + 1) * 16], in0=ones_f[0:1, 0:16], scalar1=bsp_f[0:1, g * 128:g * 128 + 1], scalar2=None, op0=ALU.mult), CR(), CW())
    tcopy("dve", bsp0_b[0:1, :], bsp0_f[0:1, :], CR(), CW())
    D16 = A.bf(4 * 16).rearrange("p (g n) -> p g n", g=4)
    for g in range(4):
        dve(lambda e, g=g: e.tensor_scalar(out=D16[0:16, g, :], in0=ident_f[0:16, 0:16], scalar1=w00_bc[0:16, g:g + 1], scalar2=None, op0=ALU.mult), CR(), CW())
    stat = A.f32(8 * 8).rearrange("p (s n) -> p s n", s=8)
    Bstat = [Buf("st%d" % i) for i in range(8)]
    stat_i = [0]
    S.op("dve", lambda e: e.memset(stat[:, 7, 7:8], 0.0), reads=CR(), writes=[Bc])
    CONST_TOP = A.top
    print('CONST_TOP', CONST_TOP)

    if stop == 'const':
        return finalize()
    def rstd_of(ssq_ap, n, inv_n, sti, Bs):
        s_ = stat[:n, sti, :]
        dve(lambda e: e.tensor_scalar(out=s_[:, 1:2], in0=ssq_ap, scalar1=inv_n, scalar2=EPS, op0=ALU.mult, op1=ALU.add), [Bs], [Bs])
        act(s_[:, 2:3], s_[:, 1:2], AF.Ln, [Bs], [Bs])
        act(s_[:, 3:4], s_[:, 2:3], AF.Exp, [Bs], [Bs], scale=-0.5)
        return s_[:, 3:4]

    def norm_rows(xt_ap, n, Bx, out_bf, Bo):
        sti = stat_i[0] % 8
        stat_i[0] += 1
        Bs = Bstat[sti]
        act(out_bf, xt_ap, AF.Square, [Bx], [Bo, Bs], accum_out=stat[:n, sti, 0:1])
        r = rstd_of(stat[:n, sti, 0:1], n, 1.0 / 1024, sti, Bs)
        act(out_bf, xt_ap, AF.Copy, [Bx, Bs], [Bo], scale=r)

    def transpose_rows(src_bf, n, Bsrc, dstT, Bdst, g_bc):
        bk, Bk = nextbank()
        pt = bk.bitcast(BF16).rearrange("p (c t) -> p c t", c=8)
        for c in range(8):
            S.op("pe", lambda e, c=c: e.transpose(out=pt[:, c, 0:n], in_=src_bf[0:n, c * 128:(c + 1) * 128], identity=ident_b[0:n, 0:n]), reads=[Bsrc, Bc], writes=[Bk])
        dve(lambda e: e.tensor_tensor(out=dstT, in0=pt[:, :, 0:n], in1=g_bc[:, :, 0:n], op=ALU.mult), [Bk, Bc], [Bdst])

    xnT = A.bf(8 * HX).rearrange("p (c n) -> p c n", c=8)
    BxnT = Buf("xnT")
    xeT = A.bf(8 * 128).rearrange("p (c n) -> p c n", c=8)
    BxeT = Buf("xeT")
    P1 = A.top
    QT = A.bf(6 * NCOL).rearrange("p (c n) -> p c n", c=6)
    BQT = Buf("QT")
    KT = [A.bf(2 * KLEN[g]).rearrange("p (c n) -> p c n", c=2) for g in range(3)]
    BKT = [Buf("KT%d" % g) for g in range(3)]
    KTs = A.bf(6 * 18).rearrange("p (c n) -> p c n", c=6)
    VTs = A.bf(6 * 18).rearrange("p (c n) -> p c n", c=6)
    BKTs = Buf("KTs")
    NVB = 79
    Vb = A.bf(NVB * 256).rearrange("p (b n) -> p b n", b=NVB)
    BV = [Buf("V%d" % i) for i in range(NVB + 3)]
    vidx = {}
    PW = A.top
    wqkv = A.bf(8 * 2304).rearrange("p (c n) -> p c n", c=8)
    Bw = Buf("wqkv")
    NKV = 5
    kvst = [A.f32(512) for _ in range(NKV)]
    Bkvst = [Buf("kvst%d" % i) for i in range(NKV)]
    kvst_i = [0]
    PB_TOP = A.top

    Bwk, Bwv = Buf("wk"), Buf("wv")
    wv0 = w_in.rearrange("(c p) n -> p c n", p=128)
    dma("pool", wqkv[:, :, 768:1536], wv0[:, :, 768:1536], [], [Bwk], sem="w0k")
    dma("pool", wqkv[:, :, 1536:2304], wv0[:, :, 1536:2304], [], [Bwv], sem="w0v")
    dma("pool", wqkv[:, :, 0:768], wv0[:, :, 0:768], [], [Bw], sem="w0")

    def wcol(kind, g, c):
        return kind * 768 + g * 256 + c * 128

    def norm_T_batch(items, xts, Bxts, xbs, Bxbs, semp, group_hook=None):
        sets = xts if isinstance(xts[0], list) else None
        G = len(xts[0]) if sets is not None else len(xts)
        all_sets = (xts, Bxts, xbs, Bxbs)
        for g0 in range(0, len(items), G):
            grp = items[g0:g0 + G]
            si_ = 0
            if sets is not None:
                si_ = (g0 // G) % len(sets)
                xts, Bxts, xbs, Bxbs = (all_sets[0][si_], all_sets[1][si_], all_sets[2][si_], all_sets[3][si_])
            stis = []
            srcs = []
            for k, (src_rows, n, dstT, Bdst, g_bc, pre) in enumerate(grp):
                if src_rows is not None:
                    dma("sp", xts[k][0:n, :], src_rows, [], [Bxts[k]], sem="%s%d%d" % (semp, si_, k))
                srcs.append((xts[k], Bxts[k]))
            for k, (src_rows, n, dstT, Bdst, g_bc, pre) in enumerate(grp):
                if pre is not None:
                    srcs[k] = pre(k)
            for k, (src_rows, n, dstT, Bdst, g_bc, pre) in enumerate(grp):
                sti = stat_i[0] % 8
                stat_i[0] += 1
                stis.append(sti)
                act(xbs[k][0:n, :], srcs[k][0][0:n, :], AF.Square, [srcs[k][1]], [Bxbs[k], Bstat[sti]], accum_out=stat[:n, sti, 0:1])
            for k, (src_rows, n, dstT, Bdst, g_bc, pre) in enumerate(grp):
                s_ = stat[:n, stis[k], :]
                dve(lambda e, s_=s_: e.tensor_scalar(out=s_[:, 1:2], in0=s_[:, 0:1], scalar1=1.0 / 1024, scalar2=EPS, op0=ALU.mult, op1=ALU.add), [Bstat[stis[k]]], [Bstat[stis[k]]])
            for k, (src_rows, n, dstT, Bdst, g_bc, pre) in enumerate(grp):
                s_ = stat[:n, stis[k], :]
                act(s_[:, 2:3], s_[:, 1:2], AF.Ln, [Bstat[stis[k]]], [Bstat[stis[k]]])
            for k, (src_rows, n, dstT, Bdst, g_bc, pre) in enumerate(grp):
                s_ = stat[:n, stis[k], :]
                act(s_[:, 3:4], s_[:, 2:3], AF.Exp, [Bstat[stis[k]]], [Bstat[stis[k]]], scale=-0.5)
            for k, (src_rows, n, dstT, Bdst, g_bc, pre) in enumerate(grp):
                s_ = stat[:n, stis[k], :]
                dve(lambda e, k=k, n=n, s_=s_, xbs=xbs, src_=srcs[k][0]: e.tensor_scalar(out=xbs[k][0:n, :], in0=src_[0:n, :], scalar1=s_[:, 3:4], scalar2=None, op0=ALU.mult), [srcs[k][1], Bstat[stis[k]]], [Bxbs[k]])
            if group_hook is not None:
                group_hook(g0 // G)
            for k, (src_rows, n, dstT, Bdst, g_bc, pre) in enumerate(grp):
                transpose_rows(xbs[k], n, Bxbs[k], dstT, Bdst, g_bc)

    def proj_fm(dst, Bdst, wt, Bwt, col0, src, Bsrc, c0, n, nk=8, func=AF.Copy, **kw):
        bk, Bk = nextbank()
        for kc in range(nk):
            mm(bk[:, 0:n], wt[:, kc, col0:col0 + 128], src[:, kc, c0:c0 + n], kc == 0, kc == nk - 1, [Bwt, Bsrc], [Bk])
        act(dst, bk[:, 0:n], func, [Bk], [Bdst], **kw)

    def vblock(g, start, step, n, src, Bsrc):
        idx = len(vidx)
        vidx[(g, start, step, "m" if src is xnT_main_marker[0] else "h")] = idx
        bk, Bk = nextbank()
        for kc in range(8):
            mm(bk[0:n, 0:256], src[:, kc, start:start + step * (n - 1) + 1:step], wqkv[:, kc, wcol(2, g, 0):wcol(2, g, 0) + 256], kc == 0, kc == 7, [Bsrc, Bwv], [Bk])
        tcopy("dve", Vb[0:n, idx, :], bk[0:n, 0:256], [Bk], [BV[idx]])
        return idx

    xnT_main_marker = [None]

    S.label = 'B1'
    QT_f32 = arena[:, P1:P1 + 6144]
    xtA = [QT_f32[:, k * 1024:(k + 1) * 1024] for k in range(4)]
    xbA = [QT_f32[:, 4096 + k * 512:4096 + (k + 1) * 512].bitcast(BF16) for k in range(4)]
    BxtA = [Buf("xtA%d" % k) for k in range(4)]
    BxbA = [Buf("xbA%d" % k) for k in range(4)]
    VBa = PW - NVB * 128
    Va_f32 = arena[:, VBa:VBa + 6144]
    xtA2 = [Va_f32[:, k * 1024:(k + 1) * 1024] for k in range(4)]
    xbA2 = [Va_f32[:, 4096 + k * 512:4096 + (k + 1) * 512].bitcast(BF16) for k in range(4)]
    BxtA2 = [Buf("xtA2%d" % k) for k in range(4)]
    BxbA2 = [Buf("xbA2%d" % k) for k in range(4)]
    norm_T_batch([(xall[t * 128:(t + 1) * 128, :], 128, xnT[:, :, t * 128:(t + 1) * 128], BxnT, gm_bc, None) for t in range(17)],
                 [xtA, xtA2], [BxtA, BxtA2], [xbA, xbA2], [BxbA, BxbA2], "xa")
    if stop == 'B1a':
        return finalize()
    for g in range(3):
        lt = KBASE[g]
        while lt < HX:
            n = min(512, HX - lt)
            for c in range(2):
                proj_fm(KT[g][:, c, lt - KBASE[g]:lt - KBASE[g] + n], BKT[g], wqkv, Bwk, wcol(1, g, c), xnT, BxnT, lt, n)
            lt += n
    if stop == 'B1k':
        return finalize()
    S.barrier()
    for r in range(16):
        vblock(2, 128 + r, 16, 128, xnT, BxnT)
    for r in range(4):
        vblock(1, 1664 + r, 4, 128, xnT, BxnT)
    vblock(0, 2048, 1, 128, xnT, BxnT)
    vblock(2, 126, 16, 128, xnT, BxnT)
    vblock(2, 127, 16, 128, xnT, BxnT)
    vblock(1, 1662, 4, 128, xnT, BxnT)
    vblock(1, 1663, 4, 128, xnT, BxnT)
    vblock(0, 2046, 1, 128, xnT, BxnT)
    vblock(2, 2174, 1, 1, xnT, BxnT)
    vblock(2, 2175, 1, 1, xnT, BxnT)
    vblock(1, 2174, 1, 1, xnT, BxnT)
    vblock(1, 2175, 1, 1, xnT, BxnT)
    vblock(0, 2174, 1, 2, xnT, BxnT)
    tcopy("dve", xeT, xnT[:, :, 2048:2176], [BxnT], [BxeT])

    if stop == 'B1':
        return finalize()
    S.label = 'B2'
    xnT_main_marker[0] = xnT
    S.barrier()
    VB0 = PW - NVB * 128 + 31 * 128
    Vm_f32 = arena[:, VB0:VB0 + 6144]
    xtB = [Vm_f32[:, k * 1024:(k + 1) * 1024] for k in range(4)]
    xbB = [Vm_f32[:, 4096 + k * 512:4096 + (k + 1) * 512].bitcast(BF16) for k in range(4)]
    BxtB = [Buf("xtB%d" % k) for k in range(4)]
    BxbB = [Buf("xbB%d" % k) for k in range(4)]
    norm_T_batch([(xall[HX + t * 128:HX + (t + 1) * 128, :], 128, xnT[:, :, t * 128:(t + 1) * 128], BxnT, gm_bc, None) for t in range(16)],
                 [xtB, xtA], [BxtB, BxtA], [xbB, xbA], [BxbB, BxbA], "xb")
    tcopy("dve", xnT[:, :, SM0:SM0 + 2], xeT[:, :, 126:128], [BxeT], [BxnT])
    norm_T_batch([(xs, NS, xnT[:, :, SM0 + 2:SM0 + 2 + NS], BxnT, gm_bc, None)], xtB, BxtB, xbB, BxbB, "xb")
    slices = [(i * 512, 512) for i in range(4)] + [(SM0, 18)]
    if stop == 'B2a':
        return finalize()
    S.barrier()
    for (c0, n) in slices:
        for gc in range(6):
            proj_fm(QT[:, gc, c0:c0 + n], BQT, wqkv, Bw, wcol(0, gc // 2, gc % 2), xnT, BxnT, c0, n)
    for (c0, n) in slices[:4]:
        for g in range(3):
            for c in range(2):
                kc0 = HX + c0 - KBASE[g]
                proj_fm(KT[g][:, c, kc0:kc0 + n], BKT[g], wqkv, Bwk, wcol(1, g, c), xnT, BxnT, c0, n)
    for gc in range(6):
        proj_fm(KTs[:, gc, :], BKTs, wqkv, Bwk, wcol(1, gc // 2, gc % 2), xnT, BxnT, SM0, 18)
        proj_fm(VTs[:, gc, :], BKTs, wqkv, Bwv, wcol(2, gc // 2, gc % 2), xnT, BxnT, SM0, 18)
    if stop == 'B2q':
        return finalize()
    assert len(vidx) == 31, len(vidx)
    S.barrier()
    for t in range(16):
        vblock(0, t * 128, 1, 128, xnT, BxnT)
    for i in range(4):
        for r in range(4):
            vblock(1, 512 * i + r, 4, 128, xnT, BxnT)
    for r in range(16):
        vblock(2, r, 16, 128, xnT, BxnT)

    if stop == 'B2v':
        return finalize()
    S.label = 'kvtok'
    def kv_tok(col0, n, g, dst_rows):
        i = kvst_i[0] % NKV
        kvst_i[0] += 1
        bk, Bk = nextbank()
        for half in range(2):
            for kc in range(8):
                mm(bk[0:n, half * 256:(half + 1) * 256], xnT[:, kc, col0:col0 + n], wqkv[:, kc, wcol(1 + half, g, 0):wcol(1 + half, g, 0) + 256], kc == 0, kc == 7, [BxnT, Bwk if half == 0 else Bwv], [Bk])
        tcopy("dve", kvst[i][0:n, :], bk[0:n, :], [Bk], [Bkvst[i]])
        dma("sp" if n == 128 else "pool", dst_rows, kvst[i][0:n, :], [Bkvst[i]], [], sem=("ko%d" if n == 128 else "kp%d") % i)
        out_bufs.append(Bkvst[i])

    for t in range(16):
        kv_tok(t * 128, 128, 2, kvo[2][t * 128:(t + 1) * 128, :])
    if stop == 'kv1':
        return finalize()
    for t in range(12, 16):
        kv_tok(t * 128, 128, 1, kvo[1][(t - 12) * 128:(t - 11) * 128, :])
    kv_tok(15 * 128, 128, 0, kvo[0])
    if stop == 'kv2':
        return finalize()
    for g in range(3):
        kv_tok(SM0 + 2, NS, g, kvs[g])

    if stop == 'B':
        return finalize()
    S.barrier()
    A.top = PW
    ACC0 = A.top
    acc_n = A.f32(2 * NCOL).rearrange("p (c n) -> p c n", c=2)
    acc_d = A.f32(2 * NCOL).rearrange("p (c n) -> p c n", c=2)
    acc_all = arena[:, ACC0:ACC0 + 4 * NCOL].rearrange("p (x c n) -> p x c n", x=2, c=2)
    NPT = 2
    PTb = [A.bf(1024).rearrange("p (h k q) -> p h k q", h=4, k=2) for _ in range(NPT)]
    BPT = [Buf("PT%d" % i) for i in range(NPT)]
    pt_i = [0]
    BOUT0 = A.top
    boutT = A.bf(2 * NCOL).rearrange("p (c n) -> p c n", c=2)
    BboutT = Buf("boutT")
    ATT_TOP = A.top
    A.top = BOUT0
    ckb = [[A.bf(512) for _ in range(3)] for _ in range(2)]
    Bckb = [[Buf("ckb%d%d" % (i, g)) for g in range(3)] for i in range(2)]
    KTc = [A.bf(6 * 128).rearrange("p (c n) -> p c n", c=6) for _ in range(2)]
    BKTc = [Buf("KTc%d" % i) for i in range(2)]
    PTs = [A.bf(16) for _ in range(2)]
    BPTs = [Buf("PTs%d" % i) for i in range(2)]
    prodf = A.f32(6 * 16).rearrange("p (c n) -> p c n", c=6)
    pself = A.f32(6 * 16).rearrange("p (c n) -> p c n", c=6)
    Bpr = Buf("prod")
    assert A.top <= NF, A.top
    acc_hist = {0: [], 1: [], 2: [], "x": [Buf("accx")], "s": []}
    mpair_main = cst_b[:, 1:3, :]
    mpair_edge = cst_b[:, 3:5, :]

    def acc_update(pair, Bn, Bd, nq, cols, first, key):
        if key == "x":
            rd_prev, wr = acc_hist["x"], acc_hist["x"]
        else:
            nb = Buf("acc%s" % str(key))
            rd_prev = [] if (key == "s" or key == 0) else acc_hist[key - 1]
            wr = [nb]
            acc_hist[key].append(nb)
        p4 = pair.rearrange("p (x h q) -> p x h q", x=2, h=4)
        for h2 in range(2):
            i_ap = p4[h2 * 64:(h2 + 1) * 64, :, h2::2, 0:nq]
            o_ap = acc_all[h2 * 64:(h2 + 1) * 64, :, :, cols]
            if first:
                dve(lambda e, i_ap=i_ap, o_ap=o_ap: e.tensor_copy(out=o_ap, in_=i_ap), [Bn, Bd] + rd_prev, wr)
            else:
                dve(lambda e, i_ap=i_ap, o_ap=o_ap: e.tensor_tensor(out=o_ap, in0=i_ap, in1=o_ap, op=ALU.add), [Bn, Bd] + rd_prev, wr)

    def band_p1(g, qsrc, Bq, qcols, nq, chunks, mpair):
        pi = pt_i[0] % NPT
        pt_i[0] += 1
        PT, Bp = PTb[pi], BPT[pi]
        b0, B0 = nextbank()
        b1, B1 = nextbank()
        sb = [b0, b1]
        SBf = [B0, B1]
        for h in range(4):
            c, h2 = h // 2, h % 2
            rows = slice(h2 * 64, (h2 + 1) * 64)
            for ci, (Kap, BK, vi, nk, mask) in enumerate(chunks):
                o = sb[h2][0:nk, (c * 2 + ci) * 128:(c * 2 + ci) * 128 + nq]
                mm(o, Kap[rows, c, :], qsrc[rows, g * 2 + c, qcols], True, True, [BK, Bq], [SBf[h2]])
        full = (nq == 128 and len(chunks) == 2 and all(ch[3] == 128 for ch in chunks))
        if full:
            for h2 in range(2):
                src = sb[h2].rearrange("p (h k q) -> p h k q", h=2, k=2)
                act(PT[:, h2::2, :, :], src, AF.Exp, [SBf[h2]], [Bp], scale=0.125)
            mb = mpair.unsqueeze(1).to_broadcast([128, 4, 2, 128])
            dve(lambda e, PT=PT, mb=mb: e.tensor_tensor(out=PT, in0=PT, in1=mb, op=ALU.mult), [Bp, Bc], [Bp])
        else:
            for ci, (Kap, BK, vi, nk, mask) in enumerate(chunks):
                for h2 in range(2):
                    src = sb[h2][0:nk, :].rearrange("p (h k q) -> p h k q", h=2, k=2)[:, :, ci, 0:nq]
                    act(PT[0:nk, h2::2, ci, 0:nq], src, AF.Exp, [SBf[h2]], [Bp], scale=0.125)
                if mask is not None:
                    for h in range(4):
                        dve(lambda e, h=h, ci=ci, nk=nk, mask=mask, PT=PT: e.tensor_tensor(out=PT[0:nk, h, ci, 0:nq], in0=PT[0:nk, h, ci, 0:nq], in1=mask[0:nk, 0:nq], op=ALU.mult), [Bp, Bc], [Bp])
        return PT, Bp

    def band_p2(PT, Bp, nq, chunks, acc_cols, first, key):
        nch = len(chunks)
        bn, Bn, bd, Bd, pair = nextpair()
        for h in range(4):
            c = h // 2
            for ci, (Kap, BK, vi, nk, mask) in enumerate(chunks):
                mm(bn[:, h * 128:h * 128 + nq], Vb[0:nk, vi, c * 128:(c + 1) * 128], PT[0:nk, h, ci, 0:nq], ci == 0, ci == nch - 1, [BV[vi], Bp], [Bn])
            for ci, (Kap, BK, vi, nk, mask) in enumerate(chunks):
                mm(bd[:, h * 128:h * 128 + nq], ones_b[0:nk, :], PT[0:nk, h, ci, 0:nq], ci == 0, ci == nch - 1, [Bc, Bp], [Bd])
        acc_update(pair, Bn, Bd, nq, acc_cols, first, key)

    def kslice(g, lt0, step, n):
        a = lt0 - KBASE[g]
        return KT[g][:, :, a:a + step * (n - 1) + 1:step]

    def samp_s0(b):
        sl = b % 2
        for g in range(3):
            L, d = WIN[g]
            dma("pool", ckb[sl][g], ck[g][b, 0:L:d, :], [], [Bckb[sl][g]], sem="ck%d%d" % (sl, g))

    def samp_s1(b):
        sl = b % 2
        bk, Bk = nextbank()
        pt = bk.bitcast(BF16).rearrange("p (c t) -> p c t", c=8)
        for g in range(3):
            for c in range(2):
                S.op("pe", lambda e, c=c, g=g, pt=pt, sl=sl: e.transpose(out=pt[:, g * 2 + c, :], in_=ckb[sl][g][:, c * 128:(c + 1) * 128], identity=ident_b), reads=[Bckb[sl][g], Bc], writes=[Bk])
        tcopy("dve", KTc[sl], pt[:, 0:6, :], [Bk], [BKTc[sl]])

    def samp_s2(b):
        sl = b % 2
        col = SM0 + 2 + b
        bs0, BS0 = nextbank()
        bs1, BS1 = nextbank()
        bsx = [bs0, bs1]
        BSx = [BS0, BS1]
        for g in range(3):
            for h in range(4):
                c, h2 = h // 2, h % 2
                rows = slice(h2 * 64, (h2 + 1) * 64)
                mm(bsx[h2][:, g * 2 + c:g * 2 + c + 1], KTc[sl][rows, g * 2 + c, :], QT[rows, g * 2 + c, col:col + 1], True, True, [BKTc[sl], BQT], [BSx[h2]])
        PTs3 = PTs[sl][:, 0:12].rearrange("p (g c t) -> p g c t", g=3, c=2)
        for h2 in range(2):
            act(PTs3[:, :, :, h2], bsx[h2][:, 0:6].rearrange("p (g c) -> p g c", g=3), AF.Exp, [BSx[h2]], [BPTs[sl]], scale=0.125)

    def samp_s3(b):
        sl = b % 2
        col = SM0 + 2 + b
        bn, Bn, bd, Bd, pair = nextpair()
        for h in range(4):
            c = h // 2
            for g in range(3):
                mm(bn[:, h * 128:h * 128 + 1], ckb[sl][g][:, 256 + c * 128:256 + (c + 1) * 128], PTs[sl][:, g * 4 + h:g * 4 + h + 1], g == 0, g == 2, [Bckb[sl][g], BPTs[sl]], [Bn])
            for g in range(3):
                mm(bd[:, h * 128:h * 128 + 1], ones_b, PTs[sl][:, g * 4 + h:g * 4 + h + 1], g == 0, g == 2, [Bc, BPTs[sl]], [Bd])
        acc_update(pair, Bn, Bd, 1, slice(col, col + 1), True, "s")

    S.label = 'att-main'
    mblocks = []
    for g in range(3):
        step = WIN[g][1]
        if g == 0:
            starts = [t * 128 for t in range(16)]
        elif g == 1:
            starts = [512 * i + r for i in range(4) for r in range(4)]
        else:
            starts = list(range(16))
        for m0 in starts:
            lt_q = HX + m0
            lt_p = lt_q - 128 * step
            if lt_p < HX:
                vp = vidx[(g, lt_p, step, "h")]
                mp = mpair_edge
            else:
                vp = vidx[(g, lt_p - HX, step, "m")]
                mp = mpair_main
            vc = vidx[(g, m0, step, "m")]
            chunks = [(kslice(g, lt_p, step, 128), BKT[g], vp, 128, None),
                      (kslice(g, lt_q, step, 128), BKT[g], vc, 128, None)]
            qc = slice(m0, m0 + step * 127 + 1, step)
            mblocks.append((g, qc, chunks, mp))
    samp_s0(0)
    pend = band_p1(mblocks[0][0], QT, BQT, mblocks[0][1], 128, mblocks[0][2], mblocks[0][3])
    for m, (g, qc, chunks, mp) in enumerate(mblocks):
        nxt = None
        if m + 1 < len(mblocks):
            g2_, qc2, ch2, mp2 = mblocks[m + 1]
            nxt = band_p1(g2_, QT, BQT, qc2, 128, ch2, mp2)
        band_p2(pend[0], pend[1], 128, chunks, qc, g == 0, g)
        pend = nxt
        b, k = m // 3, m % 3
        if k == 0:
            samp_s1(b)
            if b + 1 < NS:
                samp_s0(b + 1)
        elif k == 1:
            samp_s2(b)
        else:
            samp_s3(b)
    if stop == 'att-main':
        return finalize()
    S.label = 'att-ext2'
    vp = vidx[(0, 2046, 1, "h")]
    vc = vidx[(0, 2174, 1, "h")]
    chx = [(kslice(0, 2046, 1, 128), BKT[0], vp, 128, mprev_b), (kslice(0, 2174, 1, 2), BKT[0], vc, 2, mcur_b)]
    p_ = band_p1(0, QT, BQT, slice(SM0, SM0 + 2), 2, chx, None)
    band_p2(p_[0], p_[1], 2, chx, slice(SM0, SM0 + 2), True, "x")
    for g in (1, 2):
        step = WIN[g][1]
        for j in range(2):
            ltq = 2174 + j
            vp = vidx[(g, ltq - 128 * step, step, "h")]
            vc = vidx[(g, ltq, 1, "h")]
            chx = [(kslice(g, ltq - 128 * step, step, 128), BKT[g], vp, 128, None), (kslice(g, ltq, 1, 1), BKT[g], vc, 1, None)]
            p_ = band_p1(g, QT, BQT, slice(SM0 + j, SM0 + j + 1), 1, chx, None)
            band_p2(p_[0], p_[1], 1, chx, slice(SM0 + j, SM0 + j + 1), False, "x")
    Bacc = Buf("accall")
    S.op("dve", lambda e: e.memset(prodf[:, 0, 0:1], 0.0), reads=[b_ for k_ in acc_hist for b_ in acc_hist[k_]], writes=[Bacc, Bpr])
    S.label = 'att-self'
    dve(lambda e: e.tensor_tensor(out=prodf, in0=QT[:, :, SM0 + 2:SM0 + 18], in1=KTs[:, :, 2:18], op=ALU.mult), [BQT, BKTs], [Bpr])
    bk, Bk = nextbank()
    mm(bk[:, 0:96], blockones_f, prodf.rearrange("p c n -> p (c n)"), True, True, [Bc, Bpr], [Bk])
    act(pself.rearrange("p c n -> p (c n)"), bk[:, 0:96], AF.Exp, [Bk], [Bpr], scale=0.125)
    dve(lambda e: e.tensor_tensor(out=prodf, in0=pself, in1=VTs[:, :, 2:18], op=ALU.mult), [Bpr, BKTs], [Bpr])
    for g in range(3):
        for c in range(2):
            dve(lambda e, g=g, c=c: e.tensor_tensor(out=acc_n[:, c, SM0 + 2:SM0 + 18], in0=acc_n[:, c, SM0 + 2:SM0 + 18], in1=prodf[:, g * 2 + c, :], op=ALU.add), [Bacc, Bpr], [Bacc])
            dve(lambda e, g=g, c=c: e.tensor_tensor(out=acc_d[:, c, SM0 + 2:SM0 + 18], in0=acc_d[:, c, SM0 + 2:SM0 + 18], in1=pself[:, g * 2 + c, :], op=ALU.add), [Bacc, Bpr], [Bacc])
    if stop == 'att-self':
        return finalize()
    S.barrier()
    A.top = P1
    boutT2 = A.bf(2 * NCOL).rearrange("p (c n) -> p c n", c=2)
    assert A.top <= PW
    aoutT = A.bf(4 * NCOL).rearrange("p (c n) -> p c n", c=4)
    BaoutT = Buf("aoutT")
    C_TOP = A.top
    wuv = A.bf(8 * 1024).rearrange("p (c n) -> p c n", c=8)
    Bwuv = Buf("wuv")
    wv_in = w_in.rearrange("(c p) n -> p c n", p=128)
    dma("pool", wuv, wv_in[:, :, 2304:3328], [], [Bwuv], sem="w1")
    S.label = 'att-norm'
    for c in range(2):
        for (c0, n) in slices:
            dve(lambda e, c=c, c0=c0, n=n: e.reciprocal(out=acc_d[:, c, c0:c0 + n], in_=acc_d[:, c, c0:c0 + n]), [Bacc], [Bacc])
            dve(lambda e, c=c, c0=c0, n=n: e.tensor_tensor(out=boutT2[:, c, c0:c0 + n], in0=acc_n[:, c, c0:c0 + n], in1=acc_d[:, c, c0:c0 + n], op=ALU.mult), [Bacc], [BboutT])
    if stop == 'att':
        return finalize()
    S.label = 'C1'
    MT0 = NF - 4 * NCOL
    W2A = MT0 - (8192 + 2048 + 1024)
    wg = arena[:, W2A:W2A + 8192].bitcast(BF16).rearrange("p (c n) -> p c n", c=8)
    wpa = arena[:, W2A + 8192:W2A + 10240].bitcast(BF16).rearrange("p (c n) -> p c n", c=4)
    wpb = arena[:, W2A + 10240:W2A + 11264].bitcast(BF16).rearrange("p (c n) -> p c n", c=2)
    Bwg = Buf("wg")
    dma("pool", wg, wv_in[:, :, 3328:5376], [], [Bwg, Bacc], sem="w2")
    dma("pool", wpa, w_pa.rearrange("(c p) n -> p c n", p=128), [], [Bwg, Bacc], sem="w3")
    dma("pool", wpb, w_pb.rearrange("(c p) n -> p c n", p=128), [], [Bwg, Bacc], sem="w4")
    uT = A.bf(4 * NCOL).rearrange("p (c n) -> p c n", c=4)
    BuT = Buf("uT")
    NG = 3
    gv = [A.f32(512) for _ in range(NG)]
    Bgv = [Buf("gv%d" % i) for i in range(NG)]
    vn = [A.f32(512) for _ in range(NG)]
    Bvn = [Buf("vn%d" % i) for i in range(NG)]
    vnb = [A.bf(512) for _ in range(NG)]
    Bvnb = [Buf("vnb%d" % i) for i in range(NG)]
    uxe = A.bf(4 * 128).rearrange("p (c n) -> p c n", c=4)
    aoe = A.bf(4 * 128).rearrange("p (c n) -> p c n", c=4)
    Buxe = Buf("uxe")
    C1_TOP = A.top
    for (c0, n) in slices:
        for c in range(4):
            proj_fm(uT[:, c, c0:c0 + n], BuT, wuv, Bwuv, c * 128, xnT, BxnT, c0, n, func=AF.Gelu_apprx_tanh)
    for c in range(4):
        proj_fm(uxe[:, c, :], Buxe, wuv, Bwuv, c * 128, xeT, BxeT, 0, 128, func=AF.Gelu_apprx_tanh)
    gi = [0]

    def gmlp_batch(items):
        G = len(gv)

        def stage1(grp, par):
            bks_ = []
            for k, (src, Bsrc, c0, n, sample, u_ap, Bu, out_ap, Bout, vn_dst) in enumerate(grp):
                bk, Bk = pbank[3 * par + k], PB[3 * par + k]
                bks_.append((bk, Bk))
                for kc in range(8):
                    mm(bk[0:n, :], src[:, kc, c0:c0 + n], wuv[:, kc, 512:1024], kc == 0, kc == 7, [Bsrc, Bwuv], [Bk])
            return bks_

        groups = [items[g0:g0 + G] for g0 in range(0, len(items), G)]
        banks_next = stage1(groups[0], 0)
        for gi_, grp in enumerate(groups):
            banks = banks_next
            if gi_ + 1 < len(groups):
                banks_next = stage1(groups[gi_ + 1], (gi_ + 1) % 2)
            stis = []
            for k, (src, Bsrc, c0, n, sample, u_ap, Bu, out_ap, Bout, vn_dst) in enumerate(grp):
                sti = stat_i[0] % 8
                stat_i[0] += 1
                stis.append(sti)
            for k, (src, Bsrc, c0, n, sample, u_ap, Bu, out_ap, Bout, vn_dst) in enumerate(grp):
                s_ = stat[:n, stis[k], :]
                act(gv[k][0:n, :], banks[k][0][0:n, :], AF.Gelu_apprx_tanh, [banks[k][1]], [Bgv[k], Bstat[stis[k]]], accum_out=s_[:, 4:5])
            for k, (src, Bsrc, c0, n, sample, u_ap, Bu, out_ap, Bout, vn_dst) in enumerate(grp):
                s_ = stat[:n, stis[k], :]
                dve(lambda e, s_=s_: e.tensor_scalar(out=s_[:, 5:6], in0=s_[:, 4:5], scalar1=-1.0 / 512, scalar2=None, op0=ALU.mult), [Bstat[stis[k]]], [Bstat[stis[k]]])
            for k, (src, Bsrc, c0, n, sample, u_ap, Bu, out_ap, Bout, vn_dst) in enumerate(grp):
                s_ = stat[:n, stis[k], :]
                act(vn[k][0:n, :], gv[k][0:n, :], AF.Identity, [Bgv[k], Bstat[stis[k]]], [Bvn[k]], bias=s_[:, 5:6], scale=1.0)
            for k, (src, Bsrc, c0, n, sample, u_ap, Bu, out_ap, Bout, vn_dst) in enumerate(grp):
                s_ = stat[:n, stis[k], :]
                act(gv[k][0:n, :], vn[k][0:n, :], AF.Square, [Bvn[k]], [Bgv[k], Bstat[stis[k]]], accum_out=s_[:, 0:1])
            for k, (src, Bsrc, c0, n, sample, u_ap, Bu, out_ap, Bout, vn_dst) in enumerate(grp):
                s_ = stat[:n, stis[k], :]
                dve(lambda e, s_=s_: e.tensor_scalar(out=s_[:, 1:2], in0=s_[:, 0:1], scalar1=1.0 / 512, scalar2=EPS, op0=ALU.mult, op1=ALU.add), [Bstat[stis[k]]], [Bstat[stis[k]]])
            for k, (src, Bsrc, c0, n, sample, u_ap, Bu, out_ap, Bout, vn_dst) in enumerate(grp):
                s_ = stat[:n, stis[k], :]
                act(s_[:, 2:3], s_[:, 1:2], AF.Ln, [Bstat[stis[k]]], [Bstat[stis[k]]])
            for k, (src, Bsrc, c0, n, sample, u_ap, Bu, out_ap, Bout, vn_dst) in enumerate(grp):
                s_ = stat[:n, stis[k], :]
                act(s_[:, 3:4], s_[:, 2:3], AF.Exp, [Bstat[stis[k]]], [Bstat[stis[k]]], scale=-0.5)
            for k, (src, Bsrc, c0, n, sample, u_ap, Bu, out_ap, Bout, vn_dst) in enumerate(grp):
                s_ = stat[:n, stis[k], :]
                dve(lambda e, k=k, n=n, s_=s_: e.scalar_tensor_tensor(out=vn[k][0:n, :], in0=vn[k][0:n, :], scalar=s_[:, 3:4], in1=lng_bc[0:n, :], op0=ALU.mult, op1=ALU.mult), [Bvn[k], Bstat[stis[k]], Bc], [Bvn[k]])
            for k, (src, Bsrc, c0, n, sample, u_ap, Bu, out_ap, Bout, vn_dst) in enumerate(grp):
                dve(lambda e, k=k, n=n: e.tensor_tensor(out=vn[k][0:n, :], in0=vn[k][0:n, :], in1=lnb_bc[0:n, :], op=ALU.add), [Bvn[k], Bc], [Bvn[k]])
            for k, (src, Bsrc, c0, n, sample, u_ap, Bu, out_ap, Bout, vn_dst) in enumerate(grp):
                tcopy("dve", vnb[k][0:n, :], vn[k][0:n, :], [Bvn[k]], [Bvnb[k]])
                if vn_dst is not None:
                    dma("sp" if n == 128 else "pool", vn_dst, vn[k][0:n, :], [Bvn[k]], [], sem=("vo%d" if n == 128 else "vp%d") % k)
                    out_bufs.append(Bvn[k])
            mbanks = []
            for k, (src, Bsrc, c0, n, sample, u_ap, Bu, out_ap, Bout, vn_dst) in enumerate(grp):
                mb_i = (6, 7, 3 * (gi_ % 2))[k]
                bm, Bm = pbank[mb_i], PB[mb_i]
                mbanks.append((bm, Bm))
                nt = 16 if sample else 128
                for g in range(4):
                    o = bm[:, g * 128:g * 128 + nt]
                    if sample:
                        mm(o, vnb[k][0:n, g * 128:(g + 1) * 128], D16[0:16, g, :], True, False, [Bvnb[k], Bc], [Bm])
                        mm(o, ones_b[0:1, :], bsp0_b[0:1, g * 16:(g + 1) * 16], False, True, [Bc], [Bm])
                    else:
                        mm(o, vnb[k][0:n, g * 128:(g + 1) * 128], WmT[:, g, :], True, False, [Bvnb[k], Bc], [Bm])
                        mm(o, ones_b[0:1, :], bsp_b[0:1, g * 128:(g + 1) * 128], False, True, [Bc], [Bm])
            for k, (src, Bsrc, c0, n, sample, u_ap, Bu, out_ap, Bout, vn_dst) in enumerate(grp):
                nt = 16 if sample else 128
                m4 = mbanks[k][0].rearrange("p (g t) -> p g t", g=4)[:, :, 0:nt]
                dve(lambda e, m4=m4, out_ap=out_ap, u_ap=u_ap: e.tensor_tensor(out=out_ap, in0=m4, in1=u_ap, op=ALU.mult), [mbanks[k][1], Bu], [Bout])

    gitems = []
    for t in range(16):
        cs = slice(t * 128, (t + 1) * 128)
        gitems.append((xnT, BxnT, t * 128, 128, False, uT[:, :, cs], BuT, aoutT[:, :, cs], BaoutT, vch if t == 15 else None))
    gitems.append((xeT, BxeT, 0, 128, False, uxe, Buxe, aoe, Buxe, None))
    gitems.append((xnT, BxnT, SM0 + 2, NS, True, uT[:, :, SM0 + 2:SM0 + 18], BuT, aoutT[:, :, SM0 + 2:SM0 + 18], BaoutT, vchs))
    gmlp_batch(gitems)
    tcopy("dve", aoutT[:, :, SM0:SM0 + 2], aoe[:, :, 126:128], [Buxe], [BaoutT])
    if stop == 'C1':
        return finalize()
    S.label = 'C2a'
    S.barrier()
    A.top = C_TOP
    assert C1_TOP <= W2A, (C1_TOP, W2A)
    tg = [A.f32(512) for _ in range(2)]
    Btg = [Buf("tg0"), Buf("tg1")]
    t1 = [A.f32(512) for _ in range(2)]
    Bt1 = [Buf("t10"), Buf("t11")]
    mt = arena[:, MT0:NF].bitcast(BF16).rearrange("p (c n) -> p c n", c=8)
    Bmt = Buf("mT")
    oc_i = [0]
    for si, (c0, n) in enumerate(slices):
        for oc in range(8):
            i = oc_i[0] % 2
            oc_i[0] += 1
            ba, Ba = nextbank()
            for kc in range(4):
                mm(ba[:, 0:n], wpa[:, kc, oc * 128:(oc + 1) * 128], aoutT[:, kc, c0:c0 + n], kc == 0, kc == 3, [Bwg, BaoutT], [Ba])
            bb, Bb = nextbank()
            for kc in range(2):
                mm(bb[:, 0:n], wpb[:, kc, oc * 128:(oc + 1) * 128], boutT2[:, kc, c0:c0 + n], kc == 0, kc == 1, [Bwg, BboutT], [Bb])
            proj_fm(tg[i][:, 0:n], Btg[i], wg, Bwg, oc * 128, xnT, BxnT, c0, n, func=AF.Tanh, scale=0.5)
            dve(lambda e, i=i, ba=ba, n=n: e.scalar_tensor_tensor(out=t1[i][:, 0:n], in0=tg[i][:, 0:n], scalar=1.0, in1=ba[:, 0:n], op0=ALU.add, op1=ALU.mult), [Btg[i], Ba], [Bt1[i]])
            proj_fm(tg[i][:, 0:n], Btg[i], wg, Bwg, 1024 + oc * 128, xnT, BxnT, c0, n, func=AF.Tanh, scale=0.5)
            dve(lambda e, i=i, bb=bb, n=n: e.scalar_tensor_tensor(out=tg[i][:, 0:n], in0=tg[i][:, 0:n], scalar=1.0, in1=bb[:, 0:n], op0=ALU.add, op1=ALU.mult), [Btg[i], Bb], [Btg[i]])
            dve(lambda e, i=i, n=n, oc=oc, c0=c0: e.tensor_tensor(out=mt[:, oc, c0:c0 + n], in0=t1[i][:, 0:n], in1=tg[i][:, 0:n], op=ALU.add), [Bt1[i], Btg[i]], [Bmt])

    if stop == 'C2a':
        return finalize()
    S.label = 'C2b'
    S.barrier()
    A.top = CONST_TOP
    hnT = A.bf(8 * NCOL).rearrange("p (c n) -> p c n", c=8)
    BhnT = Buf("hnT")
    hbuf = A.f32(16 * 1024).rearrange("p (t n) -> p t n", t=16)
    hsm = A.f32(1024)
    Bh = [Buf("h%d" % t) for t in range(16)]
    Bhsm = Buf("hsm")
    HB_TOP = A.top
    wo = A.bf(8 * 1024).rearrange("p (c n) -> p c n", c=8)
    Bwo = Buf("wo")
    dma("pool", wo, w_out.rearrange("(c p) n -> p c n", p=128), [], [Bwo], sem="w5")
    NX = 3
    xt = [A.f32(1024) for _ in range(NX)]
    Bxt = [Buf("xt%db" % i) for i in range(NX)]
    xb = [A.bf(1024) for _ in range(NX)]
    Bxb = [Buf("xb%db" % i) for i in range(NX)]
    assert A.top <= MT0, (A.top, MT0)

    def mk_pre(t, c0o, m):
        def pre(k):
            if t >= 0:
                hdst, Bhd = hbuf[:, t, :], Bh[t]
            else:
                hdst, Bhd = hsm, Bhsm
            for half in range(2):
                bk, Bk = nextbank()
                for kc in range(8):
                    mm(bk[0:m, :], mt[:, kc, c0o:c0o + m], wo[:, kc, half * 512:(half + 1) * 512], kc == 0, kc == 7, [Bmt, Bwo], [Bk])
                dve(lambda e, bk=bk, half=half, k=k, hdst=hdst: e.scalar_tensor_tensor(out=hdst[0:m, half * 512:(half + 1) * 512], in0=bk[0:m, :], scalar=0.5, in1=xt[k][0:m, half * 512:(half + 1) * 512], op0=ALU.mult, op1=ALU.add), [Bk, Bxt[k]], [Bhd])
            return (hdst, Bhd)
        return pre

    HS0 = 42404
    assert A.top <= HS0 and HS0 + 1408 + 1024 <= MT0, (A.top, MT0)
    hsT = arena[:, HS0:HS0 + 1408].rearrange("p (c k n) -> p c k n", c=44, k=2)
    BhsT = Buf("hsT")
    scst2 = [arena[:, HS0 + 1408 + i * 512:HS0 + 1408 + (i + 1) * 512].rearrange("p (k n) -> p k n", k=2) for i in range(2)]
    Bscst2 = [Buf("scst0"), Buf("scst1")]
    def sconv_piece(q):
        sc_, Bsc_ = scst2[q % 2], Bscst2[q % 2]
        dma("sp", sc_[0:16, :, :], sconv[:, :, q * 256:(q + 1) * 256], [], [Bsc_], sem="sc%d" % (q % 2))
        for cc in range(2):
            ch = q * 2 + cc
            bk, Bk = nextbank()
            for k in range(2):
                mm(bk[:, k * 16:(k + 1) * 16], sc_[0:16, k, cc * 128:(cc + 1) * 128], ident_f[0:16, 0:16], True, True, [Bsc_, Bc], [Bk])
            tcopy("dve", hsT[:, ch, :, :], bk[:, 0:32].rearrange("p (k n) -> p k n", k=2), [Bk], [BhsT])
        dma("pool", convs[:, 0, q * 256:(q + 1) * 256], sc_[0:16, 1, :], [Bsc_], [], sem="sco%d" % (q % 2))
        out_bufs.append(Bsc_)


    sc_done = [0]

    def sc_hook(gi_):
        for _ in range(4):
            if sc_done[0] < 22:
                sconv_piece(sc_done[0])
                sc_done[0] += 1

    citems = [(xall[HX + t * 128:HX + (t + 1) * 128, :], 128, hnT[:, :, t * 128:(t + 1) * 128], BhnT, gf_bc, mk_pre(t, t * 128, 128)) for t in range(16)]
    norm_T_batch(citems, xt, Bxt, xb, Bxb, "xc", group_hook=sc_hook)
    dma("sp", xt[0][0:2, :], xall[HX - 2:HX, :], [], [Bxt[0]], sem="xc0")
    dma("sp", xt[0][2:18, :], xs, [], [Bxt[0]], sem="xq0")
    norm_T_batch([(None, 18, hnT[:, :, SM0:SM0 + 18], BhnT, gf_bc, mk_pre(-1, SM0, 18))], xt, Bxt, xb, Bxb, "xc")
    if stop == 'C2b':
        return finalize()
    for q in range(sc_done[0], 22):
        sconv_piece(q)

    S.label = 'D'
    S.barrier()
    A.top = HB_TOP
    HT = 1024
    KG = [(0, 4), (4, 4), (8, 4), (12, 4), (16, 3), (19, 3)]
    cbuf = [[A.f32(1024) for _ in range(2)] for _ in range(3)]
    Bcb = [[Buf("c%d%d" % (a_, b_)) for b_ in range(2)] for a_ in range(3)]
    upb = [[A.f32(2 + HT) for _ in range(2)] for _ in range(2)]
    Bupb = [[Buf("up%d%d" % (a_, b_)) for b_ in range(2)] for a_ in range(2)]
    prodS = A.bf(22 * 18).rearrange("p (j n) -> p j n", j=22)
    BprodS = Buf("prodS")
    ups = [[A.f32(18) for _ in range(2)] for _ in range(3)]
    cs_ = [[A.f32(18) for _ in range(2)] for _ in range(3)]
    Bcs = [[Buf("cs%d%d" % (a_, b_)) for b_ in range(2)] for a_ in range(3)]
    hist = A.f32(44 * 2).rearrange("p (c n) -> p c n", c=44)
    Bhist = Buf("hist")
    upst = [A.f32(256) for _ in range(2)]
    Bupst = [Buf("upst0"), Buf("upst1")]
    assert A.top <= HS0, A.top
    A.top = HS0 + 1408
    prodT = A.bf(4 * HT).rearrange("p (j n) -> p j n", j=4)
    BprodT = [Buf("prodT%d" % i) for i in range(6)]
    NWU = 3
    wup = [A.bf(8 * 256).rearrange("p (c n) -> p c n", c=8) for _ in range(NWU)]
    Bwup = [Buf("wup%d" % i) for i in range(NWU)]
    wdns = [A.bf(4 * 1024).rearrange("p (j n) -> p j n", j=4) for _ in range(2)]
    Bwdns = [Buf("wdn0"), Buf("wdn1")]
    wdi = [0]
    wdq = []

    def load_wdn(j0, gs):
        i = wdi[0] % 2
        wdi[0] += 1
        dma("pool", wdns[i][:, 0:gs, :], w_dn.rearrange("(j p) n -> p j n", p=128)[:, j0:j0 + gs, :], [], [Bwdns[i]], sem="wd%d" % i)
        wdq.append((wdns[i], Bwdns[i]))
    assert A.top <= NF, A.top

    wv = w_up.rearrange("(c p) n -> p c n", p=128)
    wslot = {}
    wi = [0]

    def load_wup(j):
        wsl = wi[0] % NWU
        wi[0] += 1
        w_, Bw_ = wup[wsl], Bwup[wsl]
        dma("pool", w_[:, :, 0:128], wv[:, :, j * 128:(j + 1) * 128], [], [Bw_], sem="wu%da" % wsl)
        dma("pool", w_[:, :, 128:256], wv[:, :, 2816 + j * 128:2816 + (j + 1) * 128], [], [Bw_], sem="wu%db" % wsl)
        return w_, Bw_

    def stageA(H, j):
        base = H * HT
        w_, Bw_ = wslot[(H, j)]
        sl = j % 2
        s3 = j % 3
        for gv_ in range(2):
            ch = gv_ * 22 + j
            ub, Bub = upb[sl][gv_], Bupb[sl][gv_]
            if H == 0:
                if gv_ == 0:
                    bks_, Bks_ = nextbank()
                so = gv_ * 32
                for kc in range(8):
                    mm(bks_[:, so:so + 18], w_[:, kc, gv_ * 128:(gv_ + 1) * 128], hnT[:, kc, SM0:SM0 + 18], kc == 0, kc == 7, [Bw_, BhnT], [Bks_])
                if gv_ == 1:
                    for kc in range(8):
                        mm(bks_[0:20, 64:320], hnT[:, kc, NM - 2:NM + 18], w_[:, kc, :], kc == 0, kc == 7, [BhnT, Bw_], [Bks_])
                    for g2_ in range(2):
                        ub2, Bub2 = upb[sl][g2_], Bupb[sl][g2_]
                        ch2 = g2_ * 22 + j
                        so2 = g2_ * 32
                        act(ub2[:, 0:2], bks_[:, so2:so2 + 2], AF.Copy, [Bks_, Bc], [Bub2], scale=flag[:, 0:1])
                        act(ups[s3][g2_], bks_[:, so2:so2 + 18], AF.Copy, [Bks_], [Bcs[s3][g2_]])
                        cs = cs_[s3][g2_]
                        act(cs, ups[s3][g2_], AF.Identity, [Bcs[s3][g2_], Bc], [Bcs[s3][g2_]], scale=cwT[:, 2, ch2:ch2 + 1], bias=cwT[:, 3, ch2:ch2 + 1])
                        for k in range(2):
                            dve(lambda e, cs=cs, ch2=ch2, k=k: e.scalar_tensor_tensor(out=cs[:, 2:18], in0=hsT[:, ch2, k, :], scalar=cwT[:, k, ch2:ch2 + 1], in1=cs[:, 2:18], op0=ALU.mult, op1=ALU.add), [BhsT, Bc, Bcs[s3][g2_]], [Bcs[s3][g2_]])
                    ui = j % 2
                    tcopy("dve", upst[ui][0:20, :], bks_[0:20, 64:320], [Bks_], [Bupst[ui]])
                    for g2_ in range(2):
                        cc0 = g2_ * 2816 + j * 128
                        dma("pool", convp[:, cc0:cc0 + 128], upst[ui][0:2, g2_ * 128:(g2_ + 1) * 128], [Bupst[ui]], [], sem="uo%d" % ui)
                        dma("pool", convs[:, 1, cc0:cc0 + 128], upst[ui][4:20, g2_ * 128:(g2_ + 1) * 128], [Bupst[ui]], [], sem="uo%d" % ui)
                    out_bufs.append(Bupst[ui])
            else:
                tcopy("dve", ub[:, 0:2], hist[:, ch, :], [Bhist], [Bub])
            for s2 in range(2):
                c0 = base + s2 * 512
                bk, Bk = nextbank()
                for kc in range(8):
                    mm(bk, w_[:, kc, gv_ * 128:(gv_ + 1) * 128], hnT[:, kc, c0:c0 + 512], kc == 0, kc == 7, [Bw_, BhnT], [Bk])
                act(ub[:, 2 + s2 * 512:2 + (s2 + 1) * 512], bk, AF.Copy, [Bk], [Bub])
            if H == 0:
                tcopy("dve", hist[:, ch, :], ub[:, HT:HT + 2], [Bub], [Bhist])
        for gv_ in range(2):
            ch = gv_ * 22 + j
            ub, Bub = upb[sl][gv_], Bupb[sl][gv_]
            cc, Bcc = cbuf[s3][gv_], Bcb[s3][gv_]
            act(cc, ub[:, 2:2 + HT], AF.Identity, [Bub, Bc], [Bcc], scale=cwT[:, 2, ch:ch + 1], bias=cwT[:, 3, ch:ch + 1])
        for gv_ in range(2):
            ch = gv_ * 22 + j
            ub, Bub = upb[sl][gv_], Bupb[sl][gv_]
            cc, Bcc = cbuf[s3][gv_], Bcb[s3][gv_]
            for k in range(2):
                dve(lambda e, cc=cc, ub=ub, k=k, ch=ch: e.scalar_tensor_tensor(out=cc, in0=ub[:, k:k + HT], scalar=cwT[:, k, ch:ch + 1], in1=cc, op0=ALU.mult, op1=ALU.add), [Bub, Bc, Bcc], [Bcc])

    def stageB(H, j, jj):
        s3 = j % 3
        cg, cv = cbuf[s3][0], cbuf[s3][1]
        act(cg, cg, AF.Gelu_apprx_tanh, [Bcb[s3][0]], [Bcb[s3][0]])
        dve(lambda e, cg=cg, cv=cv, jj=jj: e.tensor_tensor(out=prodT[:, jj, :], in0=cg, in1=cv, op=ALU.mult), [Bcb[s3][0], Bcb[s3][1]], [BprodT[jj]])
        if H == 0:
            act(cs_[s3][0], cs_[s3][0], AF.Gelu_apprx_tanh, [Bcs[s3][0]], [Bcs[s3][0]])
            dve(lambda e, s3=s3, j=j: e.tensor_tensor(out=prodS[:, j, :], in0=cs_[s3][0], in1=cs_[s3][1], op=ALU.mult), [Bcs[s3][0], Bcs[s3][1]], [BprodS])

    def wdown_group(H, j0, gs):
        wdn, Bwdn = wdq.pop(0)
        for tb in range(4):
            ths = [(tb * 2 + q_, half) for q_ in range(2) for half in range(2)]
            bks = [nextbank() for _ in ths]
            for (tl, half), (bk, Bk) in zip(ths, bks):
                for jj in range(gs - 1):
                    mm(bk, prodT[:, jj, tl * 128:(tl + 1) * 128], wdn[:, jj, half * 512:(half + 1) * 512], jj == 0, False, [BprodT[jj], Bwdn], [Bk])
            for (tl, half), (bk, Bk) in zip(ths, bks):
                jj = gs - 1
                mm(bk, prodT[:, jj, tl * 128:(tl + 1) * 128], wdn[:, jj, half * 512:(half + 1) * 512], False, True, [BprodT[jj], Bwdn], [Bk])
            for (tl, half), (bk, Bk) in zip(ths, bks):
                t = H * 8 + tl
                dve(lambda e, bk=bk, t=t, half=half: e.tensor_tensor(out=hbuf[:, t, half * 512:(half + 1) * 512], in0=bk, in1=hbuf[:, t, half * 512:(half + 1) * 512], op=ALU.add), [Bk, Bh[t]], [Bh[t]])
        if H == 0:
            for half in range(2):
                bk, Bk = nextbank()
                for jj in range(gs):
                    mm(bk[0:18, :], prodS[:, j0 + jj, :], wdn[:, jj, half * 512:(half + 1) * 512], jj == 0, jj == gs - 1, [BprodS, Bwdn], [Bk])
                dve(lambda e, bk=bk, half=half: e.tensor_tensor(out=hsm[0:18, half * 512:(half + 1) * 512], in0=bk[0:18, :], in1=hsm[0:18, half * 512:(half + 1) * 512], op=ALU.add), [Bk, Bhsm], [Bhsm])

    def final_rows(haps, dsts):
        stis = []
        for k, (hap, m, Bh_) in enumerate(haps):
            sti = stat_i[0] % 8
            stat_i[0] += 1
            stis.append(sti)
            scr = cbuf[k % 3][(k // 3) % 2]
            Bscr = Bcb[k % 3][(k // 3) % 2]
            act(scr.bitcast(BF16)[0:m, 0:1024], hap, AF.Square, [Bh_], [Bscr, Bstat[sti]], accum_out=stat[:m, sti, 0:1])
        rs = []
        for k, (hap, m, Bh_) in enumerate(haps):
            rs.append(rstd_of(stat[:m, stis[k], 0:1], m, 1.0 / 1024, stis[k], Bstat[stis[k]]))
        for k, (hap, m, Bh_) in enumerate(haps):
            dve(lambda e, hap=hap, r=rs[k], m=m: e.scalar_tensor_tensor(out=hap, in0=hap, scalar=r, in1=gfin_bc[0:m, :], op0=ALU.mult, op1=ALU.mult), [Bh_, Bstat[stis[k]], Bc], [Bh_])
        for k, (hap, m, Bh_) in enumerate(haps):
            dst, r0 = dsts[k]
            dma("sp" if m == 128 else "pool", dst, hap[r0:m, :], [Bh_], [], sem=("yo%d" if m == 128 else "yp%d") % (k % 4))
            out_bufs.append(Bh_)

    for H in range(2):
        order = [(j0, gs, jj) for (j0, gs) in KG for jj in range(gs)]
        PRE = 3
        jof = lambda q_: order[q_][0] + order[q_][2]
        for q_ in range(min(PRE, len(order))):
            wslot[(H, jof(q_))] = load_wup(jof(q_))
        doneA = set()

        def doA(q_):
            if q_ < len(order) and q_ not in doneA:
                doneA.add(q_)
                stageA(H, jof(q_))

        load_wdn(*KG[0])
        gnext = [1]
        doA(0)
        doA(1)
        for idx, (j0, gs, jj) in enumerate(order):
            j = j0 + jj
            if idx + PRE < len(order):
                wslot[(H, jof(idx + PRE))] = load_wup(jof(idx + PRE))
            if jj == gs - 1:
                stageB(H, j, jj)
                doA(idx + 2)
                doA(idx + 3)
                if gnext[0] < len(KG):
                    load_wdn(*KG[gnext[0]])
                    gnext[0] += 1
                wdown_group(H, j0, gs)
            else:
                doA(idx + 2)
                stageB(H, j, jj)
        final_rows([(hbuf[:, H * 8 + tl, :], 128, Bh[H * 8 + tl]) for tl in range(8)],
                   [(y[(H * 8 + tl) * 128:(H * 8 + tl + 1) * 128, :], 0) for tl in range(8)])
        if H == 0:
            final_rows([(hsm[0:18, :], 18, Bhsm)], [(ys, 2)])

    return finalize()


_NC_CACHE = {}


def make_in_maps(x_prompt, x_sample, cache_kv_w128, cache_kv_w512, cache_kv_w2048, state_conv_ffn,
                 norm_mix, w_in, ln_v_gain, ln_v_bias, w_spatial, b_spatial, w_proj_a, w_proj_b,
                 w_out, norm_ffn, w_up, conv_w, conv_b, w_down, norm_final, cores=None):
    f = lambda a: np.ascontiguousarray(np.asarray(a, dtype=np.float32))
    x_prompt = f(x_prompt)
    B, SEQ, D = x_prompt.shape
    xp = np.zeros((B, HX + SEQ, D), np.float32)
    xp[:, HX:] = x_prompt
    k = np.arange(128)[:, None]
    q = np.arange(128)[None, :]
    mcur = (k <= q).astype(np.float32)
    mprev = (k >= q).astype(np.float32)
    blockones = ((k // 64) == (q // 64)).astype(np.float32)
    caches = [f(cache_kv_w128)[0], f(cache_kv_w512)[0], f(cache_kv_w2048)[0]]
    common = {
        "w_in": f(w_in)[0], "w_pa": f(w_proj_a)[0], "w_pb": f(w_proj_b)[0], "w_out": f(w_out)[0],
        "w_up": f(w_up)[0], "w_dn": f(w_down)[0],
        "vec8": np.concatenate([f(norm_mix)[0].reshape(8, 128), f(norm_ffn)[0].reshape(8, 128)], 0),
        "nfin": f(norm_final), "lng": f(ln_v_gain)[0], "lnb": f(ln_v_bias)[0],
        "cwb": np.concatenate([f(conv_w)[0], f(conv_b)], 0).reshape(4, 44, 128),
        "wsp": f(w_spatial)[0], "bsp": f(b_spatial)[0],
        "w00": np.ascontiguousarray(f(w_spatial)[0][:, 0, 0]),
    }
    in_maps = []
    for c in (range(NCORES) if cores is None else cores):
        b, qi = c // 4, c % 4
        fl = 0.0 if qi == 0 else 1.0
        m = dict(common)
        m["xall"] = np.ascontiguousarray(xp[b, qi * NM:qi * NM + HX + NM])
        m["xs"] = f(x_sample)[c * NS:(c + 1) * NS, 0]
        for g in range(3):
            cg = caches[g][c * NS:(c + 1) * NS]
            m["ck%d" % g] = np.ascontiguousarray(cg.reshape(NS, cg.shape[1], 512))
        m["sconv"] = f(state_conv_ffn)[0, c * NS:(c + 1) * NS]
        m["cst"] = np.stack([np.eye(128, dtype=np.float32), mprev, mcur, mprev * fl, mcur, blockones, np.ones((128, 128), np.float32)])
        m["flagd"] = np.full((128, 1), fl, np.float32)
        in_maps.append(m)
    return in_maps


def assemble(R, B=2):
    y_prompt = np.stack([np.concatenate([R[b * 4 + qi]["y"] for qi in range(4)], 0) for b in range(B)])
    y_sample = np.concatenate([R[c]["ys"] for c in range(NCORES)], 0)[:, None, :]
    outs = [y_prompt, y_sample]
    for g in range(3):
        L = WIN[g][0]
        kvp = np.stack([R[b * 4 + 3]["kv%d" % g] for b in range(B)]).reshape(1, B, L, 2, 4, 64)
        kvsm = np.concatenate([R[c]["kvs%d" % g] for c in range(NCORES)], 0).reshape(1, NCORES * NS, 1, 2, 4, 64)
        outs += [kvp, kvsm]
    vcp = np.stack([R[b * 4 + 3]["vch"] for b in range(B)])[None]
    vcs = np.concatenate([R[c]["vchs"] for c in range(NCORES)], 0)[None, :, None, :]
    cp = np.stack([R[b * 4 + 3]["convp"] for b in range(B)])[None]
    cs = np.concatenate([R[c]["convs"] for c in range(NCORES)], 0)[None]
    outs += [vcp, vcs, cp, cs]
    return tuple(np.ascontiguousarray(np.asarray(o, dtype=np.float32)) for o in outs)


def kernel(**inputs):
    in_maps = make_in_maps(**inputs)
    if "nc" not in _NC_CACHE:
        _NC_CACHE["nc"] = build_program()
    res = run_bass_kernel_spmd(_NC_CACHE["nc"], in_maps, core_ids=list(range(NCORES)))
    return assemble(res.results)
```

```python
import numpy as np
from contextlib import ExitStack
import concourse.bass as bass
import concourse.mybir as mybir
from concourse.bass_utils import run_bass_kernel_spmd

F32 = mybir.dt.float32
BF16 = mybir.dt.bfloat16
AF = mybir.ActivationFunctionType
ALU = mybir.AluOpType

NCORES = 8
HX = 2176
NM = 2048
NS = 16
SM0 = NM
NCOL = NM + 2 + NS
EPS = 1e-6
WIN = ((128, 1), (512, 4), (2048, 16))
KBASE = (1920, 1536, 0)
KLEN = (2304, 2688, 4224)


class Buf:
    __slots__ = ("name", "w", "r")

    def __init__(self, name=""):
        self.name = name
        self.w = None
        self.r = {}


class Sched:
    ENG = ("pe", "act", "dve", "pool", "sp")

    def __init__(self, nc, stack):
        self.nc = nc
        self.stack = stack
        self.prog = {e: [] for e in self.ENG}
        self.sems = {}
        self.cnt = {}
        self.seen = {e: {} for e in self.ENG}
        self.label = ""
        for e in self.ENG:
            self._sem("E_" + e)

    def _sem(self, name):
        if name not in self.sems:
            self.sems[name] = self.stack.enter_context(self.nc.semaphore(name))
            self.cnt[name] = 0
        return name

    def _need(self, eng, waits, tok):
        if tok is None:
            return
        sem, val = tok
        if eng == "pe" and sem == "E_pe":
            return
        if self.seen[eng].get(sem, 0) >= val:
            return
        self.seen[eng][sem] = val
        waits.append((sem, val))

    def op(self, eng, fn, reads=(), writes=(), dma=None):
        waits = []
        for b in reads:
            self._need(eng, waits, b.w)
        for b in writes:
            self._need(eng, waits, b.w)
            for s, v in b.r.items():
                self._need(eng, waits, (s, v))
        if dma is not None:
            sem = self._sem(dma)
            inc = 16
        else:
            sem = "E_" + eng
            inc = 1
        self.cnt[sem] += inc
        tok = (sem, self.cnt[sem])
        self.prog[eng].append((waits, fn, (sem, inc), self.label + " r:" + ",".join(b.name for b in reads) + " w:" + ",".join(b.name for b in writes)))
        for b in reads:
            if b.r.get(sem, 0) < tok[1]:
                b.r[sem] = tok[1]
        for b in writes:
            b.w = tok
            b.r = {}
        return tok

    def barrier(self):
        allw = [(s, v) for s, v in self.cnt.items() if v > 0]
        for e in self.ENG:
            waits = []
            for t in allw:
                self._need(e, waits, t)
            self.prog[e].append((waits, None, None, ""))

    def emit(self):
        nc = self.nc
        with nc.Block() as block:
            def run(name, e):
                for waits, fn, inc, lab in self.prog[name]:
                    for s, v in waits:
                        e.wait_ge(self.sems[s], v)
                    if fn is not None:
                        with nc.named_scope(lab.split(" ")[0] or "none"):
                            ins = fn(e)
                        ins.then_inc(self.sems[inc[0]], inc[1])

            @block.tensor
            def _(e):
                run("pe", e)

            @block.scalar
            def _(e):
                run("act", e)

            @block.vector
            def _(e):
                run("dve", e)

            @block.gpsimd
            def _(e):
                run("pool", e)

            @block.sync
            def _(e):
                run("sp", e)


def build_program(stop=None):
    nc = bass.Bass("TRN2", target_bir_lowering=False)

    def din(name, shape):
        return nc.dram_tensor(name, list(shape), F32, kind="ExternalInput").ap()

    def dout(name, shape):
        return nc.dram_tensor(name, list(shape), F32, kind="ExternalOutput").ap()

    xall = din("xall", [HX + NM, 1024])
    xs = din("xs", [NS, 1024])
    ck = [din("ck0", [NS, 128, 512]), din("ck1", [NS, 512, 512]), din("ck2", [NS, 2048, 512])]
    sconv = din("sconv", [NS, 2, 5632])
    w_in = din("w_in", [1024, 5376])
    w_pa = din("w_pa", [512, 1024])
    w_pb = din("w_pb", [256, 1024])
    w_out = din("w_out", [1024, 1024])
    w_up = din("w_up", [1024, 5632])
    w_dn = din("w_dn", [2816, 1024])
    vec8 = din("vec8", [16, 128])
    nfin = din("nfin", [1024])
    lng = din("lng", [512])
    lnb = din("lnb", [512])
    cwb = din("cwb", [4, 44, 128])
    wsp = din("wsp", [4, 128, 128])
    bsp = din("bsp", [4, 128])
    w00 = din("w00", [4])
    cst = din("cst", [7, 128, 128])
    flagd = din("flagd", [128, 1])

    y = dout("y", [NM, 1024])
    ys = dout("ys", [NS, 1024])
    kvo = [dout("kv0", [128, 512]), dout("kv1", [512, 512]), dout("kv2", [2048, 512])]
    kvs = [dout("kvs0", [NS, 512]), dout("kvs1", [NS, 512]), dout("kvs2", [NS, 512])]
    vch = dout("vch", [128, 512])
    vchs = dout("vchs", [NS, 512])
    convp = dout("convp", [2, 5632])
    convs = dout("convs", [NS, 2, 5632])

    st = ExitStack()
    S = Sched(nc, st)
    NF = 53100
    arena = st.enter_context(nc.sbuf_tensor("arena", [128, NF], F32))
    psum_all = st.enter_context(nc.psum_tensor("psall", [128, 4096], F32))
    pbank = [psum_all[:, i * 512:(i + 1) * 512] for i in range(8)]
    PB = [Buf("pb%d" % i) for i in range(8)]
    bank_i = [0]

    def nextbank():
        i = bank_i[0] % 8
        bank_i[0] += 1
        return pbank[i], PB[i]

    def nextpair():
        if bank_i[0] % 2:
            bank_i[0] += 1
        i = bank_i[0] % 8
        bank_i[0] += 2
        return pbank[i], PB[i], pbank[i + 1], PB[i + 1], psum_all[:, i * 512:(i + 2) * 512]

    class Arena:
        def __init__(self):
            self.top = 0

        def f32(self, n):
            a = arena[:, self.top:self.top + n]
            self.top += n
            assert self.top <= NF, self.top
            return a

        def bf(self, n):
            n2 = (n + 1) // 2
            a = arena[:, self.top:self.top + n2].bitcast(BF16)
            self.top += n2
            assert self.top <= NF, self.top
            return a

    A = Arena()
    TT = [NF - 2200]

    def tmp_f32(n):
        a = arena[:, TT[0]:TT[0] + n]
        TT[0] += n
        assert TT[0] <= NF
        return a
    out_bufs = []
    uid = [0]

    def finalize():
        S.barrier()
        S.emit()
        st.close()
        return nc

    def dma(eng, out, in_, rd, wr, sem=None, **kw):
        if sem is None:
            uid[0] += 1
            sem = "d%d" % (uid[0] % 24)
        return S.op(eng, lambda e: e.dma_start(out=out, in_=in_, **kw), reads=rd, writes=wr, dma=sem)

    def mm(out, lhsT, rhs, start, stop, rd, wr):
        S.op("pe", lambda e: e.matmul(out, lhsT=lhsT, rhs=rhs, start=start, stop=stop), reads=rd, writes=wr)

    def act(out, in_, func, rd, wr, **kw):
        S.op("act", lambda e: e.activation(out=out, in_=in_, func=func, **kw), reads=rd, writes=wr)

    def dve(fn, rd, wr):
        S.op("dve", fn, reads=rd, writes=wr)

    def tcopy(eng, out, in_, rd, wr):
        S.op(eng, lambda e: e.tensor_copy(out=out, in_=in_), reads=rd, writes=wr)

    Bc = Buf("const")
    cbs = []

    def CW():
        nb = Buf("c%d" % len(cbs))
        cbs.append(nb)
        return [nb]

    def CR():
        return list(cbs)
    cst_f = A.f32(7 * 128).rearrange("p (k n) -> p k n", k=7)
    dma("sp", cst_f, cst.rearrange("k p n -> p k n"), [], CW(), sem="c0")
    ident_f = cst_f[:, 0, :]
    blockones_f = cst_f[:, 5, :]
    cst_b = A.bf(7 * 128).rearrange("p (k n) -> p k n", k=7)
    tcopy("dve", cst_b, cst_f, CR(), CW())
    ident_b = cst_b[:, 0, :]
    mprev_b = cst_b[:, 1, :]
    mcur_b = cst_b[:, 2, :]
    medge_b = cst_b[:, 3, :]
    ones_b = cst_b[:, 6, :]
    flag = A.f32(1)
    dma("sp", flag, flagd, [], CW(), sem="c1")
    gfin_bc = A.f32(1024)
    dma("sp", gfin_bc, nfin.partition_broadcast(128), [], CW(), sem="c2")
    lng_bc = A.f32(512)
    lnb_bc = A.f32(512)
    dma("sp", lng_bc, lng.partition_broadcast(128), [], CW(), sem="c3")
    dma("sp", lnb_bc, lnb.partition_broadcast(128), [], CW(), sem="c4")
    w00_bc = A.f32(4)
    dma("sp", w00_bc, w00.partition_broadcast(128), [], CW(), sem="c5")
    v8_sb = tmp_f32(128)
    dma("sp", v8_sb[0:16, :], vec8, [], CW(), sem="c6")
    cw_sb = tmp_f32(4 * 128).rearrange("p (k n) -> p k n", k=4)
    dma("sp", cw_sb[0:44, :, :], cwb.rearrange("k c p -> c k p"), [], CW(), sem="c7")
    gvec = A.f32(16)
    cwT = A.f32(4 * 44).rearrange("p (k c) -> p k c", k=4)
    bk, Bk = nextbank()
    mm(bk[:, 0:16], v8_sb[0:16, :], ident_f[0:16, 0:16], True, True, CR(), [Bk])
    tcopy("dve", gvec, bk[:, 0:16], [Bk], CW())
    bk, Bk = nextbank()
    for k in range(4):
        mm(bk[:, k * 44:(k + 1) * 44], cw_sb[0:44, k, :], ident_f[0:44, 0:44], True, True, CR(), [Bk])
    tcopy("dve", cwT, bk[:, 0:176].rearrange("p (k c) -> p k c", k=4), [Bk], CW())
    ones_f = A.f32(128)
    S.op("dve", lambda e: e.memset(ones_f, 1.0), writes=CW())
    gm_bc = A.bf(8 * 128).rearrange("p (c n) -> p c n", c=8)
    gf_bc = A.bf(8 * 128).rearrange("p (c n) -> p c n", c=8)
    for c in range(8):
        dve(lambda e, c=c: e.tensor_scalar(out=gm_bc[:, c, :], in0=ones_f, scalar1=gvec[:, c:c + 1], scalar2=None, op0=ALU.mult), CR(), CW())
        dve(lambda e, c=c: e.tensor_scalar(out=gf_bc[:, c, :], in0=ones_f, scalar1=gvec[:, 8 + c:9 + c], scalar2=None, op0=ALU.mult), CR(), CW())
    wsp_f = tmp_f32(512).rearrange("p (g n) -> p g n", g=4)
    dma("sp", wsp_f, wsp.rearrange("g t s -> t g s"), [], CW(), sem="c8")
    wsp_b = tmp_f32(256).bitcast(BF16).rearrange("p (g n) -> p g n", g=4)
    for g in range(4):
        dve(lambda e, g=g: e.tensor_tensor(out=wsp_b[:, g, :], in0=wsp_f[:, g, :], in1=cst_f[:, 1, :], op=ALU.mult), CR(), CW())
    WmT = A.bf(512).rearrange("p (g n) -> p g n", g=4)
    bk, Bk = nextbank()
    bkb = bk.bitcast(BF16).rearrange("p (g n) -> p g n", g=8)
    for g in range(4):
        S.op("pe", lambda e, g=g: e.transpose(out=bkb[:, g, :], in_=wsp_b[:, g, :], identity=ident_b), reads=CR(), writes=[Bk])
    tcopy("dve", WmT, bkb[:, 0:4, :], [Bk], CW())
    bsp_f = tmp_f32(512)
    dma("sp", bsp_f[0:1, :], bsp.rearrange("(o g) t -> o (g t)", o=1), [], CW(), sem="c9")
    bsp_b = A.bf(512)
    tcopy("dve", bsp_b[0:1, :], bsp_f[0:1, :], CR(), CW())
    bsp0_f = A.f32(4 * 16)
    bsp0_b = A.bf(4 * 16)
    for g in range(4):
        dve(lambda e, g=g: e.tensor_scalar(out=bsp0_f[0:1, g * 16:(g + 1) * 16], in0=ones_f[0:1, 0:16], scalar1=bsp_f[0:1, g * 128:g * 128 + 1], scalar2=None, op0=ALU.mult), CR(), CW())
    tcopy("dve", bsp0_b[0:1, :], bsp0_f[0:1, :], CR(), CW())
    D16 = A.bf(4 * 16).rearrange("p (g n) -> p g n", g=4)
    for g in range(4):
        dve(lambda e, g=g: e.tensor_scalar(out=D16[0:16, g, :], in0=ident_f[0:16, 0:16], scalar1=w00_bc[0:16, g:g + 1], scalar2=None, op0=ALU.mult), CR(), CW())
    stat = A.f32(8 * 8).rearrange("p (s n) -> p s n", s=8)
    Bstat = [Buf("st%d" % i) for i in range(8)]
    stat_i = [0]
    S.op("dve", lambda e: e.memset(stat[:, 7, 7:8], 0.0), reads=CR(), writes=[Bc])
    CONST_TOP = A.top
    print('CONST_TOP', CONST_TOP)

    if stop == 'const':
        return finalize()
    def rstd_of(ssq_ap, n, inv_n, sti, Bs):
        s_ = stat[:n, sti, :]
        dve(lambda e: e.tensor_scalar(out=s_[:, 1:2], in0=ssq_ap, scalar1=inv_n, scalar2=EPS, op0=ALU.mult, op1=ALU.add), [Bs], [Bs])
        act(s_[:, 2:3], s_[:, 1:2], AF.Ln, [Bs], [Bs])
        act(s_[:, 3:4], s_[:, 2:3], AF.Exp, [Bs], [Bs], scale=-0.5)
        return s_[:, 3:4]

    def norm_rows(xt_ap, n, Bx, out_bf, Bo):
        sti = stat_i[0] % 8
        stat_i[0] += 1
        Bs = Bstat[sti]
        act(out_bf, xt_ap, AF.Square, [Bx], [Bo, Bs], accum_out=stat[:n, sti, 0:1])
        r = rstd_of(stat[:n, sti, 0:1], n, 1.0 / 1024, sti, Bs)
        act(out_bf, xt_ap, AF.Copy, [Bx, Bs], [Bo], scale=r)

    def transpose_rows(src_bf, n, Bsrc, dstT, Bdst, g_bc):
        bk, Bk = nextbank()
        pt = bk.bitcast(BF16).rearrange("p (c t) -> p c t", c=8)
        for c in range(8):
            S.op("pe", lambda e, c=c: e.transpose(out=pt[:, c, 0:n], in_=src_bf[0:n, c * 128:(c + 1) * 128], identity=ident_b[0:n, 0:n]), reads=[Bsrc, Bc], writes=[Bk])
        dve(lambda e: e.tensor_tensor(out=dstT, in0=pt[:, :, 0:n], in1=g_bc[:, :, 0:n], op=ALU.mult), [Bk, Bc], [Bdst])

    xnT = A.bf(8 * HX).rearrange("p (c n) -> p c n", c=8)
    BxnT = Buf("xnT")
    xeT = A.bf(8 * 128).rearrange("p (c n) -> p c n", c=8)
    BxeT = Buf("xeT")
    P1 = A.top
    QT = A.bf(6 * NCOL).rearrange("p (c n) -> p c n", c=6)
    BQT = Buf("QT")
    KT = [A.bf(2 * KLEN[g]).rearrange("p (c n) -> p c n", c=2) for g in range(3)]
    BKT = [Buf("KT%d" % g) for g in range(3)]
    KTs = A.bf(6 * 18).rearrange("p (c n) -> p c n", c=6)
    VTs = A.bf(6 * 18).rearrange("p (c n) -> p c n", c=6)
    BKTs = Buf("KTs")
    NVB = 79
    Vb = A.bf(NVB * 256).rearrange("p (b n) -> p b n", b=NVB)
    BV = [Buf("V%d" % i) for i in range(NVB + 3)]
    vidx = {}
    PW = A.top
    wqkv = A.bf(8 * 2304).rearrange("p (c n) -> p c n", c=8)
    Bw = Buf("wqkv")
    NKV = 5
    kvst = [A.f32(512) for _ in range(NKV)]
    Bkvst = [Buf("kvst%d" % i) for i in range(NKV)]
    kvst_i = [0]
    PB_TOP = A.top

    Bwk, Bwv = Buf("wk"), Buf("wv")
    wv0 = w_in.rearrange("(c p) n -> p c n", p=128)
    dma("pool", wqkv[:, :, 768:1536], wv0[:, :, 768:1536], [], [Bwk], sem="w0k")
    dma("pool", wqkv[:, :, 1536:2304], wv0[:, :, 1536:2304], [], [Bwv], sem="w0v")
    dma("pool", wqkv[:, :, 0:768], wv0[:, :, 0:768], [], [Bw], sem="w0")

    def wcol(kind, g, c):
        return kind * 768 + g * 256 + c * 128

    def norm_T_batch(items, xts, Bxts, xbs, Bxbs, semp, group_hook=None):
        sets = xts if isinstance(xts[0], list) else None
        G = len(xts[0]) if sets is not None else len(xts)
        all_sets = (xts, Bxts, xbs, Bxbs)
        for g0 in range(0, len(items), G):
            grp = items[g0:g0 + G]
            si_ = 0
            if sets is not None:
                si_ = (g0 // G) % len(sets)
                xts, Bxts, xbs, Bxbs = (all_sets[0][si_], all_sets[1][si_], all_sets[2][si_], all_sets[3][si_])
            stis = []
            srcs = []
            for k, (src_rows, n, dstT, Bdst, g_bc, pre) in enumerate(grp):
                if src_rows is not None:
                    dma("sp", xts[k][0:n, :], src_rows, [], [Bxts[k]], sem="%s%d%d" % (semp, si_, k))
                srcs.append((xts[k], Bxts[k]))
            for k, (src_rows, n, dstT, Bdst, g_bc, pre) in enumerate(grp):
                if pre is not None:
                    srcs[k] = pre(k)
            for k, (src_rows, n, dstT, Bdst, g_bc, pre) in enumerate(grp):
                sti = stat_i[0] % 8
                stat_i[0] += 1
                stis.append(sti)
                act(xbs[k][0:n, :], srcs[k][0][0:n, :], AF.Square, [srcs[k][1]], [Bxbs[k], Bstat[sti]], accum_out=stat[:n, sti, 0:1])
            for k, (src_rows, n, dstT, Bdst, g_bc, pre) in enumerate(grp):
                s_ = stat[:n, stis[k], :]
                dve(lambda e, s_=s_: e.tensor_scalar(out=s_[:, 1:2], in0=s_[:, 0:1], scalar1=1.0 / 1024, scalar2=EPS, op0=ALU.mult, op1=ALU.add), [Bstat[stis[k]]], [Bstat[stis[k]]])
            for k, (src_rows, n, dstT, Bdst, g_bc, pre) in enumerate(grp):
                s_ = stat[:n, stis[k], :]
                act(s_[:, 2:3], s_[:, 1:2], AF.Ln, [Bstat[stis[k]]], [Bstat[stis[k]]])
            for k, (src_rows, n, dstT, Bdst, g_bc, pre) in enumerate(grp):
                s_ = stat[:n, stis[k], :]
                act(s_[:, 3:4], s_[:, 2:3], AF.Exp, [Bstat[stis[k]]], [Bstat[stis[k]]], scale=-0.5)
            for k, (src_rows, n, dstT, Bdst, g_bc, pre) in enumerate(grp):
                s_ = stat[:n, stis[k], :]
                dve(lambda e, k=k, n=n, s_=s_, xbs=xbs, src_=srcs[k][0]: e.tensor_scalar(out=xbs[k][0:n, :], in0=src_[0:n, :], scalar1=s_[:, 3:4], scalar2=None, op0=ALU.mult), [srcs[k][1], Bstat[stis[k]]], [Bxbs[k]])
            if group_hook is not None:
                group_hook(g0 // G)
            for k, (src_rows, n, dstT, Bdst, g_bc, pre) in enumerate(grp):
                transpose_rows(xbs[k], n, Bxbs[k], dstT, Bdst, g_bc)

    def proj_fm(dst, Bdst, wt, Bwt, col0, src, Bsrc, c0, n, nk=8, func=AF.Copy, **kw):
        bk, Bk = nextbank()
        for kc in range(nk):
            mm(bk[:, 0:n], wt[:, kc, col0:col0 + 128], src[:, kc, c0:c0 + n], kc == 0, kc == nk - 1, [Bwt, Bsrc], [Bk])
        act(dst, bk[:, 0:n], func, [Bk], [Bdst], **kw)

    def vblock(g, start, step, n, src, Bsrc):
        idx = len(vidx)
        vidx[(g, start, step, "m" if src is xnT_main_marker[0] else "h")] = idx
        bk, Bk = nextbank()
        for kc in range(8):
            mm(bk[0:n, 0:256], src[:, kc, start:start + step * (n - 1) + 1:step], wqkv[:, kc, wcol(2, g, 0):wcol(2, g, 0) + 256], kc == 0, kc == 7, [Bsrc, Bwv], [Bk])
        tcopy("dve", Vb[0:n, idx, :], bk[0:n, 0:256], [Bk], [BV[idx]])
        return idx

    xnT_main_marker = [None]

    S.label = 'B1'
    QT_f32 = arena[:, P1:P1 + 6144]
    xtA = [QT_f32[:, k * 1024:(k + 1) * 1024] for k in range(4)]
    xbA = [QT_f32[:, 4096 + k * 512:4096 + (k + 1) * 512].bitcast(BF16) for k in range(4)]
    BxtA = [Buf("xtA%d" % k) for k in range(4)]
    BxbA = [Buf("xbA%d" % k) for k in range(4)]
    VBa = PW - NVB * 128
    Va_f32 = arena[:, VBa:VBa + 6144]
    xtA2 = [Va_f32[:, k * 1024:(k + 1) * 1024] for k in range(4)]
    xbA2 = [Va_f32[:, 4096 + k * 512:4096 + (k + 1) * 512].bitcast(BF16) for k in range(4)]
    BxtA2 = [Buf("xtA2%d" % k) for k in range(4)]
    BxbA2 = [Buf("xbA2%d" % k) for k in range(4)]
    norm_T_batch([(xall[t * 128:(t + 1) * 128, :], 128, xnT[:, :, t * 128:(t + 1) * 128], BxnT, gm_bc, None) for t in range(17)],
                 [xtA, xtA2], [BxtA, BxtA2], [xbA, xbA2], [BxbA, BxbA2], "xa")
    if stop == 'B1a':
        return finalize()
    for g in range(3):
        lt = KBASE[g]
        while lt < HX:
            n = min(512, HX - lt)
            for c in range(2):
                proj_fm(KT[g][:, c, lt - KBASE[g]:lt - KBASE[g] + n], BKT[g], wqkv, Bwk, wcol(1, g, c), xnT, BxnT, lt, n)
            lt += n
    if stop == 'B1k':
        return finalize()
    S.barrier()
    for r in range(16):
        vblock(2, 128 + r, 16, 128, xnT, BxnT)
    for r in range(4):
        vblock(1, 1664 + r, 4, 128, xnT, BxnT)
    vblock(0, 2048, 1, 128, xnT, BxnT)
    vblock(2, 126, 16, 128, xnT, BxnT)
    vblock(2, 127, 16, 128, xnT, BxnT)
    vblock(1, 1662, 4, 128, xnT, BxnT)
    vblock(1, 1663, 4, 128, xnT, BxnT)
    vblock(0, 2046, 1, 128, xnT, BxnT)
    vblock(2, 2174, 1, 1, xnT, BxnT)
    vblock(2, 2175, 1, 1, xnT, BxnT)
    vblock(1, 2174, 1, 1, xnT, BxnT)
    vblock(1, 2175, 1, 1, xnT, BxnT)
    vblock(0, 2174, 1, 2, xnT, BxnT)
    tcopy("dve", xeT, xnT[:, :, 2048:2176], [BxnT], [BxeT])

    if stop == 'B1':
        return finalize()
    S.label = 'B2'
    xnT_main_marker[0] = xnT
    S.barrier()
    VB0 = PW - NVB * 128 + 31 * 128
    Vm_f32 = arena[:, VB0:VB0 + 6144]
    xtB = [Vm_f32[:, k * 1024:(k + 1) * 1024] for k in range(4)]
    xbB = [Vm_f32[:, 4096 + k * 512:4096 + (k + 1) * 512].bitcast(BF16) for k in range(4)]
    BxtB = [Buf("xtB%d" % k) for k in range(4)]
    BxbB = [Buf("xbB%d" % k) for k in range(4)]
    norm_T_batch([(xall[HX + t * 128:HX + (t + 1) * 128, :], 128, xnT[:, :, t * 128:(t + 1) * 128], BxnT, gm_bc, None) for t in range(16)],
                 [xtB, xtA], [BxtB, BxtA], [xbB, xbA], [BxbB, BxbA], "xb")
    tcopy("dve", xnT[:, :, SM0:SM0 + 2], xeT[:, :, 126:128], [BxeT], [BxnT])
    norm_T_batch([(xs, NS, xnT[:, :, SM0 + 2:SM0 + 2 + NS], BxnT, gm_bc, None)], xtB, BxtB, xbB, BxbB, "xb")
    slices = [(i * 512, 512) for i in range(4)] + [(SM0, 18)]
    if stop == 'B2a':
        return finalize()
    S.barrier()
    for (c0, n) in slices:
        for gc in range(6):
            proj_fm(QT[:, gc, c0:c0 + n], BQT, wqkv, Bw, wcol(0, gc // 2, gc % 2), xnT, BxnT, c0, n)
    for (c0, n) in slices[:4]:
        for g in range(3):
            for c in range(2):
                kc0 = HX + c0 - KBASE[g]
                proj_fm(KT[g][:, c, kc0:kc0 + n], BKT[g], wqkv, Bwk, wcol(1, g, c), xnT, BxnT, c0, n)
    for gc in range(6):
        proj_fm(KTs[:, gc, :], BKTs, wqkv, Bwk, wcol(1, gc // 2, gc % 2), xnT, BxnT, SM0, 18)
        proj_fm(VTs[:, gc, :], BKTs, wqkv, Bwv, wcol(2, gc // 2, gc % 2), xnT, BxnT, SM0, 18)
    if stop == 'B2q':
        return finalize()
    assert len(vidx) == 31, len(vidx)
    S.barrier()
    for t in range(16):
        vblock(0, t * 128, 1, 128, xnT, BxnT)
    for i in range(4):
        for r in range(4):
            vblock(1, 512 * i + r, 4, 128, xnT, BxnT)
    for r in range(16):
        vblock(2, r, 16, 128, xnT, BxnT)

    if stop == 'B2v':
        return finalize()
    S.label = 'kvtok'
    def kv_tok(col0, n, g, dst_rows):
        i = kvst_i[0] % NKV
        kvst_i[0] += 1
        bk, Bk = nextbank()
        for half in range(2):
            for kc in range(8):
                mm(bk[0:n, half * 256:(half + 1) * 256], xnT[:, kc, col0:col0 + n], wqkv[:, kc, wcol(1 + half, g, 0):wcol(1 + half, g, 0) + 256], kc == 0, kc == 7, [BxnT, Bwk if half == 0 else Bwv], [Bk])
        tcopy("dve", kvst[i][0:n, :], bk[0:n, :], [Bk], [Bkvst[i]])
        dma("sp" if n == 128 else "pool", dst_rows, kvst[i][0:n, :], [Bkvst[i]], [], sem=("ko%d" if n == 128 else "kp%d") % i)
        out_bufs.append(Bkvst[i])

    for t in range(16):
        kv_tok(t * 128, 128, 2, kvo[2][t * 128:(t + 1) * 128, :])
    if stop == 'kv1':
        return finalize()
    for t in range(12, 16):
        kv_tok(t * 128, 128, 1, kvo[1][(t - 12) * 128:(t - 11) * 128, :])
    kv_tok(15 * 128, 128, 0, kvo[0])
    if stop == 'kv2':
        return finalize()
    for g in range(3):
        kv_tok(SM0 + 2, NS, g, kvs[g])

    if stop == 'B':
        return finalize()
    S.barrier()
    A.top = PW
    ACC0 = A.top
    acc_n = A.f32(2 * NCOL).rearrange("p (c n) -> p c n", c=2)
    acc_d = A.f32(2 * NCOL).rearrange("p (c n) -> p c n", c=2)
    acc_all = arena[:, ACC0:ACC0 + 4 * NCOL].rearrange("p (x c n) -> p x c n", x=2, c=2)
    NPT = 2
    PTb = [A.bf(1024).rearrange("p (h k q) -> p h k q", h=4, k=2) for _ in range(NPT)]
    BPT = [Buf("PT%d" % i) for i in range(NPT)]
    pt_i = [0]
    BOUT0 = A.top
    boutT = A.bf(2 * NCOL).rearrange("p (c n) -> p c n", c=2)
    BboutT = Buf("boutT")
    ATT_TOP = A.top
    A.top = BOUT0
    ckb = [[A.bf(512) for _ in range(3)] for _ in range(2)]
    Bckb = [[Buf("ckb%d%d" % (i, g)) for g in range(3)] for i in range(2)]
    KTc = [A.bf(6 * 128).rearrange("p (c n) -> p c n", c=6) for _ in range(2)]
    BKTc = [Buf("KTc%d" % i) for i in range(2)]
    PTs = [A.bf(16) for _ in range(2)]
    BPTs = [Buf("PTs%d" % i) for i in range(2)]
    prodf = A.f32(6 * 16).rearrange("p (c n) -> p c n", c=6)
    pself = A.f32(6 * 16).rearrange("p (c n) -> p c n", c=6)
    Bpr = Buf("prod")
    assert A.top <= NF, A.top
    acc_hist = {0: [], 1: [], 2: [], "x": [Buf("accx")], "s": []}
    mpair_main = cst_b[:, 1:3, :]
    mpair_edge = cst_b[:, 3:5, :]

    def acc_update(pair, Bn, Bd, nq, cols, first, key):
        if key == "x":
            rd_prev, wr = acc_hist["x"], acc_hist["x"]
        else:
            nb = Buf("acc%s" % str(key))
            rd_prev = [] if (key == "s" or key == 0) else acc_hist[key - 1]
            wr = [nb]
            acc_hist[key].append(nb)
        p4 = pair.rearrange("p (x h q) -> p x h q", x=2, h=4)
        for h2 in range(2):
            i_ap = p4[h2 * 64:(h2 + 1) * 64, :, h2::2, 0:nq]
            o_ap = acc_all[h2 * 64:(h2 + 1) * 64, :, :, cols]
            if first:
                dve(lambda e, i_ap=i_ap, o_ap=o_ap: e.tensor_copy(out=o_ap, in_=i_ap), [Bn, Bd] + rd_prev, wr)
            else:
                dve(lambda e, i_ap=i_ap, o_ap=o_ap: e.tensor_tensor(out=o_ap, in0=i_ap, in1=o_ap, op=ALU.add), [Bn, Bd] + rd_prev, wr)

    def band_p1(g, qsrc, Bq, qcols, nq, chunks, mpair):
        pi = pt_i[0] % NPT
        pt_i[0] += 1
        PT, Bp = PTb[pi], BPT[pi]
        b0, B0 = nextbank()
        b1, B1 = nextbank()
        sb = [b0, b1]
        SBf = [B0, B1]
        for h in range(4):
            c, h2 = h // 2, h % 2
            rows = slice(h2 * 64, (h2 + 1) * 64)
            for ci, (Kap, BK, vi, nk, mask) in enumerate(chunks):
                o = sb[h2][0:nk, (c * 2 + ci) * 128:(c * 2 + ci) * 128 + nq]
                mm(o, Kap[rows, c, :], qsrc[rows, g * 2 + c, qcols], True, True, [BK, Bq], [SBf[h2]])
        full = (nq == 128 and len(chunks) == 2 and all(ch[3] == 128 for ch in chunks))
        if full:
            for h2 in range(2):
                src = sb[h2].rearrange("p (h k q) -> p h k q", h=2, k=2)
                act(PT[:, h2::2, :, :], src, AF.Exp, [SBf[h2]], [Bp], scale=0.125)
            mb = mpair.unsqueeze(1).to_broadcast([128, 4, 2, 128])
            dve(lambda e, PT=PT, mb=mb: e.tensor_tensor(out=PT, in0=PT, in1=mb, op=ALU.mult), [Bp, Bc], [Bp])
        else:
            for ci, (Kap, BK, vi, nk, mask) in enumerate(chunks):
                for h2 in range(2):
                    src = sb[h2][0:nk, :].rearrange("p (h k q) -> p h k q", h=2, k=2)[:, :, ci, 0:nq]
                    act(PT[0:nk, h2::2, ci, 0:nq], src, AF.Exp, [SBf[h2]], [Bp], scale=0.125)
                if mask is not None:
                    for h in range(4):
                        dve(lambda e, h=h, ci=ci, nk=nk, mask=mask, PT=PT: e.tensor_tensor(out=PT[0:nk, h, ci, 0:nq], in0=PT[0:nk, h, ci, 0:nq], in1=mask[0:nk, 0:nq], op=ALU.mult), [Bp, Bc], [Bp])
        return PT, Bp

    def band_p2(PT, Bp, nq, chunks, acc_cols, first, key):
        nch = len(chunks)
        bn, Bn, bd, Bd, pair = nextpair()
        for h in range(4):
            c = h // 2
            for ci, (Kap, BK, vi, nk, mask) in enumerate(chunks):
                mm(bn[:, h * 128:h * 128 + nq], Vb[0:nk, vi, c * 128:(c + 1) * 128], PT[0:nk, h, ci, 0:nq], ci == 0, ci == nch - 1, [BV[vi], Bp], [Bn])
            for ci, (Kap, BK, vi, nk, mask) in enumerate(chunks):
                mm(bd[:, h * 128:h * 128 + nq], ones_b[0:nk, :], PT[0:nk, h, ci, 0:nq], ci == 0, ci == nch - 1, [Bc, Bp], [Bd])
        acc_update(pair, Bn, Bd, nq, acc_cols, first, key)

    def kslice(g, lt0, step, n):
        a = lt0 - KBASE[g]
        return KT[g][:, :, a:a + step * (n - 1) + 1:step]

    def samp_s0(b):
        sl = b % 2
        for g in range(3):
            L, d = WIN[g]
            dma("pool", ckb[sl][g], ck[g][b, 0:L:d, :], [], [Bckb[sl][g]], sem="ck%d%d" % (sl, g))

    def samp_s1(b):
        sl = b % 2
        bk, Bk = nextbank()
        pt = bk.bitcast(BF16).rearrange("p (c t) -> p c t", c=8)
        for g in range(3):
            for c in range(2):
                S.op("pe", lambda e, c=c, g=g, pt=pt, sl=sl: e.transpose(out=pt[:, g * 2 + c, :], in_=ckb[sl][g][:, c * 128:(c + 1) * 128], identity=ident_b), reads=[Bckb[sl][g], Bc], writes=[Bk])
        tcopy("dve", KTc[sl], pt[:, 0:6, :], [Bk], [BKTc[sl]])

    def samp_s2(b):
        sl = b % 2
        col = SM0 + 2 + b
        bs0, BS0 = nextbank()
        bs1, BS1 = nextbank()
        bsx = [bs0, bs1]
        BSx = [BS0, BS1]
        for g in range(3):
            for h in range(4):
                c, h2 = h // 2, h % 2
                rows = slice(h2 * 64, (h2 + 1) * 64)
                mm(bsx[h2][:, g * 2 + c:g * 2 + c + 1], KTc[sl][rows, g * 2 + c, :], QT[rows, g * 2 + c, col:col + 1], True, True, [BKTc[sl], BQT], [BSx[h2]])
        PTs3 = PTs[sl][:, 0:12].rearrange("p (g c t) -> p g c t", g=3, c=2)
        for h2 in range(2):
            act(PTs3[:, :, :, h2], bsx[h2][:, 0:6].rearrange("p (g c) -> p g c", g=3), AF.Exp, [BSx[h2]], [BPTs[sl]], scale=0.125)

    def samp_s3(b):
        sl = b % 2
        col = SM0 + 2 + b
        bn, Bn, bd, Bd, pair = nextpair()
        for h in range(4):
            c = h // 2
            for g in range(3):
                mm(bn[:, h * 128:h * 128 + 1], ckb[sl][g][:, 256 + c * 128:256 + (c + 1) * 128], PTs[sl][:, g * 4 + h:g * 4 + h + 1], g == 0, g == 2, [Bckb[sl][g], BPTs[sl]], [Bn])
            for g in range(3):
                mm(bd[:, h * 128:h * 128 + 1], ones_b, PTs[sl][:, g * 4 + h:g * 4 + h + 1], g == 0, g == 2, [Bc, BPTs[sl]], [Bd])
        acc_update(pair, Bn, Bd, 1, slice(col, col + 1), True, "s")

    S.label = 'att-main'
    mblocks = []
    for g in range(3):
        step = WIN[g][1]
        if g == 0:
            starts = [t * 128 for t in range(16)]
        elif g == 1:
            starts = [512 * i + r for i in range(4) for r in range(4)]
        else:
            starts = list(range(16))
        for m0 in starts:
            lt_q = HX + m0
            lt_p = lt_q - 128 * step
            if lt_p < HX:
                vp = vidx[(g, lt_p, step, "h")]
                mp = mpair_edge
            else:
                vp = vidx[(g, lt_p - HX, step, "m")]
                mp = mpair_main
            vc = vidx[(g, m0, step, "m")]
            chunks = [(kslice(g, lt_p, step, 128), BKT[g], vp, 128, None),
                      (kslice(g, lt_q, step, 128), BKT[g], vc, 128, None)]
            qc = slice(m0, m0 + step * 127 + 1, step)
            mblocks.append((g, qc, chunks, mp))
    samp_s0(0)
    pend = band_p1(mblocks[0][0], QT, BQT, mblocks[0][1], 128, mblocks[0][2], mblocks[0][3])
    for m, (g, qc, chunks, mp) in enumerate(mblocks):
        nxt = None
        if m + 1 < len(mblocks):
            g2_, qc2, ch2, mp2 = mblocks[m + 1]
            nxt = band_p1(g2_, QT, BQT, qc2, 128, ch2, mp2)
        band_p2(pend[0], pend[1], 128, chunks, qc, g == 0, g)
        pend = nxt
        b, k = m // 3, m % 3
        if k == 0:
            samp_s1(b)
            if b + 1 < NS:
                samp_s0(b + 1)
        elif k == 1:
            samp_s2(b)
        else:
            samp_s3(b)
    if stop == 'att-main':
        return finalize()
    S.label = 'att-ext2'
    vp = vidx[(0, 2046, 1, "h")]
    vc = vidx[(0, 2174, 1, "h")]
    chx = [(kslice(0, 2046, 1, 128), BKT[0], vp, 128, mprev_b), (kslice(0, 2174, 1, 2), BKT[0], vc, 2, mcur_b)]
    p_ = band_p1(0, QT, BQT, slice(SM0, SM0 + 2), 2, chx, None)
    band_p2(p_[0], p_[1], 2, chx, slice(SM0, SM0 + 2), True, "x")
    for g in (1, 2):
        step = WIN[g][1]
        for j in range(2):
            ltq = 2174 + j
            vp = vidx[(g, ltq - 128 * step, step, "h")]
            vc = vidx[(g, ltq, 1, "h")]
            chx = [(kslice(g, ltq - 128 * step, step, 128), BKT[g], vp, 128, None), (kslice(g, ltq, 1, 1), BKT[g], vc, 1, None)]
            p_ = band_p1(g, QT, BQT, slice(SM0 + j, SM0 + j + 1), 1, chx, None)
            band_p2(p_[0], p_[1], 1, chx, slice(SM0 + j, SM0 + j + 1), False, "x")
    Bacc = Buf("accall")
    S.op("dve", lambda e: e.memset(prodf[:, 0, 0:1], 0.0), reads=[b_ for k_ in acc_hist for b_ in acc_hist[k_]], writes=[Bacc, Bpr])
    S.label = 'att-self'
    dve(lambda e: e.tensor_tensor(out=prodf, in0=QT[:, :, SM0 + 2:SM0 + 18], in1=KTs[:, :, 2:18], op=ALU.mult), [BQT, BKTs], [Bpr])
    bk, Bk = nextbank()
    mm(bk[:, 0:96], blockones_f, prodf.rearrange("p c n -> p (c n)"), True, True, [Bc, Bpr], [Bk])
    act(pself.rearrange("p c n -> p (c n)"), bk[:, 0:96], AF.Exp, [Bk], [Bpr], scale=0.125)
    dve(lambda e: e.tensor_tensor(out=prodf, in0=pself, in1=VTs[:, :, 2:18], op=ALU.mult), [Bpr, BKTs], [Bpr])
    for g in range(3):
        for c in range(2):
            dve(lambda e, g=g, c=c: e.tensor_tensor(out=acc_n[:, c, SM0 + 2:SM0 + 18], in0=acc_n[:, c, SM0 + 2:SM0 + 18], in1=prodf[:, g * 2 + c, :], op=ALU.add), [Bacc, Bpr], [Bacc])
            dve(lambda e, g=g, c=c: e.tensor_tensor(out=acc_d[:, c, SM0 + 2:SM0 + 18], in0=acc_d[:, c, SM0 + 2:SM0 + 18], in1=pself[:, g * 2 + c, :], op=ALU.add), [Bacc, Bpr], [Bacc])
    if stop == 'att-self':
        return finalize()
    S.barrier()
    A.top = P1
    boutT2 = A.bf(2 * NCOL).rearrange("p (c n) -> p c n", c=2)
    assert A.top <= PW
    aoutT = A.bf(4 * NCOL).rearrange("p (c n) -> p c n", c=4)
    BaoutT = Buf("aoutT")
    C_TOP = A.top
    wuv = A.bf(8 * 1024).rearrange("p (c n) -> p c n", c=8)
    Bwuv = Buf("wuv")
    wv_in = w_in.rearrange("(c p) n -> p c n", p=128)
    dma("pool", wuv, wv_in[:, :, 2304:3328], [], [Bwuv], sem="w1")
    S.label = 'att-norm'
    for c in range(2):
        for (c0, n) in slices:
            dve(lambda e, c=c, c0=c0, n=n: e.reciprocal(out=acc_d[:, c, c0:c0 + n], in_=acc_d[:, c, c0:c0 + n]), [Bacc], [Bacc])
            dve(lambda e, c=c, c0=c0, n=n: e.tensor_tensor(out=boutT2[:, c, c0:c0 + n], in0=acc_n[:, c, c0:c0 + n], in1=acc_d[:, c, c0:c0 + n], op=ALU.mult), [Bacc], [BboutT])
    if stop == 'att':
        return finalize()
    S.label = 'C1'
    MT0 = NF - 4 * NCOL
    W2A = MT0 - (8192 + 2048 + 1024)
    wg = arena[:, W2A:W2A + 8192].bitcast(BF16).rearrange("p (c n) -> p c n", c=8)
    wpa = arena[:, W2A + 8192:W2A + 10240].bitcast(BF16).rearrange("p (c n) -> p c n", c=4)
    wpb = arena[:, W2A + 10240:W2A + 11264].bitcast(BF16).rearrange("p (c n) -> p c n", c=2)
    Bwg = Buf("wg")
    dma("pool", wg, wv_in[:, :, 3328:5376], [], [Bwg, Bacc], sem="w2")
    dma("pool", wpa, w_pa.rearrange("(c p) n -> p c n", p=128), [], [Bwg, Bacc], sem="w3")
    dma("pool", wpb, w_pb.rearrange("(c p) n -> p c n", p=128), [], [Bwg, Bacc], sem="w4")
    uT = A.bf(4 * NCOL).rearrange("p (c n) -> p c n", c=4)
    BuT = Buf("uT")
    NG = 3
    gv = [A.f32(512) for _ in range(NG)]
    Bgv = [Buf("gv%d" % i) for i in range(NG)]
    vn = [A.f32(512) for _ in range(NG)]
    Bvn = [Buf("vn%d" % i) for i in range(NG)]
    vnb = [A.bf(512) for _ in range(NG)]
    Bvnb = [Buf("vnb%d" % i) for i in range(NG)]
    uxe = A.bf(4 * 128).rearrange("p (c n) -> p c n", c=4)
    aoe = A.bf(4 * 128).rearrange("p (c n) -> p c n", c=4)
    Buxe = Buf("uxe")
    C1_TOP = A.top
    for (c0, n) in slices:
        for c in range(4):
            proj_fm(uT[:, c, c0:c0 + n], BuT, wuv, Bwuv, c * 128, xnT, BxnT, c0, n, func=AF.Gelu_apprx_tanh)
    for c in range(4):
        proj_fm(uxe[:, c, :], Buxe, wuv, Bwuv, c * 128, xeT, BxeT, 0, 128, func=AF.Gelu_apprx_tanh)
    gi = [0]

    def gmlp_batch(items):
        G = len(gv)

        def stage1(grp, par):
            bks_ = []
            for k, (src, Bsrc, c0, n, sample, u_ap, Bu, out_ap, Bout, vn_dst) in enumerate(grp):
                bk, Bk = pbank[3 * par + k], PB[3 * par + k]
                bks_.append((bk, Bk))
                for kc in range(8):
                    mm(bk[0:n, :], src[:, kc, c0:c0 + n], wuv[:, kc, 512:1024], kc == 0, kc == 7, [Bsrc, Bwuv], [Bk])
            return bks_

        groups = [items[g0:g0 + G] for g0 in range(0, len(items), G)]
        banks_next = stage1(groups[0], 0)
        for gi_, grp in enumerate(groups):
            banks = banks_next
            if gi_ + 1 < len(groups):
                banks_next = stage1(groups[gi_ + 1], (gi_ + 1) % 2)
            stis = []
            for k, (src, Bsrc, c0, n, sample, u_ap, Bu, out_ap, Bout, vn_dst) in enumerate(grp):
                sti = stat_i[0] % 8
                stat_i[0] += 1
                stis.append(sti)
            for k, (src, Bsrc, c0, n, sample, u_ap, Bu, out_ap, Bout, vn_dst) in enumerate(grp):
                s_ = stat[:n, stis[k], :]
                act(gv[k][0:n, :], banks[k][0][0:n, :], AF.Gelu_apprx_tanh, [banks[k][1]], [Bgv[k], Bstat[stis[k]]], accum_out=s_[:, 4:5])
            for k, (src, Bsrc, c0, n, sample, u_ap, Bu, out_ap, Bout, vn_dst) in enumerate(grp):
                s_ = stat[:n, stis[k], :]
                dve(lambda e, s_=s_: e.tensor_scalar(out=s_[:, 5:6], in0=s_[:, 4:5], scalar1=-1.0 / 512, scalar2=None, op0=ALU.mult), [Bstat[stis[k]]], [Bstat[stis[k]]])
            for k, (src, Bsrc, c0, n, sample, u_ap, Bu, out_ap, Bout, vn_dst) in enumerate(grp):
                s_ = stat[:n, stis[k], :]
                act(vn[k][0:n, :], gv[k][0:n, :], AF.Identity, [Bgv[k], Bstat[stis[k]]], [Bvn[k]], bias=s_[:, 5:6], scale=1.0)
            for k, (src, Bsrc, c0, n, sample, u_ap, Bu, out_ap, Bout, vn_dst) in enumerate(grp):
                s_ = stat[:n, stis[k], :]
                act(gv[k][0:n, :], vn[k][0:n, :], AF.Square, [Bvn[k]], [Bgv[k], Bstat[stis[k]]], accum_out=s_[:, 0:1])
            for k, (src, Bsrc, c0, n, sample, u_ap, Bu, out_ap, Bout, vn_dst) in enumerate(grp):
                s_ = stat[:n, stis[k], :]
                dve(lambda e, s_=s_: e.tensor_scalar(out=s_[:, 1:2], in0=s_[:, 0:1], scalar1=1.0 / 512, scalar2=EPS, op0=ALU.mult, op1=ALU.add), [Bstat[stis[k]]], [Bstat[stis[k]]])
            for k, (src, Bsrc, c0, n, sample, u_ap, Bu, out_ap, Bout, vn_dst) in enumerate(grp):
                s_ = stat[:n, stis[k], :]
                act(s_[:, 2:3], s_[:, 1:2], AF.Ln, [Bstat[stis[k]]], [Bstat[stis[k]]])
            for k, (src, Bsrc, c0, n, sample, u_ap, Bu, out_ap, Bout, vn_dst) in enumerate(grp):
                s_ = stat[:n, stis[k], :]
                act(s_[:, 3:4], s_[:, 2:3], AF.Exp, [Bstat[stis[k]]], [Bstat[stis[k]]], scale=-0.5)
            for k, (src, Bsrc, c0, n, sample, u_ap, Bu, out_ap, Bout, vn_dst) in enumerate(grp):
                s_ = stat[:n, stis[k], :]
                dve(lambda e, k=k, n=n, s_=s_: e.scalar_tensor_tensor(out=vn[k][0:n, :], in0=vn[k][0:n, :], scalar=s_[:, 3:4], in1=lng_bc[0:n, :], op0=ALU.mult, op1=ALU.mult), [Bvn[k], Bstat[stis[k]], Bc], [Bvn[k]])
            for k, (src, Bsrc, c0, n, sample, u_ap, Bu, out_ap, Bout, vn_dst) in enumerate(grp):
                dve(lambda e, k=k, n=n: e.tensor_tensor(out=vn[k][0:n, :], in0=vn[k][0:n, :], in1=lnb_bc[0:n, :], op=ALU.add), [Bvn[k], Bc], [Bvn[k]])
            for k, (src, Bsrc, c0, n, sample, u_ap, Bu, out_ap, Bout, vn_dst) in enumerate(grp):
                tcopy("dve", vnb[k][0:n, :], vn[k][0:n, :], [Bvn[k]], [Bvnb[k]])
                if vn_dst is not None:
                    dma("sp" if n == 128 else "pool", vn_dst, vn[k][0:n, :], [Bvn[k]], [], sem=("vo%d" if n == 128 else "vp%d") % k)
                    out_bufs.append(Bvn[k])
            mbanks = []
            for k, (src, Bsrc, c0, n, sample, u_ap, Bu, out_ap, Bout, vn_dst) in enumerate(grp):
                mb_i = (6, 7, 3 * (gi_ % 2))[k]
                bm, Bm = pbank[mb_i], PB[mb_i]
                mbanks.append((bm, Bm))
                nt = 16 if sample else 128
                for g in range(4):
                    o = bm[:, g * 128:g * 128 + nt]
                    if sample:
                        mm(o, vnb[k][0:n, g * 128:(g + 1) * 128], D16[0:16, g, :], True, False, [Bvnb[k], Bc], [Bm])
                        mm(o, ones_b[0:1, :], bsp0_b[0:1, g * 16:(g + 1) * 16], False, True, [Bc], [Bm])
                    else:
                        mm(o, vnb[k][0:n, g * 128:(g + 1) * 128], WmT[:, g, :], True, False, [Bvnb[k], Bc], [Bm])
                        mm(o, ones_b[0:1, :], bsp_b[0:1, g * 128:(g + 1) * 128], False, True, [Bc], [Bm])
            for k, (src, Bsrc, c0, n, sample, u_ap, Bu, out_ap, Bout, vn_dst) in enumerate(grp):
                nt = 16 if sample else 128
                m4 = mbanks[k][0].rearrange("p (g t) -> p g t", g=4)[:, :, 0:nt]
                dve(lambda e, m4=m4, out_ap=out_ap, u_ap=u_ap: e.tensor_tensor(out=out_ap, in0=m4, in1=u_ap, op=ALU.mult), [mbanks[k][1], Bu], [Bout])

    gitems = []
    for t in range(16):
        cs = slice(t * 128, (t + 1) * 128)
        gitems.append((xnT, BxnT, t * 128, 128, False, uT[:, :, cs], BuT, aoutT[:, :, cs], BaoutT, vch if t == 15 else None))
    gitems.append((xeT, BxeT, 0, 128, False, uxe, Buxe, aoe, Buxe, None))
    gitems.append((xnT, BxnT, SM0 + 2, NS, True, uT[:, :, SM0 + 2:SM0 + 18], BuT, aoutT[:, :, SM0 + 2:SM0 + 18], BaoutT, vchs))
    gmlp_batch(gitems)
    tcopy("dve", aoutT[:, :, SM0:SM0 + 2], aoe[:, :, 126:128], [Buxe], [BaoutT])
    if stop == 'C1':
        return finalize()
    S.label = 'C2a'
    S.barrier()
    A.top = C_TOP
    assert C1_TOP <= W2A, (C1_TOP, W2A)
    tg = [A.f32(512) for _ in range(2)]
    Btg = [Buf("tg0"), Buf("tg1")]
    t1 = [A.f32(512) for _ in range(2)]
    Bt1 = [Buf("t10"), Buf("t11")]
    mt = arena[:, MT0:NF].bitcast(BF16).rearrange("p (c n) -> p c n", c=8)
    Bmt = Buf("mT")
    oc_i = [0]
    for si, (c0, n) in enumerate(slices):
        for oc in range(8):
            i = oc_i[0] % 2
            oc_i[0] += 1
            ba, Ba = nextbank()
            for kc in range(4):
                mm(ba[:, 0:n], wpa[:, kc, oc * 128:(oc + 1) * 128], aoutT[:, kc, c0:c0 + n], kc == 0, kc == 3, [Bwg, BaoutT], [Ba])
            bb, Bb = nextbank()
            for kc in range(2):
                mm(bb[:, 0:n], wpb[:, kc, oc * 128:(oc + 1) * 128], boutT2[:, kc, c0:c0 + n], kc == 0, kc == 1, [Bwg, BboutT], [Bb])
            proj_fm(tg[i][:, 0:n], Btg[i], wg, Bwg, oc * 128, xnT, BxnT, c0, n, func=AF.Tanh, scale=0.5)
            dve(lambda e, i=i, ba=ba, n=n: e.scalar_tensor_tensor(out=t1[i][:, 0:n], in0=tg[i][:, 0:n], scalar=1.0, in1=ba[:, 0:n], op0=ALU.add, op1=ALU.mult), [Btg[i], Ba], [Bt1[i]])
            proj_fm(tg[i][:, 0:n], Btg[i], wg, Bwg, 1024 + oc * 128, xnT, BxnT, c0, n, func=AF.Tanh, scale=0.5)
            dve(lambda e, i=i, bb=bb, n=n: e.scalar_tensor_tensor(out=tg[i][:, 0:n], in0=tg[i][:, 0:n], scalar=1.0, in1=bb[:, 0:n], op0=ALU.add, op1=ALU.mult), [Btg[i], Bb], [Btg[i]])
            dve(lambda e, i=i, n=n, oc=oc, c0=c0: e.tensor_tensor(out=mt[:, oc, c0:c0 + n], in0=t1[i][:, 0:n], in1=tg[i][:, 0:n], op=ALU.add), [Bt1[i], Btg[i]], [Bmt])

    if stop == 'C2a':
        return finalize()
    S.label = 'C2b'
    S.barrier()
    A.top = CONST_TOP
    hnT = A.bf(8 * NCOL).rearrange("p (c n) -> p c n", c=8)
    BhnT = Buf("hnT")
    hbuf = A.f32(16 * 1024).rearrange("p (t n) -> p t n", t=16)
    hsm = A.f32(1024)
    Bh = [Buf("h%d" % t) for t in range(16)]
    Bhsm = Buf("hsm")
    HB_TOP = A.top
    wo = A.bf(8 * 1024).rearrange("p (c n) -> p c n", c=8)
    Bwo = Buf("wo")
    dma("pool", wo, w_out.rearrange("(c p) n -> p c n", p=128), [], [Bwo], sem="w5")
    NX = 3
    xt = [A.f32(1024) for _ in range(NX)]
    Bxt = [Buf("xt%db" % i) for i in range(NX)]
    xb = [A.bf(1024) for _ in range(NX)]
    Bxb = [Buf("xb%db" % i) for i in range(NX)]
    assert A.top <= MT0, (A.top, MT0)

    def mk_pre(t, c0o, m):
        def pre(k):
            if t >= 0:
                hdst, Bhd = hbuf[:, t, :], Bh[t]
            else:
                hdst, Bhd = hsm, Bhsm
            for half in range(2):
                bk, Bk = nextbank()
                for kc in range(8):
                    mm(bk[0:m, :], mt[:, kc, c0o:c0o + m], wo[:, kc, half * 512:(half + 1) * 512], kc == 0, kc == 7, [Bmt, Bwo], [Bk])
                dve(lambda e, bk=bk, half=half, k=k, hdst=hdst: e.scalar_tensor_tensor(out=hdst[0:m, half * 512:(half + 1) * 512], in0=bk[0:m, :], scalar=0.5, in1=xt[k][0:m, half * 512:(half + 1) * 512], op0=ALU.mult, op1=ALU.add), [Bk, Bxt[k]], [Bhd])
            return (hdst, Bhd)
        return pre

    HS0 = 42404
    assert A.top <= HS0 and HS0 + 1408 + 1024 <= MT0, (A.top, MT0)
    hsT = arena[:, HS0:HS0 + 1408].rearrange("p (c k n) -> p c k n", c=44, k=2)
    BhsT = Buf("hsT")
    scst2 = [arena[:, HS0 + 1408 + i * 512:HS0 + 1408 + (i + 1) * 512].rearrange("p (k n) -> p k n", k=2) for i in range(2)]
    Bscst2 = [Buf("scst0"), Buf("scst1")]
    def sconv_piece(q):
        sc_, Bsc_ = scst2[q % 2], Bscst2[q % 2]
        dma("sp", sc_[0:16, :, :], sconv[:, :, q * 256:(q + 1) * 256], [], [Bsc_], sem="sc%d" % (q % 2))
        for cc in range(2):
            ch = q * 2 + cc
            bk, Bk = nextbank()
            for k in range(2):
                mm(bk[:, k * 16:(k + 1) * 16], sc_[0:16, k, cc * 128:(cc + 1) * 128], ident_f[0:16, 0:16], True, True, [Bsc_, Bc], [Bk])
            tcopy("dve", hsT[:, ch, :, :], bk[:, 0:32].rearrange("p (k n) -> p k n", k=2), [Bk], [BhsT])
        dma("pool", convs[:, 0, q * 256:(q + 1) * 256], sc_[0:16, 1, :], [Bsc_], [], sem="sco%d" % (q % 2))
        out_bufs.append(Bsc_)


    sc_done = [0]

    def sc_hook(gi_):
        for _ in range(4):
            if sc_done[0] < 22:
                sconv_piece(sc_done[0])
                sc_done[0] += 1

    citems = [(xall[HX + t * 128:HX + (t + 1) * 128, :], 128, hnT[:, :, t * 128:(t + 1) * 128], BhnT, gf_bc, mk_pre(t, t * 128, 128)) for t in range(16)]
    norm_T_batch(citems, xt, Bxt, xb, Bxb, "xc", group_hook=sc_hook)
    dma("sp", xt[0][0:2, :], xall[HX - 2:HX, :], [], [Bxt[0]], sem="xc0")
    dma("sp", xt[0][2:18, :], xs, [], [Bxt[0]], sem="xq0")
    norm_T_batch([(None, 18, hnT[:, :, SM0:SM0 + 18], BhnT, gf_bc, mk_pre(-1, SM0, 18))], xt, Bxt, xb, Bxb, "xc")
    if stop == 'C2b':
        return finalize()
    for q in range(sc_done[0], 22):
        sconv_piece(q)

    S.label = 'D'
    S.barrier()
    A.top = HB_TOP
    HT = 1024
    KG = [(0, 4), (4, 4), (8, 4), (12, 4), (16, 3), (19, 3)]
    cbuf = [[A.f32(1024) for _ in range(2)] for _ in range(3)]
    Bcb = [[Buf("c%d%d" % (a_, b_)) for b_ in range(2)] for a_ in range(3)]
    upb = [[A.f32(2 + HT) for _ in range(2)] for _ in range(2)]
    Bupb = [[Buf("up%d%d" % (a_, b_)) for b_ in range(2)] for a_ in range(2)]
    prodS = A.bf(22 * 18).rearrange("p (j n) -> p j n", j=22)
    BprodS = Buf("prodS")
    ups = [[A.f32(18) for _ in range(2)] for _ in range(3)]
    cs_ = [[A.f32(18) for _ in range(2)] for _ in range(3)]
    Bcs = [[Buf("cs%d%d" % (a_, b_)) for b_ in range(2)] for a_ in range(3)]
    hist = A.f32(44 * 2).rearrange("p (c n) -> p c n", c=44)
    Bhist = Buf("hist")
    upst = [A.f32(256) for _ in range(2)]
    Bupst = [Buf("upst0"), Buf("upst1")]
    assert A.top <= HS0, A.top
    A.top = HS0 + 1408
    prodT = A.bf(4 * HT).rearrange("p (j n) -> p j n", j=4)
    BprodT = [Buf("prodT%d" % i) for i in range(6)]
    NWU = 3
    wup = [A.bf(8 * 256).rearrange("p (c n) -> p c n", c=8) for _ in range(NWU)]
    Bwup = [Buf("wup%d" % i) for i in range(NWU)]
    wdns = [A.bf(4 * 1024).rearrange("p (j n) -> p j n", j=4) for _ in range(2)]
    Bwdns = [Buf("wdn0"), Buf("wdn1")]
    wdi = [0]
    wdq = []

    def load_wdn(j0, gs):
        i = wdi[0] % 2
        wdi[0] += 1
        dma("pool", wdns[i][:, 0:gs, :], w_dn.rearrange("(j p) n -> p j n", p=128)[:, j0:j0 + gs, :], [], [Bwdns[i]], sem="wd%d" % i)
        wdq.append((wdns[i], Bwdns[i]))
    assert A.top <= NF, A.top

    wv = w_up.rearrange("(c p) n -> p c n", p=128)
    wslot = {}
    wi = [0]

    def load_wup(j):
        wsl = wi[0] % NWU
        wi[0] += 1
        w_, Bw_ = wup[wsl], Bwup[wsl]
        dma("pool", w_, wv[:, :, j * 256:(j + 1) * 256], [], [Bw_], sem="wu%d" % wsl)
        return w_, Bw_

    def stageA(H, j):
        base = H * HT
        w_, Bw_ = wslot[(H, j)]
        sl = j % 2
        s3 = j % 3
        for gv_ in range(2):
            ch = gv_ * 22 + j
            ub, Bub = upb[sl][gv_], Bupb[sl][gv_]
            if H == 0:
                if gv_ == 0:
                    bks_, Bks_ = nextbank()
                so = gv_ * 32
                for kc in range(8):
                    mm(bks_[:, so:so + 18], w_[:, kc, gv_ * 128:(gv_ + 1) * 128], hnT[:, kc, SM0:SM0 + 18], kc == 0, kc == 7, [Bw_, BhnT], [Bks_])
                if gv_ == 1:
                    for kc in range(8):
                        mm(bks_[0:20, 64:320], hnT[:, kc, NM - 2:NM + 18], w_[:, kc, :], kc == 0, kc == 7, [BhnT, Bw_], [Bks_])
                    for g2_ in range(2):
                        ub2, Bub2 = upb[sl][g2_], Bupb[sl][g2_]
                        ch2 = g2_ * 22 + j
                        so2 = g2_ * 32
                        act(ub2[:, 0:2], bks_[:, so2:so2 + 2], AF.Copy, [Bks_, Bc], [Bub2], scale=flag[:, 0:1])
                        act(ups[s3][g2_], bks_[:, so2:so2 + 18], AF.Copy, [Bks_], [Bcs[s3][g2_]])
                        cs = cs_[s3][g2_]
                        act(cs, ups[s3][g2_], AF.Identity, [Bcs[s3][g2_], Bc], [Bcs[s3][g2_]], scale=cwT[:, 2, ch2:ch2 + 1], bias=cwT[:, 3, ch2:ch2 + 1])
                        for k in range(2):
                            dve(lambda e, cs=cs, ch2=ch2, k=k: e.scalar_tensor_tensor(out=cs[:, 2:18], in0=hsT[:, ch2, k, :], scalar=cwT[:, k, ch2:ch2 + 1], in1=cs[:, 2:18], op0=ALU.mult, op1=ALU.add), [BhsT, Bc, Bcs[s3][g2_]], [Bcs[s3][g2_]])
                    ui = j % 2
                    tcopy("dve", upst[ui][0:20, :], bks_[0:20, 64:320], [Bks_], [Bupst[ui]])
                    for g2_ in range(2):
                        cc0 = g2_ * 2816 + j * 128
                        dma("pool", convp[:, cc0:cc0 + 128], upst[ui][0:2, g2_ * 128:(g2_ + 1) * 128], [Bupst[ui]], [], sem="uo%d" % ui)
                        dma("pool", convs[:, 1, cc0:cc0 + 128], upst[ui][4:20, g2_ * 128:(g2_ + 1) * 128], [Bupst[ui]], [], sem="uo%d" % ui)
                    out_bufs.append(Bupst[ui])
            else:
                tcopy("dve", ub[:, 0:2], hist[:, ch, :], [Bhist], [Bub])
            for s2 in range(2):
                c0 = base + s2 * 512
                bk, Bk = nextbank()
                for kc in range(8):
                    mm(bk, w_[:, kc, gv_ * 128:(gv_ + 1) * 128], hnT[:, kc, c0:c0 + 512], kc == 0, kc == 7, [Bw_, BhnT], [Bk])
                act(ub[:, 2 + s2 * 512:2 + (s2 + 1) * 512], bk, AF.Copy, [Bk], [Bub])
            if H == 0:
                tcopy("dve", hist[:, ch, :], ub[:, HT:HT + 2], [Bub], [Bhist])
        for gv_ in range(2):
            ch = gv_ * 22 + j
            ub, Bub = upb[sl][gv_], Bupb[sl][gv_]
            cc, Bcc = cbuf[s3][gv_], Bcb[s3][gv_]
            act(cc, ub[:, 2:2 + HT], AF.Identity, [Bub, Bc], [Bcc], scale=cwT[:, 2, ch:ch + 1], bias=cwT[:, 3, ch:ch + 1])
        for gv_ in range(2):
            ch = gv_ * 22 + j
            ub, Bub = upb[sl][gv_], Bupb[sl][gv_]
            cc, Bcc = cbuf[s3][gv_], Bcb[s3][gv_]
            for k in range(2):
                dve(lambda e, cc=cc, ub=ub, k=k, ch=ch: e.scalar_tensor_tensor(out=cc, in0=ub[:, k:k + HT], scalar=cwT[:, k, ch:ch + 1], in1=cc, op0=ALU.mult, op1=ALU.add), [Bub, Bc, Bcc], [Bcc])

    def stageB(H, j, jj):
        s3 = j % 3
        cg, cv = cbuf[s3][0], cbuf[s3][1]
        act(cg, cg, AF.Gelu_apprx_tanh, [Bcb[s3][0]], [Bcb[s3][0]])
        dve(lambda e, cg=cg, cv=cv, jj=jj: e.tensor_tensor(out=prodT[:, jj, :], in0=cg, in1=cv, op=ALU.mult), [Bcb[s3][0], Bcb[s3][1]], [BprodT[jj]])
        if H == 0:
            act(cs_[s3][0], cs_[s3][0], AF.Gelu_apprx_tanh, [Bcs[s3][0]], [Bcs[s3][0]])
            dve(lambda e, s3=s3, j=j: e.tensor_tensor(out=prodS[:, j, :], in0=cs_[s3][0], in1=cs_[s3][1], op=ALU.mult), [Bcs[s3][0], Bcs[s3][1]], [BprodS])

    def wdown_group(H, j0, gs):
        wdn, Bwdn = wdq.pop(0)
        for tb in range(4):
            ths = [(tb * 2 + q_, half) for q_ in range(2) for half in range(2)]
            bks = [nextbank() for _ in ths]
            for (tl, half), (bk, Bk) in zip(ths, bks):
                for jj in range(gs - 1):
                    mm(bk, prodT[:, jj, tl * 128:(tl + 1) * 128], wdn[:, jj, half * 512:(half + 1) * 512], jj == 0, False, [BprodT[jj], Bwdn], [Bk])
            for (tl, half), (bk, Bk) in zip(ths, bks):
                jj = gs - 1
                mm(bk, prodT[:, jj, tl * 128:(tl + 1) * 128], wdn[:, jj, half * 512:(half + 1) * 512], False, True, [BprodT[jj], Bwdn], [Bk])
            for (tl, half), (bk, Bk) in zip(ths, bks):
                t = H * 8 + tl
                dve(lambda e, bk=bk, t=t, half=half: e.tensor_tensor(out=hbuf[:, t, half * 512:(half + 1) * 512], in0=bk, in1=hbuf[:, t, half * 512:(half + 1) * 512], op=ALU.add), [Bk, Bh[t]], [Bh[t]])
        if H == 0:
            for half in range(2):
                bk, Bk = nextbank()
                for jj in range(gs):
                    mm(bk[0:18, :], prodS[:, j0 + jj, :], wdn[:, jj, half * 512:(half + 1) * 512], jj == 0, jj == gs - 1, [BprodS, Bwdn], [Bk])
                dve(lambda e, bk=bk, half=half: e.tensor_tensor(out=hsm[0:18, half * 512:(half + 1) * 512], in0=bk[0:18, :], in1=hsm[0:18, half * 512:(half + 1) * 512], op=ALU.add), [Bk, Bhsm], [Bhsm])

    def final_rows(haps, dsts):
        stis = []
        for k, (hap, m, Bh_) in enumerate(haps):
            sti = stat_i[0] % 8
            stat_i[0] += 1
            stis.append(sti)
            scr = cbuf[k % 3][(k // 3) % 2]
            Bscr = Bcb[k % 3][(k // 3) % 2]
            act(scr.bitcast(BF16)[0:m, 0:1024], hap, AF.Square, [Bh_], [Bscr, Bstat[sti]], accum_out=stat[:m, sti, 0:1])
        rs = []
        for k, (hap, m, Bh_) in enumerate(haps):
            rs.append(rstd_of(stat[:m, stis[k], 0:1], m, 1.0 / 1024, stis[k], Bstat[stis[k]]))
        for k, (hap, m, Bh_) in enumerate(haps):
            dve(lambda e, hap=hap, r=rs[k], m=m: e.scalar_tensor_tensor(out=hap, in0=hap, scalar=r, in1=gfin_bc[0:m, :], op0=ALU.mult, op1=ALU.mult), [Bh_, Bstat[stis[k]], Bc], [Bh_])
        for k, (hap, m, Bh_) in enumerate(haps):
            dst, r0 = dsts[k]
            dma("sp" if m == 128 else "pool", dst, hap[r0:m, :], [Bh_], [], sem=("yo%d" if m == 128 else "yp%d") % (k % 4))
            out_bufs.append(Bh_)

    for H in range(2):
        order = [(j0, gs, jj) for (j0, gs) in KG for jj in range(gs)]
        PRE = 3
        jof = lambda q_: order[q_][0] + order[q_][2]
        for q_ in range(min(PRE, len(order))):
            wslot[(H, jof(q_))] = load_wup(jof(q_))
        doneA = set()

        def doA(q_):
            if q_ < len(order) and q_ not in doneA:
                doneA.add(q_)
                stageA(H, jof(q_))

        load_wdn(*KG[0])
        gnext = [1]
        doA(0)
        doA(1)
        for idx, (j0, gs, jj) in enumerate(order):
            j = j0 + jj
            if idx + PRE < len(order):
                wslot[(H, jof(idx + PRE))] = load_wup(jof(idx + PRE))
            if jj == gs - 1:
                stageB(H, j, jj)
                doA(idx + 2)
                doA(idx + 3)
                if gnext[0] < len(KG):
                    load_wdn(*KG[gnext[0]])
                    gnext[0] += 1
                wdown_group(H, j0, gs)
            else:
                doA(idx + 2)
                stageB(H, j, jj)
        final_rows([(hbuf[:, H * 8 + tl, :], 128, Bh[H * 8 + tl]) for tl in range(8)],
                   [(y[(H * 8 + tl) * 128:(H * 8 + tl + 1) * 128, :], 0) for tl in range(8)])
        if H == 0:
            final_rows([(hsm[0:18, :], 18, Bhsm)], [(ys, 2)])

    return finalize()


_NC_CACHE = {}


def make_in_maps(x_prompt, x_sample, cache_kv_w128, cache_kv_w512, cache_kv_w2048, state_conv_ffn,
                 norm_mix, w_in, ln_v_gain, ln_v_bias, w_spatial, b_spatial, w_proj_a, w_proj_b,
                 w_out, norm_ffn, w_up, conv_w, conv_b, w_down, norm_final, cores=None):
    f = lambda a: np.ascontiguousarray(np.asarray(a, dtype=np.float32))
    x_prompt = f(x_prompt)
    B, SEQ, D = x_prompt.shape
    xp = np.zeros((B, HX + SEQ, D), np.float32)
    xp[:, HX:] = x_prompt
    k = np.arange(128)[:, None]
    q = np.arange(128)[None, :]
    mcur = (k <= q).astype(np.float32)
    mprev = (k >= q).astype(np.float32)
    blockones = ((k // 64) == (q // 64)).astype(np.float32)
    caches = [f(cache_kv_w128)[0], f(cache_kv_w512)[0], f(cache_kv_w2048)[0]]
    common = {
        "w_in": f(w_in)[0], "w_pa": f(w_proj_a)[0], "w_pb": f(w_proj_b)[0], "w_out": f(w_out)[0],
        "w_up": np.ascontiguousarray(f(w_up)[0].reshape(1024, 2, 22, 128).transpose(0, 2, 1, 3).reshape(1024, 5632)),
        "w_dn": f(w_down)[0],
        "vec8": np.concatenate([f(norm_mix)[0].reshape(8, 128), f(norm_ffn)[0].reshape(8, 128)], 0),
        "nfin": f(norm_final), "lng": f(ln_v_gain)[0], "lnb": f(ln_v_bias)[0],
        "cwb": np.concatenate([f(conv_w)[0], f(conv_b)], 0).reshape(4, 44, 128),
        "wsp": f(w_spatial)[0], "bsp": f(b_spatial)[0],
        "w00": np.ascontiguousarray(f(w_spatial)[0][:, 0, 0]),
    }
    in_maps = []
    for c in (range(NCORES) if cores is None else cores):
        b, qi = c // 4, c % 4
        fl = 0.0 if qi == 0 else 1.0
        m = dict(common)
        m["xall"] = np.ascontiguousarray(xp[b, qi * NM:qi * NM + HX + NM])
        m["xs"] = f(x_sample)[c * NS:(c + 1) * NS, 0]
        for g in range(3):
            cg = caches[g][c * NS:(c + 1) * NS]
            m["ck%d" % g] = np.ascontiguousarray(cg.reshape(NS, cg.shape[1], 512))
        m["sconv"] = f(state_conv_ffn)[0, c * NS:(c + 1) * NS]
        m["cst"] = np.stack([np.eye(128, dtype=np.float32), mprev, mcur, mprev * fl, mcur, blockones, np.ones((128, 128), np.float32)])
        m["flagd"] = np.full((128, 1), fl, np.float32)
        in_maps.append(m)
    return in_maps


def assemble(R, B=2):
    y_prompt = np.stack([np.concatenate([R[b * 4 + qi]["y"] for qi in range(4)], 0) for b in range(B)])
    y_sample = np.concatenate([R[c]["ys"] for c in range(NCORES)], 0)[:, None, :]
    outs = [y_prompt, y_sample]
    for g in range(3):
        L = WIN[g][0]
        kvp = np.stack([R[b * 4 + 3]["kv%d" % g] for b in range(B)]).reshape(1, B, L, 2, 4, 64)
        kvsm = np.concatenate([R[c]["kvs%d" % g] for c in range(NCORES)], 0).reshape(1, NCORES * NS, 1, 2, 4, 64)
        outs += [kvp, kvsm]
    vcp = np.stack([R[b * 4 + 3]["vch"] for b in range(B)])[None]
    vcs = np.concatenate([R[c]["vchs"] for c in range(NCORES)], 0)[None, :, None, :]
    cp = np.stack([R[b * 4 + 3]["convp"] for b in range(B)])[None]
    cs = np.concatenate([R[c]["convs"] for c in range(NCORES)], 0)[None]
    outs += [vcp, vcs, cp, cs]
    return tuple(np.ascontiguousarray(np.asarray(o, dtype=np.float32)) for o in outs)


def kernel(**inputs):
    in_maps = make_in_maps(**inputs)
    if "nc" not in _NC_CACHE:
        _NC_CACHE["nc"] = build_program()
    res = run_bass_kernel_spmd(_NC_CACHE["nc"], in_maps, core_ids=list(range(NCORES)))
    return assemble(res.results)
```
